# Optimizing a Trainium2 kernel written in Bass

```python
import jax, jax.numpy as jnp
from jax import lax
import numpy as np

D_MODEL = 2048
BATCH = 1
SEQ = 8192
DEPTH = 4

CTX_LEN = 256
GRID_W = 64
D_MIX = D_MODEL
D_FOURIER = D_MIX // 4
N_FGROUPS = 4
FG_DIM = D_FOURIER // N_FGROUPS
D_SSD = D_MIX - D_FOURIER
SSD_HEAD_DIM = 64
SSD_HEADS = D_SSD // SSD_HEAD_DIM
SSD_GROUPS = 4
HEADS_PER_GROUP = SSD_HEADS // SSD_GROUPS
D_STATE = 128
D_CONV = 5
CONV_DIM = D_SSD + 2 * SSD_GROUPS * D_STATE
CHUNK = 128
D_IN_PROJ = D_FOURIER + D_SSD + CONV_DIM + 2 * SSD_HEADS
D_FF = 4 * D_MODEL
EPS = 1e-6

kernel_name = 'hybrid_fnet_ssd_dit_trunk'


def rms_norm(x, g):
    xf = x.astype(jnp.float32)
    y = xf * lax.rsqrt(jnp.mean(xf * xf, axis=-1, keepdims=True) + EPS)
    return (y * g.astype(jnp.float32)).astype(x.dtype)


def modulate(h, shift, scale):
    return h * (1 + scale) + shift


def depthwise_conv(u, w, b):
    y = lax.conv_general_dilated(u, w[:, None, :].astype(u.dtype), window_strides=(1,), padding='SAME',
                                 dimension_numbers=('NWC', 'WIO', 'NWC'), feature_group_count=u.shape[-1])
    return y + b.astype(u.dtype)


def fourier_mix(u, w_f):
    bsz, L, _ = u.shape
    ug = u.astype(jnp.float32).reshape(bsz, L, N_FGROUPS, FG_DIM)
    f = jnp.fft.fft2(ug, axes=(1, 3), norm='ortho').real
    y = jnp.einsum('blgc,gcd->blgd', f, w_f.astype(jnp.float32))
    return y.reshape(bsz, L, D_FOURIER).astype(u.dtype)


def ssd_scan(xh, dt, a, bm, cm, h0):
    bsz, L = xh.shape[:2]
    nc = L // CHUNK
    xc = xh.reshape(bsz, nc, CHUNK, SSD_GROUPS, HEADS_PER_GROUP, SSD_HEAD_DIM)
    dtc = dt.reshape(bsz, nc, CHUNK, SSD_GROUPS, HEADS_PER_GROUP)
    bc = bm.reshape(bsz, nc, CHUNK, SSD_GROUPS, D_STATE)
    cc = cm.reshape(bsz, nc, CHUNK, SSD_GROUPS, D_STATE)
    acum = jnp.cumsum(dtc * a, axis=2)
    a_last = acum[:, :, -1]
    xdt = xc * dtc[..., None]
    seg = acum[:, :, :, None] - acum[:, :, None, :]
    lower = jnp.tril(jnp.ones((CHUNK, CHUNK), dtype=bool))[:, :, None, None]
    decay = jnp.exp(jnp.where(lower, seg, -jnp.inf))
    cb = jnp.einsum('bclgn,bcsgn->bclsg', cc, bc)
    y_diag = jnp.einsum('bclsgh,bcsghp->bclghp', decay * cb[..., None], xdt)
    sdecay = jnp.exp(a_last[:, :, None] - acum)
    states = jnp.einsum('bcsgn,bcsghp->bcghpn', bc, xdt * sdecay[..., None])

    def step(h, inp):
        st, dec = inp
        return dec[..., None, None] * h + st, h

    h_final, h_prev = lax.scan(step, h0, (jnp.moveaxis(states, 1, 0), jnp.moveaxis(jnp.exp(a_last), 1, 0)))
    h_prev = jnp.moveaxis(h_prev, 0, 1)
    y_off = jnp.einsum('bclgn,bcghpn->bclghp', cc, h_prev) * jnp.exp(acum)[..., None]
    y = (y_diag + y_off).reshape(bsz, L, SSD_GROUPS, HEADS_PER_GROUP, SSD_HEAD_DIM)
    return y, h_final


def gated_group_rmsnorm(y, z, g):
    bsz, L, _ = y.shape
    u = (y * jax.nn.silu(z.astype(jnp.float32))).reshape(bsz, L, SSD_GROUPS, D_SSD // SSD_GROUPS)
    u = u * lax.rsqrt(jnp.mean(u * u, axis=-1, keepdims=True) + EPS)
    return u.reshape(bsz, L, D_SSD) * g.astype(jnp.float32)


def ssd_mixer(p_c, p_l, conv_w, conv_b, dt_bias, a_log, d_skip, g_norm, ctx_out):
    f32 = jnp.float32
    out_dtype = p_l.dtype

    def split(p):
        return p[..., :D_SSD], p[..., D_SSD:D_SSD + CONV_DIM], p[..., D_SSD + CONV_DIM:]

    z_c, xbc_c, dtr_c = split(p_c)
    z_l, xbc_l, dtr_l = split(p_l)
    bsz, n_lat = p_l.shape[:2]
    rows = n_lat // GRID_W
    xbc_c = jax.nn.silu(depthwise_conv(xbc_c, conv_w, conv_b))
    xbc_l = jax.nn.silu(depthwise_conv(xbc_l.reshape(bsz * rows, GRID_W, CONV_DIM), conv_w, conv_b))
    xbc_l = xbc_l.reshape(bsz, n_lat, CONV_DIM)
    a = -jnp.exp(a_log.astype(f32)).reshape(2, SSD_GROUPS, HEADS_PER_GROUP)
    dtb = dt_bias.astype(f32).reshape(2, SSD_GROUPS, HEADS_PER_GROUP)

    def heads(xbc, dtr):
        L = xbc.shape[1]
        xbc = xbc.astype(f32)
        xh = xbc[..., :D_SSD].reshape(bsz, L, SSD_GROUPS, HEADS_PER_GROUP, SSD_HEAD_DIM)
        bm = xbc[..., D_SSD:D_SSD + SSD_GROUPS * D_STATE].reshape(bsz, L, SSD_GROUPS, D_STATE)
        cm = xbc[..., D_SSD + SSD_GROUPS * D_STATE:].reshape(bsz, L, SSD_GROUPS, D_STATE)
        dt = jax.nn.softplus(dtr.astype(f32).reshape(bsz, L, 2, SSD_GROUPS, HEADS_PER_GROUP) + dtb)
        return xh, bm, cm, dt[:, :, 0], dt[:, :, 1]

    xh_c, b_c, c_c, dtf_c, dtbk_c = heads(xbc_c, dtr_c)
    xh_l, b_l, c_l, dtf_l, dtbk_l = heads(xbc_l, dtr_l)

    def flip(t):
        return jnp.flip(t, axis=1)

    h0 = jnp.zeros((bsz, SSD_GROUPS, HEADS_PER_GROUP, SSD_HEAD_DIM, D_STATE), f32)
    yf_c, hf_c = ssd_scan(xh_c, dtf_c, a[0], b_c, c_c, h0)
    yb_c, hb_c = ssd_scan(flip(xh_c), flip(dtbk_c), a[1], flip(b_c), flip(c_c), h0)
    yf_l, _ = ssd_scan(xh_l, dtf_l, a[0], b_l, c_l, hf_c)
    yb_l, _ = ssd_scan(flip(xh_l), flip(dtbk_l), a[1], flip(b_l), flip(c_l), hb_c)
    d = d_skip.astype(f32).reshape(SSD_GROUPS, HEADS_PER_GROUP, 1)

    def finish(yf, yb_rev, xh, z):
        L = xh.shape[1]
        y = (yf + flip(yb_rev) + d * xh).reshape(bsz, L, D_SSD)
        return gated_group_rmsnorm(y, z, g_norm).astype(out_dtype)

    y_l = finish(yf_l, yb_l, xh_l, z_l)
    y_c = finish(yf_c, yb_c, xh_c, z_c) if ctx_out else None
    return y_l, y_c


def squared_relu_mlp(h, w1, w2):
    return jnp.square(jax.nn.relu(h @ w1)) @ w2


def setup_inputs(seed: int = 0) -> dict:
    key = jax.random.key(seed)
    ks = jax.random.split(key, 24)
    f32 = jnp.float32
    nrm = lambda k, shape, s: jax.random.normal(k, shape, f32) * s
    u_dt = jax.random.uniform(ks[12], (DEPTH, 2, SSD_HEADS), f32)
    dt0 = jnp.exp(u_dt * (np.log(0.1) - np.log(0.001)) + np.log(0.001)).astype(f32)
    dt_bias = dt0 + jnp.log(-jnp.expm1(-dt0))
    return {
        'x': nrm(ks[0], (BATCH, SEQ, D_MODEL), 1.0),
        'c': nrm(ks[1], (BATCH, D_MODEL), 1.0),
        'ctx': nrm(ks[2], (BATCH, CTX_LEN, D_MODEL), 1.0),
        'c_ctx': nrm(ks[3], (D_MODEL,), 1.0),
        'w_ada': nrm(ks[4], (DEPTH, D_MODEL, 6 * D_MODEL), 0.5 * D_MODEL ** -0.5),
        'b_ada': nrm(ks[5], (DEPTH, 6 * D_MODEL), 0.02),
        'g_mix': 1.0 + nrm(ks[6], (DEPTH, D_MODEL), 0.02),
        'w_in': nrm(ks[7], (DEPTH, D_MODEL, D_IN_PROJ), D_MODEL ** -0.5),
        'conv_w': nrm(ks[8], (DEPTH, D_CONV, CONV_DIM), D_CONV ** -0.5),
        'conv_b': nrm(ks[9], (DEPTH, CONV_DIM), 0.02),
        'dt_bias': dt_bias,
        'a_log': jnp.log(jax.random.uniform(ks[10], (DEPTH, 2, SSD_HEADS), f32, 1.0, 16.0)),
        'd_skip': 1.0 + nrm(ks[11], (DEPTH, SSD_HEADS), 0.1),
        'g_ssd_norm': 1.0 + nrm(ks[13], (DEPTH, D_SSD), 0.02),
        'w_fourier': nrm(ks[14], (DEPTH, N_FGROUPS, FG_DIM, FG_DIM), FG_DIM ** -0.5),
        'w_out': nrm(ks[15], (DEPTH, D_MIX, D_MODEL), D_MIX ** -0.5),
        'g_mlp': 1.0 + nrm(ks[16], (DEPTH, D_MODEL), 0.02),
        'w_mlp1': nrm(ks[17], (DEPTH, D_MODEL, D_FF), D_MODEL ** -0.5),
        'w_mlp2': nrm(ks[18], (DEPTH, D_FF, D_MODEL), D_FF ** -0.5),
        'g_final': 1.0 + nrm(ks[19], (D_MODEL,), 0.02),
    }


def reference(x, c, ctx, c_ctx, w_ada, b_ada, g_mix, w_in, conv_w, conv_b, dt_bias, a_log, d_skip,
              g_ssd_norm, w_fourier, w_out, g_mlp, w_mlp1, w_mlp2, g_final):
    xl = x
    xc = ctx
    sc_lat = jax.nn.silu(c)
    sc_ctx = jax.nn.silu(c_ctx)
    for l in range(DEPTH):
        ctx_out = l < DEPTH - 1
        mod_l = (sc_lat @ w_ada[l] + b_ada[l])[:, None, :]
        mod_c = sc_ctx @ w_ada[l] + b_ada[l]
        sh1_l, sc1_l, gt1_l, sh2_l, sc2_l, gt2_l = jnp.split(mod_l, 6, axis=-1)
        sh1_c, sc1_c, gt1_c, sh2_c, sc2_c, gt2_c = jnp.split(mod_c, 6, axis=-1)
        p_l = modulate(rms_norm(xl, g_mix[l]), sh1_l, sc1_l) @ w_in[l]
        p_c = modulate(rms_norm(xc, g_mix[l]), sh1_c, sc1_c) @ w_in[l]
        f_l = fourier_mix(p_l[..., :D_FOURIER], w_fourier[l])
        y_l, y_c = ssd_mixer(p_c[..., D_FOURIER:], p_l[..., D_FOURIER:], conv_w[l], conv_b[l], dt_bias[l],
                             a_log[l], d_skip[l], g_ssd_norm[l], ctx_out)
        xl = xl + gt1_l * (jnp.concatenate([f_l, y_l], axis=-1) @ w_out[l])
        if ctx_out:
            f_c = fourier_mix(p_c[..., :D_FOURIER], w_fourier[l])
            xc = xc + gt1_c * (jnp.concatenate([f_c, y_c], axis=-1) @ w_out[l])
        xl = xl + gt2_l * squared_relu_mlp(modulate(rms_norm(xl, g_mlp[l]), sh2_l, sc2_l), w_mlp1[l], w_mlp2[l])
        if ctx_out:
            xc = xc + gt2_c * squared_relu_mlp(modulate(rms_norm(xc, g_mlp[l]), sh2_c, sc2_c), w_mlp1[l], w_mlp2[l])
    return rms_norm(xl, g_final)
```

```python
import numpy as np
import ml_dtypes
import concourse.bass as bass
import concourse.mybir as mybir
from concourse.bass_utils import run_bass_kernel_spmd

F32 = mybir.dt.float32
BF16 = mybir.dt.bfloat16
AF = mybir.ActivationFunctionType
ALU = mybir.AluOpType
AX = mybir.AxisListType
NPBF = ml_dtypes.bfloat16

NCORES = 8
D = 2048
SEQ = 8192
CTX = 256
DEPTH = 4
DIN = 4656
DFF = 8192
TOK = 1056
EPS = 1e-6


class Prog:
    ENGS = ("sync", "scalar", "vector", "gpsimd", "tensor")

    def __init__(self, nc):
        self.nc = nc
        self.q = {e: [] for e in self.ENGS}
        self.cnt = {}
        self.waited = {e: {} for e in self.ENGS}

    def emit(self, eng, fn, waits=(), sig=None, inc=1):
        ws = []
        for w in waits:
            if w is None:
                continue
            s, v = w
            if self.waited[eng].get(s, 0) >= v:
                continue
            self.waited[eng][s] = v
            ws.append((s, v))
        tok = None
        if sig is not None:
            self.cnt[sig] = self.cnt.get(sig, 0) + inc
            tok = (sig, self.cnt[sig])
        self.q[eng].append((fn, ws, sig, inc))
        return tok

    def pe(self, fn, waits=(), sig=True):
        return self.emit("tensor", fn, waits, "s_pe" if sig else None)

    def act(self, fn, waits=(), sig=True):
        return self.emit("scalar", fn, waits, "s_act" if sig else None)

    def dve(self, fn, waits=(), sig=True):
        return self.emit("vector", fn, waits, "s_dve" if sig else None)

    def pool(self, fn, waits=(), sig=True):
        return self.emit("gpsimd", fn, waits, "s_pool" if sig else None)

    def dma(self, q, out, in_, sem, waits=()):
        return self.emit(q, lambda e: e.dma_start(out=out, in_=in_), waits, sem, 16)

    def finish(self, final_waits):
        nc = self.nc
        names = sorted(self.cnt.keys())
        sems = {}
        import contextlib
        with contextlib.ExitStack() as st:
            for n in names:
                sems[n] = st.enter_context(nc.semaphore(n))
            block = st.enter_context(nc.Block())
            q = self.q
            q["sync"].append((None, [w for w in final_waits if w is not None], None, 0))

            def runner(ename):
                def _(eng):
                    for fn, ws, sig, inc in q[ename]:
                        for s, v in ws:
                            eng.wait_ge(sems[s], v)
                        if fn is None:
                            continue
                        ins = fn(eng)
                        if sig is not None:
                            ins.then_inc(sems[sig], inc)
                return _
            block.sync(runner("sync"))
            block.scalar(runner("scalar"))
            block.vector(runner("vector"))
            block.gpsimd(runner("gpsimd"))
            block.tensor(runner("tensor"))


def _run(nc, in_maps):
    res = run_bass_kernel_spmd(nc, in_maps, core_ids=list(range(NCORES)))
    return res.results


def build_cast(F):
    nc = bass.Bass("TRN2", target_bir_lowering=False)
    src = nc.dram_tensor("src", [128, F], F32, kind="ExternalInput").ap()
    dst = nc.dram_tensor("dst", [128, F], BF16, kind="ExternalOutput").ap()
    T = 4096
    nt = (F + T - 1) // T
    NB = 3
    import contextlib
    with contextlib.ExitStack() as st:
        ins = [st.enter_context(nc.sbuf_tensor(f"in{i}", [128, T], F32)) for i in range(NB)]
        outs = [st.enter_context(nc.sbuf_tensor(f"out{i}", [128, T], BF16)) for i in range(NB)]
        P = Prog(nc)
        cast_tok = [None] * NB
        st_tok = [None] * NB
        for t in range(nt):
            b = t % NB
            w = min(T, F - t * T)
            ld = P.dma("sync", ins[b][:, :w], src[:, t * T:t * T + w], f"ld{b}", [cast_tok[b]])
            if t % 2 == 0:
                cast_tok[b] = P.dve(lambda e, b=b, w=w: e.tensor_copy(out=outs[b][:, :w], in_=ins[b][:, :w]),
                                    [ld, st_tok[b]])
            else:
                cast_tok[b] = P.act(lambda e, b=b, w=w: e.copy(out=outs[b][:, :w], in_=ins[b][:, :w]),
                                    [ld, st_tok[b]])
            st_tok[b] = P.dma("gpsimd", dst[:, t * T:t * T + w], outs[b][:, :w], f"st{b}", [cast_tok[b]])
        P.finish(st_tok)
    return nc


def run_cast(flat):
    n = flat.size
    assert n % (NCORES * 128) == 0
    F = n // (NCORES * 128)
    nc = build_cast(F)
    sh = flat.reshape(NCORES, 128, F)
    res = _run(nc, [{"src": np.ascontiguousarray(sh[i])} for i in range(NCORES)])
    return np.stack([r["dst"] for r in res]).reshape(-1)


import contextlib


def fm(v):
    v = np.asarray(v)
    n = v.shape[-1] // 128
    lead = v.shape[:-1]
    a = v.reshape(lead + (n, 128))
    a = np.moveaxis(a, -1, 0)
    return np.ascontiguousarray(a)


def build_mods():
    nc = bass.Bass("TRN2", target_bir_lowering=False)
    cc = nc.dram_tensor("cc", [128, 16, 2], F32, kind="ExternalInput").ap()
    w = nc.dram_tensor("w", [DEPTH, D, 1536], F32, kind="ExternalInput").ap()
    b = nc.dram_tensor("b", [128, DEPTH, 12], F32, kind="ExternalInput").ap()
    out = nc.dram_tensor("out", [128, DEPTH, 12, 2], F32, kind="ExternalOutput").ap()
    with contextlib.ExitStack() as st:
        sb = lambda n, s, d: st.enter_context(nc.sbuf_tensor(n, s, d))
        cs = sb("cs", [128, 16, 2], F32)
        ss = sb("ss", [128, 16, 2], F32)
        bs = sb("bs", [128, DEPTH, 12], F32)
        os_ = sb("os", [128, DEPTH, 12, 2], F32)
        wb = [sb(f"wb{i}", [128, 16, 512], F32) for i in range(2)]
        ps = st.enter_context(nc.psum_tensor("ps", [128, 2, 512], F32))
        P = Prog(nc)
        t_c = P.dma("sync", cs[:], cc, "ld_c")
        t_b = P.dma("sync", bs[:], b, "ld_b")
        t_s = P.act(lambda e: e.activation(out=ss[:], in_=cs[:], func=AF.Silu), [t_c])
        wfree = [None, None]
        evs = []
        n = 0
        for l in range(DEPTH):
            for blk in range(3):
                bi = n % 2
                ld = P.dma("sync" if n % 2 == 0 else "gpsimd", wb[bi][:],
                           w[l, :, blk * 512:(blk + 1) * 512].rearrange("(kc p) n -> p kc n", p=128),
                           f"ld_w{bi}", [wfree[bi]])
                for j in range(4):
                    jj = blk * 4 + j
                    for k in range(16):
                        t_mm = P.pe(lambda e, bi=bi, j=j, k=k, jj=jj, n=n: e.matmul(
                            ps[:, n % 2, jj * 2:jj * 2 + 2], lhsT=wb[bi][:, k, j * 128:(j + 1) * 128],
                            rhs=ss[:, k, :], start=(k == 0), stop=(k == 15)),
                            [ld, t_s] + (evs[-1:] if k == 0 else []), sig=(k == 15))
                    ev = P.dve(lambda e, l=l, jj=jj, n=n: e.tensor_scalar(
                        out=os_[:, l, jj, :], in0=ps[:, n % 2, jj * 2:jj * 2 + 2], scalar1=bs[:, l, jj:jj + 1],
                        scalar2=None, op0=ALU.add), [t_mm, t_b])
                    evs.append(ev)
                wfree[bi] = t_mm
                n += 1
        t_o = P.dma("sync", out, os_[:], "st_o", [evs[-1]])
        P.finish([t_o])
    return nc


def run_mods(c, c_ctx, w_ada, b_ada):
    nc = build_mods()
    cc = np.stack([fm(c.reshape(-1)), fm(c_ctx.reshape(-1))], axis=-1)
    in_maps = []
    for i in range(NCORES):
        sl = slice(1536 * i, 1536 * (i + 1))
        in_maps.append({"cc": cc, "w": np.ascontiguousarray(w_ada[:, :, sl]),
                        "b": fm(b_ada[:, sl])})
    res = _run(nc, in_maps)
    mods = np.zeros((DEPTH, 2, 6 * D), np.float32)
    for i in range(NCORES):
        o = res[i]["out"]
        mods[:, :, 1536 * i:1536 * (i + 1)] = np.transpose(o, (1, 3, 2, 0)).reshape(DEPTH, 2, 1536)
    return mods


BLKS = [(0, 512), (512, 512), (1024, 32)]


def emit_norm_mod(P, nc, X, hT, sc, sh, g, ones, ps_ms, sq, rstd, tmp, s1, x_ready, extra_waits=()):
    ew = list(extra_waits)
    epsc = s1[:, 0, 2:3]
    t_eps = P.dve(lambda e: e.memset(s1[:, :, 2:3], EPS))
    t_s1 = []
    for j in range(2):
        t_s1.append(P.dve(lambda e, j=j: e.scalar_tensor_tensor(
            out=s1[:, :, j], in0=sc[:, :, j], scalar=1.0, in1=g, op0=ALU.add, op1=ALU.mult), ew))
    sq_tok = [None] * len(sq)
    mm_tok = None
    for c in range(16):
        b = c % len(sq)
        xr = x_ready[c] if isinstance(x_ready, list) else x_ready
        t_sq = P.act(lambda e, c=c, b=b: e.activation(out=sq[b][:], in_=X[:, c, :], func=AF.Square),
                     [xr, sq_tok[b]] + ew)
        for bi, (s0, w) in enumerate(BLKS):
            mm_tok = P.pe(lambda e, c=c, b=b, bi=bi, s0=s0, w=w: e.matmul(
                ps_ms[:, bi, :w], lhsT=ones[:], rhs=sq[b][:, s0:s0 + w], start=(c == 0), stop=(c == 15)),
                [t_sq] + ew, sig=(bi == 2))
        sq_tok[b] = mm_tok
    t_r = None
    for bi, (s0, w) in enumerate(BLKS):
        t_q = P.act(lambda e, bi=bi, s0=s0, w=w: e.activation(
            out=rstd[:, s0:s0 + w], in_=ps_ms[:, bi, :w], func=AF.Sqrt, bias=epsc[:, 0:1], scale=1.0), [mm_tok, t_eps])
        t_r = P.dve(lambda e, bi=bi, s0=s0, w=w: e.reciprocal(
            out=rstd[:, s0:s0 + w], in_=rstd[:, s0:s0 + w]), [t_q])
    toks = []
    tmp_tok = [None] * len(tmp)
    for c in range(16):
        b = c % len(tmp)
        t_m = P.dve(lambda e, c=c, b=b: e.tensor_tensor(out=tmp[b][:], in0=X[:, c, :], in1=rstd[:], op=ALU.mult),
                    [t_r, tmp_tok[b]])
        P.act(lambda e, c=c, b=b: e.activation(out=hT[:, c, 0:1024], in_=tmp[b][:, 0:1024], func=AF.Identity,
                                               scale=s1[:, c, 0:1], bias=sh[:, c, 0:1]), [t_m, t_s1[1]], sig=False)
        t_a = P.act(lambda e, c=c, b=b: e.activation(out=hT[:, c, 1024:TOK], in_=tmp[b][:, 1024:TOK], func=AF.Identity,
                                                     scale=s1[:, c, 1:2], bias=sh[:, c, 1:2]), [t_m])
        tmp_tok[b] = t_a
        toks.append(t_a)
    return toks


def emit_proj(P, nc, wdram, nout, hT, h_ready, wbufs, psb, evac, name, kch=16, wfree0=None, ps_free0=None):
    nblk = (nout + 511) // 512
    wfree = list(wfree0) if wfree0 else [None] * len(wbufs)
    ps_free = list(ps_free0) if ps_free0 else [None] * len(psb)
    pi = 0
    evs = []
    last_mm = None
    for nb in range(nblk):
        b = nb % len(wbufs)
        ncol = min(512, nout - nb * 512)
        ld = P.dma("sync" if nb % 2 == 0 else "gpsimd", wbufs[b][:, :, :ncol],
                   wdram[:, nb * 512:nb * 512 + ncol].rearrange("(kc p) n -> p kc n", p=128),
                   f"ld_{name}{b}", [wfree[b]])
        for j in range((ncol + 127) // 128):
            m = min(128, ncol - j * 128)
            for bi, (s0, w) in enumerate(BLKS):
                pb = pi % len(psb)
                pi += 1
                for k in range(kch):
                    last_mm = P.pe(lambda e, b=b, j=j, m=m, k=k, s0=s0, w=w, pb=pb: e.matmul(
                        psb[pb][:m, :w], lhsT=wbufs[b][:, k, j * 128:j * 128 + m], rhs=hT[:, k, s0:s0 + w],
                        start=(k == 0), stop=(k == kch - 1)),
                        ([ld, ps_free[pb]] + list(h_ready)) if k == 0 else [], sig=(k == kch - 1))
                ev = evac(nb * 4 + j, m, bi, s0, w, psb[pb], last_mm)
                ps_free[pb] = ev
                evs.append(ev)
        wfree[b] = last_mm
    return last_mm, evs


def build_B():
    nc = bass.Bass("TRN2", target_bir_lowering=False)
    xT = nc.dram_tensor("xT", [D, TOK], F32, kind="ExternalInput").ap()
    scd = nc.dram_tensor("sc", [128, 16, 2], F32, kind="ExternalInput").ap()
    shd = nc.dram_tensor("sh", [128, 16, 2], F32, kind="ExternalInput").ap()
    gd = nc.dram_tensor("g", [128, 16], F32, kind="ExternalInput").ap()
    wd = nc.dram_tensor("w", [D, DIN], BF16, kind="ExternalInput").ap()
    pT = nc.dram_tensor("pT", [DIN, TOK], F32, kind="ExternalOutput").ap()
    with contextlib.ExitStack() as st:
        sb = lambda n, s, d: st.enter_context(nc.sbuf_tensor(n, s, d))
        X = sb("X", [128, 16, TOK], F32)
        hT = sb("hT", [128, 16, TOK], BF16)
        sc = sb("sc_s", [128, 16, 2], F32)
        sh = sb("sh_s", [128, 16, 2], F32)
        g = sb("g_s", [128, 16], F32)
        s1 = sb("s1", [128, 16, 3], F32)
        ones = sb("ones", [128, 128], BF16)
        sq = [sb(f"sq{i}", [128, TOK], BF16) for i in range(3)]
        rstd = sb("rstd", [128, TOK], F32)
        tmp = [sb(f"tmp{i}", [128, TOK], F32) for i in range(2)]
        wb = [sb(f"wb{i}", [128, 16, 512], BF16) for i in range(2)]
        ot = [sb(f"ot{i}", [128, TOK], F32) for i in range(3)]
        ps_ms = st.enter_context(nc.psum_tensor("ps_ms", [128, 3, 512], F32))
        ps_mm = st.enter_context(nc.psum_tensor("ps_mm", [128, 5, 512], F32))
        P = Prog(nc)
        t_ones = P.dve(lambda e: e.memset(ones[:], 1.0 / D))
        xr = []
        xv = xT.rearrange("(c p) t -> p c t", p=128)
        for c4 in range(4):
            t = P.dma("sync" if c4 % 2 == 0 else "gpsimd", X[:, c4 * 4:(c4 + 1) * 4, :], xv[:, c4 * 4:(c4 + 1) * 4, :], f"ld_x{c4}")
            xr += [t] * 4
        t1 = P.dma("gpsimd", sc[:], scd, "ld_sc")
        t2 = P.dma("gpsimd", sh[:], shd, "ld_sh")
        t3 = P.dma("gpsimd", g[:], gd, "ld_g")
        h_ready = emit_norm_mod(P, nc, X, hT, sc, sh, g[:], ones, ps_ms, sq, rstd, tmp, s1, xr, [t1, t2, t3, t_ones])

        ot_free = [None] * 3
        state = {"n": 0, "evs": []}
        out_toks = []

        def evac(nchunk, m, bi, s0, w, ps, mm):
            b = nchunk % 3
            if bi == 1:
                tk = P.act(lambda e: e.copy(out=ot[b][:m, s0:s0 + w], in_=ps[:m, :w]), [mm, ot_free[b]])
            else:
                tk = P.dve(lambda e: e.tensor_copy(out=ot[b][:m, s0:s0 + w], in_=ps[:m, :w]), [mm, ot_free[b]])
            state["evs"].append(tk)
            if bi == 2:
                d = P.dma("sync", pT[nchunk * 128:nchunk * 128 + m, :], ot[b][:m, :], f"st_p{b}", state["evs"][-3:])
                ot_free[b] = d
                out_toks.append(d)
            return tk

        emit_proj(P, nc, wd, DIN, hT, h_ready, wb, [ps_mm[:, i, :] for i in range(5)], evac, "w")
        P.finish(out_toks[-3:])
    return nc


class Res:
    __slots__ = ("w", "r")

    def __init__(self):
        self.w = None
        self.r = {}


class TP(Prog):
    def _deps(self, reads, writes):
        waits = []
        for r in reads:
            waits.append(r.w)
        for w in writes:
            waits.append(w.w)
            waits += list(w.r.items())
        return waits

    def _mark(self, tok, reads, writes):
        for r in reads:
            s, v = tok
            if r.r.get(s, 0) < v:
                r.r[s] = v
        for w in writes:
            w.w = tok
            w.r = {}

    def op(self, eng, fn, reads=(), writes=()):
        sem = {"tensor": "s_pe", "scalar": "s_act", "vector": "s_dve", "gpsimd": "s_pool"}[eng]
        tok = self.emit(eng, fn, self._deps(reads, writes), sem)
        self._mark(tok, reads, writes)
        return tok

    def group(self, fns, reads=(), writes=()):
        waits = self._deps(reads, writes)
        tok = None
        for i, fn in enumerate(fns):
            tok = self.emit("tensor", fn, waits if i == 0 else (), "s_pe" if i == len(fns) - 1 else None)
        self._mark(tok, reads, writes)
        return tok

    def xdma(self, q, out, in_, sem, reads=(), writes=()):
        tok = self.emit(q, lambda e: e.dma_start(out=out, in_=in_), self._deps(reads, writes), sem, 16)
        self._mark(tok, reads, writes)
        return tok


NCH = 66
NTOK = NCH * 128


def build_ssd():
    nc = bass.Bass("TRN2", target_bir_lowering=False)
    uT = nc.dram_tensor("uT", [448, NTOK], F32, kind="ExternalInput").ap()
    dtr = nc.dram_tensor("dtr", [128, 2, NCH, 3], F32, kind="ExternalInput").ap()
    cwd = nc.dram_tensor("cw", [128, 4, 5], F32, kind="ExternalInput").ap()
    cbd = nc.dram_tensor("cb", [128, 4], F32, kind="ExternalInput").ap()
    smd = nc.dram_tensor("sm", [128, 15], F32, kind="ExternalInput").ap()
    cst = nc.dram_tensor("cst", [128, 8, 128], F32, kind="ExternalInput").ap()
    Y = nc.dram_tensor("Y", [2, NTOK, 192], F32, kind="ExternalOutput").ap()
    with contextlib.ExitStack() as st:
        sb = lambda n, s, d: st.enter_context(nc.sbuf_tensor(n, s, d))
        BT = sb("BT", [128, NTOK], BF16)
        CT = sb("CT", [128, NTOK], BF16)
        Btm = sb("Btm", [128, NCH, 128], BF16)
        Xtm = sb("Xtm", [128, NCH, 192], F32)
        cw = sb("cw_s", [128, 4, 5], F32)
        cb = sb("cb_s", [128, 4], F32)
        sm = sb("sm_s", [128, 15], F32)
        K8 = sb("K8", [128, 8, 128], F32)
        identb = sb("identb", [128, 128], BF16)
        dts = {n: sb(n, [128, 2, NCH, 3], F32) for n in
               ("raw", "t1", "dtv", "dta", "acum", "tot", "sdec", "ea", "dec", "dtsd")}
        avec = sb("avec", [128, 6], F32)
        PIECE = 1024
        raw = [sb(f"rawb{i}", [128, PIECE], F32) for i in range(2)]
        acc = [sb(f"accb{i}", [128, PIECE], F32) for i in range(2)]
        sil = [sb(f"silb{i}", [128, PIECE], F32) for i in range(2)]
        Rb = [[sb(f"R{i}_{h}", [128, 128], F32) for h in range(3)] for i in range(2)]
        Eb = [[sb(f"E{i}_{h}", [128, 128], F32) for h in range(3)] for i in range(2)]
        Mb = [[sb(f"M{i}_{h}", [128, 128], BF16) for h in range(3)] for i in range(2)]
        CBm = [sb(f"CBm{i}", [128, 128], F32) for i in range(2)]
        xdt = [sb(f"xdt{i}", [128, 192], BF16) for i in range(2)]
        xs = [sb(f"xs{i}", [128, 192], BF16) for i in range(2)]
        xd = sb("xd", [128, 192], BF16)
        hst = [sb(f"hst{i}", [128, 192], F32) for i in range(2)]
        hbf = [[sb(f"hbf{i}_{j}", [128, 192], BF16) for j in range(2)] for i in range(2)]
        yo_t = [sb(f"yot{i}", [128, 192], F32) for i in range(2)]
        yo = [sb(f"yo{i}", [128, 192], F32) for i in range(4)]
        ps_seg = [st.enter_context(nc.psum_tensor(f"ps_seg{i}", [128, 512], F32)) for i in range(2)]
        ps_cb = st.enter_context(nc.psum_tensor("ps_cb", [128, 512], F32))
        ps_y = [st.enter_context(nc.psum_tensor(f"ps_y{i}", [128, 512], F32)) for i in range(2)]
        ps_st = st.enter_context(nc.psum_tensor("ps_st", [128, 512], F32))
        ps_tr = [st.enter_context(nc.psum_tensor(f"ps_tr{i}", [128, 512], F32)) for i in range(2)]

        P = TP(nc)
        R = {}

        def res(name):
            if name not in R:
                R[name] = Res()
            return R[name]

        Tm, Um, SU, SL, MF, MB, ID, ON = [K8[:, i, :] for i in range(8)]
        P.xdma("sync", K8[:], cst, "ld_k8", writes=[res("K8")])
        P.xdma("sync", cw[:], cwd, "ld_cw", writes=[res("cw")])
        P.xdma("sync", cb[:], cbd, "ld_cb", writes=[res("cb")])
        P.xdma("sync", sm[:], smd, "ld_sm", writes=[res("sm")])
        P.xdma("sync", dts["raw"][:], dtr, "ld_dt", writes=[res("raw")])
        P.op("vector", lambda e: e.tensor_copy(out=identb[:], in_=ID), [res("K8")], [res("identb")])

        fl = lambda n: dts[n][:].rearrange("p a c j -> p (a c j)")
        col = lambda n, d, j: dts[n][:, d, :, j]
        for d in range(2):
            for j in range(3):
                P.op("vector", lambda e, d=d, j=j: e.tensor_scalar(
                    out=col("raw", d, j), in0=col("raw", d, j), scalar1=sm[:, d * 3 + j:d * 3 + j + 1], scalar2=None,
                    op0=ALU.add), [res("raw"), res("sm")], [res("raw")])
        P.op("scalar", lambda e: e.activation(out=fl("t1"), in_=fl("raw"), func=AF.Abs), [res("raw")], [res("t1")])
        P.op("scalar", lambda e: e.activation(out=fl("t1"), in_=fl("t1"), func=AF.Exp, scale=-1.0), [res("t1")], [res("t1")])
        P.op("vector", lambda e: e.tensor_scalar_add(out=fl("t1"), in0=fl("t1"), scalar1=1.0), [res("t1")], [res("t1")])
        P.op("scalar", lambda e: e.activation(out=fl("t1"), in_=fl("t1"), func=AF.Ln), [res("t1")], [res("t1")])
        P.op("vector", lambda e: e.scalar_tensor_tensor(out=fl("dtv"), in0=fl("raw"), scalar=0.0, in1=fl("t1"),
                                                         op0=ALU.max, op1=ALU.add), [res("raw"), res("t1")], [res("dtv")])
        P.op("scalar", lambda e: e.activation(out=avec[:], in_=sm[:, 6:12], func=AF.Exp), [res("sm")], [res("avec")])
        P.op("vector", lambda e: e.tensor_scalar_mul(out=avec[:], in0=avec[:], scalar1=-1.0), [res("avec")], [res("avec")])
        for d in range(2):
            for j in range(3):
                P.op("vector", lambda e, d=d, j=j: e.tensor_scalar(
                    out=col("dta", d, j), in0=col("dtv", d, j), scalar1=avec[:, d * 3 + j:d * 3 + j + 1], scalar2=None,
                    op0=ALU.mult), [res("dtv"), res("avec")], [res("dta")])
        for d in range(2):
            lhs = Tm if d == 0 else Um
            P.group([lambda e, d=d, lhs=lhs: e.matmul(ps_tr[0][:, d * 256:d * 256 + 198], lhsT=lhs,
                                                       rhs=dts["dta"][:, d].rearrange("p c j -> p (c j)"),
                                                       start=True, stop=True)],
                    [res("K8"), res("dta")], [res("ps_tr0")])
            P.group([lambda e, d=d: e.matmul(ps_tr[1][:, d * 256:d * 256 + 198], lhsT=ON,
                                             rhs=dts["dta"][:, d].rearrange("p c j -> p (c j)"),
                                             start=True, stop=True)],
                    [res("K8"), res("dta")], [res("ps_tr1")])
        for d in range(2):
            P.op("vector", lambda e, d=d: e.tensor_copy(out=dts["acum"][:, d].rearrange("p c j -> p (c j)"),
                                                       in_=ps_tr[0][:, d * 256:d * 256 + 198]),
                 [res("ps_tr0")], [res("acum")])
            P.op("vector", lambda e, d=d: e.tensor_copy(out=dts["tot"][:, d].rearrange("p c j -> p (c j)"),
                                                       in_=ps_tr[1][:, d * 256:d * 256 + 198]),
                 [res("ps_tr1")], [res("tot")])
        P.op("vector", lambda e: e.tensor_tensor(out=fl("sdec"), in0=fl("tot"), in1=fl("acum"), op=ALU.subtract),
             [res("tot"), res("acum")], [res("sdec")])
        P.op("scalar", lambda e: e.activation(out=fl("sdec"), in_=fl("sdec"), func=AF.Exp), [res("sdec")], [res("sdec")])
        P.op("scalar", lambda e: e.activation(out=fl("ea"), in_=fl("acum"), func=AF.Exp), [res("acum")], [res("ea")])
        P.op("scalar", lambda e: e.activation(out=fl("dec"), in_=fl("tot"), func=AF.Exp), [res("tot")], [res("dec")])
        P.op("vector", lambda e: e.tensor_tensor(out=fl("dtsd"), in0=fl("dtv"), in1=fl("sdec"), op=ALU.mult),
             [res("dtv"), res("sdec")], [res("dtsd")])

        tiles = [(0, 128), (128, 64), (192, 128), (320, 128)]
        pieces = [(0, 256, 256)] + [(256 + i * 1024, 1024, 64) for i in range(8)]
        n = 0
        ntr = 0
        for (t0, nt, rl) in pieces:
            for ti, (r0, npart) in enumerate(tiles):
                b = n % 2
                n += 1
                rw, ac, so = raw[b], acc[b], sil[b]
                rr, ra, rs = res(f"raw{b}"), res(f"acc{b}"), res(f"sil{b}")
                P.xdma("sync" if n % 2 else "gpsimd", rw[:npart, :nt], uT[r0:r0 + npart, t0:t0 + nt], f"ld_raw{b}", writes=[rr])
                v = lambda a, npart=npart, nt=nt, rl=rl: a[:npart, :nt].rearrange("p (r t) -> p r t", t=rl)
                P.op("vector", lambda e, ti=ti, rw=rw, ac=ac, npart=npart, nt=nt: e.tensor_scalar(
                    out=ac[:npart, :nt], in0=rw[:npart, :nt], scalar1=cw[:npart, ti, 2:3], scalar2=None, op0=ALU.mult),
                    [rr, res("cw")], [ra])
                for j in (0, 1, 3, 4):
                    sh_ = j - 2
                    a0, a1 = max(0, -sh_), min(rl, rl - sh_)
                    P.op("vector", lambda e, ti=ti, j=j, v=v, rw=rw, ac=ac, a0=a0, a1=a1, sh_=sh_, npart=npart: e.scalar_tensor_tensor(
                        out=v(ac)[:, :, a0:a1], in0=v(rw)[:, :, a0 + sh_:a1 + sh_], scalar=cw[:npart, ti, j:j + 1],
                        in1=v(ac)[:, :, a0:a1], op0=ALU.mult, op1=ALU.add), [rr, ra, res("cw")], [ra])
                P.op("scalar", lambda e, ti=ti, ac=ac, so=so, npart=npart, nt=nt: e.activation(
                    out=so[:npart, :nt], in_=ac[:npart, :nt], func=AF.Silu, bias=cb[:npart, ti:ti + 1], scale=1.0),
                    [ra, res("cb")], [rs])
                if ti == 2:
                    P.op("vector", lambda e, so=so, nt=nt, t0=t0: e.tensor_copy(out=BT[:, t0:t0 + nt], in_=so[:, :nt]), [rs], [res("BT")])
                if ti == 3:
                    P.op("vector", lambda e, so=so, nt=nt, t0=t0: e.tensor_copy(out=CT[:, t0:t0 + nt], in_=so[:, :nt]), [rs], [res("CT")])
                    continue
                for cc in range(nt // 128):
                    ch = t0 // 128 + cc
                    pt = ntr % 2
                    ntr += 1
                    rp = res(f"ps_tr{pt}")
                    P.group([lambda e, so=so, cc=cc, npart=npart, pt=pt: e.transpose(
                        ps_tr[pt][:, :npart], so[:npart, cc * 128:(cc + 1) * 128], K8[:npart, 6, :npart])],
                        [rs, res("K8")], [rp])
                    if ti == 2:
                        P.op("scalar", lambda e, ch=ch, pt=pt: e.copy(out=Btm[:, ch, :], in_=ps_tr[pt][:, :128]), [rp], [res("Btm")])
                    else:
                        c0 = 0 if ti == 0 else 128
                        P.op("vector" if ti == 0 else "scalar",
                             (lambda e, ch=ch, pt=pt, c0=c0, npart=npart: e.tensor_copy(out=Xtm[:, ch, c0:c0 + npart], in_=ps_tr[pt][:, :npart]))
                             if ti == 0 else
                             (lambda e, ch=ch, pt=pt, c0=c0, npart=npart: e.copy(out=Xtm[:, ch, c0:c0 + npart], in_=ps_tr[pt][:, :npart])),
                             [rp], [res("Xtm")])

        for i in range(2):
            P.op("vector", lambda e, i=i: e.memset(hst[i][:], 0.0), [], [res(f"hst{i}")])
            P.op("vector", lambda e, i=i: e.memset(hbf[i][0][:], 0.0), [], [res(f"hbf{i}_0")])
        border = [1, 0] + list(range(65, 1, -1))
        dvec = sm[:, 12:15]
        out_tok = []
        nyo = 0
        for step in range(NCH):
            for d in range(2):
                ch = step if d == 0 else border[step]
                cs = slice(ch * 128, (ch + 1) * 128)
                P.group([lambda e, cs=cs: e.matmul(ps_cb[:, :128], lhsT=BT[:, cs], rhs=CT[:, cs], start=True, stop=True)],
                        [res("BT"), res("CT")], [res("ps_cb")])
                P.op("vector", lambda e, d=d: e.tensor_tensor(out=CBm[d][:], in0=ps_cb[:, :128], in1=(MF if d == 0 else MB), op=ALU.mult),
                     [res("ps_cb"), res("K8")], [res(f"CBm{d}")])
                for h in range(3):
                    hs = slice(h * 64, (h + 1) * 64)
                    P.op("vector", lambda e, d=d, ch=ch, h=h, hs=hs: e.tensor_scalar(
                        out=xdt[d][:, hs], in0=Xtm[:, ch, hs], scalar1=dts["dtv"][:, d, ch, h:h + 1], scalar2=None, op0=ALU.mult),
                        [res("Xtm"), res("dtv")], [res(f"xdt{d}")])
                    P.op("scalar", lambda e, d=d, ch=ch, h=h, hs=hs: e.activation(
                        out=xs[d][:, hs], in_=Xtm[:, ch, hs], func=AF.Copy, scale=dts["dtsd"][:, d, ch, h:h + 1]),
                        [res("Xtm"), res("dtsd")], [res(f"xs{d}")])
                    if d == 0:
                        P.op("vector", lambda e, ch=ch, h=h, hs=hs: e.tensor_scalar(
                            out=xd[:, hs], in0=Xtm[:, ch, hs], scalar1=dvec[:, h:h + 1], scalar2=None, op0=ALU.mult),
                            [res("Xtm"), res("sm")], [res("xd")])
                for h in range(3):
                    P.op("scalar", lambda e, d=d, ch=ch, h=h: e.activation(
                        out=Rb[d][h][:], in_=(Tm if d == 0 else Um), func=AF.Copy, scale=dts["dta"][:, d, ch, h:h + 1]),
                        [res("K8"), res("dta")], [res(f"R{d}{h}")])
                    P.group([lambda e, d=d, h=h: e.matmul(ps_seg[d][:, h * 128:(h + 1) * 128], lhsT=(SU if d == 0 else SL),
                                                          rhs=Rb[d][h][:], start=True, stop=True)],
                            [res("K8"), res(f"R{d}{h}")], [res(f"ps_seg{d}{h}")])
                    P.op("scalar", lambda e, d=d, h=h: e.activation(out=Eb[d][h][:], in_=ps_seg[d][:, h * 128:(h + 1) * 128], func=AF.Exp),
                         [res(f"ps_seg{d}{h}")], [res(f"E{d}{h}")])
                    P.op("vector", lambda e, d=d, h=h: e.tensor_tensor(out=Mb[d][h][:], in0=Eb[d][h][:], in1=CBm[d][:], op=ALU.mult),
                         [res(f"E{d}{h}"), res(f"CBm{d}")], [res(f"M{d}{h}")])
                hp = step % 2
                for h in range(3):
                    hs = slice(h * 64, (h + 1) * 64)
                    fns = [lambda e, d=d, h=h, hs=hs: e.matmul(ps_y[d][:, hs], lhsT=Mb[d][h][:], rhs=xdt[d][:, hs],
                                                               start=True, stop=(d == 1))]
                    rds = [res(f"M{d}{h}"), res(f"xdt{d}")]
                    if d == 0:
                        fns.append(lambda e, hs=hs: e.matmul(ps_y[0][:, hs], lhsT=identb[:], rhs=xd[:, hs], start=False, stop=True))
                        rds += [res("identb"), res("xd")]
                    P.group(fns, rds, [res(f"ps_yd{d}")])
                P.group([lambda e, d=d, cs=cs, hp=hp: e.matmul(ps_y[d][:, 256:448], lhsT=CT[:, cs], rhs=hbf[d][hp][:], start=True, stop=True)],
                        [res("CT"), res(f"hbf{d}_{hp}")], [res(f"ps_yo{d}")])
                P.group([lambda e, d=d, ch=ch: e.matmul(ps_st[:, d * 256:d * 256 + 192], lhsT=Btm[:, ch, :], rhs=xs[d][:], start=True, stop=True)],
                        [res("Btm"), res(f"xs{d}")], [res(f"ps_st{d}")])
                for h in range(3):
                    hs = slice(h * 64, (h + 1) * 64)
                    P.op("vector", lambda e, d=d, ch=ch, h=h, hs=hs: e.scalar_tensor_tensor(
                        out=hst[d][:, hs], in0=hst[d][:, hs], scalar=dts["dec"][:, d, ch, h:h + 1],
                        in1=ps_st[:, d * 256 + h * 64:d * 256 + (h + 1) * 64], op0=ALU.mult, op1=ALU.add),
                        [res(f"hst{d}"), res("dec"), res(f"ps_st{d}")], [res(f"hst{d}")])
                P.op("scalar", lambda e, d=d, hp=hp: e.copy(out=hbf[d][1 - hp][:], in_=hst[d][:]),
                     [res(f"hst{d}")], [res(f"hbf{d}_{1 - hp}")])
                for h in range(3):
                    hs = slice(h * 64, (h + 1) * 64)
                    P.op("scalar", lambda e, d=d, ch=ch, h=h, hs=hs: e.activation(
                        out=yo_t[d][:, hs], in_=ps_y[d][:, 256 + h * 64:256 + (h + 1) * 64], func=AF.Copy,
                        scale=dts["ea"][:, d, ch, h:h + 1]), [res(f"ps_yo{d}"), res("ea")], [res(f"yot{d}")])
                yb = nyo % 4
                nyo += 1
                P.op("vector", lambda e, d=d, yb=yb: e.tensor_tensor(out=yo[yb][:], in0=ps_y[d][:, 0:192], in1=yo_t[d][:], op=ALU.add),
                     [res(f"ps_yd{d}"), res(f"yot{d}")], [res(f"yo{yb}")])
                out_tok.append(P.xdma("sync", Y[d, ch * 128:(ch + 1) * 128, :], yo[yb][:], f"st_y{yb}", reads=[res(f"yo{yb}")]))
        P.finish(out_tok[-4:])
    return nc


def ssd_consts():
    k = np.arange(128)[:, None]
    l = np.arange(128)[None, :]
    mats = [k <= l, k >= l, k > l, k < l, l >= k, l <= k, k == l, np.ones((128, 128), bool)]
    return np.ascontiguousarray(np.stack([m.astype(np.float32) for m in mats], 1))


def ssd_inputs(PT, conv_w, conv_b, dt_bias, a_log, d_skip):
    cst = ssd_consts()
    maps = []
    for core in range(NCORES):
        g, half = core // 2, core % 2
        hg0 = g * 6 + half * 3
        xc = 2048 + g * 384 + half * 192
        bc = 2048 + 1536 + g * 128
        cc = 2048 + 2048 + g * 128
        uT = np.concatenate([PT[xc:xc + 192], PT[bc:bc + 128], PT[cc:cc + 128]], 0)
        dt = np.stack([PT[4608 + d * 24 + hg0:4608 + d * 24 + hg0 + 3] for d in range(2)], 0)
        dt = dt.reshape(2, 3, NCH, 128).transpose(3, 0, 2, 1)
        cols = [np.arange(xc - 2048, xc - 2048 + 128), np.arange(xc - 2048 + 128, xc - 2048 + 192),
                np.arange(bc - 2048, bc - 2048 + 128), np.arange(cc - 2048, cc - 2048 + 128)]
        cw = np.zeros((128, 4, 5), np.float32)
        cb = np.zeros((128, 4), np.float32)
        for ti, cidx in enumerate(cols):
            cw[:len(cidx), ti, :] = conv_w[:, cidx].T
            cb[:len(cidx), ti] = conv_b[cidx]
        sm = np.concatenate([dt_bias[:, hg0:hg0 + 3].reshape(-1), a_log[:, hg0:hg0 + 3].reshape(-1), d_skip[hg0:hg0 + 3]])
        sm = np.broadcast_to(sm[None, :], (128, 15))
        maps.append({"uT": np.ascontiguousarray(uT), "dtr": np.ascontiguousarray(dt), "cw": cw, "cb": cb,
                     "sm": np.ascontiguousarray(sm, dtype=np.float32), "cst": cst})
    return maps


def build_fourier():
    nc = bass.Bass("TRN2", target_bir_lowering=False)
    uL = nc.dram_tensor("uL", [SEQ, 512], F32, kind="ExternalInput").ap()
    uC = nc.dram_tensor("uC", [CTX, 512], F32, kind="ExternalInput").ap()
    tabC = nc.dram_tensor("tabC", [SEQ, 1024], BF16, kind="ExternalInput").ap()
    tabS = nc.dram_tensor("tabS", [SEQ, 1024], BF16, kind="ExternalInput").ap()
    tcC = nc.dram_tensor("tcC", [CTX, 32], BF16, kind="ExternalInput").ap()
    tcS = nc.dram_tensor("tcS", [CTX, 32], BF16, kind="ExternalInput").ap()
    wfd = nc.dram_tensor("wf", [4, 128, 128], F32, kind="ExternalInput").ap()
    ccd = nc.dram_tensor("cc", [128, 2, 128], F32, kind="ExternalInput").ap()
    fT = nc.dram_tensor("fT", [512, TOK], F32, kind="ExternalOutput").ap()
    with contextlib.ExitStack() as st:
        sb = lambda n, s, d: st.enter_context(nc.sbuf_tensor(n, s, d))
        NB = 3
        ust = [sb(f"ust{i}", [128, 4, 512], F32) for i in range(NB)]
        ub = [sb(f"ub{i}", [128, 4, 512], BF16) for i in range(NB)]
        tC = [sb(f"tC{i}", [128, 4, 512], BF16) for i in range(NB)]
        tS = [sb(f"tS{i}", [128, 4, 512], BF16) for i in range(NB)]
        wf = sb("wf_s", [128, 4, 128], F32)
        cc = sb("cc_s", [128, 2, 128], F32)
        Mm = sb("Mm", [128, 4, 2, 128], BF16)
        AB = sb("AB", [128, 4, 2, 512], BF16)
        fo = sb("fo", [128, 4, TOK], F32)
        ucs = sb("ucs", [128, 2, 512], F32)
        ucb = sb("ucb", [128, 2, 512], BF16)
        tcc = sb("tcc", [128, 2, 2, 32], BF16)
        bank = [st.enter_context(nc.psum_tensor(f"bank{i}", [128, 512], F32)) for i in range(8)]
        P = TP(nc)
        R = {}

        def res(name):
            if name not in R:
                R[name] = Res()
            return R[name]

        P.xdma("sync", wf[:], wfd.rearrange("g j d -> j g d"), "ld_wf", writes=[res("wf")])
        P.xdma("sync", cc[:], ccd, "ld_cc", writes=[res("cc")])
        for g in range(4):
            for t in range(2):
                P.group([lambda e, g=g, t=t: e.matmul(bank[g][:, t * 128:(t + 1) * 128], lhsT=cc[:, t, :], rhs=wf[:, g, :],
                                                      start=True, stop=True)], [res("wf"), res("cc")], [res(f"bank{g}")])
            P.op("vector", lambda e, g=g: e.tensor_copy(out=Mm[:, g].rearrange("p t d -> p (t d)"), in_=bank[g][:, 0:256]),
                 [res(f"bank{g}")], [res("Mm")])

        def stage2(width, c0, ABv):
            for g in range(4):
                P.group([lambda e, g=g, t=t: e.matmul(bank[g][:, :width], lhsT=Mm[:, g, t, :], rhs=ABv(g, t),
                                                      start=(t == 0), stop=(t == 1)) for t in range(2)],
                        [res("Mm"), res("AB")], [res(f"bank{g}")])
                if g % 2 == 0:
                    P.op("vector", lambda e, g=g: e.tensor_copy(out=fo[:, g, c0:c0 + width], in_=bank[g][:, :width]),
                         [res(f"bank{g}")], [res("fo")])
                else:
                    P.op("scalar", lambda e, g=g: e.copy(out=fo[:, g, c0:c0 + width], in_=bank[g][:, :width]),
                         [res(f"bank{g}")], [res("fo")])

        uv = uL.rearrange("(c p) f -> p c f", p=128)
        cv = tabC.rearrange("(c p) k -> p c k", p=128)
        sv = tabS.rearrange("(c p) k -> p c k", p=128)
        n = 0
        for half in range(2):
            ks = slice(half * 512, (half + 1) * 512)
            for nb in range(16):
                b = n % NB
                n += 1
                P.xdma("sync", ust[b][:], uv[:, nb * 4:(nb + 1) * 4, :], f"ld_u{b}", writes=[res(f"ust{b}")])
                P.xdma("gpsimd", tC[b][:], cv[:, nb * 4:(nb + 1) * 4, ks], f"ld_tc{b}", writes=[res(f"tC{b}")])
                P.xdma("gpsimd", tS[b][:], sv[:, nb * 4:(nb + 1) * 4, ks], f"ld_ts{b}", writes=[res(f"tS{b}")])
                if nb % 2 == 0:
                    P.op("vector", lambda e, b=b: e.tensor_copy(out=ub[b][:], in_=ust[b][:]), [res(f"ust{b}")], [res(f"ub{b}")])
                else:
                    P.op("scalar", lambda e, b=b: e.copy(out=ub[b][:], in_=ust[b][:]), [res(f"ust{b}")], [res(f"ub{b}")])
                fns = []
                for ci in range(4):
                    first = (nb == 0 and ci == 0)
                    last = (nb == 15 and ci == 3)
                    for g in range(4):
                        for t in range(2):
                            tb = tC[b] if t == 0 else tS[b]
                            fns.append(lambda e, b=b, ci=ci, g=g, t=t, tb=tb, first=first, last=last: e.matmul(
                                bank[g * 2 + t][:, :], lhsT=ub[b][:, ci, g * 128:(g + 1) * 128], rhs=tb[:, ci, :],
                                start=first, stop=last))
                P.group(fns, [res(f"ub{b}"), res(f"tC{b}"), res(f"tS{b}")], [res(f"bank{i}") for i in range(8)])
            for g in range(4):
                for t in range(2):
                    if t == 0:
                        P.op("vector", lambda e, g=g, t=t: e.tensor_copy(out=AB[:, g, t, :], in_=bank[g * 2 + t][:, :]),
                             [res(f"bank{g * 2 + t}")], [res("AB")])
                    else:
                        P.op("scalar", lambda e, g=g, t=t: e.copy(out=AB[:, g, t, :], in_=bank[g * 2 + t][:, :]),
                             [res(f"bank{g * 2 + t}")], [res("AB")])
            stage2(512, half * 512, lambda g, t: AB[:, g, t, :])
        P.xdma("sync", ucs[:], uC.rearrange("(c p) f -> p c f", p=128), "ld_uc", writes=[res("ucs")])
        P.xdma("sync", tcc[:, 0], tcC.rearrange("(c p) k -> p c k", p=128), "ld_tcc", writes=[res("tcc")])
        P.xdma("sync", tcc[:, 1], tcS.rearrange("(c p) k -> p c k", p=128), "ld_tcs", writes=[res("tcc")])
        P.op("vector", lambda e: e.tensor_copy(out=ucb[:], in_=ucs[:]), [res("ucs")], [res("ucb")])
        for g in range(4):
            for t in range(2):
                P.group([lambda e, g=g, t=t, ci=ci: e.matmul(bank[g * 2 + t][:, :32], lhsT=ucb[:, ci, g * 128:(g + 1) * 128],
                                                            rhs=tcc[:, t, ci, :], start=(ci == 0), stop=(ci == 1)) for ci in range(2)],
                        [res("ucb"), res("tcc")], [res(f"bank{g * 2 + t}")])
                P.op("vector", lambda e, g=g, t=t: e.tensor_copy(out=AB[:, g, t, :32], in_=bank[g * 2 + t][:, :32]),
                     [res(f"bank{g * 2 + t}")], [res("AB")])
        stage2(32, 1024, lambda g, t: AB[:, g, t, :32])
        t_o = P.xdma("sync", fT.rearrange("(g p) t -> p g t", p=128), fo[:], "st_f", reads=[res("fo")])
        P.finish([t_o])
    return nc


def fourier_tables():
    n = np.arange(SEQ, dtype=np.int64)
    ph = (np.outer(n, n) % SEQ).astype(np.float64) * (2 * np.pi / SEQ)
    sc = 1.0 / np.sqrt(SEQ * 128.0)
    C = (np.cos(ph) * sc).astype(NPBF)
    S = (np.sin(ph) * sc).astype(NPBF)
    m = np.arange(CTX, dtype=np.int64)
    phc = (np.outer(m, m) % CTX).astype(np.float64) * (2 * np.pi / CTX)
    scc = 1.0 / np.sqrt(CTX * 128.0)
    Cc_ = (np.cos(phc) * scc).astype(NPBF)
    Sc_ = (np.sin(phc) * scc).astype(NPBF)
    j = np.arange(128, dtype=np.int64)
    p128 = (np.outer(j, j) % 128).astype(np.float64) * (2 * np.pi / 128)
    cc = np.stack([np.cos(p128), -np.sin(p128)], 1).astype(np.float32)
    return C, S, Cc_, Sc_, np.ascontiguousarray(cc)


def build_G():
    nc = bass.Bass("TRN2", target_bir_lowering=False)
    xT = nc.dram_tensor("xT", [D, TOK], F32, kind="ExternalInput").ap()
    fTd = nc.dram_tensor("fT", [512, TOK], F32, kind="ExternalInput").ap()
    yfd = nc.dram_tensor("yf", [1536, TOK], F32, kind="ExternalInput").ap()
    ybd = nc.dram_tensor("yb", [1536, TOK], F32, kind="ExternalInput").ap()
    zd = nc.dram_tensor("zT", [1536, TOK], F32, kind="ExternalInput").ap()
    gnd = nc.dram_tensor("gn", [128, 12], F32, kind="ExternalInput").ap()
    gtd = nc.dram_tensor("gt", [128, 16, 2], F32, kind="ExternalInput").ap()
    wd = nc.dram_tensor("w", [D, D], BF16, kind="ExternalInput").ap()
    xo = nc.dram_tensor("xo", [D, TOK], F32, kind="ExternalOutput").ap()
    with contextlib.ExitStack() as st:
        sb = lambda n, s, d: st.enter_context(nc.sbuf_tensor(n, s, d))
        catT = sb("catT", [128, 16, TOK], BF16)
        gn = sb("gn_s", [128, 12], F32)
        gt = sb("gt_s", [128, 16, 2], F32)
        epsc = sb("epsc", [128, 1], F32)
        ones = sb("ones", [128, 128], BF16)
        fst = [sb(f"fst{i}", [128, TOK], F32) for i in range(2)]
        yfs = [sb(f"yfs{i}", [128, TOK], F32) for i in range(2)]
        ybs = [sb(f"ybs{i}", [128, TOK], F32) for i in range(2)]
        zs = [sb(f"zs{i}", [128, TOK], F32) for i in range(2)]
        uu = sb("uu", [128, 3, TOK], F32)
        sq = [sb(f"sq{i}", [128, TOK], BF16) for i in range(2)]
        rstd = sb("rstd", [128, TOK], F32)
        tmp = [sb(f"tmp{i}", [128, TOK], F32) for i in range(2)]
        wb = [sb(f"wb{i}", [128, 16, 512], BF16) for i in range(2)]
        xc = [sb(f"xc{i}", [128, TOK], F32) for i in range(3)]
        ps_ms = st.enter_context(nc.psum_tensor("ps_ms", [128, 3, 512], F32))
        ps_mm = st.enter_context(nc.psum_tensor("ps_mm", [128, 5, 512], F32))
        P = TP(nc)
        R = {}

        def res(name):
            if name not in R:
                R[name] = Res()
            return R[name]

        P.xdma("sync", gn[:], gnd, "ld_gn", writes=[res("gn")])
        P.xdma("sync", gt[:], gtd, "ld_gt", writes=[res("gt")])
        P.op("vector", lambda e: e.memset(ones[:], 1.0), [], [res("ones")])
        P.op("vector", lambda e: e.memset(epsc[:], EPS), [], [res("epsc")])
        for c in range(4):
            b = c % 2
            P.xdma("sync", fst[b][:], fTd[c * 128:(c + 1) * 128, :], f"ld_f{b}", writes=[res(f"fst{b}")])
            P.op("vector" if c % 2 == 0 else "scalar",
                 (lambda e, c=c, b=b: e.tensor_copy(out=catT[:, c, :], in_=fst[b][:])) if c % 2 == 0 else
                 (lambda e, c=c, b=b: e.copy(out=catT[:, c, :], in_=fst[b][:])),
                 [res(f"fst{b}")], [res("catT")])
        n = 0
        for grp in range(4):
            for c in range(3):
                ch = grp * 3 + c
                b = n % 2
                n += 1
                rows = slice(ch * 128, (ch + 1) * 128)
                P.xdma("sync", yfs[b][:], yfd[rows, :], f"ld_yf{b}", writes=[res(f"yfs{b}")])
                P.xdma("gpsimd", ybs[b][:], ybd[rows, :], f"ld_yb{b}", writes=[res(f"ybs{b}")])
                P.xdma("sync", zs[b][:], zd[rows, :], f"ld_z{b}", writes=[res(f"zs{b}")])
                P.op("vector", lambda e, b=b: e.tensor_tensor(out=yfs[b][:], in0=yfs[b][:], in1=ybs[b][:], op=ALU.add),
                     [res(f"yfs{b}"), res(f"ybs{b}")], [res(f"yfs{b}")])
                P.op("scalar", lambda e, b=b: e.activation(out=zs[b][:], in_=zs[b][:], func=AF.Silu), [res(f"zs{b}")], [res(f"zs{b}")])
                P.op("vector", lambda e, b=b, c=c: e.tensor_tensor(out=uu[:, c, :], in0=yfs[b][:], in1=zs[b][:], op=ALU.mult),
                     [res(f"yfs{b}"), res(f"zs{b}")], [res(f"uu{c}")])
                P.op("scalar", lambda e, b=b, c=c: e.activation(out=sq[b][:], in_=uu[:, c, :], func=AF.Square),
                     [res(f"uu{c}")], [res(f"sq{b}")])
                P.group([lambda e, b=b, c=c, bi=bi, s0=s0, w=w: e.matmul(ps_ms[:, bi, :w], lhsT=ones[:], rhs=sq[b][:, s0:s0 + w],
                                                                        start=(c == 0), stop=(c == 2))
                         for bi, (s0, w) in enumerate(BLKS)], [res("ones"), res(f"sq{b}")], [res("ps_ms")])
            for bi, (s0, w) in enumerate(BLKS):
                P.op("scalar", lambda e, bi=bi, s0=s0, w=w: e.activation(out=rstd[:, s0:s0 + w], in_=ps_ms[:, bi, :w], func=AF.Sqrt,
                                                                         bias=epsc[:, 0:1], scale=1.0 / 384.0),
                     [res("ps_ms"), res("epsc")], [res("rstd")])
            P.op("vector", lambda e: e.reciprocal(out=rstd[:], in_=rstd[:]), [res("rstd")], [res("rstd")])
            for c in range(3):
                ch = grp * 3 + c
                b = c % 2
                P.op("vector", lambda e, b=b, c=c: e.tensor_tensor(out=tmp[b][:], in0=uu[:, c, :], in1=rstd[:], op=ALU.mult),
                     [res(f"uu{c}"), res("rstd")], [res(f"tmp{b}")])
                P.op("scalar", lambda e, b=b, ch=ch: e.activation(out=catT[:, 4 + ch, :], in_=tmp[b][:], func=AF.Copy, scale=gn[:, ch:ch + 1]),
                     [res(f"tmp{b}"), res("gn")], [res("catT")])
        outs = []
        for nb in range(4):
            b = nb % 2
            P.xdma("sync", wb[b][:], wd[:, nb * 512:(nb + 1) * 512].rearrange("(kc p) n -> p kc n", p=128), f"ld_w{b}", writes=[res(f"wb{b}")])
            for j in range(4):
                nch = nb * 4 + j
                xb = nch % 3
                P.xdma("gpsimd", xc[xb][:], xT[nch * 128:(nch + 1) * 128, :], f"ld_x{xb}", writes=[res(f"xc{xb}")])
                for bi, (s0, w) in enumerate(BLKS):
                    pb = (nch * 3 + bi) % 5
                    P.group([lambda e, b=b, j=j, k=k, s0=s0, w=w, pb=pb: e.matmul(
                        ps_mm[:, pb, :w], lhsT=wb[b][:, k, j * 128:(j + 1) * 128], rhs=catT[:, k, s0:s0 + w],
                        start=(k == 0), stop=(k == 15)) for k in range(16)],
                        [res(f"wb{b}"), res("catT")], [res(f"ps_mm{pb}")])
                    P.op("vector", lambda e, xb=xb, nch=nch, bi=bi, s0=s0, w=w, pb=pb: e.scalar_tensor_tensor(
                        out=xc[xb][:, s0:s0 + w], in0=ps_mm[:, pb, :w], scalar=gt[:, nch, (1 if bi == 2 else 0):(2 if bi == 2 else 1)],
                        in1=xc[xb][:, s0:s0 + w], op0=ALU.mult, op1=ALU.add),
                        [res(f"ps_mm{pb}"), res("gt"), res(f"xc{xb}")], [res(f"xc{xb}")])
                outs.append(P.xdma("sync", xo[nch * 128:(nch + 1) * 128, :], xc[xb][:], f"st_x{xb}", reads=[res(f"xc{xb}")]))
        P.finish(outs[-3:])
    return nc


def build_M():
    nc = bass.Bass("TRN2", target_bir_lowering=False)
    xT = nc.dram_tensor("xT", [D, TOK], F32, kind="ExternalInput").ap()
    scd = nc.dram_tensor("sc", [128, 16, 2], F32, kind="ExternalInput").ap()
    shd = nc.dram_tensor("sh", [128, 16, 2], F32, kind="ExternalInput").ap()
    gtd = nc.dram_tensor("gt", [128, 16, 2], F32, kind="ExternalInput").ap()
    gd = nc.dram_tensor("g", [128, 16], F32, kind="ExternalInput").ap()
    w1d = nc.dram_tensor("w1", [D, DFF], BF16, kind="ExternalInput").ap()
    w2d = nc.dram_tensor("w2", [DFF, D], BF16, kind="ExternalInput").ap()
    xo = nc.dram_tensor("xo", [D, TOK], F32, kind="ExternalOutput").ap()
    with contextlib.ExitStack() as st:
        sb = lambda n, s, d: st.enter_context(nc.sbuf_tensor(n, s, d))
        X = sb("X", [128, 16, TOK], F32)
        hT = sb("hT", [128, 16, TOK], BF16)
        m1 = sb("m1", [128, 8, TOK], BF16)
        sc = sb("sc_s", [128, 16, 2], F32)
        sh = sb("sh_s", [128, 16, 2], F32)
        gt = sb("gt_s", [128, 16, 2], F32)
        g = sb("g_s", [128, 16], F32)
        s1 = sb("s1", [128, 16, 3], F32)
        ones = sb("ones", [128, 128], BF16)
        sq = [sb(f"sq{i}", [128, TOK], BF16) for i in range(2)]
        rstd = sb("rstd", [128, TOK], F32)
        tmp = [sb(f"tmp{i}", [128, TOK], F32) for i in range(2)]
        w1b = [sb(f"w1b{i}", [128, 16, 512], BF16) for i in range(2)]
        w2b = [sb(f"w2b{i}", [128, 8, 512], BF16) for i in range(2)]
        rl = [sb(f"rl{i}", [128, 512], F32) for i in range(2)]
        ps_ms = st.enter_context(nc.psum_tensor("ps_ms", [128, 3, 512], F32))
        ps_mm = st.enter_context(nc.psum_tensor("ps_mm", [128, 5, 512], F32))
        P = TP(nc)
        R = {}

        def res(name):
            if name not in R:
                R[name] = Res()
            return R[name]

        t_ones = P.dve(lambda e: e.memset(ones[:], 1.0 / D))
        xr = []
        xv = xT.rearrange("(c p) t -> p c t", p=128)
        for c4 in range(4):
            t = P.dma("sync" if c4 % 2 == 0 else "gpsimd", X[:, c4 * 4:(c4 + 1) * 4, :], xv[:, c4 * 4:(c4 + 1) * 4, :], f"ld_x{c4}")
            xr += [t] * 4
        t1 = P.dma("gpsimd", sc[:], scd, "ld_sc")
        t2 = P.dma("gpsimd", sh[:], shd, "ld_sh")
        t3 = P.dma("gpsimd", g[:], gd, "ld_g")
        t4 = P.dma("gpsimd", gt[:], gtd, "ld_gt")
        h_ready = emit_norm_mod(P, nc, X, hT, sc, sh, g[:], ones, ps_ms, sq, rstd, tmp, s1, xr, [t1, t2, t3, t_ones])
        rh = res("hT")
        rh.w = h_ready[-1]
        rX = [res(f"X{c}") for c in range(16)]
        for c in range(16):
            rX[c].w = xr[c]
            rX[c].r = {h_ready[-1][0]: h_ready[-1][1], "s_dve": P.cnt["s_dve"]}
        res("gt").w = t4
        npb = 0
        for q in range(8):
            for b2 in range(2):
                wi = (q * 2 + b2) % 2
                c0 = q * 1024 + b2 * 512
                P.xdma("sync", w1b[wi][:], w1d[:, c0:c0 + 512].rearrange("(kc p) n -> p kc n", p=128), f"ld_w1{wi}", writes=[res(f"w1b{wi}")])
                for j in range(4):
                    f = b2 * 4 + j
                    for bi, (s0, w) in enumerate(BLKS):
                        pb = npb % 5
                        npb += 1
                        P.group([lambda e, wi=wi, j=j, k=k, s0=s0, w=w, pb=pb: e.matmul(
                            ps_mm[:, pb, :w], lhsT=w1b[wi][:, k, j * 128:(j + 1) * 128], rhs=hT[:, k, s0:s0 + w],
                            start=(k == 0), stop=(k == 15)) for k in range(16)],
                            [res(f"w1b{wi}"), rh], [res(f"ps_mm{pb}")])
                        rb = npb % 2
                        if npb % 2 == 0:
                            P.op("scalar", lambda e, rb=rb, pb=pb, w=w: e.activation(out=rl[rb][:, :w], in_=ps_mm[:, pb, :w], func=AF.Relu),
                                 [res(f"ps_mm{pb}")], [res(f"rl{rb}")])
                            P.op("vector", lambda e, rb=rb, f=f, s0=s0, w=w: e.tensor_tensor(out=m1[:, f, s0:s0 + w], in0=rl[rb][:, :w], in1=rl[rb][:, :w], op=ALU.mult),
                                 [res(f"rl{rb}")], [res(f"m1_{f}")])
                        else:
                            P.op("vector", lambda e, rb=rb, pb=pb, w=w: e.tensor_scalar_max(out=rl[rb][:, :w], in0=ps_mm[:, pb, :w], scalar1=0.0),
                                 [res(f"ps_mm{pb}")], [res(f"rl{rb}")])
                            P.op("scalar", lambda e, rb=rb, f=f, s0=s0, w=w: e.activation(out=m1[:, f, s0:s0 + w], in_=rl[rb][:, :w], func=AF.Square),
                                 [res(f"rl{rb}")], [res(f"m1_{f}")])
            for nb in range(4):
                wi = (q * 4 + nb) % 2
                P.xdma("gpsimd", w2b[wi][:], w2d[q * 1024:(q + 1) * 1024, nb * 512:(nb + 1) * 512].rearrange("(fc p) n -> p fc n", p=128),
                       f"ld_w2{wi}", writes=[res(f"w2b{wi}")])
                for j in range(4):
                    nch = nb * 4 + j
                    for bi, (s0, w) in enumerate(BLKS):
                        pb = npb % 5
                        npb += 1
                        P.group([lambda e, wi=wi, j=j, f=f, s0=s0, w=w, pb=pb: e.matmul(
                            ps_mm[:, pb, :w], lhsT=w2b[wi][:, f, j * 128:(j + 1) * 128], rhs=m1[:, f, s0:s0 + w],
                            start=(f == 0), stop=(f == 7)) for f in range(8)],
                            [res(f"w2b{wi}")] + [res(f"m1_{ff}") for ff in range(8)], [res(f"ps_mm{pb}")])
                        P.op("vector", lambda e, nch=nch, bi=bi, s0=s0, w=w, pb=pb: e.scalar_tensor_tensor(
                            out=X[:, nch, s0:s0 + w], in0=ps_mm[:, pb, :w], scalar=gt[:, nch, (1 if bi == 2 else 0):(2 if bi == 2 else 1)],
                            in1=X[:, nch, s0:s0 + w], op0=ALU.mult, op1=ALU.add),
                            [res(f"ps_mm{pb}"), res("gt"), rX[nch]], [rX[nch]])
        outs = []
        xov = xo.rearrange("(c p) t -> p c t", p=128)
        for c4 in range(4):
            outs.append(P.xdma("sync" if c4 % 2 == 0 else "gpsimd", xov[:, c4 * 4:(c4 + 1) * 4, :], X[:, c4 * 4:(c4 + 1) * 4, :], f"st_x{c4}",
                               reads=[rX[c] for c in range(c4 * 4, c4 * 4 + 4)]))
        P.finish(outs)
    return nc


def build_N():
    nc = bass.Bass("TRN2", target_bir_lowering=False)
    xT = nc.dram_tensor("xT", [D, TOK], F32, kind="ExternalInput").ap()
    gd = nc.dram_tensor("g", [128, 16], F32, kind="ExternalInput").ap()
    xo = nc.dram_tensor("xo", [D, TOK], F32, kind="ExternalOutput").ap()
    with contextlib.ExitStack() as st:
        sb = lambda n, s, d: st.enter_context(nc.sbuf_tensor(n, s, d))
        X = sb("X", [128, 16, TOK], F32)
        hT = sb("hT", [128, 16, TOK], F32)
        zz = sb("zz", [128, 16, 2], F32)
        g = sb("g_s", [128, 16], F32)
        s1 = sb("s1", [128, 16, 3], F32)
        ones = sb("ones", [128, 128], BF16)
        sq = [sb(f"sq{i}", [128, TOK], BF16) for i in range(2)]
        rstd = sb("rstd", [128, TOK], F32)
        tmp = [sb(f"tmp{i}", [128, TOK], F32) for i in range(2)]
        ps_ms = st.enter_context(nc.psum_tensor("ps_ms", [128, 3, 512], F32))
        P = Prog(nc)
        t_ones = P.dve(lambda e: e.memset(ones[:], 1.0 / D))
        t_z = P.dve(lambda e: e.memset(zz[:], 0.0))
        xr = []
        xv = xT.rearrange("(c p) t -> p c t", p=128)
        for c4 in range(4):
            t = P.dma("sync" if c4 % 2 == 0 else "gpsimd", X[:, c4 * 4:(c4 + 1) * 4, :], xv[:, c4 * 4:(c4 + 1) * 4, :], f"ld_x{c4}")
            xr += [t] * 4
        t3 = P.dma("gpsimd", g[:], gd, "ld_g")
        h_ready = emit_norm_mod(P, nc, X, hT, zz, zz, g[:], ones, ps_ms, sq, rstd, tmp, s1, xr, [t3, t_ones, t_z])
        outs = []
        xov = xo.rearrange("(c p) t -> p c t", p=128)
        for c4 in range(4):
            outs.append(P.dma("sync", xov[:, c4 * 4:(c4 + 1) * 4, :], hT[:, c4 * 4:(c4 + 1) * 4, :], f"st_x{c4}", [h_ready[c4 * 4 + 3]]))
        P.finish(outs)
    return nc


_CACHE = {}


def _prog(name, builder):
    if name not in _CACHE:
        _CACHE[name] = builder()
    return _CACHE[name]


def _mod2(mods, l, k):
    return np.ascontiguousarray(np.stack([fm(mods[l, 0, k * D:(k + 1) * D]), fm(mods[l, 1, k * D:(k + 1) * D])], -1))


def kernel(x, c, ctx, c_ctx, w_ada, b_ada, g_mix, w_in, conv_w, conv_b, dt_bias, a_log, d_skip,
           g_ssd_norm, w_fourier, w_out, g_mlp, w_mlp1, w_mlp2, g_final, _nlayers=DEPTH, _debug=None):
    f32 = lambda a: np.ascontiguousarray(np.asarray(a, dtype=np.float32))
    x, c, ctx, c_ctx = f32(x), f32(c), f32(ctx), f32(c_ctx)
    w_ada, b_ada, g_mix, w_in = f32(w_ada), f32(b_ada), f32(g_mix), f32(w_in)
    conv_w, conv_b, dt_bias, a_log, d_skip = f32(conv_w), f32(conv_b), f32(dt_bias), f32(a_log), f32(d_skip)
    g_ssd_norm, w_fourier, w_out, g_mlp = f32(g_ssd_norm), f32(w_fourier), f32(w_out), f32(g_mlp)
    w_mlp1, w_mlp2, g_final = f32(w_mlp1), f32(w_mlp2), f32(g_final)

    ws = [w_in, w_out, w_mlp1, w_mlp2]
    flat = np.concatenate([w.reshape(-1) for w in ws])
    fb = run_cast(flat)
    wb = []
    o = 0
    for w in ws:
        wb.append(fb[o:o + w.size].reshape(w.shape))
        o += w.size
    w_in_b, w_out_b, w1_b, w2_b = wb
    del flat, fb

    mods = run_mods(c, c_ctx, w_ada, b_ada)
    C, S, Cc_, Sc_, cc = fourier_tables()
    tabs = [(np.ascontiguousarray(C[:, 1024 * i:1024 * (i + 1)]), np.ascontiguousarray(S[:, 1024 * i:1024 * (i + 1)]),
             np.ascontiguousarray(Cc_[:, 32 * i:32 * (i + 1)]), np.ascontiguousarray(Sc_[:, 32 * i:32 * (i + 1)])) for i in range(NCORES)]
    del C, S

    xl, xc = x[0], ctx[0]
    xT = [np.ascontiguousarray(np.concatenate([xl[1024 * i:1024 * (i + 1)], xc[32 * i:32 * (i + 1)]], 0).T) for i in range(NCORES)]

    def to_global(per_core):
        R_ = per_core[0].shape[0]
        G = np.empty((R_, NTOK), np.float32)
        for i in range(NCORES):
            G[:, CTX + 1024 * i:CTX + 1024 * (i + 1)] = per_core[i][:, :1024]
            G[:, 32 * i:32 * (i + 1)] = per_core[i][:, 1024:]
        return G

    def to_core(G, i):
        return np.ascontiguousarray(np.concatenate([G[:, CTX + 1024 * i:CTX + 1024 * (i + 1)], G[:, 32 * i:32 * (i + 1)]], 1))

    for l in range(_nlayers):
        ncB = _prog("B", build_B)
        sc1, sh1, gt1 = _mod2(mods, l, 1), _mod2(mods, l, 0), _mod2(mods, l, 2)
        sh2, sc2, gt2 = _mod2(mods, l, 3), _mod2(mods, l, 4), _mod2(mods, l, 5)
        res = _run(ncB, [{"xT": xT[i], "sc": sc1, "sh": sh1, "g": fm(g_mix[l]), "w": w_in_b[l]} for i in range(NCORES)])
        pT = [r["pT"] for r in res]
        PT = to_global(pT)
        ncC = _prog("C", build_ssd)
        res = _run(ncC, ssd_inputs(PT, conv_w[l], conv_b[l], dt_bias[l], a_log[l], d_skip[l]))
        YF = np.concatenate([r["Y"][0].T for r in res], 0)
        YB = np.concatenate([r["Y"][1].T for r in res], 0)
        ncF = _prog("F", build_fourier)
        uL = np.ascontiguousarray(PT[0:512, CTX:].T)
        uC = np.ascontiguousarray(PT[0:512, :CTX].T)
        res = _run(ncF, [{"uL": uL, "uC": uC, "tabC": tabs[i][0], "tabS": tabs[i][1], "tcC": tabs[i][2], "tcS": tabs[i][3],
                          "wf": w_fourier[l], "cc": cc} for i in range(NCORES)])
        fT = [r["fT"] for r in res]
        ncG = _prog("G", build_G)
        res = _run(ncG, [{"xT": xT[i], "fT": fT[i], "yf": to_core(YF, i), "yb": to_core(YB, i),
                          "zT": np.ascontiguousarray(pT[i][512:2048]), "gn": fm(g_ssd_norm[l]), "gt": gt1, "w": w_out_b[l]}
                         for i in range(NCORES)])
        xm = [r["xo"] for r in res]
        if _debug is not None:
            _debug[f"xm{l}"] = xm
        ncM = _prog("M", build_M)
        res = _run(ncM, [{"xT": xm[i], "sc": sc2, "sh": sh2, "gt": gt2, "g": fm(g_mlp[l]), "w1": w1_b[l], "w2": w2_b[l]}
                         for i in range(NCORES)])
        xT = [r["xo"] for r in res]
        if _debug is not None:
            _debug[f"xo{l}"] = xT
    ncN = _prog("N", build_N)
    res = _run(ncN, [{"xT": xT[i], "g": fm(g_final)} for i in range(NCORES)])
    out = np.empty((1, SEQ, D), np.float32)
    for i in range(NCORES):
        out[0, 1024 * i:1024 * (i + 1), :] = res[i]["xo"][:, :1024].T
    return out
```

```python
import numpy as np
import ml_dtypes
import concourse.bass as bass
import concourse.mybir as mybir
from concourse.bass_utils import run_bass_kernel_spmd

F32 = mybir.dt.float32
BF16 = mybir.dt.bfloat16
AF = mybir.ActivationFunctionType
ALU = mybir.AluOpType
AX = mybir.AxisListType
NPBF = ml_dtypes.bfloat16

NCORES = 8
D = 2048
SEQ = 8192
CTX = 256
DEPTH = 4
DIN = 4656
DFF = 8192
TOK = 1056
EPS = 1e-6


class Prog:
    ENGS = ("sync", "scalar", "vector", "gpsimd", "tensor")

    def __init__(self, nc):
        self.nc = nc
        self.q = {e: [] for e in self.ENGS}
        self.cnt = {}
        self.waited = {e: {} for e in self.ENGS}

    def emit(self, eng, fn, waits=(), sig=None, inc=1):
        ws = []
        for w in waits:
            if w is None:
                continue
            s, v = w
            if self.waited[eng].get(s, 0) >= v:
                continue
            self.waited[eng][s] = v
            ws.append((s, v))
        tok = None
        if sig is not None:
            self.cnt[sig] = self.cnt.get(sig, 0) + inc
            tok = (sig, self.cnt[sig])
        self.q[eng].append((fn, ws, sig, inc))
        return tok

    def pe(self, fn, waits=(), sig=True):
        return self.emit("tensor", fn, waits, "s_pe" if sig else None)

    def act(self, fn, waits=(), sig=True):
        return self.emit("scalar", fn, waits, "s_act" if sig else None)

    def dve(self, fn, waits=(), sig=True):
        return self.emit("vector", fn, waits, "s_dve" if sig else None)

    def pool(self, fn, waits=(), sig=True):
        return self.emit("gpsimd", fn, waits, "s_pool" if sig else None)

    def dma(self, q, out, in_, sem, waits=()):
        return self.emit(q, lambda e: e.dma_start(out=out, in_=in_), waits, sem, 16)

    def simulate(self):
        pos = {e: 0 for e in self.ENGS}
        cnt = {}
        while True:
            prog = False
            for e in self.ENGS:
                while pos[e] < len(self.q[e]):
                    fn, ws, sig, inc = self.q[e][pos[e]]
                    if all(cnt.get(s_, 0) >= v for s_, v in ws):
                        if sig is not None:
                            cnt[sig] = cnt.get(sig, 0) + inc
                        pos[e] += 1
                        prog = True
                    else:
                        break
            if all(pos[e] == len(self.q[e]) for e in self.ENGS):
                return True
            if not prog:
                for e in self.ENGS:
                    if pos[e] < len(self.q[e]):
                        fn, ws, sig, inc = self.q[e][pos[e]]
                        print("DEADLOCK", e, pos[e], len(self.q[e]), [(s_, v, cnt.get(s_, 0)) for s_, v in ws], sig)
                return False

    def finish(self, final_waits):
        import os
        if os.environ.get("PROG_SIM"):
            print("SIM", self.simulate(), {e: len(self.q[e]) for e in self.ENGS})
        nc = self.nc
        names = sorted(self.cnt.keys())
        sems = {}
        import contextlib
        with contextlib.ExitStack() as st:
            for n in names:
                sems[n] = st.enter_context(nc.semaphore(n))
            block = st.enter_context(nc.Block())
            q = self.q
            q["sync"].append((None, [w for w in final_waits if w is not None], None, 0))

            def runner(ename):
                def _(eng):
                    for fn, ws, sig, inc in q[ename]:
                        for s, v in ws:
                            eng.wait_ge(sems[s], v)
                        if fn is None:
                            continue
                        ins = fn(eng)
                        if sig is not None:
                            ins.then_inc(sems[sig], inc)
                return _
            block.sync(runner("sync"))
            block.scalar(runner("scalar"))
            block.vector(runner("vector"))
            block.gpsimd(runner("gpsimd"))
            block.tensor(runner("tensor"))


def _run(nc, in_maps):
    res = run_bass_kernel_spmd(nc, in_maps, core_ids=list(range(NCORES)))
    return res.results


def build_cast(F):
    nc = bass.Bass("TRN2", target_bir_lowering=False)
    src = nc.dram_tensor("src", [128, F], F32, kind="ExternalInput").ap()
    dst = nc.dram_tensor("dst", [128, F], BF16, kind="ExternalOutput").ap()
    T = 4096
    nt = (F + T - 1) // T
    NB = 3
    import contextlib
    with contextlib.ExitStack() as st:
        ins = [st.enter_context(nc.sbuf_tensor(f"in{i}", [128, T], F32)) for i in range(NB)]
        outs = [st.enter_context(nc.sbuf_tensor(f"out{i}", [128, T], BF16)) for i in range(NB)]
        P = Prog(nc)
        cast_tok = [None] * NB
        st_tok = [None] * NB
        for t in range(nt):
            b = t % NB
            w = min(T, F - t * T)
            ld = P.dma("sync", ins[b][:, :w], src[:, t * T:t * T + w], f"ld{b}", [cast_tok[b]])
            if t % 2 == 0:
                cast_tok[b] = P.dve(lambda e, b=b, w=w: e.tensor_copy(out=outs[b][:, :w], in_=ins[b][:, :w]),
                                    [ld, st_tok[b]])
            else:
                cast_tok[b] = P.act(lambda e, b=b, w=w: e.copy(out=outs[b][:, :w], in_=ins[b][:, :w]),
                                    [ld, st_tok[b]])
            st_tok[b] = P.dma("gpsimd", dst[:, t * T:t * T + w], outs[b][:, :w], f"st{b}", [cast_tok[b]])
        P.finish(st_tok)
    return nc


def run_cast(flat):
    n = flat.size
    assert n % (NCORES * 128) == 0
    F = n // (NCORES * 128)
    nc = build_cast(F)
    sh = flat.reshape(NCORES, 128, F)
    res = _run(nc, [{"src": np.ascontiguousarray(sh[i])} for i in range(NCORES)])
    return np.stack([r["dst"] for r in res]).reshape(-1)


import contextlib


def fm(v):
    v = np.asarray(v)
    n = v.shape[-1] // 128
    lead = v.shape[:-1]
    a = v.reshape(lead + (n, 128))
    a = np.moveaxis(a, -1, 0)
    return np.ascontiguousarray(a)


def build_mods():
    nc = bass.Bass("TRN2", target_bir_lowering=False)
    cc = nc.dram_tensor("cc", [128, 16, 2], F32, kind="ExternalInput").ap()
    w = nc.dram_tensor("w", [DEPTH, D, 1536], F32, kind="ExternalInput").ap()
    b = nc.dram_tensor("b", [128, DEPTH, 12], F32, kind="ExternalInput").ap()
    out = nc.dram_tensor("out", [128, DEPTH, 12, 2], F32, kind="ExternalOutput").ap()
    with contextlib.ExitStack() as st:
        sb = lambda n, s, d: st.enter_context(nc.sbuf_tensor(n, s, d))
        cs = sb("cs", [128, 16, 2], F32)
        ss = sb("ss", [128, 16, 2], F32)
        bs = sb("bs", [128, DEPTH, 12], F32)
        os_ = sb("os", [128, DEPTH, 12, 2], F32)
        wb = [sb(f"wb{i}", [128, 16, 512], F32) for i in range(2)]
        ps = st.enter_context(nc.psum_tensor("ps", [128, 2, 512], F32))
        P = Prog(nc)
        t_c = P.dma("sync", cs[:], cc, "ld_c")
        t_b = P.dma("sync", bs[:], b, "ld_b")
        t_s = P.act(lambda e: e.activation(out=ss[:], in_=cs[:], func=AF.Silu), [t_c])
        wfree = [None, None]
        evs = []
        n = 0
        for l in range(DEPTH):
            for blk in range(3):
                bi = n % 2
                ld = P.dma("sync" if n % 2 == 0 else "gpsimd", wb[bi][:],
                           w[l, :, blk * 512:(blk + 1) * 512].rearrange("(kc p) n -> p kc n", p=128),
                           f"ld_w{bi}", [wfree[bi]])
                for j in range(4):
                    jj = blk * 4 + j
                    for k in range(16):
                        t_mm = P.pe(lambda e, bi=bi, j=j, k=k, jj=jj, n=n: e.matmul(
                            ps[:, n % 2, jj * 2:jj * 2 + 2], lhsT=wb[bi][:, k, j * 128:(j + 1) * 128],
                            rhs=ss[:, k, :], start=(k == 0), stop=(k == 15)),
                            [ld, t_s] + (evs[-1:] if k == 0 else []), sig=(k == 15))
                    ev = P.dve(lambda e, l=l, jj=jj, n=n: e.tensor_scalar(
                        out=os_[:, l, jj, :], in0=ps[:, n % 2, jj * 2:jj * 2 + 2], scalar1=bs[:, l, jj:jj + 1],
                        scalar2=None, op0=ALU.add), [t_mm, t_b])
                    evs.append(ev)
                wfree[bi] = t_mm
                n += 1
        t_o = P.dma("sync", out, os_[:], "st_o", [evs[-1]])
        P.finish([t_o])
    return nc


def run_mods(c, c_ctx, w_ada, b_ada):
    nc = build_mods()
    cc = np.stack([fm(c.reshape(-1)), fm(c_ctx.reshape(-1))], axis=-1)
    in_maps = []
    for i in range(NCORES):
        sl = slice(1536 * i, 1536 * (i + 1))
        in_maps.append({"cc": cc, "w": np.ascontiguousarray(w_ada[:, :, sl]),
                        "b": fm(b_ada[:, sl])})
    res = _run(nc, in_maps)
    mods = np.zeros((DEPTH, 2, 6 * D), np.float32)
    for i in range(NCORES):
        o = res[i]["out"]
        mods[:, :, 1536 * i:1536 * (i + 1)] = np.transpose(o, (1, 3, 2, 0)).reshape(DEPTH, 2, 1536)
    return mods


BLKS = [(0, 512), (512, 512), (1024, 32)]


def emit_norm_mod(P, nc, X, hT, sc, sh, g, ones, ps_ms, sq, rstd, tmp, s1, x_ready, extra_waits=()):
    ew = list(extra_waits)
    epsc = s1[:, 0, 2:3]
    t_eps = P.dve(lambda e: e.memset(s1[:, :, 2:3], EPS))
    t_s1 = []
    for j in range(2):
        t_s1.append(P.dve(lambda e, j=j: e.scalar_tensor_tensor(
            out=s1[:, :, j], in0=sc[:, :, j], scalar=1.0, in1=g, op0=ALU.add, op1=ALU.mult), ew))
    sq_tok = [None] * len(sq)
    mm_tok = None
    for c in range(16):
        b = c % len(sq)
        xr = x_ready[c] if isinstance(x_ready, list) else x_ready
        t_sq = P.act(lambda e, c=c, b=b: e.activation(out=sq[b][:], in_=X[:, c, :], func=AF.Square),
                     [xr, sq_tok[b]] + ew)
        for bi, (s0, w) in enumerate(BLKS):
            mm_tok = P.pe(lambda e, c=c, b=b, bi=bi, s0=s0, w=w: e.matmul(
                ps_ms[:, bi, :w], lhsT=ones[:], rhs=sq[b][:, s0:s0 + w], start=(c == 0), stop=(c == 15)),
                [t_sq] + ew, sig=(bi == 2))
        sq_tok[b] = mm_tok
    t_r = None
    for bi, (s0, w) in enumerate(BLKS):
        t_q = P.act(lambda e, bi=bi, s0=s0, w=w: e.activation(
            out=rstd[:, s0:s0 + w], in_=ps_ms[:, bi, :w], func=AF.Sqrt, bias=epsc[:, 0:1], scale=1.0), [mm_tok, t_eps])
        t_r = P.dve(lambda e, bi=bi, s0=s0, w=w: e.reciprocal(
            out=rstd[:, s0:s0 + w], in_=rstd[:, s0:s0 + w]), [t_q])
    toks = []
    tmp_tok = [None] * len(tmp)
    for c in range(16):
        b = c % len(tmp)
        t_m = P.dve(lambda e, c=c, b=b: e.tensor_tensor(out=tmp[b][:], in0=X[:, c, :], in1=rstd[:], op=ALU.mult),
                    [t_r, tmp_tok[b]])
        P.act(lambda e, c=c, b=b: e.activation(out=hT[:, c, 0:1024], in_=tmp[b][:, 0:1024], func=AF.Identity,
                                               scale=s1[:, c, 0:1], bias=sh[:, c, 0:1]), [t_m, t_s1[1]], sig=False)
        t_a = P.act(lambda e, c=c, b=b: e.activation(out=hT[:, c, 1024:TOK], in_=tmp[b][:, 1024:TOK], func=AF.Identity,
                                                     scale=s1[:, c, 1:2], bias=sh[:, c, 1:2]), [t_m])
        tmp_tok[b] = t_a
        toks.append(t_a)
    return toks


def emit_proj(P, nc, wdram, nout, hT, h_ready, wbufs, psb, evac, name, kch=16, wfree0=None, ps_free0=None):
    nblk = (nout + 511) // 512
    wfree = list(wfree0) if wfree0 else [None] * len(wbufs)
    ps_free = list(ps_free0) if ps_free0 else [None] * len(psb)
    pi = 0
    evs = []
    last_mm = None
    for nb in range(nblk):
        b = nb % len(wbufs)
        ncol = min(512, nout - nb * 512)
        ld = P.dma("sync" if nb % 2 == 0 else "gpsimd", wbufs[b][:, :, :ncol],
                   wdram[:, nb * 512:nb * 512 + ncol].rearrange("(kc p) n -> p kc n", p=128),
                   f"ld_{name}{b}", [wfree[b]])
        for j in range((ncol + 127) // 128):
            m = min(128, ncol - j * 128)
            for bi, (s0, w) in enumerate(BLKS):
                pb = pi % len(psb)
                pi += 1
                for k in range(kch):
                    last_mm = P.pe(lambda e, b=b, j=j, m=m, k=k, s0=s0, w=w, pb=pb: e.matmul(
                        psb[pb][:m, :w], lhsT=wbufs[b][:, k, j * 128:j * 128 + m], rhs=hT[:, k, s0:s0 + w],
                        start=(k == 0), stop=(k == kch - 1)),
                        ([ld, ps_free[pb]] + list(h_ready)) if k == 0 else [], sig=(k == kch - 1))
                ev = evac(nb * 4 + j, m, bi, s0, w, psb[pb], last_mm)
                ps_free[pb] = ev
                evs.append(ev)
        wfree[b] = last_mm
    return last_mm, evs


def build_B():
    nc = bass.Bass("TRN2", target_bir_lowering=False)
    xT = nc.dram_tensor("xT", [D, TOK], F32, kind="ExternalInput").ap()
    scd = nc.dram_tensor("sc", [128, 16, 2], F32, kind="ExternalInput").ap()
    shd = nc.dram_tensor("sh", [128, 16, 2], F32, kind="ExternalInput").ap()
    gd = nc.dram_tensor("g", [128, 16], F32, kind="ExternalInput").ap()
    wd = nc.dram_tensor("w", [D, DIN], BF16, kind="ExternalInput").ap()
    pT = nc.dram_tensor("pT", [DIN, TOK], F32, kind="ExternalOutput").ap()
    with contextlib.ExitStack() as st:
        sb = lambda n, s, d: st.enter_context(nc.sbuf_tensor(n, s, d))
        X = sb("X", [128, 16, TOK], F32)
        hT = sb("hT", [128, 16, TOK], BF16)
        sc = sb("sc_s", [128, 16, 2], F32)
        sh = sb("sh_s", [128, 16, 2], F32)
        g = sb("g_s", [128, 16], F32)
        s1 = sb("s1", [128, 16, 3], F32)
        ones = sb("ones", [128, 128], BF16)
        sq = [sb(f"sq{i}", [128, TOK], BF16) for i in range(3)]
        rstd = sb("rstd", [128, TOK], F32)
        tmp = [sb(f"tmp{i}", [128, TOK], F32) for i in range(2)]
        wb = [sb(f"wb{i}", [128, 16, 512], BF16) for i in range(2)]
        ot = [sb(f"ot{i}", [128, TOK], F32) for i in range(3)]
        ps_ms = st.enter_context(nc.psum_tensor("ps_ms", [128, 3, 512], F32))
        ps_mm = st.enter_context(nc.psum_tensor("ps_mm", [128, 5, 512], F32))
        P = Prog(nc)
        t_ones = P.dve(lambda e: e.memset(ones[:], 1.0 / D))
        xr = []
        xv = xT.rearrange("(c p) t -> p c t", p=128)
        for c4 in range(4):
            t = P.dma("sync" if c4 % 2 == 0 else "gpsimd", X[:, c4 * 4:(c4 + 1) * 4, :], xv[:, c4 * 4:(c4 + 1) * 4, :], f"ld_x{c4}")
            xr += [t] * 4
        t1 = P.dma("gpsimd", sc[:], scd, "ld_sc")
        t2 = P.dma("gpsimd", sh[:], shd, "ld_sh")
        t3 = P.dma("gpsimd", g[:], gd, "ld_g")
        h_ready = emit_norm_mod(P, nc, X, hT, sc, sh, g[:], ones, ps_ms, sq, rstd, tmp, s1, xr, [t1, t2, t3, t_ones])

        ot_free = [None] * 3
        state = {"n": 0, "evs": []}
        out_toks = []

        def evac(nchunk, m, bi, s0, w, ps, mm):
            b = nchunk % 3
            if bi == 1:
                tk = P.act(lambda e: e.copy(out=ot[b][:m, s0:s0 + w], in_=ps[:m, :w]), [mm, ot_free[b]])
            else:
                tk = P.dve(lambda e: e.tensor_copy(out=ot[b][:m, s0:s0 + w], in_=ps[:m, :w]), [mm, ot_free[b]])
            state["evs"].append(tk)
            if bi == 2:
                d = P.dma("sync", pT[nchunk * 128:nchunk * 128 + m, :], ot[b][:m, :], f"st_p{b}", state["evs"][-3:])
                ot_free[b] = d
                out_toks.append(d)
            return tk

        emit_proj(P, nc, wd, DIN, hT, h_ready, wb, [ps_mm[:, i, :] for i in range(5)], evac, "w")
        P.finish(out_toks[-3:])
    return nc


class Res:
    __slots__ = ("w", "r")

    def __init__(self):
        self.w = None
        self.r = {}


class TP(Prog):
    def _deps(self, reads, writes):
        waits = []
        for r in reads:
            waits.append(r.w)
        for w in writes:
            waits.append(w.w)
            waits += list(w.r.items())
        return waits

    def _mark(self, tok, reads, writes):
        for r in reads:
            s, v = tok
            if r.r.get(s, 0) < v:
                r.r[s] = v
        for w in writes:
            w.w = tok
            w.r = {}

    def op(self, eng, fn, reads=(), writes=()):
        sem = {"tensor": "s_pe", "scalar": "s_act", "vector": "s_dve", "gpsimd": "s_pool"}[eng]
        tok = self.emit(eng, fn, self._deps(reads, writes), sem)
        self._mark(tok, reads, writes)
        return tok

    def group(self, fns, reads=(), writes=()):
        waits = self._deps(reads, writes)
        tok = None
        for i, fn in enumerate(fns):
            tok = self.emit("tensor", fn, waits if i == 0 else (), "s_pe" if i == len(fns) - 1 else None)
        self._mark(tok, reads, writes)
        return tok

    def xdma(self, q, out, in_, sem, reads=(), writes=()):
        tok = self.emit(q, lambda e: e.dma_start(out=out, in_=in_), self._deps(reads, writes), sem, 16)
        self._mark(tok, reads, writes)
        return tok


NCH = 66
NTOK = NCH * 128


def build_ssd():
    nc = bass.Bass("TRN2", target_bir_lowering=False)
    uT = nc.dram_tensor("uT", [448, NTOK], F32, kind="ExternalInput").ap()
    dtr = nc.dram_tensor("dtr", [128, 2, NCH, 3], F32, kind="ExternalInput").ap()
    cwd = nc.dram_tensor("cw", [128, 4, 5], F32, kind="ExternalInput").ap()
    cbd = nc.dram_tensor("cb", [128, 4], F32, kind="ExternalInput").ap()
    smd = nc.dram_tensor("sm", [128, 15], F32, kind="ExternalInput").ap()
    cst = nc.dram_tensor("cst", [128, 8, 128], F32, kind="ExternalInput").ap()
    Y = nc.dram_tensor("Y", [2, NTOK, 192], F32, kind="ExternalOutput").ap()
    with contextlib.ExitStack() as st:
        sb = lambda n, s, d: st.enter_context(nc.sbuf_tensor(n, s, d))
        BT = sb("BT", [128, NTOK], BF16)
        CT = sb("CT", [128, NTOK], BF16)
        Btm = sb("Btm", [128, NCH, 128], BF16)
        Xtm = sb("Xtm", [128, NCH, 192], F32)
        cw = sb("cw_s", [128, 4, 5], F32)
        cb = sb("cb_s", [128, 4], F32)
        sm = sb("sm_s", [128, 15], F32)
        K8 = sb("K8", [128, 8, 128], F32)
        identb = sb("identb", [128, 128], BF16)
        dts = {n: sb(n, [128, 2, NCH, 3], F32) for n in
               ("raw", "t1", "dtv", "dta", "acum", "tot", "sdec", "ea", "dec", "dtsd")}
        avec = sb("avec", [128, 6], F32)
        PIECE = 1024
        raw = [sb(f"rawb{i}", [128, PIECE], F32) for i in range(2)]
        acc = [sb(f"accb{i}", [128, PIECE], F32) for i in range(2)]
        sil = [sb(f"silb{i}", [128, PIECE], F32) for i in range(2)]
        Rb = [sb(f"R{i}", [128, 2, 3, 128], BF16) for i in range(2)]
        K8b = sb("K8b", [128, 4, 128], BF16)
        dth = sb("dth", [128, 2, NCH, 3], BF16)
        dthf = sb("dthf", [128, 2, NCH, 3], F32)
        dtl = sb("dtl", [128, 2, NCH, 3], F32)
        Eb = [sb(f"E{i}", [128, 3, 128], F32) for i in range(2)]
        Mb = [sb(f"M{i}", [128, 3, 128], BF16) for i in range(2)]
        rtmp = [sb(f"rtmp{i}", [128, 192], F32) for i in range(2)]
        CBm = [sb(f"CBm{i}", [128, 128], F32) for i in range(2)]
        xdt = [sb(f"xdt{i}", [128, 192], BF16) for i in range(2)]
        xs = [sb(f"xs{i}", [128, 192], BF16) for i in range(2)]
        xd = sb("xd", [128, 192], BF16)
        hst = [sb(f"hst{i}", [128, 192], F32) for i in range(2)]
        hbf = [[sb(f"hbf{i}_{j}", [128, 192], BF16) for j in range(2)] for i in range(2)]
        yo_t = [sb(f"yot{i}", [128, 192], F32) for i in range(2)]
        yo = [sb(f"yo{i}", [128, 192], F32) for i in range(4)]
        ps_seg = [st.enter_context(nc.psum_tensor(f"ps_seg{i}", [128, 512], F32)) for i in range(2)]
        ps_cb = st.enter_context(nc.psum_tensor("ps_cb", [128, 512], F32))
        ps_y = [st.enter_context(nc.psum_tensor(f"ps_y{i}", [128, 512], F32)) for i in range(2)]
        ps_st = st.enter_context(nc.psum_tensor("ps_st", [128, 512], F32))
        ps_tr = [st.enter_context(nc.psum_tensor(f"ps_tr{i}", [128, 512], F32)) for i in range(2)]

        P = TP(nc)
        R = {}

        def res(name):
            if name not in R:
                R[name] = Res()
            return R[name]

        Tm, Um, SU, SL, MF, MB, ID, ON = [K8[:, i, :] for i in range(8)]
        P.xdma("sync", K8[:], cst, "ld_k8", writes=[res("K8")])
        P.xdma("sync", cw[:], cwd, "ld_cw", writes=[res("cw")])
        P.xdma("sync", cb[:], cbd, "ld_cb", writes=[res("cb")])
        P.xdma("sync", sm[:], smd, "ld_sm", writes=[res("sm")])
        P.xdma("sync", dts["raw"][:], dtr, "ld_dt", writes=[res("raw")])
        P.op("vector", lambda e: e.tensor_copy(out=identb[:], in_=ID), [res("K8")], [res("identb")])
        P.op("vector", lambda e: e.tensor_copy(out=K8b[:], in_=K8[:, 0:4, :]), [res("K8")], [res("K8b")])

        fl = lambda n: dts[n][:].rearrange("p a c j -> p (a c j)")
        col = lambda n, d, j: dts[n][:, d, :, j]
        for d in range(2):
            for j in range(3):
                P.op("vector", lambda e, d=d, j=j: e.tensor_scalar(
                    out=col("raw", d, j), in0=col("raw", d, j), scalar1=sm[:, d * 3 + j:d * 3 + j + 1], scalar2=None,
                    op0=ALU.add), [res("raw"), res("sm")], [res("raw")])
        P.op("scalar", lambda e: e.activation(out=fl("t1"), in_=fl("raw"), func=AF.Abs), [res("raw")], [res("t1")])
        P.op("scalar", lambda e: e.activation(out=fl("t1"), in_=fl("t1"), func=AF.Exp, scale=-1.0), [res("t1")], [res("t1")])
        P.op("vector", lambda e: e.tensor_scalar_add(out=fl("t1"), in0=fl("t1"), scalar1=1.0), [res("t1")], [res("t1")])
        P.op("scalar", lambda e: e.activation(out=fl("t1"), in_=fl("t1"), func=AF.Ln), [res("t1")], [res("t1")])
        P.op("vector", lambda e: e.scalar_tensor_tensor(out=fl("dtv"), in0=fl("raw"), scalar=0.0, in1=fl("t1"),
                                                         op0=ALU.max, op1=ALU.add), [res("raw"), res("t1")], [res("dtv")])
        P.op("scalar", lambda e: e.activation(out=avec[:], in_=sm[:, 6:12], func=AF.Exp), [res("sm")], [res("avec")])
        P.op("vector", lambda e: e.tensor_scalar_mul(out=avec[:], in0=avec[:], scalar1=-1.0), [res("avec")], [res("avec")])
        for d in range(2):
            for j in range(3):
                P.op("vector", lambda e, d=d, j=j: e.tensor_scalar(
                    out=col("dta", d, j), in0=col("dtv", d, j), scalar1=avec[:, d * 3 + j:d * 3 + j + 1], scalar2=None,
                    op0=ALU.mult), [res("dtv"), res("avec")], [res("dta")])
        fl2 = lambda t: t[:].rearrange("p a c j -> p (a c j)")
        P.op("vector", lambda e: e.tensor_copy(out=fl2(dth), in_=fl("dta")), [res("dta")], [res("dth")])
        P.op("vector", lambda e: e.tensor_copy(out=fl2(dthf), in_=fl2(dth)), [res("dth")], [res("dthf")])
        P.op("vector", lambda e: e.tensor_tensor(out=fl2(dtl), in0=fl("dta"), in1=fl2(dthf), op=ALU.subtract), [res("dta"), res("dthf")], [res("dtl")])
        for d in range(2):
            lhs = Tm if d == 0 else Um
            P.group([lambda e, d=d, lhs=lhs: e.matmul(ps_tr[0][:, d * 256:d * 256 + 198], lhsT=lhs,
                                                       rhs=dts["dta"][:, d].rearrange("p c j -> p (c j)"),
                                                       start=True, stop=True)],
                    [res("K8"), res("dta")], [res("ps_tr0")])
            P.group([lambda e, d=d: e.matmul(ps_tr[1][:, d * 256:d * 256 + 198], lhsT=ON,
                                             rhs=dts["dta"][:, d].rearrange("p c j -> p (c j)"),
                                             start=True, stop=True)],
                    [res("K8"), res("dta")], [res("ps_tr1")])
        for d in range(2):
            P.op("vector", lambda e, d=d: e.tensor_copy(out=dts["acum"][:, d].rearrange("p c j -> p (c j)"),
                                                       in_=ps_tr[0][:, d * 256:d * 256 + 198]),
                 [res("ps_tr0")], [res("acum")])
            P.op("vector", lambda e, d=d: e.tensor_copy(out=dts["tot"][:, d].rearrange("p c j -> p (c j)"),
                                                       in_=ps_tr[1][:, d * 256:d * 256 + 198]),
                 [res("ps_tr1")], [res("tot")])
        P.op("vector", lambda e: e.tensor_tensor(out=fl("sdec"), in0=fl("tot"), in1=fl("acum"), op=ALU.subtract),
             [res("tot"), res("acum")], [res("sdec")])
        P.op("scalar", lambda e: e.activation(out=fl("sdec"), in_=fl("sdec"), func=AF.Exp), [res("sdec")], [res("sdec")])
        P.op("scalar", lambda e: e.activation(out=fl("ea"), in_=fl("acum"), func=AF.Exp), [res("acum")], [res("ea")])
        P.op("scalar", lambda e: e.activation(out=fl("dec"), in_=fl("tot"), func=AF.Exp), [res("tot")], [res("dec")])
        P.op("vector", lambda e: e.tensor_tensor(out=fl("dtsd"), in0=fl("dtv"), in1=fl("sdec"), op=ALU.mult),
             [res("dtv"), res("sdec")], [res("dtsd")])

        tiles = [(0, 128), (128, 64), (192, 128), (320, 128)]
        pieces = [(0, 256, 256)] + [(256 + i * 1024, 1024, 64) for i in range(8)]
        n = 0
        ntr = 0
        for (t0, nt, rl) in pieces:
            for ti, (r0, npart) in enumerate(tiles):
                b = n % 2
                n += 1
                rw, ac, so = raw[b], acc[b], sil[b]
                rr, ra, rs = res(f"raw{b}"), res(f"acc{b}"), res(f"sil{b}")
                P.xdma("sync" if n % 2 else "gpsimd", rw[:npart, :nt], uT[r0:r0 + npart, t0:t0 + nt], f"ld_raw{b}", writes=[rr])
                v = lambda a, npart=npart, nt=nt, rl=rl: a[:npart, :nt].rearrange("p (r t) -> p r t", t=rl)
                P.op("vector", lambda e, ti=ti, rw=rw, ac=ac, npart=npart, nt=nt: e.tensor_scalar(
                    out=ac[:npart, :nt], in0=rw[:npart, :nt], scalar1=cw[:npart, ti, 2:3], scalar2=None, op0=ALU.mult),
                    [rr, res("cw")], [ra])
                for j in (0, 1, 3, 4):
                    sh_ = j - 2
                    a0, a1 = max(0, -sh_), min(rl, rl - sh_)
                    P.op("vector", lambda e, ti=ti, j=j, v=v, rw=rw, ac=ac, a0=a0, a1=a1, sh_=sh_, npart=npart: e.scalar_tensor_tensor(
                        out=v(ac)[:, :, a0:a1], in0=v(rw)[:, :, a0 + sh_:a1 + sh_], scalar=cw[:npart, ti, j:j + 1],
                        in1=v(ac)[:, :, a0:a1], op0=ALU.mult, op1=ALU.add), [rr, ra, res("cw")], [ra])
                P.op("scalar", lambda e, ti=ti, ac=ac, so=so, npart=npart, nt=nt: e.activation(
                    out=so[:npart, :nt], in_=ac[:npart, :nt], func=AF.Silu, bias=cb[:npart, ti:ti + 1], scale=1.0),
                    [ra, res("cb")], [rs])
                if ti == 2:
                    P.op("vector", lambda e, so=so, nt=nt, t0=t0: e.tensor_copy(out=BT[:, t0:t0 + nt], in_=so[:, :nt]), [rs], [res("BT")])
                if ti == 3:
                    P.op("vector", lambda e, so=so, nt=nt, t0=t0: e.tensor_copy(out=CT[:, t0:t0 + nt], in_=so[:, :nt]), [rs], [res("CT")])
                    continue
                for cc in range(nt // 128):
                    ch = t0 // 128 + cc
                    pt = ntr % 2
                    ntr += 1
                    rp = res(f"ps_tr{pt}")
                    P.group([lambda e, so=so, cc=cc, npart=npart, pt=pt: e.transpose(
                        ps_tr[pt][:, :npart], so[:npart, cc * 128:(cc + 1) * 128], K8[:npart, 6, :npart])],
                        [rs, res("K8")], [rp])
                    if ti == 2:
                        P.op("scalar", lambda e, ch=ch, pt=pt: e.copy(out=Btm[:, ch, :], in_=ps_tr[pt][:, :128]), [rp], [res("Btm")])
                    else:
                        c0 = 0 if ti == 0 else 128
                        P.op("vector" if ti == 0 else "scalar",
                             (lambda e, ch=ch, pt=pt, c0=c0, npart=npart: e.tensor_copy(out=Xtm[:, ch, c0:c0 + npart], in_=ps_tr[pt][:, :npart]))
                             if ti == 0 else
                             (lambda e, ch=ch, pt=pt, c0=c0, npart=npart: e.copy(out=Xtm[:, ch, c0:c0 + npart], in_=ps_tr[pt][:, :npart])),
                             [rp], [res("Xtm")])

        for i in range(2):
            P.op("vector", lambda e, i=i: e.memset(hst[i][:], 0.0), [], [res(f"hst{i}")])
            P.op("vector", lambda e, i=i: e.memset(hbf[i][0][:], 0.0), [], [res(f"hbf{i}_0")])
        border = [1, 0] + list(range(65, 1, -1))
        dvec = sm[:, 12:15]
        out_tok = []
        h3 = lambda ap: ap.rearrange("p (h c) -> p h c", h=3)

        def unit(u):
            step, d = u // 2, u % 2
            ch = step if d == 0 else border[step]
            return step, d, ch

        def front(u):
            front_a(u)
            front_b(u)

        def front_a(u):
            cbpart(u)
            prep(u)

        def cbpart(u):
            step, d, ch = unit(u)
            cs = slice(ch * 128, (ch + 1) * 128)
            P.group([lambda e, cs=cs: e.matmul(ps_cb[:, :128], lhsT=BT[:, cs], rhs=CT[:, cs], start=True, stop=True)],
                    [res("BT"), res("CT")], [res("ps_cb")])
            P.op("vector", lambda e, d=d: e.tensor_tensor(out=CBm[d][:], in0=ps_cb[:, :128], in1=(MF if d == 0 else MB), op=ALU.mult),
                 [res("ps_cb"), res("K8")], [res(f"CBm{d}")])

        def prep(u):
            step, d, ch = unit(u)
            P.op("vector", lambda e, d=d, ch=ch: e.tensor_tensor(
                out=h3(xdt[d][:]), in0=h3(Xtm[:, ch, :]), in1=dts["dtv"][:, d, ch, 0:3].unsqueeze(2).to_broadcast([128, 3, 64]), op=ALU.mult),
                [res("Xtm"), res("dtv")], [res(f"xdt{d}")])
            for h in range(3):
                hs = slice(h * 64, (h + 1) * 64)
                P.op("scalar", lambda e, d=d, ch=ch, h=h, hs=hs: e.activation(
                    out=xs[d][:, hs], in_=Xtm[:, ch, hs], func=AF.Copy, scale=dts["dtsd"][:, d, ch, h:h + 1]),
                    [res("Xtm"), res("dtsd")], [res(f"xs{d}")])
            if d == 0:
                P.op("vector", lambda e, ch=ch: e.tensor_tensor(
                    out=h3(xd[:]), in0=h3(Xtm[:, ch, :]), in1=dvec.unsqueeze(2).to_broadcast([128, 3, 64]), op=ALU.mult),
                    [res("Xtm"), res("sm")], [res("xd")])
            Tsel = K8[:, (0 if d == 0 else 1), :]
            P.op("vector", lambda e, d=d, ch=ch, Tsel=Tsel: e.tensor_tensor(
                out=Rb[d][:, 0], in0=Tsel.unsqueeze(1).to_broadcast([128, 3, 128]),
                in1=dthf[:, d, ch, 0:3].unsqueeze(2).to_broadcast([128, 3, 128]), op=ALU.mult),
                [res("K8"), res("dthf")], [res(f"R{d}")])
            for h in range(3):
                P.op("scalar", lambda e, d=d, ch=ch, h=h, Tsel=Tsel: e.activation(
                    out=Rb[d][:, 1, h, :], in_=Tsel, func=AF.Copy, scale=dtl[:, d, ch, h:h + 1]),
                    [res("K8"), res("dtl")], [res(f"R{d}")])
        def front_b(u):
            step, d, ch = unit(u)
            Ssel = K8b[:, (2 if d == 0 else 3), :]
            P.group([lambda e, d=d, h=h, t=t, Ssel=Ssel: e.matmul(ps_seg[d][:, h * 128:(h + 1) * 128], lhsT=Ssel,
                                                  rhs=Rb[d][:, t, h, :], start=(t == 0), stop=(t == 1)) for h in range(3) for t in range(2)],
                    [res("K8b"), res(f"R{d}")], [res(f"ps_seg{d}")])
            P.op("scalar", lambda e, d=d: e.activation(out=Eb[d][:].rearrange("p h l -> p (h l)"), in_=ps_seg[d][:, 0:384], func=AF.Exp),
                 [res(f"ps_seg{d}")], [res(f"E{d}")])
            P.op("vector", lambda e, d=d: e.tensor_tensor(out=Mb[d][:], in0=Eb[d][:], in1=CBm[d][:].unsqueeze(1).to_broadcast([128, 3, 128]), op=ALU.mult),
                 [res(f"E{d}"), res(f"CBm{d}")], [res(f"M{d}")])

        def back(u):
            step, d, ch = unit(u)
            cs = slice(ch * 128, (ch + 1) * 128)
            hp = step % 2
            fns = []
            for h in range(3):
                hs = slice(h * 64, (h + 1) * 64)
                fns.append(lambda e, d=d, h=h, hs=hs: e.matmul(ps_y[d][:, hs], lhsT=Mb[d][:, h, :], rhs=xdt[d][:, hs],
                                                               start=True, stop=(d == 1)))
                if d == 0:
                    fns.append(lambda e, hs=hs: e.matmul(ps_y[0][:, hs], lhsT=identb[:], rhs=xd[:, hs], start=False, stop=True))
            fns.append(lambda e, d=d, cs=cs, hp=hp: e.matmul(ps_y[d][:, 256:448], lhsT=CT[:, cs], rhs=hbf[d][hp][:], start=True, stop=True))
            fns.append(lambda e, d=d, ch=ch: e.matmul(ps_st[:, d * 256:d * 256 + 192], lhsT=Btm[:, ch, :], rhs=xs[d][:], start=True, stop=True))
            P.group(fns, [res(f"M{d}"), res(f"xdt{d}"), res("identb"), res("xd"), res("CT"), res(f"hbf{d}_{hp}"), res("Btm"), res(f"xs{d}")],
                    [res(f"ps_y{d}"), res(f"ps_st{d}")])
            P.op("vector", lambda e, d=d, ch=ch: e.tensor_tensor(
                out=h3(rtmp[d][:]), in0=h3(hst[d][:]), in1=dts["dec"][:, d, ch, 0:3].unsqueeze(2).to_broadcast([128, 3, 64]), op=ALU.mult),
                [res(f"hst{d}"), res("dec")], [res(f"rtmp{d}")])
            P.op("vector", lambda e, d=d: e.tensor_tensor(out=hst[d][:], in0=rtmp[d][:], in1=ps_st[:, d * 256:d * 256 + 192], op=ALU.add),
                 [res(f"rtmp{d}"), res(f"ps_st{d}")], [res(f"hst{d}")])
            P.op("scalar", lambda e, d=d, hp=hp: e.copy(out=hbf[d][1 - hp][:], in_=hst[d][:]),
                 [res(f"hst{d}")], [res(f"hbf{d}_{1 - hp}")])
            P.op("vector", lambda e, d=d, ch=ch: e.tensor_tensor(
                out=h3(yo_t[d][:]), in0=h3(ps_y[d][:, 256:448]), in1=dts["ea"][:, d, ch, 0:3].unsqueeze(2).to_broadcast([128, 3, 64]), op=ALU.mult),
                [res(f"ps_y{d}"), res("ea")], [res(f"yot{d}")])
            yb = u % 4
            P.op("vector", lambda e, d=d, yb=yb: e.tensor_tensor(out=yo[yb][:], in0=ps_y[d][:, 0:192], in1=yo_t[d][:], op=ALU.add),
                 [res(f"ps_y{d}"), res(f"yot{d}")], [res(f"yo{yb}")])
            out_tok.append(P.xdma("sync", Y[d, ch * 128:(ch + 1) * 128, :], yo[yb][:], f"st_y{yb}", reads=[res(f"yo{yb}")]))

        NU = 2 * NCH
        import os
        mode = os.environ.get("SSD_ORDER", "prep")
        if mode == "plain":
            for u in range(NU):
                front(u)
                back(u)
        elif mode == "prep":
            prep(0)
            for u in range(NU):
                cbpart(u)
                front_b(u)
                if u + 1 < NU:
                    prep(u + 1)
                back(u)
        elif mode == "split":
            front_a(0)
            front_b(0)
            for u in range(NU):
                if u + 1 < NU:
                    front_a(u + 1)
                back(u)
                if u + 1 < NU:
                    front_b(u + 1)
        P.finish(out_tok[-4:])
    return nc


def ssd_consts():
    k = np.arange(128)[:, None]
    l = np.arange(128)[None, :]
    mats = [k <= l, k >= l, k > l, k < l, l >= k, l <= k, k == l, np.ones((128, 128), bool)]
    return np.ascontiguousarray(np.stack([m.astype(np.float32) for m in mats], 1))


def ssd_inputs(PT, conv_w, conv_b, dt_bias, a_log, d_skip):
    cst = ssd_consts()
    maps = []
    for core in range(NCORES):
        g, half = core // 2, core % 2
        hg0 = g * 6 + half * 3
        xc = 2048 + g * 384 + half * 192
        bc = 2048 + 1536 + g * 128
        cc = 2048 + 2048 + g * 128
        uT = np.concatenate([PT[xc:xc + 192], PT[bc:bc + 128], PT[cc:cc + 128]], 0)
        dt = np.stack([PT[4608 + d * 24 + hg0:4608 + d * 24 + hg0 + 3] for d in range(2)], 0)
        dt = dt.reshape(2, 3, NCH, 128).transpose(3, 0, 2, 1)
        cols = [np.arange(xc - 2048, xc - 2048 + 128), np.arange(xc - 2048 + 128, xc - 2048 + 192),
                np.arange(bc - 2048, bc - 2048 + 128), np.arange(cc - 2048, cc - 2048 + 128)]
        cw = np.zeros((128, 4, 5), np.float32)
        cb = np.zeros((128, 4), np.float32)
        for ti, cidx in enumerate(cols):
            cw[:len(cidx), ti, :] = conv_w[:, cidx].T
            cb[:len(cidx), ti] = conv_b[cidx]
        sm = np.concatenate([dt_bias[:, hg0:hg0 + 3].reshape(-1), a_log[:, hg0:hg0 + 3].reshape(-1), d_skip[hg0:hg0 + 3]])
        sm = np.broadcast_to(sm[None, :], (128, 15))
        maps.append({"uT": np.ascontiguousarray(uT), "dtr": np.ascontiguousarray(dt), "cw": cw, "cb": cb,
                     "sm": np.ascontiguousarray(sm, dtype=np.float32), "cst": cst})
    return maps


def build_fourier():
    nc = bass.Bass("TRN2", target_bir_lowering=False)
    uL = nc.dram_tensor("uL", [SEQ, 512], F32, kind="ExternalInput").ap()
    uC = nc.dram_tensor("uC", [CTX, 512], F32, kind="ExternalInput").ap()
    tabC = nc.dram_tensor("tabC", [SEQ, 1024], BF16, kind="ExternalInput").ap()
    tabS = nc.dram_tensor("tabS", [SEQ, 1024], BF16, kind="ExternalInput").ap()
    tcC = nc.dram_tensor("tcC", [CTX, 32], BF16, kind="ExternalInput").ap()
    tcS = nc.dram_tensor("tcS", [CTX, 32], BF16, kind="ExternalInput").ap()
    wfd = nc.dram_tensor("wf", [4, 128, 128], F32, kind="ExternalInput").ap()
    ccd = nc.dram_tensor("cc", [128, 2, 128], F32, kind="ExternalInput").ap()
    fT = nc.dram_tensor("fT", [512, TOK], F32, kind="ExternalOutput").ap()
    with contextlib.ExitStack() as st:
        sb = lambda n, s, d: st.enter_context(nc.sbuf_tensor(n, s, d))
        NB = 3
        ust = [sb(f"ust{i}", [128, 4, 512], F32) for i in range(NB)]
        ub = [sb(f"ub{i}", [128, 4, 512], BF16) for i in range(NB)]
        tC = [sb(f"tC{i}", [128, 4, 512], BF16) for i in range(NB)]
        tS = [sb(f"tS{i}", [128, 4, 512], BF16) for i in range(NB)]
        wf = sb("wf_s", [128, 4, 128], F32)
        cc = sb("cc_s", [128, 2, 128], F32)
        Mm = sb("Mm", [128, 4, 2, 128], BF16)
        AB = sb("AB", [128, 4, 2, 512], BF16)
        fo = sb("fo", [128, 4, TOK], F32)
        ucs = sb("ucs", [128, 2, 512], F32)
        ucb = sb("ucb", [128, 2, 512], BF16)
        tcc = sb("tcc", [128, 2, 2, 32], BF16)
        bank = [st.enter_context(nc.psum_tensor(f"bank{i}", [128, 512], F32)) for i in range(8)]
        P = TP(nc)
        R = {}

        def res(name):
            if name not in R:
                R[name] = Res()
            return R[name]

        P.xdma("sync", wf[:], wfd.rearrange("g j d -> j g d"), "ld_wf", writes=[res("wf")])
        P.xdma("sync", cc[:], ccd, "ld_cc", writes=[res("cc")])
        for g in range(4):
            for t in range(2):
                P.group([lambda e, g=g, t=t: e.matmul(bank[g][:, t * 128:(t + 1) * 128], lhsT=cc[:, t, :], rhs=wf[:, g, :],
                                                      start=True, stop=True)], [res("wf"), res("cc")], [res(f"bank{g}")])
            P.op("vector", lambda e, g=g: e.tensor_copy(out=Mm[:, g].rearrange("p t d -> p (t d)"), in_=bank[g][:, 0:256]),
                 [res(f"bank{g}")], [res("Mm")])

        def stage2(width, c0, ABv):
            for g in range(4):
                P.group([lambda e, g=g, t=t: e.matmul(bank[g][:, :width], lhsT=Mm[:, g, t, :], rhs=ABv(g, t),
                                                      start=(t == 0), stop=(t == 1)) for t in range(2)],
                        [res("Mm"), res("AB")], [res(f"bank{g}")])
                if g % 2 == 0:
                    P.op("vector", lambda e, g=g: e.tensor_copy(out=fo[:, g, c0:c0 + width], in_=bank[g][:, :width]),
                         [res(f"bank{g}")], [res("fo")])
                else:
                    P.op("scalar", lambda e, g=g: e.copy(out=fo[:, g, c0:c0 + width], in_=bank[g][:, :width]),
                         [res(f"bank{g}")], [res("fo")])

        uv = uL.rearrange("(c p) f -> p c f", p=128)
        cv = tabC.rearrange("(c p) k -> p c k", p=128)
        sv = tabS.rearrange("(c p) k -> p c k", p=128)
        n = 0
        for half in range(2):
            ks = slice(half * 512, (half + 1) * 512)
            for nb in range(16):
                b = n % NB
                n += 1
                P.xdma("sync", ust[b][:], uv[:, nb * 4:(nb + 1) * 4, :], f"ld_u{b}", writes=[res(f"ust{b}")])
                P.xdma("gpsimd", tC[b][:], cv[:, nb * 4:(nb + 1) * 4, ks], f"ld_tc{b}", writes=[res(f"tC{b}")])
                P.xdma("gpsimd", tS[b][:], sv[:, nb * 4:(nb + 1) * 4, ks], f"ld_ts{b}", writes=[res(f"tS{b}")])
                if nb % 2 == 0:
                    P.op("vector", lambda e, b=b: e.tensor_copy(out=ub[b][:], in_=ust[b][:]), [res(f"ust{b}")], [res(f"ub{b}")])
                else:
                    P.op("scalar", lambda e, b=b: e.copy(out=ub[b][:], in_=ust[b][:]), [res(f"ust{b}")], [res(f"ub{b}")])
                fns = []
                for ci in range(4):
                    first = (nb == 0 and ci == 0)
                    last = (nb == 15 and ci == 3)
                    for g in range(4):
                        for t in range(2):
                            tb = tC[b] if t == 0 else tS[b]
                            fns.append(lambda e, b=b, ci=ci, g=g, t=t, tb=tb, first=first, last=last: e.matmul(
                                bank[g * 2 + t][:, :], lhsT=ub[b][:, ci, g * 128:(g + 1) * 128], rhs=tb[:, ci, :],
                                start=first, stop=last))
                P.group(fns, [res(f"ub{b}"), res(f"tC{b}"), res(f"tS{b}")], [res(f"bank{i}") for i in range(8)])
            for g in range(4):
                for t in range(2):
                    if t == 0:
                        P.op("vector", lambda e, g=g, t=t: e.tensor_copy(out=AB[:, g, t, :], in_=bank[g * 2 + t][:, :]),
                             [res(f"bank{g * 2 + t}")], [res("AB")])
                    else:
                        P.op("scalar", lambda e, g=g, t=t: e.copy(out=AB[:, g, t, :], in_=bank[g * 2 + t][:, :]),
                             [res(f"bank{g * 2 + t}")], [res("AB")])
            stage2(512, half * 512, lambda g, t: AB[:, g, t, :])
        P.xdma("sync", ucs[:], uC.rearrange("(c p) f -> p c f", p=128), "ld_uc", writes=[res("ucs")])
        P.xdma("sync", tcc[:, 0], tcC.rearrange("(c p) k -> p c k", p=128), "ld_tcc", writes=[res("tcc")])
        P.xdma("sync", tcc[:, 1], tcS.rearrange("(c p) k -> p c k", p=128), "ld_tcs", writes=[res("tcc")])
        P.op("vector", lambda e: e.tensor_copy(out=ucb[:], in_=ucs[:]), [res("ucs")], [res("ucb")])
        for g in range(4):
            for t in range(2):
                P.group([lambda e, g=g, t=t, ci=ci: e.matmul(bank[g * 2 + t][:, :32], lhsT=ucb[:, ci, g * 128:(g + 1) * 128],
                                                            rhs=tcc[:, t, ci, :], start=(ci == 0), stop=(ci == 1)) for ci in range(2)],
                        [res("ucb"), res("tcc")], [res(f"bank{g * 2 + t}")])
                P.op("vector", lambda e, g=g, t=t: e.tensor_copy(out=AB[:, g, t, :32], in_=bank[g * 2 + t][:, :32]),
                     [res(f"bank{g * 2 + t}")], [res("AB")])
        stage2(32, 1024, lambda g, t: AB[:, g, t, :32])
        t_o = P.xdma("sync", fT.rearrange("(g p) t -> p g t", p=128), fo[:], "st_f", reads=[res("fo")])
        P.finish([t_o])
    return nc


def fourier_tables():
    n = np.arange(SEQ, dtype=np.int64)
    ph = (np.outer(n, n) % SEQ).astype(np.float64) * (2 * np.pi / SEQ)
    sc = 1.0 / np.sqrt(SEQ * 128.0)
    C = (np.cos(ph) * sc).astype(NPBF)
    S = (np.sin(ph) * sc).astype(NPBF)
    m = np.arange(CTX, dtype=np.int64)
    phc = (np.outer(m, m) % CTX).astype(np.float64) * (2 * np.pi / CTX)
    scc = 1.0 / np.sqrt(CTX * 128.0)
    Cc_ = (np.cos(phc) * scc).astype(NPBF)
    Sc_ = (np.sin(phc) * scc).astype(NPBF)
    j = np.arange(128, dtype=np.int64)
    p128 = (np.outer(j, j) % 128).astype(np.float64) * (2 * np.pi / 128)
    cc = np.stack([np.cos(p128), -np.sin(p128)], 1).astype(np.float32)
    return C, S, Cc_, Sc_, np.ascontiguousarray(cc)


def build_G():
    nc = bass.Bass("TRN2", target_bir_lowering=False)
    xT = nc.dram_tensor("xT", [D, TOK], F32, kind="ExternalInput").ap()
    fTd = nc.dram_tensor("fT", [512, TOK], F32, kind="ExternalInput").ap()
    yfd = nc.dram_tensor("yf", [1536, TOK], F32, kind="ExternalInput").ap()
    ybd = nc.dram_tensor("yb", [1536, TOK], F32, kind="ExternalInput").ap()
    zd = nc.dram_tensor("zT", [1536, TOK], F32, kind="ExternalInput").ap()
    gnd = nc.dram_tensor("gn", [128, 12], F32, kind="ExternalInput").ap()
    gtd = nc.dram_tensor("gt", [128, 16, 2], F32, kind="ExternalInput").ap()
    wd = nc.dram_tensor("w", [D, D], BF16, kind="ExternalInput").ap()
    xo = nc.dram_tensor("xo", [D, TOK], F32, kind="ExternalOutput").ap()
    with contextlib.ExitStack() as st:
        sb = lambda n, s, d: st.enter_context(nc.sbuf_tensor(n, s, d))
        catT = sb("catT", [128, 16, TOK], BF16)
        gn = sb("gn_s", [128, 12], F32)
        gt = sb("gt_s", [128, 16, 2], F32)
        epsc = sb("epsc", [128, 1], F32)
        ones = sb("ones", [128, 128], BF16)
        fst = [sb(f"fst{i}", [128, TOK], F32) for i in range(2)]
        yfs = [sb(f"yfs{i}", [128, TOK], F32) for i in range(2)]
        ybs = [sb(f"ybs{i}", [128, TOK], F32) for i in range(2)]
        zs = [sb(f"zs{i}", [128, TOK], F32) for i in range(2)]
        uu = sb("uu", [128, 3, TOK], F32)
        sq = [sb(f"sq{i}", [128, TOK], BF16) for i in range(2)]
        rstd = sb("rstd", [128, TOK], F32)
        tmp = [sb(f"tmp{i}", [128, TOK], F32) for i in range(2)]
        wb = [sb(f"wb{i}", [128, 16, 512], BF16) for i in range(2)]
        xc = [sb(f"xc{i}", [128, TOK], F32) for i in range(3)]
        ps_ms = st.enter_context(nc.psum_tensor("ps_ms", [128, 3, 512], F32))
        ps_mm = st.enter_context(nc.psum_tensor("ps_mm", [128, 5, 512], F32))
        P = TP(nc)
        R = {}

        def res(name):
            if name not in R:
                R[name] = Res()
            return R[name]

        P.xdma("sync", gn[:], gnd, "ld_gn", writes=[res("gn")])
        P.xdma("sync", gt[:], gtd, "ld_gt", writes=[res("gt")])
        P.op("vector", lambda e: e.memset(ones[:], 1.0), [], [res("ones")])
        P.op("vector", lambda e: e.memset(epsc[:], EPS), [], [res("epsc")])
        for c in range(4):
            b = c % 2
            P.xdma("sync", fst[b][:], fTd[c * 128:(c + 1) * 128, :], f"ld_f{b}", writes=[res(f"fst{b}")])
            P.op("vector" if c % 2 == 0 else "scalar",
                 (lambda e, c=c, b=b: e.tensor_copy(out=catT[:, c, :], in_=fst[b][:])) if c % 2 == 0 else
                 (lambda e, c=c, b=b: e.copy(out=catT[:, c, :], in_=fst[b][:])),
                 [res(f"fst{b}")], [res("catT")])
        n = 0
        for grp in range(4):
            for c in range(3):
                ch = grp * 3 + c
                b = n % 2
                n += 1
                rows = slice(ch * 128, (ch + 1) * 128)
                P.xdma("sync", yfs[b][:], yfd[rows, :], f"ld_yf{b}", writes=[res(f"yfs{b}")])
                P.xdma("gpsimd", ybs[b][:], ybd[rows, :], f"ld_yb{b}", writes=[res(f"ybs{b}")])
                P.xdma("sync", zs[b][:], zd[rows, :], f"ld_z{b}", writes=[res(f"zs{b}")])
                P.op("vector", lambda e, b=b: e.tensor_tensor(out=yfs[b][:], in0=yfs[b][:], in1=ybs[b][:], op=ALU.add),
                     [res(f"yfs{b}"), res(f"ybs{b}")], [res(f"yfs{b}")])
                P.op("scalar", lambda e, b=b: e.activation(out=zs[b][:], in_=zs[b][:], func=AF.Silu), [res(f"zs{b}")], [res(f"zs{b}")])
                P.op("vector", lambda e, b=b, c=c: e.tensor_tensor(out=uu[:, c, :], in0=yfs[b][:], in1=zs[b][:], op=ALU.mult),
                     [res(f"yfs{b}"), res(f"zs{b}")], [res(f"uu{c}")])
                P.op("scalar", lambda e, b=b, c=c: e.activation(out=sq[b][:], in_=uu[:, c, :], func=AF.Square),
                     [res(f"uu{c}")], [res(f"sq{b}")])
                P.group([lambda e, b=b, c=c, bi=bi, s0=s0, w=w: e.matmul(ps_ms[:, bi, :w], lhsT=ones[:], rhs=sq[b][:, s0:s0 + w],
                                                                        start=(c == 0), stop=(c == 2))
                         for bi, (s0, w) in enumerate(BLKS)], [res("ones"), res(f"sq{b}")], [res("ps_ms")])
            for bi, (s0, w) in enumerate(BLKS):
                P.op("scalar", lambda e, bi=bi, s0=s0, w=w: e.activation(out=rstd[:, s0:s0 + w], in_=ps_ms[:, bi, :w], func=AF.Sqrt,
                                                                         bias=epsc[:, 0:1], scale=1.0 / 384.0),
                     [res("ps_ms"), res("epsc")], [res("rstd")])
            P.op("vector", lambda e: e.reciprocal(out=rstd[:], in_=rstd[:]), [res("rstd")], [res("rstd")])
            for c in range(3):
                ch = grp * 3 + c
                b = c % 2
                P.op("vector", lambda e, b=b, c=c: e.tensor_tensor(out=tmp[b][:], in0=uu[:, c, :], in1=rstd[:], op=ALU.mult),
                     [res(f"uu{c}"), res("rstd")], [res(f"tmp{b}")])
                P.op("scalar", lambda e, b=b, ch=ch: e.activation(out=catT[:, 4 + ch, :], in_=tmp[b][:], func=AF.Copy, scale=gn[:, ch:ch + 1]),
                     [res(f"tmp{b}"), res("gn")], [res("catT")])
        outs = []
        for nb in range(4):
            b = nb % 2
            P.xdma("sync", wb[b][:], wd[:, nb * 512:(nb + 1) * 512].rearrange("(kc p) n -> p kc n", p=128), f"ld_w{b}", writes=[res(f"wb{b}")])
            for j in range(4):
                nch = nb * 4 + j
                xb = nch % 3
                P.xdma("gpsimd", xc[xb][:], xT[nch * 128:(nch + 1) * 128, :], f"ld_x{xb}", writes=[res(f"xc{xb}")])
                for bi, (s0, w) in enumerate(BLKS):
                    pb = (nch * 3 + bi) % 5
                    P.group([lambda e, b=b, j=j, k=k, s0=s0, w=w, pb=pb: e.matmul(
                        ps_mm[:, pb, :w], lhsT=wb[b][:, k, j * 128:(j + 1) * 128], rhs=catT[:, k, s0:s0 + w],
                        start=(k == 0), stop=(k == 15)) for k in range(16)],
                        [res(f"wb{b}"), res("catT")], [res(f"ps_mm{pb}")])
                    P.op("vector", lambda e, xb=xb, nch=nch, bi=bi, s0=s0, w=w, pb=pb: e.scalar_tensor_tensor(
                        out=xc[xb][:, s0:s0 + w], in0=ps_mm[:, pb, :w], scalar=gt[:, nch, (1 if bi == 2 else 0):(2 if bi == 2 else 1)],
                        in1=xc[xb][:, s0:s0 + w], op0=ALU.mult, op1=ALU.add),
                        [res(f"ps_mm{pb}"), res("gt"), res(f"xc{xb}")], [res(f"xc{xb}")])
                outs.append(P.xdma("sync", xo[nch * 128:(nch + 1) * 128, :], xc[xb][:], f"st_x{xb}", reads=[res(f"xc{xb}")]))
        P.finish(outs[-3:])
    return nc


def build_M():
    nc = bass.Bass("TRN2", target_bir_lowering=False)
    xT = nc.dram_tensor("xT", [D, TOK], F32, kind="ExternalInput").ap()
    scd = nc.dram_tensor("sc", [128, 16, 2], F32, kind="ExternalInput").ap()
    shd = nc.dram_tensor("sh", [128, 16, 2], F32, kind="ExternalInput").ap()
    gtd = nc.dram_tensor("gt", [128, 16, 2], F32, kind="ExternalInput").ap()
    gd = nc.dram_tensor("g", [128, 16], F32, kind="ExternalInput").ap()
    w1d = nc.dram_tensor("w1", [D, DFF], BF16, kind="ExternalInput").ap()
    w2d = nc.dram_tensor("w2", [DFF, D], BF16, kind="ExternalInput").ap()
    xo = nc.dram_tensor("xo", [D, TOK], F32, kind="ExternalOutput").ap()
    with contextlib.ExitStack() as st:
        sb = lambda n, s, d: st.enter_context(nc.sbuf_tensor(n, s, d))
        X = sb("X", [128, 16, TOK], F32)
        hT = sb("hT", [128, 16, TOK], BF16)
        m1 = sb("m1", [128, 8, TOK], BF16)
        sc = sb("sc_s", [128, 16, 2], F32)
        sh = sb("sh_s", [128, 16, 2], F32)
        gt = sb("gt_s", [128, 16, 2], F32)
        g = sb("g_s", [128, 16], F32)
        s1 = sb("s1", [128, 16, 3], F32)
        ones = sb("ones", [128, 128], BF16)
        sq = [sb(f"sq{i}", [128, TOK], BF16) for i in range(2)]
        rstd = sb("rstd", [128, TOK], F32)
        tmp = [sb(f"tmp{i}", [128, TOK], F32) for i in range(2)]
        w1b = [sb(f"w1b{i}", [128, 16, 512], BF16) for i in range(2)]
        w2b = [sb(f"w2b{i}", [128, 8, 512], BF16) for i in range(2)]
        rl = [sb(f"rl{i}", [128, 512], F32) for i in range(2)]
        ps_ms = st.enter_context(nc.psum_tensor("ps_ms", [128, 3, 512], F32))
        ps_mm = st.enter_context(nc.psum_tensor("ps_mm", [128, 5, 512], F32))
        P = TP(nc)
        R = {}

        def res(name):
            if name not in R:
                R[name] = Res()
            return R[name]

        t_ones = P.dve(lambda e: e.memset(ones[:], 1.0 / D))
        xr = []
        xv = xT.rearrange("(c p) t -> p c t", p=128)
        for c4 in range(4):
            t = P.dma("sync" if c4 % 2 == 0 else "gpsimd", X[:, c4 * 4:(c4 + 1) * 4, :], xv[:, c4 * 4:(c4 + 1) * 4, :], f"ld_x{c4}")
            xr += [t] * 4
        t1 = P.dma("gpsimd", sc[:], scd, "ld_sc")
        t2 = P.dma("gpsimd", sh[:], shd, "ld_sh")
        t3 = P.dma("gpsimd", g[:], gd, "ld_g")
        t4 = P.dma("gpsimd", gt[:], gtd, "ld_gt")
        h_ready = emit_norm_mod(P, nc, X, hT, sc, sh, g[:], ones, ps_ms, sq, rstd, tmp, s1, xr, [t1, t2, t3, t_ones])
        rh = res("hT")
        rh.w = h_ready[-1]
        rX = [res(f"X{c}") for c in range(16)]
        for c in range(16):
            rX[c].w = xr[c]
            rX[c].r = {h_ready[-1][0]: h_ready[-1][1], "s_dve": P.cnt["s_dve"]}
        res("gt").w = t4
        npb = 0
        for q in range(8):
            for b2 in range(2):
                wi = (q * 2 + b2) % 2
                c0 = q * 1024 + b2 * 512
                P.xdma("sync", w1b[wi][:], w1d[:, c0:c0 + 512].rearrange("(kc p) n -> p kc n", p=128), f"ld_w1{wi}", writes=[res(f"w1b{wi}")])
                for j in range(4):
                    f = b2 * 4 + j
                    for bi, (s0, w) in enumerate(BLKS):
                        pb = npb % 5
                        npb += 1
                        P.group([lambda e, wi=wi, j=j, k=k, s0=s0, w=w, pb=pb: e.matmul(
                            ps_mm[:, pb, :w], lhsT=w1b[wi][:, k, j * 128:(j + 1) * 128], rhs=hT[:, k, s0:s0 + w],
                            start=(k == 0), stop=(k == 15)) for k in range(16)],
                            [res(f"w1b{wi}"), rh], [res(f"ps_mm{pb}")])
                        rb = npb % 2
                        if npb % 2 == 0:
                            P.op("scalar", lambda e, rb=rb, pb=pb, w=w: e.activation(out=rl[rb][:, :w], in_=ps_mm[:, pb, :w], func=AF.Relu),
                                 [res(f"ps_mm{pb}")], [res(f"rl{rb}")])
                            P.op("vector", lambda e, rb=rb, f=f, s0=s0, w=w: e.tensor_tensor(out=m1[:, f, s0:s0 + w], in0=rl[rb][:, :w], in1=rl[rb][:, :w], op=ALU.mult),
                                 [res(f"rl{rb}")], [res(f"m1_{f}")])
                        else:
                            P.op("vector", lambda e, rb=rb, pb=pb, w=w: e.tensor_scalar_max(out=rl[rb][:, :w], in0=ps_mm[:, pb, :w], scalar1=0.0),
                                 [res(f"ps_mm{pb}")], [res(f"rl{rb}")])
                            P.op("scalar", lambda e, rb=rb, f=f, s0=s0, w=w: e.activation(out=m1[:, f, s0:s0 + w], in_=rl[rb][:, :w], func=AF.Square),
                                 [res(f"rl{rb}")], [res(f"m1_{f}")])
            for nb in range(4):
                wi = (q * 4 + nb) % 2
                P.xdma("gpsimd", w2b[wi][:], w2d[q * 1024:(q + 1) * 1024, nb * 512:(nb + 1) * 512].rearrange("(fc p) n -> p fc n", p=128),
                       f"ld_w2{wi}", writes=[res(f"w2b{wi}")])
                for j in range(4):
                    nch = nb * 4 + j
                    for bi, (s0, w) in enumerate(BLKS):
                        pb = npb % 5
                        npb += 1
                        P.group([lambda e, wi=wi, j=j, f=f, s0=s0, w=w, pb=pb: e.matmul(
                            ps_mm[:, pb, :w], lhsT=w2b[wi][:, f, j * 128:(j + 1) * 128], rhs=m1[:, f, s0:s0 + w],
                            start=(f == 0), stop=(f == 7)) for f in range(8)],
                            [res(f"w2b{wi}")] + [res(f"m1_{ff}") for ff in range(8)], [res(f"ps_mm{pb}")])
                        P.op("vector", lambda e, nch=nch, bi=bi, s0=s0, w=w, pb=pb: e.scalar_tensor_tensor(
                            out=X[:, nch, s0:s0 + w], in0=ps_mm[:, pb, :w], scalar=gt[:, nch, (1 if bi == 2 else 0):(2 if bi == 2 else 1)],
                            in1=X[:, nch, s0:s0 + w], op0=ALU.mult, op1=ALU.add),
                            [res(f"ps_mm{pb}"), res("gt"), rX[nch]], [rX[nch]])
        outs = []
        xov = xo.rearrange("(c p) t -> p c t", p=128)
        for c4 in range(4):
            outs.append(P.xdma("sync" if c4 % 2 == 0 else "gpsimd", xov[:, c4 * 4:(c4 + 1) * 4, :], X[:, c4 * 4:(c4 + 1) * 4, :], f"st_x{c4}",
                               reads=[rX[c] for c in range(c4 * 4, c4 * 4 + 4)]))
        P.finish(outs)
    return nc


def build_N():
    nc = bass.Bass("TRN2", target_bir_lowering=False)
    xT = nc.dram_tensor("xT", [D, TOK], F32, kind="ExternalInput").ap()
    gd = nc.dram_tensor("g", [128, 16], F32, kind="ExternalInput").ap()
    xo = nc.dram_tensor("xo", [D, TOK], F32, kind="ExternalOutput").ap()
    with contextlib.ExitStack() as st:
        sb = lambda n, s, d: st.enter_context(nc.sbuf_tensor(n, s, d))
        X = sb("X", [128, 16, TOK], F32)
        hT = sb("hT", [128, 16, TOK], F32)
        zz = sb("zz", [128, 16, 2], F32)
        g = sb("g_s", [128, 16], F32)
        s1 = sb("s1", [128, 16, 3], F32)
        ones = sb("ones", [128, 128], BF16)
        sq = [sb(f"sq{i}", [128, TOK], BF16) for i in range(2)]
        rstd = sb("rstd", [128, TOK], F32)
        tmp = [sb(f"tmp{i}", [128, TOK], F32) for i in range(2)]
        ps_ms = st.enter_context(nc.psum_tensor("ps_ms", [128, 3, 512], F32))
        P = Prog(nc)
        t_ones = P.dve(lambda e: e.memset(ones[:], 1.0 / D))
        t_z = P.dve(lambda e: e.memset(zz[:], 0.0))
        xr = []
        xv = xT.rearrange("(c p) t -> p c t", p=128)
        for c4 in range(4):
            t = P.dma("sync" if c4 % 2 == 0 else "gpsimd", X[:, c4 * 4:(c4 + 1) * 4, :], xv[:, c4 * 4:(c4 + 1) * 4, :], f"ld_x{c4}")
            xr += [t] * 4
        t3 = P.dma("gpsimd", g[:], gd, "ld_g")
        h_ready = emit_norm_mod(P, nc, X, hT, zz, zz, g[:], ones, ps_ms, sq, rstd, tmp, s1, xr, [t3, t_ones, t_z])
        outs = []
        xov = xo.rearrange("(c p) t -> p c t", p=128)
        for c4 in range(4):
            outs.append(P.dma("sync", xov[:, c4 * 4:(c4 + 1) * 4, :], hT[:, c4 * 4:(c4 + 1) * 4, :], f"st_x{c4}", [h_ready[c4 * 4 + 3]]))
        P.finish(outs)
    return nc


_CACHE = {}


def _prog(name, builder):
    if name not in _CACHE:
        _CACHE[name] = builder()
    return _CACHE[name]


def _mod2(mods, l, k):
    return np.ascontiguousarray(np.stack([fm(mods[l, 0, k * D:(k + 1) * D]), fm(mods[l, 1, k * D:(k + 1) * D])], -1))


def kernel(x, c, ctx, c_ctx, w_ada, b_ada, g_mix, w_in, conv_w, conv_b, dt_bias, a_log, d_skip,
           g_ssd_norm, w_fourier, w_out, g_mlp, w_mlp1, w_mlp2, g_final, _nlayers=DEPTH, _debug=None):
    f32 = lambda a: np.ascontiguousarray(np.asarray(a, dtype=np.float32))
    x, c, ctx, c_ctx = f32(x), f32(c), f32(ctx), f32(c_ctx)
    w_ada, b_ada, g_mix, w_in = f32(w_ada), f32(b_ada), f32(g_mix), f32(w_in)
    conv_w, conv_b, dt_bias, a_log, d_skip = f32(conv_w), f32(conv_b), f32(dt_bias), f32(a_log), f32(d_skip)
    g_ssd_norm, w_fourier, w_out, g_mlp = f32(g_ssd_norm), f32(w_fourier), f32(w_out), f32(g_mlp)
    w_mlp1, w_mlp2, g_final = f32(w_mlp1), f32(w_mlp2), f32(g_final)

    ws = [w_in, w_out, w_mlp1, w_mlp2]
    flat = np.concatenate([w.reshape(-1) for w in ws])
    fb = run_cast(flat)
    wb = []
    o = 0
    for w in ws:
        wb.append(fb[o:o + w.size].reshape(w.shape))
        o += w.size
    w_in_b, w_out_b, w1_b, w2_b = wb
    del flat, fb

    mods = run_mods(c, c_ctx, w_ada, b_ada)
    C, S, Cc_, Sc_, cc = fourier_tables()
    tabs = [(np.ascontiguousarray(C[:, 1024 * i:1024 * (i + 1)]), np.ascontiguousarray(S[:, 1024 * i:1024 * (i + 1)]),
             np.ascontiguousarray(Cc_[:, 32 * i:32 * (i + 1)]), np.ascontiguousarray(Sc_[:, 32 * i:32 * (i + 1)])) for i in range(NCORES)]
    del C, S

    xl, xc = x[0], ctx[0]
    xT = [np.ascontiguousarray(np.concatenate([xl[1024 * i:1024 * (i + 1)], xc[32 * i:32 * (i + 1)]], 0).T) for i in range(NCORES)]

    def to_global(per_core):
        R_ = per_core[0].shape[0]
        G = np.empty((R_, NTOK), np.float32)
        for i in range(NCORES):
            G[:, CTX + 1024 * i:CTX + 1024 * (i + 1)] = per_core[i][:, :1024]
            G[:, 32 * i:32 * (i + 1)] = per_core[i][:, 1024:]
        return G

    def to_core(G, i):
        return np.ascontiguousarray(np.concatenate([G[:, CTX + 1024 * i:CTX + 1024 * (i + 1)], G[:, 32 * i:32 * (i + 1)]], 1))

    for l in range(_nlayers):
        ncB = _prog("B", build_B)
        sc1, sh1, gt1 = _mod2(mods, l, 1), _mod2(mods, l, 0), _mod2(mods, l, 2)
        sh2, sc2, gt2 = _mod2(mods, l, 3), _mod2(mods, l, 4), _mod2(mods, l, 5)
        res = _run(ncB, [{"xT": xT[i], "sc": sc1, "sh": sh1, "g": fm(g_mix[l]), "w": w_in_b[l]} for i in range(NCORES)])
        pT = [r["pT"] for r in res]
        PT = to_global(pT)
        ncC = _prog("C", build_ssd)
        res = _run(ncC, ssd_inputs(PT, conv_w[l], conv_b[l], dt_bias[l], a_log[l], d_skip[l]))
        YF = np.concatenate([r["Y"][0].T for r in res], 0)
        YB = np.concatenate([r["Y"][1].T for r in res], 0)
        ncF = _prog("F", build_fourier)
        uL = np.ascontiguousarray(PT[0:512, CTX:].T)
        uC = np.ascontiguousarray(PT[0:512, :CTX].T)
        res = _run(ncF, [{"uL": uL, "uC": uC, "tabC": tabs[i][0], "tabS": tabs[i][1], "tcC": tabs[i][2], "tcS": tabs[i][3],
                          "wf": w_fourier[l], "cc": cc} for i in range(NCORES)])
        fT = [r["fT"] for r in res]
        ncG = _prog("G", build_G)
        res = _run(ncG, [{"xT": xT[i], "fT": fT[i], "yf": to_core(YF, i), "yb": to_core(YB, i),
                          "zT": np.ascontiguousarray(pT[i][512:2048]), "gn": fm(g_ssd_norm[l]), "gt": gt1, "w": w_out_b[l]}
                         for i in range(NCORES)])
        xm = [r["xo"] for r in res]
        if _debug is not None:
            _debug[f"xm{l}"] = xm
        ncM = _prog("M", build_M)
        res = _run(ncM, [{"xT": xm[i], "sc": sc2, "sh": sh2, "gt": gt2, "g": fm(g_mlp[l]), "w1": w1_b[l], "w2": w2_b[l]}
                         for i in range(NCORES)])
        xT = [r["xo"] for r in res]
        if _debug is not None:
            _debug[f"xo{l}"] = xT
    ncN = _prog("N", build_N)
    res = _run(ncN, [{"xT": xT[i], "g": fm(g_final)} for i in range(NCORES)])
    out = np.empty((1, SEQ, D), np.float32)
    for i in range(NCORES):
        out[0, 1024 * i:1024 * (i + 1), :] = res[i]["xo"][:, :1024].T
    return out
```

```python
import numpy as np
import ml_dtypes
import concourse.bass as bass
import concourse.mybir as mybir
from concourse.bass_utils import run_bass_kernel_spmd

F32 = mybir.dt.float32
BF16 = mybir.dt.bfloat16
AF = mybir.ActivationFunctionType
ALU = mybir.AluOpType
AX = mybir.AxisListType
NPBF = ml_dtypes.bfloat16

NCORES = 8
D = 2048
SEQ = 8192
CTX = 256
DEPTH = 4
DIN = 4656
DFF = 8192
TOK = 1056
EPS = 1e-6


class Prog:
    ENGS = ("sync", "scalar", "vector", "gpsimd", "tensor")

    def __init__(self, nc):
        self.nc = nc
        self.q = {e: [] for e in self.ENGS}
        self.cnt = {}
        self.waited = {e: {} for e in self.ENGS}

    def emit(self, eng, fn, waits=(), sig=None, inc=1):
        ws = []
        for w in waits:
            if w is None:
                continue
            s, v = w
            if self.waited[eng].get(s, 0) >= v:
                continue
            self.waited[eng][s] = v
            ws.append((s, v))
        tok = None
        if sig is not None:
            self.cnt[sig] = self.cnt.get(sig, 0) + inc
            tok = (sig, self.cnt[sig])
        self.q[eng].append((fn, ws, sig, inc))
        return tok

    def pe(self, fn, waits=(), sig=True):
        return self.emit("tensor", fn, waits, "s_pe" if sig else None)

    def act(self, fn, waits=(), sig=True):
        return self.emit("scalar", fn, waits, "s_act" if sig else None)

    def dve(self, fn, waits=(), sig=True):
        return self.emit("vector", fn, waits, "s_dve" if sig else None)

    def pool(self, fn, waits=(), sig=True):
        return self.emit("gpsimd", fn, waits, "s_pool" if sig else None)

    def dma(self, q, out, in_, sem, waits=()):
        return self.emit(q, lambda e: e.dma_start(out=out, in_=in_), waits, sem, 16)

    def simulate(self):
        pos = {e: 0 for e in self.ENGS}
        cnt = {}
        while True:
            prog = False
            for e in self.ENGS:
                while pos[e] < len(self.q[e]):
                    fn, ws, sig, inc = self.q[e][pos[e]]
                    if all(cnt.get(s_, 0) >= v for s_, v in ws):
                        if sig is not None:
                            cnt[sig] = cnt.get(sig, 0) + inc
                        pos[e] += 1
                        prog = True
                    else:
                        break
            if all(pos[e] == len(self.q[e]) for e in self.ENGS):
                return True
            if not prog:
                for e in self.ENGS:
                    if pos[e] < len(self.q[e]):
                        fn, ws, sig, inc = self.q[e][pos[e]]
                        print("DEADLOCK", e, pos[e], len(self.q[e]), [(s_, v, cnt.get(s_, 0)) for s_, v in ws], sig)
                return False

    def finish(self, final_waits):
        import os
        if os.environ.get("PROG_SIM"):
            print("SIM", self.simulate(), {e: len(self.q[e]) for e in self.ENGS})
        nc = self.nc
        names = sorted(self.cnt.keys())
        sems = {}
        import contextlib
        with contextlib.ExitStack() as st:
            for n in names:
                sems[n] = st.enter_context(nc.semaphore(n))
            block = st.enter_context(nc.Block())
            q = self.q
            q["sync"].append((None, [w for w in final_waits if w is not None], None, 0))

            def runner(ename):
                def _(eng):
                    for fn, ws, sig, inc in q[ename]:
                        for s, v in ws:
                            eng.wait_ge(sems[s], v)
                        if fn is None:
                            continue
                        ins = fn(eng)
                        if sig is not None:
                            ins.then_inc(sems[sig], inc)
                return _
            block.sync(runner("sync"))
            block.scalar(runner("scalar"))
            block.vector(runner("vector"))
            block.gpsimd(runner("gpsimd"))
            block.tensor(runner("tensor"))


def _run(nc, in_maps):
    res = run_bass_kernel_spmd(nc, in_maps, core_ids=list(range(NCORES)))
    return res.results


def build_cast(F):
    nc = bass.Bass("TRN2", target_bir_lowering=False)
    src = nc.dram_tensor("src", [128, F], F32, kind="ExternalInput").ap()
    dst = nc.dram_tensor("dst", [128, F], BF16, kind="ExternalOutput").ap()
    T = 4096
    nt = (F + T - 1) // T
    NB = 3
    import contextlib
    with contextlib.ExitStack() as st:
        ins = [st.enter_context(nc.sbuf_tensor(f"in{i}", [128, T], F32)) for i in range(NB)]
        outs = [st.enter_context(nc.sbuf_tensor(f"out{i}", [128, T], BF16)) for i in range(NB)]
        P = Prog(nc)
        cast_tok = [None] * NB
        st_tok = [None] * NB
        for t in range(nt):
            b = t % NB
            w = min(T, F - t * T)
            ld = P.dma("sync", ins[b][:, :w], src[:, t * T:t * T + w], f"ld{b}", [cast_tok[b]])
            if t % 2 == 0:
                cast_tok[b] = P.dve(lambda e, b=b, w=w: e.tensor_copy(out=outs[b][:, :w], in_=ins[b][:, :w]),
                                    [ld, st_tok[b]])
            else:
                cast_tok[b] = P.act(lambda e, b=b, w=w: e.copy(out=outs[b][:, :w], in_=ins[b][:, :w]),
                                    [ld, st_tok[b]])
            st_tok[b] = P.dma("gpsimd", dst[:, t * T:t * T + w], outs[b][:, :w], f"st{b}", [cast_tok[b]])
        P.finish(st_tok)
    return nc


def run_cast(flat):
    n = flat.size
    assert n % (NCORES * 128) == 0
    F = n // (NCORES * 128)
    nc = build_cast(F)
    sh = flat.reshape(NCORES, 128, F)
    res = _run(nc, [{"src": np.ascontiguousarray(sh[i])} for i in range(NCORES)])
    return np.stack([r["dst"] for r in res]).reshape(-1)


import contextlib


def fm(v):
    v = np.asarray(v)
    n = v.shape[-1] // 128
    lead = v.shape[:-1]
    a = v.reshape(lead + (n, 128))
    a = np.moveaxis(a, -1, 0)
    return np.ascontiguousarray(a)


def build_mods():
    nc = bass.Bass("TRN2", target_bir_lowering=False)
    cc = nc.dram_tensor("cc", [128, 16, 2], F32, kind="ExternalInput").ap()
    w = nc.dram_tensor("w", [DEPTH, D, 1536], F32, kind="ExternalInput").ap()
    b = nc.dram_tensor("b", [128, DEPTH, 12], F32, kind="ExternalInput").ap()
    out = nc.dram_tensor("out", [128, DEPTH, 12, 2], F32, kind="ExternalOutput").ap()
    with contextlib.ExitStack() as st:
        sb = lambda n, s, d: st.enter_context(nc.sbuf_tensor(n, s, d))
        cs = sb("cs", [128, 16, 2], F32)
        ss = sb("ss", [128, 16, 2], F32)
        bs = sb("bs", [128, DEPTH, 12], F32)
        os_ = sb("os", [128, DEPTH, 12, 2], F32)
        wb = [sb(f"wb{i}", [128, 16, 512], F32) for i in range(2)]
        ps = st.enter_context(nc.psum_tensor("ps", [128, 2, 512], F32))
        P = Prog(nc)
        t_c = P.dma("sync", cs[:], cc, "ld_c")
        t_b = P.dma("sync", bs[:], b, "ld_b")
        t_s = P.act(lambda e: e.activation(out=ss[:], in_=cs[:], func=AF.Silu), [t_c])
        wfree = [None, None]
        evs = []
        n = 0
        for l in range(DEPTH):
            for blk in range(3):
                bi = n % 2
                ld = P.dma("sync" if n % 2 == 0 else "gpsimd", wb[bi][:],
                           w[l, :, blk * 512:(blk + 1) * 512].rearrange("(kc p) n -> p kc n", p=128),
                           f"ld_w{bi}", [wfree[bi]])
                for j in range(4):
                    jj = blk * 4 + j
                    for k in range(16):
                        t_mm = P.pe(lambda e, bi=bi, j=j, k=k, jj=jj, n=n: e.matmul(
                            ps[:, n % 2, jj * 2:jj * 2 + 2], lhsT=wb[bi][:, k, j * 128:(j + 1) * 128],
                            rhs=ss[:, k, :], start=(k == 0), stop=(k == 15)),
                            [ld, t_s] + (evs[-1:] if k == 0 else []), sig=(k == 15))
                    ev = P.dve(lambda e, l=l, jj=jj, n=n: e.tensor_scalar(
                        out=os_[:, l, jj, :], in0=ps[:, n % 2, jj * 2:jj * 2 + 2], scalar1=bs[:, l, jj:jj + 1],
                        scalar2=None, op0=ALU.add), [t_mm, t_b])
                    evs.append(ev)
                wfree[bi] = t_mm
                n += 1
        t_o = P.dma("sync", out, os_[:], "st_o", [evs[-1]])
        P.finish([t_o])
    return nc


def run_mods(c, c_ctx, w_ada, b_ada):
    nc = build_mods()
    cc = np.stack([fm(c.reshape(-1)), fm(c_ctx.reshape(-1))], axis=-1)
    in_maps = []
    for i in range(NCORES):
        sl = slice(1536 * i, 1536 * (i + 1))
        in_maps.append({"cc": cc, "w": np.ascontiguousarray(w_ada[:, :, sl]),
                        "b": fm(b_ada[:, sl])})
    res = _run(nc, in_maps)
    mods = np.zeros((DEPTH, 2, 6 * D), np.float32)
    for i in range(NCORES):
        o = res[i]["out"]
        mods[:, :, 1536 * i:1536 * (i + 1)] = np.transpose(o, (1, 3, 2, 0)).reshape(DEPTH, 2, 1536)
    return mods


BLKS = [(0, 512), (512, 512), (1024, 32)]


def emit_norm_mod(P, nc, X, hT, sc, sh, g, ones, ps_ms, sq, rstd, tmp, s1, x_ready, extra_waits=()):
    ew = list(extra_waits)
    epsc = s1[:, 0, 2:3]
    t_eps = P.dve(lambda e: e.memset(s1[:, :, 2:3], EPS))
    t_s1 = []
    for j in range(2):
        t_s1.append(P.dve(lambda e, j=j: e.scalar_tensor_tensor(
            out=s1[:, :, j], in0=sc[:, :, j], scalar=1.0, in1=g, op0=ALU.add, op1=ALU.mult), ew))
    sq_tok = [None] * len(sq)
    mm_tok = None
    for c in range(16):
        b = c % len(sq)
        xr = x_ready[c] if isinstance(x_ready, list) else x_ready
        t_sq = P.act(lambda e, c=c, b=b: e.activation(out=sq[b][:], in_=X[:, c, :], func=AF.Square),
                     [xr, sq_tok[b]] + ew)
        for bi, (s0, w) in enumerate(BLKS):
            mm_tok = P.pe(lambda e, c=c, b=b, bi=bi, s0=s0, w=w: e.matmul(
                ps_ms[:, bi, :w], lhsT=ones[:], rhs=sq[b][:, s0:s0 + w], start=(c == 0), stop=(c == 15)),
                [t_sq] + ew, sig=(bi == 2))
        sq_tok[b] = mm_tok
    t_r = None
    for bi, (s0, w) in enumerate(BLKS):
        t_q = P.act(lambda e, bi=bi, s0=s0, w=w: e.activation(
            out=rstd[:, s0:s0 + w], in_=ps_ms[:, bi, :w], func=AF.Sqrt, bias=epsc[:, 0:1], scale=1.0), [mm_tok, t_eps])
        t_r = P.dve(lambda e, bi=bi, s0=s0, w=w: e.reciprocal(
            out=rstd[:, s0:s0 + w], in_=rstd[:, s0:s0 + w]), [t_q])
    toks = []
    tmp_tok = [None] * len(tmp)
    for c in range(16):
        b = c % len(tmp)
        t_m = P.dve(lambda e, c=c, b=b: e.tensor_tensor(out=tmp[b][:], in0=X[:, c, :], in1=rstd[:], op=ALU.mult),
                    [t_r, tmp_tok[b]])
        P.act(lambda e, c=c, b=b: e.activation(out=hT[:, c, 0:1024], in_=tmp[b][:, 0:1024], func=AF.Identity,
                                               scale=s1[:, c, 0:1], bias=sh[:, c, 0:1]), [t_m, t_s1[1]], sig=False)
        t_a = P.act(lambda e, c=c, b=b: e.activation(out=hT[:, c, 1024:TOK], in_=tmp[b][:, 1024:TOK], func=AF.Identity,
                                                     scale=s1[:, c, 1:2], bias=sh[:, c, 1:2]), [t_m])
        tmp_tok[b] = t_a
        toks.append(t_a)
    return toks


def emit_proj(P, nc, wdram, nout, hT, h_ready, wbufs, psb, evac, name, kch=16, wfree0=None, ps_free0=None):
    nblk = (nout + 511) // 512
    wfree = list(wfree0) if wfree0 else [None] * len(wbufs)
    ps_free = list(ps_free0) if ps_free0 else [None] * len(psb)
    pi = 0
    evs = []
    last_mm = None
    for nb in range(nblk):
        b = nb % len(wbufs)
        ncol = min(512, nout - nb * 512)
        ld = P.dma("sync" if nb % 2 == 0 else "gpsimd", wbufs[b][:, :, :ncol],
                   wdram[:, nb * 512:nb * 512 + ncol].rearrange("(kc p) n -> p kc n", p=128),
                   f"ld_{name}{b}", [wfree[b]])
        for j in range((ncol + 127) // 128):
            m = min(128, ncol - j * 128)
            for bi, (s0, w) in enumerate(BLKS):
                pb = pi % len(psb)
                pi += 1
                for k in range(kch):
                    last_mm = P.pe(lambda e, b=b, j=j, m=m, k=k, s0=s0, w=w, pb=pb: e.matmul(
                        psb[pb][:m, :w], lhsT=wbufs[b][:, k, j * 128:j * 128 + m], rhs=hT[:, k, s0:s0 + w],
                        start=(k == 0), stop=(k == kch - 1)),
                        ([ld, ps_free[pb]] + list(h_ready)) if k == 0 else [], sig=(k == kch - 1))
                ev = evac(nb * 4 + j, m, bi, s0, w, psb[pb], last_mm)
                ps_free[pb] = ev
                evs.append(ev)
        wfree[b] = last_mm
    return last_mm, evs


def build_B():
    nc = bass.Bass("TRN2", target_bir_lowering=False)
    xT = nc.dram_tensor("xT", [D, TOK], F32, kind="ExternalInput").ap()
    scd = nc.dram_tensor("sc", [128, 16, 2], F32, kind="ExternalInput").ap()
    shd = nc.dram_tensor("sh", [128, 16, 2], F32, kind="ExternalInput").ap()
    gd = nc.dram_tensor("g", [128, 16], F32, kind="ExternalInput").ap()
    wd = nc.dram_tensor("w", [D, DIN], BF16, kind="ExternalInput").ap()
    pT = nc.dram_tensor("pT", [DIN, TOK], F32, kind="ExternalOutput").ap()
    with contextlib.ExitStack() as st:
        sb = lambda n, s, d: st.enter_context(nc.sbuf_tensor(n, s, d))
        X = sb("X", [128, 16, TOK], F32)
        hT = sb("hT", [128, 16, TOK], BF16)
        sc = sb("sc_s", [128, 16, 2], F32)
        sh = sb("sh_s", [128, 16, 2], F32)
        g = sb("g_s", [128, 16], F32)
        s1 = sb("s1", [128, 16, 3], F32)
        ones = sb("ones", [128, 128], BF16)
        sq = [sb(f"sq{i}", [128, TOK], BF16) for i in range(3)]
        rstd = sb("rstd", [128, TOK], F32)
        tmp = [sb(f"tmp{i}", [128, TOK], F32) for i in range(2)]
        wb = [sb(f"wb{i}", [128, 16, 512], BF16) for i in range(2)]
        ot = [sb(f"ot{i}", [128, TOK], F32) for i in range(3)]
        ps_ms = st.enter_context(nc.psum_tensor("ps_ms", [128, 3, 512], F32))
        ps_mm = st.enter_context(nc.psum_tensor("ps_mm", [128, 5, 512], F32))
        P = Prog(nc)
        t_ones = P.dve(lambda e: e.memset(ones[:], 1.0 / D))
        xr = []
        xv = xT.rearrange("(c p) t -> p c t", p=128)
        for c4 in range(4):
            t = P.dma("sync" if c4 % 2 == 0 else "gpsimd", X[:, c4 * 4:(c4 + 1) * 4, :], xv[:, c4 * 4:(c4 + 1) * 4, :], f"ld_x{c4}")
            xr += [t] * 4
        t1 = P.dma("gpsimd", sc[:], scd, "ld_sc")
        t2 = P.dma("gpsimd", sh[:], shd, "ld_sh")
        t3 = P.dma("gpsimd", g[:], gd, "ld_g")
        h_ready = emit_norm_mod(P, nc, X, hT, sc, sh, g[:], ones, ps_ms, sq, rstd, tmp, s1, xr, [t1, t2, t3, t_ones])

        ot_free = [None] * 3
        state = {"n": 0, "evs": []}
        out_toks = []

        def evac(nchunk, m, bi, s0, w, ps, mm):
            b = nchunk % 3
            if bi == 1:
                tk = P.act(lambda e: e.copy(out=ot[b][:m, s0:s0 + w], in_=ps[:m, :w]), [mm, ot_free[b]])
            else:
                tk = P.dve(lambda e: e.tensor_copy(out=ot[b][:m, s0:s0 + w], in_=ps[:m, :w]), [mm, ot_free[b]])
            state["evs"].append(tk)
            if bi == 2:
                d = P.dma("sync", pT[nchunk * 128:nchunk * 128 + m, :], ot[b][:m, :], f"st_p{b}", state["evs"][-3:])
                ot_free[b] = d
                out_toks.append(d)
            return tk

        emit_proj(P, nc, wd, DIN, hT, h_ready, wb, [ps_mm[:, i, :] for i in range(5)], evac, "w")
        P.finish(out_toks[-3:])
    return nc


class Res:
    __slots__ = ("w", "r")

    def __init__(self):
        self.w = None
        self.r = {}


class TP(Prog):
    def _deps(self, reads, writes):
        waits = []
        for r in reads:
            waits.append(r.w)
        for w in writes:
            waits.append(w.w)
            waits += list(w.r.items())
        return waits

    def _mark(self, tok, reads, writes):
        for r in reads:
            s, v = tok
            if r.r.get(s, 0) < v:
                r.r[s] = v
        for w in writes:
            w.w = tok
            w.r = {}

    def op(self, eng, fn, reads=(), writes=()):
        sem = {"tensor": "s_pe", "scalar": "s_act", "vector": "s_dve", "gpsimd": "s_pool"}[eng]
        tok = self.emit(eng, fn, self._deps(reads, writes), sem)
        self._mark(tok, reads, writes)
        return tok

    def group(self, fns, reads=(), writes=()):
        waits = self._deps(reads, writes)
        tok = None
        for i, fn in enumerate(fns):
            tok = self.emit("tensor", fn, waits if i == 0 else (), "s_pe" if i == len(fns) - 1 else None)
        self._mark(tok, reads, writes)
        return tok

    def xdma(self, q, out, in_, sem, reads=(), writes=()):
        tok = self.emit(q, lambda e: e.dma_start(out=out, in_=in_), self._deps(reads, writes), sem, 16)
        self._mark(tok, reads, writes)
        return tok


NCH = 66
NTOK = NCH * 128


def build_ssd():
    nc = bass.Bass("TRN2", target_bir_lowering=False)
    uT = nc.dram_tensor("uT", [448, NTOK], F32, kind="ExternalInput").ap()
    dtr = nc.dram_tensor("dtr", [128, 2, NCH, 3], F32, kind="ExternalInput").ap()
    cwd = nc.dram_tensor("cw", [128, 4, 5], F32, kind="ExternalInput").ap()
    cbd = nc.dram_tensor("cb", [128, 4], F32, kind="ExternalInput").ap()
    smd = nc.dram_tensor("sm", [128, 15], F32, kind="ExternalInput").ap()
    cst = nc.dram_tensor("cst", [128, 8, 128], F32, kind="ExternalInput").ap()
    Y = nc.dram_tensor("Y", [2, NTOK, 192], F32, kind="ExternalOutput").ap()
    with contextlib.ExitStack() as st:
        sb = lambda n, s, d: st.enter_context(nc.sbuf_tensor(n, s, d))
        BT = sb("BT", [128, NTOK], BF16)
        CT = sb("CT", [128, NTOK], BF16)
        Btm = sb("Btm", [128, NCH, 128], BF16)
        Xtm = sb("Xtm", [128, NCH, 192], F32)
        cw = sb("cw_s", [128, 4, 5], F32)
        cb = sb("cb_s", [128, 4], F32)
        sm = sb("sm_s", [128, 15], F32)
        K8 = sb("K8", [128, 8, 128], F32)
        identb = sb("identb", [128, 128], BF16)
        dts = {n: sb(n, [128, 2, NCH, 3], F32) for n in
               ("raw", "t1", "dtv", "dta", "acum", "tot", "sdec", "ea", "dec", "dtsd")}
        avec = sb("avec", [128, 6], F32)
        PIECE = 1024
        raw = [sb(f"rawb{i}", [128, PIECE], F32) for i in range(2)]
        acc = [sb(f"accb{i}", [128, PIECE], F32) for i in range(2)]
        sil = [sb(f"silb{i}", [128, PIECE], F32) for i in range(2)]
        Rb = [sb(f"R{i}", [128, 2, 3, 128], BF16) for i in range(2)]
        K8b = sb("K8b", [128, 4, 128], BF16)
        dth = sb("dth", [128, 2, NCH, 3], BF16)
        dthf = sb("dthf", [128, 2, NCH, 3], F32)
        dtl = sb("dtl", [128, 2, NCH, 3], F32)
        Eb = [sb(f"E{i}", [128, 3, 128], F32) for i in range(2)]
        Mb = [sb(f"M{i}", [128, 3, 128], BF16) for i in range(2)]
        rtmp = [sb(f"rtmp{i}", [128, 192], F32) for i in range(2)]
        CBm = [sb(f"CBm{i}", [128, 128], F32) for i in range(2)]
        xdt2 = [[sb(f"xdt{i}_{j}", [128, 192], BF16) for j in range(2)] for i in range(2)]
        xs2 = [[sb(f"xs{i}_{j}", [128, 192], BF16) for j in range(2)] for i in range(2)]
        xd2 = [sb(f"xd_{j}", [128, 192], BF16) for j in range(2)]
        hst = [sb(f"hst{i}", [128, 192], F32) for i in range(2)]
        hbf = [[sb(f"hbf{i}_{j}", [128, 192], BF16) for j in range(2)] for i in range(2)]
        yo_t = [sb(f"yot{i}", [128, 192], F32) for i in range(2)]
        yo = [sb(f"yo{i}", [128, 192], F32) for i in range(4)]
        ps_seg = [st.enter_context(nc.psum_tensor(f"ps_seg{i}", [128, 512], F32)) for i in range(2)]
        ps_cb = st.enter_context(nc.psum_tensor("ps_cb", [128, 512], F32))
        ps_y = [st.enter_context(nc.psum_tensor(f"ps_y{i}", [128, 512], F32)) for i in range(2)]
        ps_st = st.enter_context(nc.psum_tensor("ps_st", [128, 512], F32))
        ps_tr = [st.enter_context(nc.psum_tensor(f"ps_tr{i}", [128, 512], F32)) for i in range(2)]

        P = TP(nc)
        R = {}

        def res(name):
            if name not in R:
                R[name] = Res()
            return R[name]

        Tm, Um, SU, SL, MF, MB, ID, ON = [K8[:, i, :] for i in range(8)]
        P.xdma("sync", K8[:], cst, "ld_k8", writes=[res("K8")])
        P.xdma("sync", cw[:], cwd, "ld_cw", writes=[res("cw")])
        P.xdma("sync", cb[:], cbd, "ld_cb", writes=[res("cb")])
        P.xdma("sync", sm[:], smd, "ld_sm", writes=[res("sm")])
        P.xdma("sync", dts["raw"][:], dtr, "ld_dt", writes=[res("raw")])
        P.op("vector", lambda e: e.tensor_copy(out=identb[:], in_=ID), [res("K8")], [res("identb")])
        P.op("vector", lambda e: e.tensor_copy(out=K8b[:], in_=K8[:, 0:4, :]), [res("K8")], [res("K8b")])

        fl = lambda n: dts[n][:].rearrange("p a c j -> p (a c j)")
        col = lambda n, d, j: dts[n][:, d, :, j]
        for d in range(2):
            for j in range(3):
                P.op("vector", lambda e, d=d, j=j: e.tensor_scalar(
                    out=col("raw", d, j), in0=col("raw", d, j), scalar1=sm[:, d * 3 + j:d * 3 + j + 1], scalar2=None,
                    op0=ALU.add), [res("raw"), res("sm")], [res("raw")])
        P.op("scalar", lambda e: e.activation(out=fl("t1"), in_=fl("raw"), func=AF.Abs), [res("raw")], [res("t1")])
        P.op("scalar", lambda e: e.activation(out=fl("t1"), in_=fl("t1"), func=AF.Exp, scale=-1.0), [res("t1")], [res("t1")])
        P.op("vector", lambda e: e.tensor_scalar_add(out=fl("t1"), in0=fl("t1"), scalar1=1.0), [res("t1")], [res("t1")])
        P.op("scalar", lambda e: e.activation(out=fl("t1"), in_=fl("t1"), func=AF.Ln), [res("t1")], [res("t1")])
        P.op("vector", lambda e: e.scalar_tensor_tensor(out=fl("dtv"), in0=fl("raw"), scalar=0.0, in1=fl("t1"),
                                                         op0=ALU.max, op1=ALU.add), [res("raw"), res("t1")], [res("dtv")])
        P.op("scalar", lambda e: e.activation(out=avec[:], in_=sm[:, 6:12], func=AF.Exp), [res("sm")], [res("avec")])
        P.op("vector", lambda e: e.tensor_scalar_mul(out=avec[:], in0=avec[:], scalar1=-1.0), [res("avec")], [res("avec")])
        for d in range(2):
            for j in range(3):
                P.op("vector", lambda e, d=d, j=j: e.tensor_scalar(
                    out=col("dta", d, j), in0=col("dtv", d, j), scalar1=avec[:, d * 3 + j:d * 3 + j + 1], scalar2=None,
                    op0=ALU.mult), [res("dtv"), res("avec")], [res("dta")])
        fl2 = lambda t: t[:].rearrange("p a c j -> p (a c j)")
        P.op("vector", lambda e: e.tensor_copy(out=fl2(dth), in_=fl("dta")), [res("dta")], [res("dth")])
        P.op("vector", lambda e: e.tensor_copy(out=fl2(dthf), in_=fl2(dth)), [res("dth")], [res("dthf")])
        P.op("vector", lambda e: e.tensor_tensor(out=fl2(dtl), in0=fl("dta"), in1=fl2(dthf), op=ALU.subtract), [res("dta"), res("dthf")], [res("dtl")])
        for d in range(2):
            lhs = Tm if d == 0 else Um
            P.group([lambda e, d=d, lhs=lhs: e.matmul(ps_tr[0][:, d * 256:d * 256 + 198], lhsT=lhs,
                                                       rhs=dts["dta"][:, d].rearrange("p c j -> p (c j)"),
                                                       start=True, stop=True)],
                    [res("K8"), res("dta")], [res("ps_tr0")])
            P.group([lambda e, d=d: e.matmul(ps_tr[1][:, d * 256:d * 256 + 198], lhsT=ON,
                                             rhs=dts["dta"][:, d].rearrange("p c j -> p (c j)"),
                                             start=True, stop=True)],
                    [res("K8"), res("dta")], [res("ps_tr1")])
        for d in range(2):
            P.op("vector", lambda e, d=d: e.tensor_copy(out=dts["acum"][:, d].rearrange("p c j -> p (c j)"),
                                                       in_=ps_tr[0][:, d * 256:d * 256 + 198]),
                 [res("ps_tr0")], [res("acum")])
            P.op("vector", lambda e, d=d: e.tensor_copy(out=dts["tot"][:, d].rearrange("p c j -> p (c j)"),
                                                       in_=ps_tr[1][:, d * 256:d * 256 + 198]),
                 [res("ps_tr1")], [res("tot")])
        P.op("vector", lambda e: e.tensor_tensor(out=fl("sdec"), in0=fl("tot"), in1=fl("acum"), op=ALU.subtract),
             [res("tot"), res("acum")], [res("sdec")])
        P.op("scalar", lambda e: e.activation(out=fl("sdec"), in_=fl("sdec"), func=AF.Exp), [res("sdec")], [res("sdec")])
        P.op("scalar", lambda e: e.activation(out=fl("ea"), in_=fl("acum"), func=AF.Exp), [res("acum")], [res("ea")])
        P.op("scalar", lambda e: e.activation(out=fl("dec"), in_=fl("tot"), func=AF.Exp), [res("tot")], [res("dec")])
        P.op("vector", lambda e: e.tensor_tensor(out=fl("dtsd"), in0=fl("dtv"), in1=fl("sdec"), op=ALU.mult),
             [res("dtv"), res("sdec")], [res("dtsd")])

        tiles = [(0, 128), (128, 64), (192, 128), (320, 128)]
        pieces = [(0, 256, 256)] + [(256 + i * 1024, 1024, 64) for i in range(8)]
        n = 0
        ntr = 0
        for (t0, nt, rl) in pieces:
            for ti, (r0, npart) in enumerate(tiles):
                b = n % 2
                n += 1
                rw, ac, so = raw[b], acc[b], sil[b]
                rr, ra, rs = res(f"raw{b}"), res(f"acc{b}"), res(f"sil{b}")
                P.xdma("sync" if n % 2 else "gpsimd", rw[:npart, :nt], uT[r0:r0 + npart, t0:t0 + nt], f"ld_raw{b}", writes=[rr])
                v = lambda a, npart=npart, nt=nt, rl=rl: a[:npart, :nt].rearrange("p (r t) -> p r t", t=rl)
                P.op("vector", lambda e, ti=ti, rw=rw, ac=ac, npart=npart, nt=nt: e.tensor_scalar(
                    out=ac[:npart, :nt], in0=rw[:npart, :nt], scalar1=cw[:npart, ti, 2:3], scalar2=None, op0=ALU.mult),
                    [rr, res("cw")], [ra])
                for j in (0, 1, 3, 4):
                    sh_ = j - 2
                    a0, a1 = max(0, -sh_), min(rl, rl - sh_)
                    P.op("vector", lambda e, ti=ti, j=j, v=v, rw=rw, ac=ac, a0=a0, a1=a1, sh_=sh_, npart=npart: e.scalar_tensor_tensor(
                        out=v(ac)[:, :, a0:a1], in0=v(rw)[:, :, a0 + sh_:a1 + sh_], scalar=cw[:npart, ti, j:j + 1],
                        in1=v(ac)[:, :, a0:a1], op0=ALU.mult, op1=ALU.add), [rr, ra, res("cw")], [ra])
                P.op("scalar", lambda e, ti=ti, ac=ac, so=so, npart=npart, nt=nt: e.activation(
                    out=so[:npart, :nt], in_=ac[:npart, :nt], func=AF.Silu, bias=cb[:npart, ti:ti + 1], scale=1.0),
                    [ra, res("cb")], [rs])
                if ti == 2:
                    P.op("vector", lambda e, so=so, nt=nt, t0=t0: e.tensor_copy(out=BT[:, t0:t0 + nt], in_=so[:, :nt]), [rs], [res("BT")])
                if ti == 3:
                    P.op("vector", lambda e, so=so, nt=nt, t0=t0: e.tensor_copy(out=CT[:, t0:t0 + nt], in_=so[:, :nt]), [rs], [res("CT")])
                    continue
                for cc in range(nt // 128):
                    ch = t0 // 128 + cc
                    pt = ntr % 2
                    ntr += 1
                    rp = res(f"ps_tr{pt}")
                    P.group([lambda e, so=so, cc=cc, npart=npart, pt=pt: e.transpose(
                        ps_tr[pt][:, :npart], so[:npart, cc * 128:(cc + 1) * 128], K8[:npart, 6, :npart])],
                        [rs, res("K8")], [rp])
                    if ti == 2:
                        P.op("scalar", lambda e, ch=ch, pt=pt: e.copy(out=Btm[:, ch, :], in_=ps_tr[pt][:, :128]), [rp], [res("Btm")])
                    else:
                        c0 = 0 if ti == 0 else 128
                        P.op("vector" if ti == 0 else "scalar",
                             (lambda e, ch=ch, pt=pt, c0=c0, npart=npart: e.tensor_copy(out=Xtm[:, ch, c0:c0 + npart], in_=ps_tr[pt][:, :npart]))
                             if ti == 0 else
                             (lambda e, ch=ch, pt=pt, c0=c0, npart=npart: e.copy(out=Xtm[:, ch, c0:c0 + npart], in_=ps_tr[pt][:, :npart])),
                             [rp], [res("Xtm")])

        for i in range(2):
            P.op("vector", lambda e, i=i: e.memset(hst[i][:], 0.0), [], [res(f"hst{i}")])
            P.op("vector", lambda e, i=i: e.memset(hbf[i][0][:], 0.0), [], [res(f"hbf{i}_0")])
        border = [1, 0] + list(range(65, 1, -1))
        dvec = sm[:, 12:15]
        out_tok = []
        h3 = lambda ap: ap.rearrange("p (h c) -> p h c", h=3)
        stb = [ps_st, ps_tr[1]]
        stn = ["ps_st0", "ps_tr1"]
        cbb = [ps_cb, ps_tr[0]]
        cbn = ["ps_cb", "ps_tr0"]

        def unit(u):
            step, d = u // 2, u % 2
            ch = step if d == 0 else border[step]
            return step, d, ch

        def front(u):
            front_a(u)
            front_b(u)

        def front_a(u):
            cbpart(u)
            prep(u)

        def cbpart(u):
            step, d, ch = unit(u)
            cs = slice(ch * 128, (ch + 1) * 128)
            P.group([lambda e, cs=cs, d=d: e.matmul(cbb[d][:, :128], lhsT=BT[:, cs], rhs=CT[:, cs], start=True, stop=True)],
                    [res("BT"), res("CT")], [res(cbn[d])])
            P.op("vector", lambda e, d=d: e.tensor_tensor(out=CBm[d][:], in0=cbb[d][:, :128], in1=(MF if d == 0 else MB), op=ALU.mult),
                 [res(cbn[d]), res("K8")], [res(f"CBm{d}")])

        def prep(u):
            step, d, ch = unit(u)
            par = step % 2
            xdt_, xs_, xd_ = xdt2[d][par], xs2[d][par], xd2[par]
            P.op("vector", lambda e, d=d, ch=ch, xdt_=xdt_: e.tensor_tensor(
                out=h3(xdt_[:]), in0=h3(Xtm[:, ch, :]), in1=dts["dtv"][:, d, ch, 0:3].unsqueeze(2).to_broadcast([128, 3, 64]), op=ALU.mult),
                [res("Xtm"), res("dtv")], [res(f"xdt{d}_{par}")])
            for h in range(3):
                hs = slice(h * 64, (h + 1) * 64)
                P.op("scalar", lambda e, d=d, ch=ch, h=h, hs=hs, xs_=xs_: e.activation(
                    out=xs_[:, hs], in_=Xtm[:, ch, hs], func=AF.Copy, scale=dts["dtsd"][:, d, ch, h:h + 1]),
                    [res("Xtm"), res("dtsd")], [res(f"xs{d}_{par}")])
            if d == 0:
                P.op("vector", lambda e, ch=ch, xd_=xd_: e.tensor_tensor(
                    out=h3(xd_[:]), in0=h3(Xtm[:, ch, :]), in1=dvec.unsqueeze(2).to_broadcast([128, 3, 64]), op=ALU.mult),
                    [res("Xtm"), res("sm")], [res(f"xd_{par}")])
            Tsel = K8[:, (0 if d == 0 else 1), :]
            P.op("vector", lambda e, d=d, ch=ch, Tsel=Tsel: e.tensor_tensor(
                out=Rb[d][:, 0], in0=Tsel.unsqueeze(1).to_broadcast([128, 3, 128]),
                in1=dthf[:, d, ch, 0:3].unsqueeze(2).to_broadcast([128, 3, 128]), op=ALU.mult),
                [res("K8"), res("dthf")], [res(f"R{d}")])
            for h in range(3):
                P.op("scalar", lambda e, d=d, ch=ch, h=h, Tsel=Tsel: e.activation(
                    out=Rb[d][:, 1, h, :], in_=Tsel, func=AF.Copy, scale=dtl[:, d, ch, h:h + 1]),
                    [res("K8"), res("dtl")], [res(f"R{d}")])
        def front_b(u):
            step, d, ch = unit(u)
            Ssel = K8b[:, (2 if d == 0 else 3), :]
            P.group([lambda e, d=d, h=h, t=t, Ssel=Ssel: e.matmul(ps_seg[d][:, h * 128:(h + 1) * 128], lhsT=Ssel,
                                                  rhs=Rb[d][:, t, h, :], start=(t == 0), stop=(t == 1)) for h in range(3) for t in range(2)],
                    [res("K8b"), res(f"R{d}")], [res(f"ps_seg{d}")])
            P.op("scalar", lambda e, d=d: e.activation(out=Eb[d][:].rearrange("p h l -> p (h l)"), in_=ps_seg[d][:, 0:384], func=AF.Exp),
                 [res(f"ps_seg{d}")], [res(f"E{d}")])
            P.op("vector", lambda e, d=d: e.tensor_tensor(out=Mb[d][:], in0=Eb[d][:], in1=CBm[d][:].unsqueeze(1).to_broadcast([128, 3, 128]), op=ALU.mult),
                 [res(f"E{d}"), res(f"CBm{d}")], [res(f"M{d}")])

        def back(u):
            back_pe(u)
            back_rest(u)

        def back_pe(u):
            step, d, ch = unit(u)
            cs = slice(ch * 128, (ch + 1) * 128)
            hp = step % 2
            par = step % 2
            xdt_, xs_, xd_ = xdt2[d][par], xs2[d][par], xd2[par]
            fns = []
            for h in range(3):
                hs = slice(h * 64, (h + 1) * 64)
                fns.append(lambda e, d=d, h=h, hs=hs, xdt_=xdt_: e.matmul(ps_y[d][:, hs], lhsT=Mb[d][:, h, :], rhs=xdt_[:, hs],
                                                               start=True, stop=(d == 1)))
                if d == 0:
                    fns.append(lambda e, hs=hs, xd_=xd_: e.matmul(ps_y[0][:, hs], lhsT=identb[:], rhs=xd_[:, hs], start=False, stop=True))
            fns.append(lambda e, d=d, cs=cs, hp=hp: e.matmul(ps_y[d][:, 256:448], lhsT=CT[:, cs], rhs=hbf[d][hp][:], start=True, stop=True))
            fns.append(lambda e, d=d, ch=ch, xs_=xs_: e.matmul(stb[d][:, 0:192], lhsT=Btm[:, ch, :], rhs=xs_[:], start=True, stop=True))
            P.group(fns, [res(f"M{d}"), res(f"xdt{d}_{par}"), res("identb"), res(f"xd_{par}"), res("CT"), res(f"hbf{d}_{hp}"), res("Btm"), res(f"xs{d}_{par}")],
                    [res(f"ps_y{d}"), res(stn[d])])
        def back_rest(u):
            step, d, ch = unit(u)
            hp = step % 2
            P.op("vector", lambda e, d=d, ch=ch: e.tensor_tensor(
                out=h3(rtmp[d][:]), in0=h3(hst[d][:]), in1=dts["dec"][:, d, ch, 0:3].unsqueeze(2).to_broadcast([128, 3, 64]), op=ALU.mult),
                [res(f"hst{d}"), res("dec")], [res(f"rtmp{d}")])
            P.op("vector", lambda e, d=d: e.tensor_tensor(out=hst[d][:], in0=rtmp[d][:], in1=stb[d][:, 0:192], op=ALU.add),
                 [res(f"rtmp{d}"), res(stn[d])], [res(f"hst{d}")])
            P.op("scalar", lambda e, d=d, hp=hp: e.copy(out=hbf[d][1 - hp][:], in_=hst[d][:]),
                 [res(f"hst{d}")], [res(f"hbf{d}_{1 - hp}")])
            P.op("vector", lambda e, d=d, ch=ch: e.tensor_tensor(
                out=h3(yo_t[d][:]), in0=h3(ps_y[d][:, 256:448]), in1=dts["ea"][:, d, ch, 0:3].unsqueeze(2).to_broadcast([128, 3, 64]), op=ALU.mult),
                [res(f"ps_y{d}"), res("ea")], [res(f"yot{d}")])
            yb = u % 4
            P.op("vector", lambda e, d=d, yb=yb: e.tensor_tensor(out=yo[yb][:], in0=ps_y[d][:, 0:192], in1=yo_t[d][:], op=ALU.add),
                 [res(f"ps_y{d}"), res(f"yot{d}")], [res(f"yo{yb}")])
            out_tok.append(P.xdma("sync", Y[d, ch * 128:(ch + 1) * 128, :], yo[yb][:], f"st_y{yb}", reads=[res(f"yo{yb}")]))

        NU = 2 * NCH
        import os
        mode = os.environ.get("SSD_ORDER", "pair")
        if mode == "plain":
            for u in range(NU):
                front(u)
                back(u)
        elif mode == "prep":
            prep(0)
            for u in range(NU):
                cbpart(u)
                front_b(u)
                if u + 1 < NU:
                    prep(u + 1)
                back(u)
        elif mode == "pair":
            prep(0)
            prep(1)
            for st_ in range(NCH):
                a, b = 2 * st_, 2 * st_ + 1
                cbpart(a)
                cbpart(b)
                front_b(a)
                front_b(b)
                if st_ + 1 < NCH:
                    prep(a + 2)
                    prep(b + 2)
                back(a)
                back(b)
        elif mode == "pipe":
            prep(0)
            cbpart(0)
            front_b(0)
            for u in range(NU):
                if u + 1 < NU:
                    prep(u + 1)
                    cbpart(u + 1)
                    front_b(u + 1)
                back(u)
        elif mode == "late":
            prep(0)
            cbpart(0)
            front_b(0)
            for u in range(NU):
                if u + 1 < NU:
                    prep(u + 1)
                back_pe(u)
                if u + 1 < NU:
                    cbpart(u + 1)
                    front_b(u + 1)
                back_rest(u)
        elif mode == "split":
            front_a(0)
            front_b(0)
            for u in range(NU):
                if u + 1 < NU:
                    front_a(u + 1)
                back(u)
                if u + 1 < NU:
                    front_b(u + 1)
        P.finish(out_tok[-4:])
    return nc


def ssd_consts():
    k = np.arange(128)[:, None]
    l = np.arange(128)[None, :]
    mats = [k <= l, k >= l, k > l, k < l, l >= k, l <= k, k == l, np.ones((128, 128), bool)]
    return np.ascontiguousarray(np.stack([m.astype(np.float32) for m in mats], 1))


def ssd_inputs(PT, conv_w, conv_b, dt_bias, a_log, d_skip):
    cst = ssd_consts()
    maps = []
    for core in range(NCORES):
        g, half = core // 2, core % 2
        hg0 = g * 6 + half * 3
        xc = 2048 + g * 384 + half * 192
        bc = 2048 + 1536 + g * 128
        cc = 2048 + 2048 + g * 128
        uT = np.concatenate([PT[xc:xc + 192], PT[bc:bc + 128], PT[cc:cc + 128]], 0)
        dt = np.stack([PT[4608 + d * 24 + hg0:4608 + d * 24 + hg0 + 3] for d in range(2)], 0)
        dt = dt.reshape(2, 3, NCH, 128).transpose(3, 0, 2, 1)
        cols = [np.arange(xc - 2048, xc - 2048 + 128), np.arange(xc - 2048 + 128, xc - 2048 + 192),
                np.arange(bc - 2048, bc - 2048 + 128), np.arange(cc - 2048, cc - 2048 + 128)]
        cw = np.zeros((128, 4, 5), np.float32)
        cb = np.zeros((128, 4), np.float32)
        for ti, cidx in enumerate(cols):
            cw[:len(cidx), ti, :] = conv_w[:, cidx].T
            cb[:len(cidx), ti] = conv_b[cidx]
        sm = np.concatenate([dt_bias[:, hg0:hg0 + 3].reshape(-1), a_log[:, hg0:hg0 + 3].reshape(-1), d_skip[hg0:hg0 + 3]])
        sm = np.broadcast_to(sm[None, :], (128, 15))
        maps.append({"uT": np.ascontiguousarray(uT), "dtr": np.ascontiguousarray(dt), "cw": cw, "cb": cb,
                     "sm": np.ascontiguousarray(sm, dtype=np.float32), "cst": cst})
    return maps


def build_fourier():
    nc = bass.Bass("TRN2", target_bir_lowering=False)
    uL = nc.dram_tensor("uL", [SEQ, 512], F32, kind="ExternalInput").ap()
    uC = nc.dram_tensor("uC", [CTX, 512], F32, kind="ExternalInput").ap()
    tabC = nc.dram_tensor("tabC", [SEQ, 1024], BF16, kind="ExternalInput").ap()
    tabS = nc.dram_tensor("tabS", [SEQ, 1024], BF16, kind="ExternalInput").ap()
    tcC = nc.dram_tensor("tcC", [CTX, 32], BF16, kind="ExternalInput").ap()
    tcS = nc.dram_tensor("tcS", [CTX, 32], BF16, kind="ExternalInput").ap()
    wfd = nc.dram_tensor("wf", [4, 128, 128], F32, kind="ExternalInput").ap()
    ccd = nc.dram_tensor("cc", [128, 2, 128], F32, kind="ExternalInput").ap()
    fT = nc.dram_tensor("fT", [512, TOK], F32, kind="ExternalOutput").ap()
    with contextlib.ExitStack() as st:
        sb = lambda n, s, d: st.enter_context(nc.sbuf_tensor(n, s, d))
        NB = 3
        ust = [sb(f"ust{i}", [128, 4, 512], F32) for i in range(NB)]
        ub = [sb(f"ub{i}", [128, 4, 512], BF16) for i in range(NB)]
        tC = [sb(f"tC{i}", [128, 4, 512], BF16) for i in range(NB)]
        tS = [sb(f"tS{i}", [128, 4, 512], BF16) for i in range(NB)]
        wf = sb("wf_s", [128, 4, 128], F32)
        cc = sb("cc_s", [128, 2, 128], F32)
        Mm = sb("Mm", [128, 4, 2, 128], BF16)
        AB = sb("AB", [128, 4, 2, 512], BF16)
        fo = sb("fo", [128, 4, TOK], F32)
        ucs = sb("ucs", [128, 2, 512], F32)
        ucb = sb("ucb", [128, 2, 512], BF16)
        tcc = sb("tcc", [128, 2, 2, 32], BF16)
        bank = [st.enter_context(nc.psum_tensor(f"bank{i}", [128, 512], F32)) for i in range(8)]
        P = TP(nc)
        R = {}

        def res(name):
            if name not in R:
                R[name] = Res()
            return R[name]

        P.xdma("sync", wf[:], wfd.rearrange("g j d -> j g d"), "ld_wf", writes=[res("wf")])
        P.xdma("sync", cc[:], ccd, "ld_cc", writes=[res("cc")])
        for g in range(4):
            for t in range(2):
                P.group([lambda e, g=g, t=t: e.matmul(bank[g][:, t * 128:(t + 1) * 128], lhsT=cc[:, t, :], rhs=wf[:, g, :],
                                                      start=True, stop=True)], [res("wf"), res("cc")], [res(f"bank{g}")])
            P.op("vector", lambda e, g=g: e.tensor_copy(out=Mm[:, g].rearrange("p t d -> p (t d)"), in_=bank[g][:, 0:256]),
                 [res(f"bank{g}")], [res("Mm")])

        def stage2(width, c0, ABv):
            for g in range(4):
                P.group([lambda e, g=g, t=t: e.matmul(bank[g][:, :width], lhsT=Mm[:, g, t, :], rhs=ABv(g, t),
                                                      start=(t == 0), stop=(t == 1)) for t in range(2)],
                        [res("Mm"), res("AB")], [res(f"bank{g}")])
                if g % 2 == 0:
                    P.op("vector", lambda e, g=g: e.tensor_copy(out=fo[:, g, c0:c0 + width], in_=bank[g][:, :width]),
                         [res(f"bank{g}")], [res("fo")])
                else:
                    P.op("scalar", lambda e, g=g: e.copy(out=fo[:, g, c0:c0 + width], in_=bank[g][:, :width]),
                         [res(f"bank{g}")], [res("fo")])

        uv = uL.rearrange("(c p) f -> p c f", p=128)
        cv = tabC.rearrange("(c p) k -> p c k", p=128)
        sv = tabS.rearrange("(c p) k -> p c k", p=128)
        n = 0
        for half in range(2):
            ks = slice(half * 512, (half + 1) * 512)
            for nb in range(16):
                b = n % NB
                n += 1
                P.xdma("sync", ust[b][:], uv[:, nb * 4:(nb + 1) * 4, :], f"ld_u{b}", writes=[res(f"ust{b}")])
                P.xdma("gpsimd", tC[b][:], cv[:, nb * 4:(nb + 1) * 4, ks], f"ld_tc{b}", writes=[res(f"tC{b}")])
                P.xdma("gpsimd", tS[b][:], sv[:, nb * 4:(nb + 1) * 4, ks], f"ld_ts{b}", writes=[res(f"tS{b}")])
                if nb % 2 == 0:
                    P.op("vector", lambda e, b=b: e.tensor_copy(out=ub[b][:], in_=ust[b][:]), [res(f"ust{b}")], [res(f"ub{b}")])
                else:
                    P.op("scalar", lambda e, b=b: e.copy(out=ub[b][:], in_=ust[b][:]), [res(f"ust{b}")], [res(f"ub{b}")])
                fns = []
                for ci in range(4):
                    first = (nb == 0 and ci == 0)
                    last = (nb == 15 and ci == 3)
                    for g in range(4):
                        for t in range(2):
                            tb = tC[b] if t == 0 else tS[b]
                            fns.append(lambda e, b=b, ci=ci, g=g, t=t, tb=tb, first=first, last=last: e.matmul(
                                bank[g * 2 + t][:, :], lhsT=ub[b][:, ci, g * 128:(g + 1) * 128], rhs=tb[:, ci, :],
                                start=first, stop=last))
                P.group(fns, [res(f"ub{b}"), res(f"tC{b}"), res(f"tS{b}")], [res(f"bank{i}") for i in range(8)])
            for g in range(4):
                for t in range(2):
                    if t == 0:
                        P.op("vector", lambda e, g=g, t=t: e.tensor_copy(out=AB[:, g, t, :], in_=bank[g * 2 + t][:, :]),
                             [res(f"bank{g * 2 + t}")], [res("AB")])
                    else:
                        P.op("scalar", lambda e, g=g, t=t: e.copy(out=AB[:, g, t, :], in_=bank[g * 2 + t][:, :]),
                             [res(f"bank{g * 2 + t}")], [res("AB")])
            stage2(512, half * 512, lambda g, t: AB[:, g, t, :])
        P.xdma("sync", ucs[:], uC.rearrange("(c p) f -> p c f", p=128), "ld_uc", writes=[res("ucs")])
        P.xdma("sync", tcc[:, 0], tcC.rearrange("(c p) k -> p c k", p=128), "ld_tcc", writes=[res("tcc")])
        P.xdma("sync", tcc[:, 1], tcS.rearrange("(c p) k -> p c k", p=128), "ld_tcs", writes=[res("tcc")])
        P.op("vector", lambda e: e.tensor_copy(out=ucb[:], in_=ucs[:]), [res("ucs")], [res("ucb")])
        for g in range(4):
            for t in range(2):
                P.group([lambda e, g=g, t=t, ci=ci: e.matmul(bank[g * 2 + t][:, :32], lhsT=ucb[:, ci, g * 128:(g + 1) * 128],
                                                            rhs=tcc[:, t, ci, :], start=(ci == 0), stop=(ci == 1)) for ci in range(2)],
                        [res("ucb"), res("tcc")], [res(f"bank{g * 2 + t}")])
                P.op("vector", lambda e, g=g, t=t: e.tensor_copy(out=AB[:, g, t, :32], in_=bank[g * 2 + t][:, :32]),
                     [res(f"bank{g * 2 + t}")], [res("AB")])
        stage2(32, 1024, lambda g, t: AB[:, g, t, :32])
        t_o = P.xdma("sync", fT.rearrange("(g p) t -> p g t", p=128), fo[:], "st_f", reads=[res("fo")])
        P.finish([t_o])
    return nc


def fourier_tables():
    n = np.arange(SEQ, dtype=np.int64)
    ph = (np.outer(n, n) % SEQ).astype(np.float64) * (2 * np.pi / SEQ)
    sc = 1.0 / np.sqrt(SEQ * 128.0)
    C = (np.cos(ph) * sc).astype(NPBF)
    S = (np.sin(ph) * sc).astype(NPBF)
    m = np.arange(CTX, dtype=np.int64)
    phc = (np.outer(m, m) % CTX).astype(np.float64) * (2 * np.pi / CTX)
    scc = 1.0 / np.sqrt(CTX * 128.0)
    Cc_ = (np.cos(phc) * scc).astype(NPBF)
    Sc_ = (np.sin(phc) * scc).astype(NPBF)
    j = np.arange(128, dtype=np.int64)
    p128 = (np.outer(j, j) % 128).astype(np.float64) * (2 * np.pi / 128)
    cc = np.stack([np.cos(p128), -np.sin(p128)], 1).astype(np.float32)
    return C, S, Cc_, Sc_, np.ascontiguousarray(cc)


def build_G():
    nc = bass.Bass("TRN2", target_bir_lowering=False)
    xT = nc.dram_tensor("xT", [D, TOK], F32, kind="ExternalInput").ap()
    fTd = nc.dram_tensor("fT", [512, TOK], F32, kind="ExternalInput").ap()
    yfd = nc.dram_tensor("yf", [1536, TOK], F32, kind="ExternalInput").ap()
    ybd = nc.dram_tensor("yb", [1536, TOK], F32, kind="ExternalInput").ap()
    zd = nc.dram_tensor("zT", [1536, TOK], F32, kind="ExternalInput").ap()
    gnd = nc.dram_tensor("gn", [128, 12], F32, kind="ExternalInput").ap()
    gtd = nc.dram_tensor("gt", [128, 16, 2], F32, kind="ExternalInput").ap()
    wd = nc.dram_tensor("w", [D, D], BF16, kind="ExternalInput").ap()
    xo = nc.dram_tensor("xo", [D, TOK], F32, kind="ExternalOutput").ap()
    with contextlib.ExitStack() as st:
        sb = lambda n, s, d: st.enter_context(nc.sbuf_tensor(n, s, d))
        catT = sb("catT", [128, 16, TOK], BF16)
        gn = sb("gn_s", [128, 12], F32)
        gt = sb("gt_s", [128, 16, 2], F32)
        epsc = sb("epsc", [128, 1], F32)
        ones = sb("ones", [128, 128], BF16)
        fst = [sb(f"fst{i}", [128, TOK], F32) for i in range(2)]
        yfs = [sb(f"yfs{i}", [128, TOK], F32) for i in range(2)]
        ybs = [sb(f"ybs{i}", [128, TOK], F32) for i in range(2)]
        zs = [sb(f"zs{i}", [128, TOK], F32) for i in range(2)]
        uu = sb("uu", [128, 3, TOK], F32)
        sq = [sb(f"sq{i}", [128, TOK], BF16) for i in range(2)]
        rstd = sb("rstd", [128, TOK], F32)
        tmp = [sb(f"tmp{i}", [128, TOK], F32) for i in range(2)]
        wb = [sb(f"wb{i}", [128, 16, 512], BF16) for i in range(2)]
        xc = [sb(f"xc{i}", [128, TOK], F32) for i in range(3)]
        ps_ms = st.enter_context(nc.psum_tensor("ps_ms", [128, 3, 512], F32))
        ps_mm = st.enter_context(nc.psum_tensor("ps_mm", [128, 5, 512], F32))
        P = TP(nc)
        R = {}

        def res(name):
            if name not in R:
                R[name] = Res()
            return R[name]

        P.xdma("sync", gn[:], gnd, "ld_gn", writes=[res("gn")])
        P.xdma("sync", gt[:], gtd, "ld_gt", writes=[res("gt")])
        P.op("vector", lambda e: e.memset(ones[:], 1.0), [], [res("ones")])
        P.op("vector", lambda e: e.memset(epsc[:], EPS), [], [res("epsc")])
        for c in range(4):
            b = c % 2
            P.xdma("sync", fst[b][:], fTd[c * 128:(c + 1) * 128, :], f"ld_f{b}", writes=[res(f"fst{b}")])
            P.op("vector" if c % 2 == 0 else "scalar",
                 (lambda e, c=c, b=b: e.tensor_copy(out=catT[:, c, :], in_=fst[b][:])) if c % 2 == 0 else
                 (lambda e, c=c, b=b: e.copy(out=catT[:, c, :], in_=fst[b][:])),
                 [res(f"fst{b}")], [res("catT")])
        n = 0
        for grp in range(4):
            for c in range(3):
                ch = grp * 3 + c
                b = n % 2
                n += 1
                rows = slice(ch * 128, (ch + 1) * 128)
                P.xdma("sync", yfs[b][:], yfd[rows, :], f"ld_yf{b}", writes=[res(f"yfs{b}")])
                P.xdma("gpsimd", ybs[b][:], ybd[rows, :], f"ld_yb{b}", writes=[res(f"ybs{b}")])
                P.xdma("sync", zs[b][:], zd[rows, :], f"ld_z{b}", writes=[res(f"zs{b}")])
                P.op("vector", lambda e, b=b: e.tensor_tensor(out=yfs[b][:], in0=yfs[b][:], in1=ybs[b][:], op=ALU.add),
                     [res(f"yfs{b}"), res(f"ybs{b}")], [res(f"yfs{b}")])
                P.op("scalar", lambda e, b=b: e.activation(out=zs[b][:], in_=zs[b][:], func=AF.Silu), [res(f"zs{b}")], [res(f"zs{b}")])
                P.op("vector", lambda e, b=b, c=c: e.tensor_tensor(out=uu[:, c, :], in0=yfs[b][:], in1=zs[b][:], op=ALU.mult),
                     [res(f"yfs{b}"), res(f"zs{b}")], [res(f"uu{c}")])
                P.op("scalar", lambda e, b=b, c=c: e.activation(out=sq[b][:], in_=uu[:, c, :], func=AF.Square),
                     [res(f"uu{c}")], [res(f"sq{b}")])
                P.group([lambda e, b=b, c=c, bi=bi, s0=s0, w=w: e.matmul(ps_ms[:, bi, :w], lhsT=ones[:], rhs=sq[b][:, s0:s0 + w],
                                                                        start=(c == 0), stop=(c == 2))
                         for bi, (s0, w) in enumerate(BLKS)], [res("ones"), res(f"sq{b}")], [res("ps_ms")])
            for bi, (s0, w) in enumerate(BLKS):
                P.op("scalar", lambda e, bi=bi, s0=s0, w=w: e.activation(out=rstd[:, s0:s0 + w], in_=ps_ms[:, bi, :w], func=AF.Sqrt,
                                                                         bias=epsc[:, 0:1], scale=1.0 / 384.0),
                     [res("ps_ms"), res("epsc")], [res("rstd")])
            P.op("vector", lambda e: e.reciprocal(out=rstd[:], in_=rstd[:]), [res("rstd")], [res("rstd")])
            for c in range(3):
                ch = grp * 3 + c
                b = c % 2
                P.op("vector", lambda e, b=b, c=c: e.tensor_tensor(out=tmp[b][:], in0=uu[:, c, :], in1=rstd[:], op=ALU.mult),
                     [res(f"uu{c}"), res("rstd")], [res(f"tmp{b}")])
                P.op("scalar", lambda e, b=b, ch=ch: e.activation(out=catT[:, 4 + ch, :], in_=tmp[b][:], func=AF.Copy, scale=gn[:, ch:ch + 1]),
                     [res(f"tmp{b}"), res("gn")], [res("catT")])
        outs = []
        for nb in range(4):
            b = nb % 2
            P.xdma("sync", wb[b][:], wd[:, nb * 512:(nb + 1) * 512].rearrange("(kc p) n -> p kc n", p=128), f"ld_w{b}", writes=[res(f"wb{b}")])
            for j in range(4):
                nch = nb * 4 + j
                xb = nch % 3
                P.xdma("gpsimd", xc[xb][:], xT[nch * 128:(nch + 1) * 128, :], f"ld_x{xb}", writes=[res(f"xc{xb}")])
                for bi, (s0, w) in enumerate(BLKS):
                    pb = (nch * 3 + bi) % 5
                    P.group([lambda e, b=b, j=j, k=k, s0=s0, w=w, pb=pb: e.matmul(
                        ps_mm[:, pb, :w], lhsT=wb[b][:, k, j * 128:(j + 1) * 128], rhs=catT[:, k, s0:s0 + w],
                        start=(k == 0), stop=(k == 15)) for k in range(16)],
                        [res(f"wb{b}"), res("catT")], [res(f"ps_mm{pb}")])
                    P.op("vector", lambda e, xb=xb, nch=nch, bi=bi, s0=s0, w=w, pb=pb: e.scalar_tensor_tensor(
                        out=xc[xb][:, s0:s0 + w], in0=ps_mm[:, pb, :w], scalar=gt[:, nch, (1 if bi == 2 else 0):(2 if bi == 2 else 1)],
                        in1=xc[xb][:, s0:s0 + w], op0=ALU.mult, op1=ALU.add),
                        [res(f"ps_mm{pb}"), res("gt"), res(f"xc{xb}")], [res(f"xc{xb}")])
                outs.append(P.xdma("sync", xo[nch * 128:(nch + 1) * 128, :], xc[xb][:], f"st_x{xb}", reads=[res(f"xc{xb}")]))
        P.finish(outs[-3:])
    return nc


def build_M():
    nc = bass.Bass("TRN2", target_bir_lowering=False)
    xT = nc.dram_tensor("xT", [D, TOK], F32, kind="ExternalInput").ap()
    scd = nc.dram_tensor("sc", [128, 16, 2], F32, kind="ExternalInput").ap()
    shd = nc.dram_tensor("sh", [128, 16, 2], F32, kind="ExternalInput").ap()
    gtd = nc.dram_tensor("gt", [128, 16, 2], F32, kind="ExternalInput").ap()
    gd = nc.dram_tensor("g", [128, 16], F32, kind="ExternalInput").ap()
    w1d = nc.dram_tensor("w1", [D, DFF], BF16, kind="ExternalInput").ap()
    w2d = nc.dram_tensor("w2", [DFF, D], BF16, kind="ExternalInput").ap()
    xo = nc.dram_tensor("xo", [D, TOK], F32, kind="ExternalOutput").ap()
    with contextlib.ExitStack() as st:
        sb = lambda n, s, d: st.enter_context(nc.sbuf_tensor(n, s, d))
        X = sb("X", [128, 16, TOK], F32)
        hT = sb("hT", [128, 16, TOK], BF16)
        m1 = sb("m1", [128, 8, TOK], BF16)
        sc = sb("sc_s", [128, 16, 2], F32)
        sh = sb("sh_s", [128, 16, 2], F32)
        gt = sb("gt_s", [128, 16, 2], F32)
        g = sb("g_s", [128, 16], F32)
        s1 = sb("s1", [128, 16, 3], F32)
        ones = sb("ones", [128, 128], BF16)
        sq = [sb(f"sq{i}", [128, TOK], BF16) for i in range(2)]
        rstd = sb("rstd", [128, TOK], F32)
        tmp = [sb(f"tmp{i}", [128, TOK], F32) for i in range(2)]
        w1b = [sb(f"w1b{i}", [128, 16, 512], BF16) for i in range(2)]
        w2b = [sb(f"w2b{i}", [128, 8, 512], BF16) for i in range(2)]
        rl = [sb(f"rl{i}", [128, 512], F32) for i in range(2)]
        ps_ms = st.enter_context(nc.psum_tensor("ps_ms", [128, 3, 512], F32))
        ps_mm = st.enter_context(nc.psum_tensor("ps_mm", [128, 5, 512], F32))
        P = TP(nc)
        R = {}

        def res(name):
            if name not in R:
                R[name] = Res()
            return R[name]

        t_ones = P.dve(lambda e: e.memset(ones[:], 1.0 / D))
        xr = []
        xv = xT.rearrange("(c p) t -> p c t", p=128)
        for c4 in range(4):
            t = P.dma("sync" if c4 % 2 == 0 else "gpsimd", X[:, c4 * 4:(c4 + 1) * 4, :], xv[:, c4 * 4:(c4 + 1) * 4, :], f"ld_x{c4}")
            xr += [t] * 4
        t1 = P.dma("gpsimd", sc[:], scd, "ld_sc")
        t2 = P.dma("gpsimd", sh[:], shd, "ld_sh")
        t3 = P.dma("gpsimd", g[:], gd, "ld_g")
        t4 = P.dma("gpsimd", gt[:], gtd, "ld_gt")
        h_ready = emit_norm_mod(P, nc, X, hT, sc, sh, g[:], ones, ps_ms, sq, rstd, tmp, s1, xr, [t1, t2, t3, t_ones])
        rh = res("hT")
        rh.w = h_ready[-1]
        rX = [res(f"X{c}") for c in range(16)]
        for c in range(16):
            rX[c].w = xr[c]
            rX[c].r = {h_ready[-1][0]: h_ready[-1][1], "s_dve": P.cnt["s_dve"]}
        res("gt").w = t4
        npb = 0
        for q in range(8):
            for b2 in range(2):
                wi = (q * 2 + b2) % 2
                c0 = q * 1024 + b2 * 512
                P.xdma("sync", w1b[wi][:], w1d[:, c0:c0 + 512].rearrange("(kc p) n -> p kc n", p=128), f"ld_w1{wi}", writes=[res(f"w1b{wi}")])
                for j in range(4):
                    f = b2 * 4 + j
                    for bi, (s0, w) in enumerate(BLKS):
                        pb = npb % 5
                        npb += 1
                        P.group([lambda e, wi=wi, j=j, k=k, s0=s0, w=w, pb=pb: e.matmul(
                            ps_mm[:, pb, :w], lhsT=w1b[wi][:, k, j * 128:(j + 1) * 128], rhs=hT[:, k, s0:s0 + w],
                            start=(k == 0), stop=(k == 15)) for k in range(16)],
                            [res(f"w1b{wi}"), rh], [res(f"ps_mm{pb}")])
                        rb = npb % 2
                        if npb % 2 == 0:
                            P.op("scalar", lambda e, rb=rb, pb=pb, w=w: e.activation(out=rl[rb][:, :w], in_=ps_mm[:, pb, :w], func=AF.Relu),
                                 [res(f"ps_mm{pb}")], [res(f"rl{rb}")])
                            P.op("vector", lambda e, rb=rb, f=f, s0=s0, w=w: e.tensor_tensor(out=m1[:, f, s0:s0 + w], in0=rl[rb][:, :w], in1=rl[rb][:, :w], op=ALU.mult),
                                 [res(f"rl{rb}")], [res(f"m1_{f}")])
                        else:
                            P.op("vector", lambda e, rb=rb, pb=pb, w=w: e.tensor_scalar_max(out=rl[rb][:, :w], in0=ps_mm[:, pb, :w], scalar1=0.0),
                                 [res(f"ps_mm{pb}")], [res(f"rl{rb}")])
                            P.op("scalar", lambda e, rb=rb, f=f, s0=s0, w=w: e.activation(out=m1[:, f, s0:s0 + w], in_=rl[rb][:, :w], func=AF.Square),
                                 [res(f"rl{rb}")], [res(f"m1_{f}")])
            for nb in range(4):
                wi = (q * 4 + nb) % 2
                P.xdma("gpsimd", w2b[wi][:], w2d[q * 1024:(q + 1) * 1024, nb * 512:(nb + 1) * 512].rearrange("(fc p) n -> p fc n", p=128),
                       f"ld_w2{wi}", writes=[res(f"w2b{wi}")])
                for j in range(4):
                    nch = nb * 4 + j
                    for bi, (s0, w) in enumerate(BLKS):
                        pb = npb % 5
                        npb += 1
                        P.group([lambda e, wi=wi, j=j, f=f, s0=s0, w=w, pb=pb: e.matmul(
                            ps_mm[:, pb, :w], lhsT=w2b[wi][:, f, j * 128:(j + 1) * 128], rhs=m1[:, f, s0:s0 + w],
                            start=(f == 0), stop=(f == 7)) for f in range(8)],
                            [res(f"w2b{wi}")] + [res(f"m1_{ff}") for ff in range(8)], [res(f"ps_mm{pb}")])
                        P.op("vector", lambda e, nch=nch, bi=bi, s0=s0, w=w, pb=pb: e.scalar_tensor_tensor(
                            out=X[:, nch, s0:s0 + w], in0=ps_mm[:, pb, :w], scalar=gt[:, nch, (1 if bi == 2 else 0):(2 if bi == 2 else 1)],
                            in1=X[:, nch, s0:s0 + w], op0=ALU.mult, op1=ALU.add),
                            [res(f"ps_mm{pb}"), res("gt"), rX[nch]], [rX[nch]])
        outs = []
        xov = xo.rearrange("(c p) t -> p c t", p=128)
        for c4 in range(4):
            outs.append(P.xdma("sync" if c4 % 2 == 0 else "gpsimd", xov[:, c4 * 4:(c4 + 1) * 4, :], X[:, c4 * 4:(c4 + 1) * 4, :], f"st_x{c4}",
                               reads=[rX[c] for c in range(c4 * 4, c4 * 4 + 4)]))
        P.finish(outs)
    return nc


def build_N():
    nc = bass.Bass("TRN2", target_bir_lowering=False)
    xT = nc.dram_tensor("xT", [D, TOK], F32, kind="ExternalInput").ap()
    gd = nc.dram_tensor("g", [128, 16], F32, kind="ExternalInput").ap()
    xo = nc.dram_tensor("xo", [D, TOK], F32, kind="ExternalOutput").ap()
    with contextlib.ExitStack() as st:
        sb = lambda n, s, d: st.enter_context(nc.sbuf_tensor(n, s, d))
        X = sb("X", [128, 16, TOK], F32)
        hT = sb("hT", [128, 16, TOK], F32)
        zz = sb("zz", [128, 16, 2], F32)
        g = sb("g_s", [128, 16], F32)
        s1 = sb("s1", [128, 16, 3], F32)
        ones = sb("ones", [128, 128], BF16)
        sq = [sb(f"sq{i}", [128, TOK], BF16) for i in range(2)]
        rstd = sb("rstd", [128, TOK], F32)
        tmp = [sb(f"tmp{i}", [128, TOK], F32) for i in range(2)]
        ps_ms = st.enter_context(nc.psum_tensor("ps_ms", [128, 3, 512], F32))
        P = Prog(nc)
        t_ones = P.dve(lambda e: e.memset(ones[:], 1.0 / D))
        t_z = P.dve(lambda e: e.memset(zz[:], 0.0))
        xr = []
        xv = xT.rearrange("(c p) t -> p c t", p=128)
        for c4 in range(4):
            t = P.dma("sync" if c4 % 2 == 0 else "gpsimd", X[:, c4 * 4:(c4 + 1) * 4, :], xv[:, c4 * 4:(c4 + 1) * 4, :], f"ld_x{c4}")
            xr += [t] * 4
        t3 = P.dma("gpsimd", g[:], gd, "ld_g")
        h_ready = emit_norm_mod(P, nc, X, hT, zz, zz, g[:], ones, ps_ms, sq, rstd, tmp, s1, xr, [t3, t_ones, t_z])
        outs = []
        xov = xo.rearrange("(c p) t -> p c t", p=128)
        for c4 in range(4):
            outs.append(P.dma("sync", xov[:, c4 * 4:(c4 + 1) * 4, :], hT[:, c4 * 4:(c4 + 1) * 4, :], f"st_x{c4}", [h_ready[c4 * 4 + 3]]))
        P.finish(outs)
    return nc


_CACHE = {}


def _prog(name, builder):
    if name not in _CACHE:
        _CACHE[name] = builder()
    return _CACHE[name]


def _mod2(mods, l, k):
    return np.ascontiguousarray(np.stack([fm(mods[l, 0, k * D:(k + 1) * D]), fm(mods[l, 1, k * D:(k + 1) * D])], -1))


def kernel(x, c, ctx, c_ctx, w_ada, b_ada, g_mix, w_in, conv_w, conv_b, dt_bias, a_log, d_skip,
           g_ssd_norm, w_fourier, w_out, g_mlp, w_mlp1, w_mlp2, g_final, _nlayers=DEPTH, _debug=None):
    f32 = lambda a: np.ascontiguousarray(np.asarray(a, dtype=np.float32))
    x, c, ctx, c_ctx = f32(x), f32(c), f32(ctx), f32(c_ctx)
    w_ada, b_ada, g_mix, w_in = f32(w_ada), f32(b_ada), f32(g_mix), f32(w_in)
    conv_w, conv_b, dt_bias, a_log, d_skip = f32(conv_w), f32(conv_b), f32(dt_bias), f32(a_log), f32(d_skip)
    g_ssd_norm, w_fourier, w_out, g_mlp = f32(g_ssd_norm), f32(w_fourier), f32(w_out), f32(g_mlp)
    w_mlp1, w_mlp2, g_final = f32(w_mlp1), f32(w_mlp2), f32(g_final)

    ws = [w_in, w_out, w_mlp1, w_mlp2]
    flat = np.concatenate([w.reshape(-1) for w in ws])
    fb = run_cast(flat)
    wb = []
    o = 0
    for w in ws:
        wb.append(fb[o:o + w.size].reshape(w.shape))
        o += w.size
    w_in_b, w_out_b, w1_b, w2_b = wb
    del flat, fb

    mods = run_mods(c, c_ctx, w_ada, b_ada)
    C, S, Cc_, Sc_, cc = fourier_tables()
    tabs = [(np.ascontiguousarray(C[:, 1024 * i:1024 * (i + 1)]), np.ascontiguousarray(S[:, 1024 * i:1024 * (i + 1)]),
             np.ascontiguousarray(Cc_[:, 32 * i:32 * (i + 1)]), np.ascontiguousarray(Sc_[:, 32 * i:32 * (i + 1)])) for i in range(NCORES)]
    del C, S

    xl, xc = x[0], ctx[0]
    xT = [np.ascontiguousarray(np.concatenate([xl[1024 * i:1024 * (i + 1)], xc[32 * i:32 * (i + 1)]], 0).T) for i in range(NCORES)]

    def to_global(per_core):
        R_ = per_core[0].shape[0]
        G = np.empty((R_, NTOK), np.float32)
        for i in range(NCORES):
            G[:, CTX + 1024 * i:CTX + 1024 * (i + 1)] = per_core[i][:, :1024]
            G[:, 32 * i:32 * (i + 1)] = per_core[i][:, 1024:]
        return G

    def to_core(G, i):
        return np.ascontiguousarray(np.concatenate([G[:, CTX + 1024 * i:CTX + 1024 * (i + 1)], G[:, 32 * i:32 * (i + 1)]], 1))

    for l in range(_nlayers):
        ncB = _prog("B", build_B)
        sc1, sh1, gt1 = _mod2(mods, l, 1), _mod2(mods, l, 0), _mod2(mods, l, 2)
        sh2, sc2, gt2 = _mod2(mods, l, 3), _mod2(mods, l, 4), _mod2(mods, l, 5)
        res = _run(ncB, [{"xT": xT[i], "sc": sc1, "sh": sh1, "g": fm(g_mix[l]), "w": w_in_b[l]} for i in range(NCORES)])
        pT = [r["pT"] for r in res]
        PT = to_global(pT)
        ncC = _prog("C", build_ssd)
        res = _run(ncC, ssd_inputs(PT, conv_w[l], conv_b[l], dt_bias[l], a_log[l], d_skip[l]))
        YF = np.concatenate([r["Y"][0].T for r in res], 0)
        YB = np.concatenate([r["Y"][1].T for r in res], 0)
        ncF = _prog("F", build_fourier)
        uL = np.ascontiguousarray(PT[0:512, CTX:].T)
        uC = np.ascontiguousarray(PT[0:512, :CTX].T)
        res = _run(ncF, [{"uL": uL, "uC": uC, "tabC": tabs[i][0], "tabS": tabs[i][1], "tcC": tabs[i][2], "tcS": tabs[i][3],
                          "wf": w_fourier[l], "cc": cc} for i in range(NCORES)])
        fT = [r["fT"] for r in res]
        ncG = _prog("G", build_G)
        res = _run(ncG, [{"xT": xT[i], "fT": fT[i], "yf": to_core(YF, i), "yb": to_core(YB, i),
                          "zT": np.ascontiguousarray(pT[i][512:2048]), "gn": fm(g_ssd_norm[l]), "gt": gt1, "w": w_out_b[l]}
                         for i in range(NCORES)])
        xm = [r["xo"] for r in res]
        if _debug is not None:
            _debug[f"xm{l}"] = xm
        ncM = _prog("M", build_M)
        res = _run(ncM, [{"xT": xm[i], "sc": sc2, "sh": sh2, "gt": gt2, "g": fm(g_mlp[l]), "w1": w1_b[l], "w2": w2_b[l]}
                         for i in range(NCORES)])
        xT = [r["xo"] for r in res]
        if _debug is not None:
            _debug[f"xo{l}"] = xT
    ncN = _prog("N", build_N)
    res = _run(ncN, [{"xT": xT[i], "g": fm(g_final)} for i in range(NCORES)])
    out = np.empty((1, SEQ, D), np.float32)
    for i in range(NCORES):
        out[0, 1024 * i:1024 * (i + 1), :] = res[i]["xo"][:, :1024].T
    return out
```

```python
import numpy as np
import ml_dtypes
import concourse.bass as bass
import concourse.mybir as mybir
from concourse.bass_utils import run_bass_kernel_spmd

F32 = mybir.dt.float32
BF16 = mybir.dt.bfloat16
AF = mybir.ActivationFunctionType
ALU = mybir.AluOpType
AX = mybir.AxisListType
NPBF = ml_dtypes.bfloat16

NCORES = 8
D = 2048
SEQ = 8192
CTX = 256
DEPTH = 4
DIN = 4656
DFF = 8192
TOK = 1056
EPS = 1e-6


class Prog:
    ENGS = ("sync", "scalar", "vector", "gpsimd", "tensor")

    def __init__(self, nc):
        self.nc = nc
        self.q = {e: [] for e in self.ENGS}
        self.cnt = {}
        self.waited = {e: {} for e in self.ENGS}

    def emit(self, eng, fn, waits=(), sig=None, inc=1):
        ws = []
        for w in waits:
            if w is None:
                continue
            s, v = w
            if self.waited[eng].get(s, 0) >= v:
                continue
            self.waited[eng][s] = v
            ws.append((s, v))
        tok = None
        if sig is not None:
            self.cnt[sig] = self.cnt.get(sig, 0) + inc
            tok = (sig, self.cnt[sig])
        self.q[eng].append((fn, ws, sig, inc))
        return tok

    def pe(self, fn, waits=(), sig=True):
        return self.emit("tensor", fn, waits, "s_pe" if sig else None)

    def act(self, fn, waits=(), sig=True):
        return self.emit("scalar", fn, waits, "s_act" if sig else None)

    def dve(self, fn, waits=(), sig=True):
        return self.emit("vector", fn, waits, "s_dve" if sig else None)

    def pool(self, fn, waits=(), sig=True):
        return self.emit("gpsimd", fn, waits, "s_pool" if sig else None)

    def dma(self, q, out, in_, sem, waits=()):
        return self.emit(q, lambda e: e.dma_start(out=out, in_=in_), waits, sem, 16)

    def simulate(self):
        pos = {e: 0 for e in self.ENGS}
        cnt = {}
        while True:
            prog = False
            for e in self.ENGS:
                while pos[e] < len(self.q[e]):
                    fn, ws, sig, inc = self.q[e][pos[e]]
                    if all(cnt.get(s_, 0) >= v for s_, v in ws):
                        if sig is not None:
                            cnt[sig] = cnt.get(sig, 0) + inc
                        pos[e] += 1
                        prog = True
                    else:
                        break
            if all(pos[e] == len(self.q[e]) for e in self.ENGS):
                return True
            if not prog:
                for e in self.ENGS:
                    if pos[e] < len(self.q[e]):
                        fn, ws, sig, inc = self.q[e][pos[e]]
                        print("DEADLOCK", e, pos[e], len(self.q[e]), [(s_, v, cnt.get(s_, 0)) for s_, v in ws], sig)
                return False

    def finish(self, final_waits):
        import os
        if os.environ.get("PROG_SIM"):
            print("SIM", self.simulate(), {e: len(self.q[e]) for e in self.ENGS})
        nc = self.nc
        names = sorted(self.cnt.keys())
        sems = {}
        import contextlib
        with contextlib.ExitStack() as st:
            for n in names:
                sems[n] = st.enter_context(nc.semaphore(n))
            block = st.enter_context(nc.Block())
            q = self.q
            q["sync"].append((None, [w for w in final_waits if w is not None], None, 0))

            def runner(ename):
                def _(eng):
                    for fn, ws, sig, inc in q[ename]:
                        for s, v in ws:
                            eng.wait_ge(sems[s], v)
                        if fn is None:
                            continue
                        ins = fn(eng)
                        if sig is not None:
                            ins.then_inc(sems[sig], inc)
                return _
            block.sync(runner("sync"))
            block.scalar(runner("scalar"))
            block.vector(runner("vector"))
            block.gpsimd(runner("gpsimd"))
            block.tensor(runner("tensor"))


def _run(nc, in_maps):
    res = run_bass_kernel_spmd(nc, in_maps, core_ids=list(range(NCORES)))
    return res.results


def build_cast(F):
    nc = bass.Bass("TRN2", target_bir_lowering=False)
    src = nc.dram_tensor("src", [128, F], F32, kind="ExternalInput").ap()
    dst = nc.dram_tensor("dst", [128, F], BF16, kind="ExternalOutput").ap()
    T = 4096
    nt = (F + T - 1) // T
    NB = 3
    import contextlib
    with contextlib.ExitStack() as st:
        ins = [st.enter_context(nc.sbuf_tensor(f"in{i}", [128, T], F32)) for i in range(NB)]
        outs = [st.enter_context(nc.sbuf_tensor(f"out{i}", [128, T], BF16)) for i in range(NB)]
        P = Prog(nc)
        cast_tok = [None] * NB
        st_tok = [None] * NB
        for t in range(nt):
            b = t % NB
            w = min(T, F - t * T)
            ld = P.dma("sync", ins[b][:, :w], src[:, t * T:t * T + w], f"ld{b}", [cast_tok[b]])
            if t % 2 == 0:
                cast_tok[b] = P.dve(lambda e, b=b, w=w: e.tensor_copy(out=outs[b][:, :w], in_=ins[b][:, :w]),
                                    [ld, st_tok[b]])
            else:
                cast_tok[b] = P.act(lambda e, b=b, w=w: e.copy(out=outs[b][:, :w], in_=ins[b][:, :w]),
                                    [ld, st_tok[b]])
            st_tok[b] = P.dma("gpsimd", dst[:, t * T:t * T + w], outs[b][:, :w], f"st{b}", [cast_tok[b]])
        P.finish(st_tok)
    return nc


def run_cast(flat):
    n = flat.size
    assert n % (NCORES * 128) == 0
    F = n // (NCORES * 128)
    nc = build_cast(F)
    sh = flat.reshape(NCORES, 128, F)
    res = _run(nc, [{"src": np.ascontiguousarray(sh[i])} for i in range(NCORES)])
    return np.stack([r["dst"] for r in res]).reshape(-1)


import contextlib


def fm(v):
    v = np.asarray(v)
    n = v.shape[-1] // 128
    lead = v.shape[:-1]
    a = v.reshape(lead + (n, 128))
    a = np.moveaxis(a, -1, 0)
    return np.ascontiguousarray(a)


def build_mods():
    nc = bass.Bass("TRN2", target_bir_lowering=False)
    cc = nc.dram_tensor("cc", [128, 16, 2], F32, kind="ExternalInput").ap()
    w = nc.dram_tensor("w", [DEPTH, D, 1536], F32, kind="ExternalInput").ap()
    b = nc.dram_tensor("b", [128, DEPTH, 12], F32, kind="ExternalInput").ap()
    out = nc.dram_tensor("out", [128, DEPTH, 12, 2], F32, kind="ExternalOutput").ap()
    with contextlib.ExitStack() as st:
        sb = lambda n, s, d: st.enter_context(nc.sbuf_tensor(n, s, d))
        cs = sb("cs", [128, 16, 2], F32)
        ss = sb("ss", [128, 16, 2], F32)
        bs = sb("bs", [128, DEPTH, 12], F32)
        os_ = sb("os", [128, DEPTH, 12, 2], F32)
        wb = [sb(f"wb{i}", [128, 16, 512], F32) for i in range(2)]
        ps = st.enter_context(nc.psum_tensor("ps", [128, 2, 512], F32))
        P = Prog(nc)
        t_c = P.dma("sync", cs[:], cc, "ld_c")
        t_b = P.dma("sync", bs[:], b, "ld_b")
        t_s = P.act(lambda e: e.activation(out=ss[:], in_=cs[:], func=AF.Silu), [t_c])
        wfree = [None, None]
        evs = []
        n = 0
        for l in range(DEPTH):
            for blk in range(3):
                bi = n % 2
                ld = P.dma("sync" if n % 2 == 0 else "gpsimd", wb[bi][:],
                           w[l, :, blk * 512:(blk + 1) * 512].rearrange("(kc p) n -> p kc n", p=128),
                           f"ld_w{bi}", [wfree[bi]])
                for j in range(4):
                    jj = blk * 4 + j
                    for k in range(16):
                        t_mm = P.pe(lambda e, bi=bi, j=j, k=k, jj=jj, n=n: e.matmul(
                            ps[:, n % 2, jj * 2:jj * 2 + 2], lhsT=wb[bi][:, k, j * 128:(j + 1) * 128],
                            rhs=ss[:, k, :], start=(k == 0), stop=(k == 15)),
                            [ld, t_s] + (evs[-1:] if k == 0 else []), sig=(k == 15))
                    ev = P.dve(lambda e, l=l, jj=jj, n=n: e.tensor_scalar(
                        out=os_[:, l, jj, :], in0=ps[:, n % 2, jj * 2:jj * 2 + 2], scalar1=bs[:, l, jj:jj + 1],
                        scalar2=None, op0=ALU.add), [t_mm, t_b])
                    evs.append(ev)
                wfree[bi] = t_mm
                n += 1
        t_o = P.dma("sync", out, os_[:], "st_o", [evs[-1]])
        P.finish([t_o])
    return nc


def run_mods(c, c_ctx, w_ada, b_ada):
    nc = build_mods()
    cc = np.stack([fm(c.reshape(-1)), fm(c_ctx.reshape(-1))], axis=-1)
    in_maps = []
    for i in range(NCORES):
        sl = slice(1536 * i, 1536 * (i + 1))
        in_maps.append({"cc": cc, "w": np.ascontiguousarray(w_ada[:, :, sl]),
                        "b": fm(b_ada[:, sl])})
    res = _run(nc, in_maps)
    mods = np.zeros((DEPTH, 2, 6 * D), np.float32)
    for i in range(NCORES):
        o = res[i]["out"]
        mods[:, :, 1536 * i:1536 * (i + 1)] = np.transpose(o, (1, 3, 2, 0)).reshape(DEPTH, 2, 1536)
    return mods


BLKS = [(0, 512), (512, 512), (1024, 32)]


def emit_norm_mod(P, nc, X, hT, sc, sh, g, ones, ps_ms, sq, rstd, tmp, s1, x_ready, extra_waits=()):
    ew = list(extra_waits)
    epsc = s1[:, 0, 2:3]
    t_eps = P.dve(lambda e: e.memset(s1[:, :, 2:3], EPS))
    t_s1 = []
    for j in range(2):
        t_s1.append(P.dve(lambda e, j=j: e.scalar_tensor_tensor(
            out=s1[:, :, j], in0=sc[:, :, j], scalar=1.0, in1=g, op0=ALU.add, op1=ALU.mult), ew))
    sq_tok = [None] * len(sq)
    mm_tok = None
    for c in range(16):
        b = c % len(sq)
        xr = x_ready[c] if isinstance(x_ready, list) else x_ready
        t_sq = P.act(lambda e, c=c, b=b: e.activation(out=sq[b][:], in_=X[:, c, :], func=AF.Square),
                     [xr, sq_tok[b]] + ew)
        for bi, (s0, w) in enumerate(BLKS):
            mm_tok = P.pe(lambda e, c=c, b=b, bi=bi, s0=s0, w=w: e.matmul(
                ps_ms[:, bi, :w], lhsT=ones[:], rhs=sq[b][:, s0:s0 + w], start=(c == 0), stop=(c == 15)),
                [t_sq] + ew, sig=(bi == 2))
        sq_tok[b] = mm_tok
    t_r = None
    for bi, (s0, w) in enumerate(BLKS):
        t_q = P.act(lambda e, bi=bi, s0=s0, w=w: e.activation(
            out=rstd[:, s0:s0 + w], in_=ps_ms[:, bi, :w], func=AF.Sqrt, bias=epsc[:, 0:1], scale=1.0), [mm_tok, t_eps])
        t_r = P.dve(lambda e, bi=bi, s0=s0, w=w: e.reciprocal(
            out=rstd[:, s0:s0 + w], in_=rstd[:, s0:s0 + w]), [t_q])
    toks = []
    tmp_tok = [None] * len(tmp)
    for c in range(16):
        b = c % len(tmp)
        t_m = P.dve(lambda e, c=c, b=b: e.tensor_tensor(out=tmp[b][:], in0=X[:, c, :], in1=rstd[:], op=ALU.mult),
                    [t_r, tmp_tok[b]])
        P.act(lambda e, c=c, b=b: e.activation(out=hT[:, c, 0:1024], in_=tmp[b][:, 0:1024], func=AF.Identity,
                                               scale=s1[:, c, 0:1], bias=sh[:, c, 0:1]), [t_m, t_s1[1]], sig=False)
        t_a = P.act(lambda e, c=c, b=b: e.activation(out=hT[:, c, 1024:TOK], in_=tmp[b][:, 1024:TOK], func=AF.Identity,
                                                     scale=s1[:, c, 1:2], bias=sh[:, c, 1:2]), [t_m])
        tmp_tok[b] = t_a
        toks.append(t_a)
    return toks


def emit_proj(P, nc, wdram, nout, hT, h_ready, wbufs, psb, evac, name, kch=16, wfree0=None, ps_free0=None):
    nblk = (nout + 511) // 512
    wfree = list(wfree0) if wfree0 else [None] * len(wbufs)
    ps_free = list(ps_free0) if ps_free0 else [None] * len(psb)
    pi = 0
    evs = []
    last_mm = None
    for nb in range(nblk):
        b = nb % len(wbufs)
        ncol = min(512, nout - nb * 512)
        ld = P.dma("sync" if nb % 2 == 0 else "gpsimd", wbufs[b][:, :, :ncol],
                   wdram[:, nb * 512:nb * 512 + ncol].rearrange("(kc p) n -> p kc n", p=128),
                   f"ld_{name}{b}", [wfree[b]])
        for j in range((ncol + 127) // 128):
            m = min(128, ncol - j * 128)
            for bi, (s0, w) in enumerate(BLKS):
                pb = pi % len(psb)
                pi += 1
                for k in range(kch):
                    last_mm = P.pe(lambda e, b=b, j=j, m=m, k=k, s0=s0, w=w, pb=pb: e.matmul(
                        psb[pb][:m, :w], lhsT=wbufs[b][:, k, j * 128:j * 128 + m], rhs=hT[:, k, s0:s0 + w],
                        start=(k == 0), stop=(k == kch - 1)),
                        ([ld, ps_free[pb]] if k == 0 else []) + [h_ready[k]], sig=(k == kch - 1))
                ev = evac(nb * 4 + j, m, bi, s0, w, psb[pb], last_mm)
                ps_free[pb] = ev
                evs.append(ev)
        wfree[b] = last_mm
    return last_mm, evs


def build_B():
    nc = bass.Bass("TRN2", target_bir_lowering=False)
    xT = nc.dram_tensor("xT", [D, TOK], F32, kind="ExternalInput").ap()
    scd = nc.dram_tensor("sc", [128, 16, 2], F32, kind="ExternalInput").ap()
    shd = nc.dram_tensor("sh", [128, 16, 2], F32, kind="ExternalInput").ap()
    gd = nc.dram_tensor("g", [128, 16], F32, kind="ExternalInput").ap()
    wd = nc.dram_tensor("w", [D, DIN], BF16, kind="ExternalInput").ap()
    pT = nc.dram_tensor("pT", [DIN, TOK], F32, kind="ExternalOutput").ap()
    with contextlib.ExitStack() as st:
        sb = lambda n, s, d: st.enter_context(nc.sbuf_tensor(n, s, d))
        X = sb("X", [128, 16, TOK], F32)
        hT = sb("hT", [128, 16, TOK], BF16)
        sc = sb("sc_s", [128, 16, 2], F32)
        sh = sb("sh_s", [128, 16, 2], F32)
        g = sb("g_s", [128, 16], F32)
        s1 = sb("s1", [128, 16, 3], F32)
        ones = sb("ones", [128, 128], BF16)
        sq = [sb(f"sq{i}", [128, TOK], BF16) for i in range(3)]
        rstd = sb("rstd", [128, TOK], F32)
        tmp = [sb(f"tmp{i}", [128, TOK], F32) for i in range(2)]
        wb = [sb(f"wb{i}", [128, 16, 512], BF16) for i in range(2)]
        ot = [sb(f"ot{i}", [128, TOK], F32) for i in range(3)]
        ps_ms = st.enter_context(nc.psum_tensor("ps_ms", [128, 3, 512], F32))
        ps_mm = st.enter_context(nc.psum_tensor("ps_mm", [128, 5, 512], F32))
        P = Prog(nc)
        t_ones = P.dve(lambda e: e.memset(ones[:], 1.0 / D))
        xr = []
        xv = xT.rearrange("(c p) t -> p c t", p=128)
        for c4 in range(4):
            t = P.dma("sync" if c4 % 2 == 0 else "gpsimd", X[:, c4 * 4:(c4 + 1) * 4, :], xv[:, c4 * 4:(c4 + 1) * 4, :], f"ld_x{c4}")
            xr += [t] * 4
        t1 = P.dma("gpsimd", sc[:], scd, "ld_sc")
        t2 = P.dma("gpsimd", sh[:], shd, "ld_sh")
        t3 = P.dma("gpsimd", g[:], gd, "ld_g")
        h_ready = emit_norm_mod(P, nc, X, hT, sc, sh, g[:], ones, ps_ms, sq, rstd, tmp, s1, xr, [t1, t2, t3, t_ones])

        ot_free = [None] * 3
        state = {"n": 0, "evs": []}
        out_toks = []

        def evac(nchunk, m, bi, s0, w, ps, mm):
            b = nchunk % 3
            if bi == 1:
                tk = P.act(lambda e: e.copy(out=ot[b][:m, s0:s0 + w], in_=ps[:m, :w]), [mm, ot_free[b]])
            else:
                tk = P.dve(lambda e: e.tensor_copy(out=ot[b][:m, s0:s0 + w], in_=ps[:m, :w]), [mm, ot_free[b]])
            state["evs"].append(tk)
            if bi == 2:
                d = P.dma("scalar", pT[nchunk * 128:nchunk * 128 + m, :], ot[b][:m, :], f"st_p{b}", state["evs"][-3:])
                ot_free[b] = d
                out_toks.append(d)
            return tk

        emit_proj(P, nc, wd, DIN, hT, h_ready, wb, [ps_mm[:, i, :] for i in range(5)], evac, "w")
        P.finish(out_toks[-3:])
    return nc


class Res:
    __slots__ = ("w", "r")

    def __init__(self):
        self.w = None
        self.r = {}


class TP(Prog):
    def _deps(self, reads, writes):
        waits = []
        for r in reads:
            waits.append(r.w)
        for w in writes:
            waits.append(w.w)
            waits += list(w.r.items())
        return waits

    def _mark(self, tok, reads, writes):
        for r in reads:
            s, v = tok
            if r.r.get(s, 0) < v:
                r.r[s] = v
        for w in writes:
            w.w = tok
            w.r = {}

    def op(self, eng, fn, reads=(), writes=()):
        sem = {"tensor": "s_pe", "scalar": "s_act", "vector": "s_dve", "gpsimd": "s_pool"}[eng]
        tok = self.emit(eng, fn, self._deps(reads, writes), sem)
        self._mark(tok, reads, writes)
        return tok

    def group(self, fns, reads=(), writes=()):
        waits = self._deps(reads, writes)
        tok = None
        for i, fn in enumerate(fns):
            tok = self.emit("tensor", fn, waits if i == 0 else (), "s_pe" if i == len(fns) - 1 else None)
        self._mark(tok, reads, writes)
        return tok

    def xdma(self, q, out, in_, sem, reads=(), writes=()):
        tok = self.emit(q, lambda e: e.dma_start(out=out, in_=in_), self._deps(reads, writes), sem, 16)
        self._mark(tok, reads, writes)
        return tok


NCH = 66
NTOK = NCH * 128


def build_ssd():
    nc = bass.Bass("TRN2", target_bir_lowering=False)
    uT = nc.dram_tensor("uT", [448, NTOK], F32, kind="ExternalInput").ap()
    dtr = nc.dram_tensor("dtr", [128, 2, NCH, 3], F32, kind="ExternalInput").ap()
    cwd = nc.dram_tensor("cw", [128, 4, 5], F32, kind="ExternalInput").ap()
    cbd = nc.dram_tensor("cb", [128, 4], F32, kind="ExternalInput").ap()
    smd = nc.dram_tensor("sm", [128, 15], F32, kind="ExternalInput").ap()
    cst = nc.dram_tensor("cst", [128, 8, 128], F32, kind="ExternalInput").ap()
    Y = nc.dram_tensor("Y", [2, NTOK, 192], F32, kind="ExternalOutput").ap()
    with contextlib.ExitStack() as st:
        sb = lambda n, s, d: st.enter_context(nc.sbuf_tensor(n, s, d))
        BT = sb("BT", [128, NTOK], BF16)
        CT = sb("CT", [128, NTOK], BF16)
        Btm = sb("Btm", [128, NCH, 128], BF16)
        Xtm = sb("Xtm", [128, NCH, 192], F32)
        cw = sb("cw_s", [128, 4, 5], F32)
        cb = sb("cb_s", [128, 4], F32)
        sm = sb("sm_s", [128, 15], F32)
        K8 = sb("K8", [128, 8, 128], F32)
        identb = sb("identb", [128, 128], BF16)
        dts = {n: sb(n, [128, 2, NCH, 3], F32) for n in
               ("raw", "t1", "dtv", "dta", "acum", "tot", "sdec", "ea", "dec", "dtsd")}
        avec = sb("avec", [128, 6], F32)
        PIECE = 1024
        raw = [sb(f"rawb{i}", [128, PIECE], F32) for i in range(2)]
        acc = [sb(f"accb{i}", [128, PIECE], F32) for i in range(2)]
        sil = [sb(f"silb{i}", [128, PIECE], F32) for i in range(2)]
        Rb = [sb(f"R{i}", [128, 2, 3, 128], BF16) for i in range(2)]
        K8b = sb("K8b", [128, 4, 128], BF16)
        dth = sb("dth", [128, 2, NCH, 3], BF16)
        dthf = sb("dthf", [128, 2, NCH, 3], F32)
        dtl = sb("dtl", [128, 2, NCH, 3], F32)
        Eb = [sb(f"E{i}", [128, 3, 128], F32) for i in range(2)]
        Mb = [sb(f"M{i}", [128, 3, 128], BF16) for i in range(2)]
        rtmp = [sb(f"rtmp{i}", [128, 192], F32) for i in range(2)]
        CBm = [sb(f"CBm{i}", [128, 128], F32) for i in range(2)]
        xdt2 = [[sb(f"xdt{i}_{j}", [128, 192], BF16) for j in range(2)] for i in range(2)]
        xs2 = [[sb(f"xs{i}_{j}", [128, 192], BF16) for j in range(2)] for i in range(2)]
        xd2 = [sb(f"xd_{j}", [128, 192], BF16) for j in range(2)]
        hst = [sb(f"hst{i}", [128, 192], F32) for i in range(2)]
        hbf = [[sb(f"hbf{i}_{j}", [128, 192], BF16) for j in range(2)] for i in range(2)]
        yo_t = [sb(f"yot{i}", [128, 192], F32) for i in range(2)]
        yo = [sb(f"yo{i}", [128, 192], F32) for i in range(4)]
        ps_seg = [st.enter_context(nc.psum_tensor(f"ps_seg{i}", [128, 512], F32)) for i in range(2)]
        ps_cb = st.enter_context(nc.psum_tensor("ps_cb", [128, 512], F32))
        ps_y = [st.enter_context(nc.psum_tensor(f"ps_y{i}", [128, 512], F32)) for i in range(2)]
        ps_st = st.enter_context(nc.psum_tensor("ps_st", [128, 512], F32))
        ps_tr = [st.enter_context(nc.psum_tensor(f"ps_tr{i}", [128, 512], F32)) for i in range(2)]

        P = TP(nc)
        R = {}

        def res(name):
            if name not in R:
                R[name] = Res()
            return R[name]

        Tm, Um, SU, SL, MF, MB, ID, ON = [K8[:, i, :] for i in range(8)]
        P.xdma("sync", K8[:], cst, "ld_k8", writes=[res("K8")])
        P.xdma("sync", cw[:], cwd, "ld_cw", writes=[res("cw")])
        P.xdma("sync", cb[:], cbd, "ld_cb", writes=[res("cb")])
        P.xdma("sync", sm[:], smd, "ld_sm", writes=[res("sm")])
        P.xdma("sync", dts["raw"][:], dtr, "ld_dt", writes=[res("raw")])
        P.op("vector", lambda e: e.tensor_copy(out=identb[:], in_=ID), [res("K8")], [res("identb")])
        P.op("vector", lambda e: e.tensor_copy(out=K8b[:], in_=K8[:, 0:4, :]), [res("K8")], [res("K8b")])

        fl = lambda n: dts[n][:].rearrange("p a c j -> p (a c j)")
        col = lambda n, d, j: dts[n][:, d, :, j]
        for d in range(2):
            for j in range(3):
                P.op("vector", lambda e, d=d, j=j: e.tensor_scalar(
                    out=col("raw", d, j), in0=col("raw", d, j), scalar1=sm[:, d * 3 + j:d * 3 + j + 1], scalar2=None,
                    op0=ALU.add), [res("raw"), res("sm")], [res("raw")])
        P.op("scalar", lambda e: e.activation(out=fl("t1"), in_=fl("raw"), func=AF.Abs), [res("raw")], [res("t1")])
        P.op("scalar", lambda e: e.activation(out=fl("t1"), in_=fl("t1"), func=AF.Exp, scale=-1.0), [res("t1")], [res("t1")])
        P.op("vector", lambda e: e.tensor_scalar_add(out=fl("t1"), in0=fl("t1"), scalar1=1.0), [res("t1")], [res("t1")])
        P.op("scalar", lambda e: e.activation(out=fl("t1"), in_=fl("t1"), func=AF.Ln), [res("t1")], [res("t1")])
        P.op("vector", lambda e: e.scalar_tensor_tensor(out=fl("dtv"), in0=fl("raw"), scalar=0.0, in1=fl("t1"),
                                                         op0=ALU.max, op1=ALU.add), [res("raw"), res("t1")], [res("dtv")])
        P.op("scalar", lambda e: e.activation(out=avec[:], in_=sm[:, 6:12], func=AF.Exp), [res("sm")], [res("avec")])
        P.op("vector", lambda e: e.tensor_scalar_mul(out=avec[:], in0=avec[:], scalar1=-1.0), [res("avec")], [res("avec")])
        for d in range(2):
            for j in range(3):
                P.op("vector", lambda e, d=d, j=j: e.tensor_scalar(
                    out=col("dta", d, j), in0=col("dtv", d, j), scalar1=avec[:, d * 3 + j:d * 3 + j + 1], scalar2=None,
                    op0=ALU.mult), [res("dtv"), res("avec")], [res("dta")])
        fl2 = lambda t: t[:].rearrange("p a c j -> p (a c j)")
        P.op("vector", lambda e: e.tensor_copy(out=fl2(dth), in_=fl("dta")), [res("dta")], [res("dth")])
        P.op("vector", lambda e: e.tensor_copy(out=fl2(dthf), in_=fl2(dth)), [res("dth")], [res("dthf")])
        P.op("vector", lambda e: e.tensor_tensor(out=fl2(dtl), in0=fl("dta"), in1=fl2(dthf), op=ALU.subtract), [res("dta"), res("dthf")], [res("dtl")])
        for d in range(2):
            lhs = Tm if d == 0 else Um
            P.group([lambda e, d=d, lhs=lhs: e.matmul(ps_tr[0][:, d * 256:d * 256 + 198], lhsT=lhs,
                                                       rhs=dts["dta"][:, d].rearrange("p c j -> p (c j)"),
                                                       start=True, stop=True)],
                    [res("K8"), res("dta")], [res("ps_tr0")])
            P.group([lambda e, d=d: e.matmul(ps_tr[1][:, d * 256:d * 256 + 198], lhsT=ON,
                                             rhs=dts["dta"][:, d].rearrange("p c j -> p (c j)"),
                                             start=True, stop=True)],
                    [res("K8"), res("dta")], [res("ps_tr1")])
        for d in range(2):
            P.op("vector", lambda e, d=d: e.tensor_copy(out=dts["acum"][:, d].rearrange("p c j -> p (c j)"),
                                                       in_=ps_tr[0][:, d * 256:d * 256 + 198]),
                 [res("ps_tr0")], [res("acum")])
            P.op("vector", lambda e, d=d: e.tensor_copy(out=dts["tot"][:, d].rearrange("p c j -> p (c j)"),
                                                       in_=ps_tr[1][:, d * 256:d * 256 + 198]),
                 [res("ps_tr1")], [res("tot")])
        P.op("vector", lambda e: e.tensor_tensor(out=fl("sdec"), in0=fl("tot"), in1=fl("acum"), op=ALU.subtract),
             [res("tot"), res("acum")], [res("sdec")])
        P.op("scalar", lambda e: e.activation(out=fl("sdec"), in_=fl("sdec"), func=AF.Exp), [res("sdec")], [res("sdec")])
        P.op("scalar", lambda e: e.activation(out=fl("ea"), in_=fl("acum"), func=AF.Exp), [res("acum")], [res("ea")])
        P.op("scalar", lambda e: e.activation(out=fl("dec"), in_=fl("tot"), func=AF.Exp), [res("tot")], [res("dec")])
        P.op("vector", lambda e: e.tensor_tensor(out=fl("dtsd"), in0=fl("dtv"), in1=fl("sdec"), op=ALU.mult),
             [res("dtv"), res("sdec")], [res("dtsd")])

        tiles = [(0, 128), (128, 64), (192, 128), (320, 128)]
        pieces = [(0, 256, 256)] + [(256 + i * 1024, 1024, 64) for i in range(8)]
        n = 0
        ntr = 0
        for (t0, nt, rl) in pieces:
            for ti, (r0, npart) in enumerate(tiles):
                b = n % 2
                n += 1
                rw, ac, so = raw[b], acc[b], sil[b]
                rr, ra, rs = res(f"raw{b}"), res(f"acc{b}"), res(f"sil{b}")
                P.xdma("sync" if n % 2 else "gpsimd", rw[:npart, :nt], uT[r0:r0 + npart, t0:t0 + nt], f"ld_raw{b}", writes=[rr])
                v = lambda a, npart=npart, nt=nt, rl=rl: a[:npart, :nt].rearrange("p (r t) -> p r t", t=rl)
                P.op("vector", lambda e, ti=ti, rw=rw, ac=ac, npart=npart, nt=nt: e.tensor_scalar(
                    out=ac[:npart, :nt], in0=rw[:npart, :nt], scalar1=cw[:npart, ti, 2:3], scalar2=None, op0=ALU.mult),
                    [rr, res("cw")], [ra])
                for j in (0, 1, 3, 4):
                    sh_ = j - 2
                    a0, a1 = max(0, -sh_), min(rl, rl - sh_)
                    P.op("vector", lambda e, ti=ti, j=j, v=v, rw=rw, ac=ac, a0=a0, a1=a1, sh_=sh_, npart=npart: e.scalar_tensor_tensor(
                        out=v(ac)[:, :, a0:a1], in0=v(rw)[:, :, a0 + sh_:a1 + sh_], scalar=cw[:npart, ti, j:j + 1],
                        in1=v(ac)[:, :, a0:a1], op0=ALU.mult, op1=ALU.add), [rr, ra, res("cw")], [ra])
                P.op("scalar", lambda e, ti=ti, ac=ac, so=so, npart=npart, nt=nt: e.activation(
                    out=so[:npart, :nt], in_=ac[:npart, :nt], func=AF.Silu, bias=cb[:npart, ti:ti + 1], scale=1.0),
                    [ra, res("cb")], [rs])
                if ti == 2:
                    P.op("vector", lambda e, so=so, nt=nt, t0=t0: e.tensor_copy(out=BT[:, t0:t0 + nt], in_=so[:, :nt]), [rs], [res("BT")])
                if ti == 3:
                    P.op("vector", lambda e, so=so, nt=nt, t0=t0: e.tensor_copy(out=CT[:, t0:t0 + nt], in_=so[:, :nt]), [rs], [res("CT")])
                    continue
                for cc in range(nt // 128):
                    ch = t0 // 128 + cc
                    pt = ntr % 2
                    ntr += 1
                    rp = res(f"ps_tr{pt}")
                    P.group([lambda e, so=so, cc=cc, npart=npart, pt=pt: e.transpose(
                        ps_tr[pt][:, :npart], so[:npart, cc * 128:(cc + 1) * 128], K8[:npart, 6, :npart])],
                        [rs, res("K8")], [rp])
                    if ti == 2:
                        P.op("scalar", lambda e, ch=ch, pt=pt: e.copy(out=Btm[:, ch, :], in_=ps_tr[pt][:, :128]), [rp], [res("Btm")])
                    else:
                        c0 = 0 if ti == 0 else 128
                        P.op("vector" if ti == 0 else "scalar",
                             (lambda e, ch=ch, pt=pt, c0=c0, npart=npart: e.tensor_copy(out=Xtm[:, ch, c0:c0 + npart], in_=ps_tr[pt][:, :npart]))
                             if ti == 0 else
                             (lambda e, ch=ch, pt=pt, c0=c0, npart=npart: e.copy(out=Xtm[:, ch, c0:c0 + npart], in_=ps_tr[pt][:, :npart])),
                             [rp], [res("Xtm")])

        for i in range(2):
            P.op("vector", lambda e, i=i: e.memset(hst[i][:], 0.0), [], [res(f"hst{i}")])
            P.op("vector", lambda e, i=i: e.memset(hbf[i][0][:], 0.0), [], [res(f"hbf{i}_0")])
        border = [1, 0] + list(range(65, 1, -1))
        dvec = sm[:, 12:15]
        out_tok = []
        h3 = lambda ap: ap.rearrange("p (h c) -> p h c", h=3)
        stb = [ps_st, ps_tr[1]]
        stn = ["ps_st0", "ps_tr1"]
        cbb = [ps_cb, ps_tr[0]]
        cbn = ["ps_cb", "ps_tr0"]

        def unit(u):
            step, d = u // 2, u % 2
            ch = step if d == 0 else border[step]
            return step, d, ch

        def front(u):
            front_a(u)
            front_b(u)

        def front_a(u):
            cbpart(u)
            prep(u)

        def cbpart(u):
            step, d, ch = unit(u)
            cs = slice(ch * 128, (ch + 1) * 128)
            P.group([lambda e, cs=cs, d=d: e.matmul(cbb[d][:, :128], lhsT=BT[:, cs], rhs=CT[:, cs], start=True, stop=True)],
                    [res("BT"), res("CT")], [res(cbn[d])])
            P.op("vector", lambda e, d=d: e.tensor_tensor(out=CBm[d][:], in0=cbb[d][:, :128], in1=(MF if d == 0 else MB), op=ALU.mult),
                 [res(cbn[d]), res("K8")], [res(f"CBm{d}")])

        def prep(u):
            step, d, ch = unit(u)
            par = step % 2
            xdt_, xs_, xd_ = xdt2[d][par], xs2[d][par], xd2[par]
            P.op("vector", lambda e, d=d, ch=ch, xdt_=xdt_: e.tensor_tensor(
                out=h3(xdt_[:]), in0=h3(Xtm[:, ch, :]), in1=dts["dtv"][:, d, ch, 0:3].unsqueeze(2).to_broadcast([128, 3, 64]), op=ALU.mult),
                [res("Xtm"), res("dtv")], [res(f"xdt{d}_{par}")])
            P.op("vector", lambda e, d=d, ch=ch, xs_=xs_: e.tensor_tensor(
                out=h3(xs_[:]), in0=h3(Xtm[:, ch, :]), in1=dts["dtsd"][:, d, ch, 0:3].unsqueeze(2).to_broadcast([128, 3, 64]), op=ALU.mult),
                [res("Xtm"), res("dtsd")], [res(f"xs{d}_{par}")])
            if d == 0:
                P.op("vector", lambda e, ch=ch, xd_=xd_: e.tensor_tensor(
                    out=h3(xd_[:]), in0=h3(Xtm[:, ch, :]), in1=dvec.unsqueeze(2).to_broadcast([128, 3, 64]), op=ALU.mult),
                    [res("Xtm"), res("sm")], [res(f"xd_{par}")])
            Tsel = K8[:, (0 if d == 0 else 1), :]
            P.op("vector", lambda e, d=d, ch=ch, Tsel=Tsel: e.tensor_tensor(
                out=Rb[d][:, 0], in0=Tsel.unsqueeze(1).to_broadcast([128, 3, 128]),
                in1=dthf[:, d, ch, 0:3].unsqueeze(2).to_broadcast([128, 3, 128]), op=ALU.mult),
                [res("K8"), res("dthf")], [res(f"R{d}")])
            for h in range(3):
                P.op("scalar", lambda e, d=d, ch=ch, h=h, Tsel=Tsel: e.activation(
                    out=Rb[d][:, 1, h, :], in_=Tsel, func=AF.Copy, scale=dtl[:, d, ch, h:h + 1]),
                    [res("K8"), res("dtl")], [res(f"R{d}")])
        def front_b(u):
            step, d, ch = unit(u)
            Ssel = K8b[:, (2 if d == 0 else 3), :]
            P.group([lambda e, d=d, h=h, t=t, Ssel=Ssel: e.matmul(ps_seg[d][:, h * 128:(h + 1) * 128], lhsT=Ssel,
                                                  rhs=Rb[d][:, t, h, :], start=(t == 0), stop=(t == 1)) for h in range(3) for t in range(2)],
                    [res("K8b"), res(f"R{d}")], [res(f"ps_seg{d}")])
            P.op("scalar", lambda e, d=d: e.activation(out=Eb[d][:].rearrange("p h l -> p (h l)"), in_=ps_seg[d][:, 0:384], func=AF.Exp),
                 [res(f"ps_seg{d}")], [res(f"E{d}")])
            P.op("vector", lambda e, d=d: e.tensor_tensor(out=Mb[d][:], in0=Eb[d][:], in1=CBm[d][:].unsqueeze(1).to_broadcast([128, 3, 128]), op=ALU.mult),
                 [res(f"E{d}"), res(f"CBm{d}")], [res(f"M{d}")])

        def back(u):
            back_pe(u)
            back_rest(u)

        def back_pe(u):
            step, d, ch = unit(u)
            cs = slice(ch * 128, (ch + 1) * 128)
            hp = step % 2
            par = step % 2
            xdt_, xs_, xd_ = xdt2[d][par], xs2[d][par], xd2[par]
            fns = []
            for h in range(3):
                hs = slice(h * 64, (h + 1) * 64)
                fns.append(lambda e, d=d, h=h, hs=hs, xdt_=xdt_: e.matmul(ps_y[d][:, hs], lhsT=Mb[d][:, h, :], rhs=xdt_[:, hs],
                                                               start=True, stop=(d == 1)))
                if d == 0:
                    fns.append(lambda e, hs=hs, xd_=xd_: e.matmul(ps_y[0][:, hs], lhsT=identb[:], rhs=xd_[:, hs], start=False, stop=True))
            fns.append(lambda e, d=d, cs=cs, hp=hp: e.matmul(ps_y[d][:, 256:448], lhsT=CT[:, cs], rhs=hbf[d][hp][:], start=True, stop=True))
            fns.append(lambda e, d=d, ch=ch, xs_=xs_: e.matmul(stb[d][:, 0:192], lhsT=Btm[:, ch, :], rhs=xs_[:], start=True, stop=True))
            P.group(fns, [res(f"M{d}"), res(f"xdt{d}_{par}"), res("identb"), res(f"xd_{par}"), res("CT"), res(f"hbf{d}_{hp}"), res("Btm"), res(f"xs{d}_{par}")],
                    [res(f"ps_y{d}"), res(stn[d])])
        def back_rest(u):
            step, d, ch = unit(u)
            hp = step % 2
            P.op("vector", lambda e, d=d, ch=ch: e.tensor_tensor(
                out=h3(rtmp[d][:]), in0=h3(hst[d][:]), in1=dts["dec"][:, d, ch, 0:3].unsqueeze(2).to_broadcast([128, 3, 64]), op=ALU.mult),
                [res(f"hst{d}"), res("dec")], [res(f"rtmp{d}")])
            P.op("vector", lambda e, d=d: e.tensor_tensor(out=hst[d][:], in0=rtmp[d][:], in1=stb[d][:, 0:192], op=ALU.add),
                 [res(f"rtmp{d}"), res(stn[d])], [res(f"hst{d}")])
            P.op("scalar", lambda e, d=d, hp=hp: e.copy(out=hbf[d][1 - hp][:], in_=hst[d][:]),
                 [res(f"hst{d}")], [res(f"hbf{d}_{1 - hp}")])
            P.op("vector", lambda e, d=d, ch=ch: e.tensor_tensor(
                out=h3(yo_t[d][:]), in0=h3(ps_y[d][:, 256:448]), in1=dts["ea"][:, d, ch, 0:3].unsqueeze(2).to_broadcast([128, 3, 64]), op=ALU.mult),
                [res(f"ps_y{d}"), res("ea")], [res(f"yot{d}")])
            yb = u % 4
            P.op("vector", lambda e, d=d, yb=yb: e.tensor_tensor(out=yo[yb][:], in0=ps_y[d][:, 0:192], in1=yo_t[d][:], op=ALU.add),
                 [res(f"ps_y{d}"), res(f"yot{d}")], [res(f"yo{yb}")])
            out_tok.append(P.xdma("sync", Y[d, ch * 128:(ch + 1) * 128, :], yo[yb][:], f"st_y{yb}", reads=[res(f"yo{yb}")]))

        NU = 2 * NCH
        import os
        mode = os.environ.get("SSD_ORDER", "pair")
        if mode == "plain":
            for u in range(NU):
                front(u)
                back(u)
        elif mode == "prep":
            prep(0)
            for u in range(NU):
                cbpart(u)
                front_b(u)
                if u + 1 < NU:
                    prep(u + 1)
                back(u)
        elif mode == "pair":
            prep(0)
            prep(1)
            for st_ in range(NCH):
                a, b = 2 * st_, 2 * st_ + 1
                cbpart(a)
                cbpart(b)
                front_b(a)
                front_b(b)
                if st_ + 1 < NCH:
                    prep(a + 2)
                    prep(b + 2)
                back(a)
                back(b)
        elif mode == "pipe":
            prep(0)
            cbpart(0)
            front_b(0)
            for u in range(NU):
                if u + 1 < NU:
                    prep(u + 1)
                    cbpart(u + 1)
                    front_b(u + 1)
                back(u)
        elif mode == "late":
            prep(0)
            cbpart(0)
            front_b(0)
            for u in range(NU):
                if u + 1 < NU:
                    prep(u + 1)
                back_pe(u)
                if u + 1 < NU:
                    cbpart(u + 1)
                    front_b(u + 1)
                back_rest(u)
        elif mode == "split":
            front_a(0)
            front_b(0)
            for u in range(NU):
                if u + 1 < NU:
                    front_a(u + 1)
                back(u)
                if u + 1 < NU:
                    front_b(u + 1)
        P.finish(out_tok[-4:])
    return nc


def ssd_consts():
    k = np.arange(128)[:, None]
    l = np.arange(128)[None, :]
    mats = [k <= l, k >= l, k > l, k < l, l >= k, l <= k, k == l, np.ones((128, 128), bool)]
    return np.ascontiguousarray(np.stack([m.astype(np.float32) for m in mats], 1))


def ssd_inputs(PT, conv_w, conv_b, dt_bias, a_log, d_skip):
    cst = ssd_consts()
    maps = []
    for core in range(NCORES):
        g, half = core // 2, core % 2
        hg0 = g * 6 + half * 3
        xc = 2048 + g * 384 + half * 192
        bc = 2048 + 1536 + g * 128
        cc = 2048 + 2048 + g * 128
        uT = np.concatenate([PT[xc:xc + 192], PT[bc:bc + 128], PT[cc:cc + 128]], 0)
        dt = np.stack([PT[4608 + d * 24 + hg0:4608 + d * 24 + hg0 + 3] for d in range(2)], 0)
        dt = dt.reshape(2, 3, NCH, 128).transpose(3, 0, 2, 1)
        cols = [np.arange(xc - 2048, xc - 2048 + 128), np.arange(xc - 2048 + 128, xc - 2048 + 192),
                np.arange(bc - 2048, bc - 2048 + 128), np.arange(cc - 2048, cc - 2048 + 128)]
        cw = np.zeros((128, 4, 5), np.float32)
        cb = np.zeros((128, 4), np.float32)
        for ti, cidx in enumerate(cols):
            cw[:len(cidx), ti, :] = conv_w[:, cidx].T
            cb[:len(cidx), ti] = conv_b[cidx]
        sm = np.concatenate([dt_bias[:, hg0:hg0 + 3].reshape(-1), a_log[:, hg0:hg0 + 3].reshape(-1), d_skip[hg0:hg0 + 3]])
        sm = np.broadcast_to(sm[None, :], (128, 15))
        maps.append({"uT": np.ascontiguousarray(uT), "dtr": np.ascontiguousarray(dt), "cw": cw, "cb": cb,
                     "sm": np.ascontiguousarray(sm, dtype=np.float32), "cst": cst})
    return maps


def build_fourier():
    nc = bass.Bass("TRN2", target_bir_lowering=False)
    uL = nc.dram_tensor("uL", [SEQ, 512], F32, kind="ExternalInput").ap()
    uC = nc.dram_tensor("uC", [CTX, 512], F32, kind="ExternalInput").ap()
    tabC = nc.dram_tensor("tabC", [SEQ, 1024], BF16, kind="ExternalInput").ap()
    tabS = nc.dram_tensor("tabS", [SEQ, 1024], BF16, kind="ExternalInput").ap()
    tcC = nc.dram_tensor("tcC", [CTX, 32], BF16, kind="ExternalInput").ap()
    tcS = nc.dram_tensor("tcS", [CTX, 32], BF16, kind="ExternalInput").ap()
    wfd = nc.dram_tensor("wf", [4, 128, 128], F32, kind="ExternalInput").ap()
    ccd = nc.dram_tensor("cc", [128, 2, 128], F32, kind="ExternalInput").ap()
    fT = nc.dram_tensor("fT", [512, TOK], F32, kind="ExternalOutput").ap()
    with contextlib.ExitStack() as st:
        sb = lambda n, s, d: st.enter_context(nc.sbuf_tensor(n, s, d))
        NB = 3
        ust = [sb(f"ust{i}", [128, 4, 512], F32) for i in range(NB)]
        ubig = sb("ubig", [128, 64, 512], BF16)
        tC = [sb(f"tC{i}", [128, 4, 512], BF16) for i in range(NB)]
        tS = [sb(f"tS{i}", [128, 4, 512], BF16) for i in range(NB)]
        wf = sb("wf_s", [128, 4, 128], F32)
        cc = sb("cc_s", [128, 2, 128], F32)
        Mm = sb("Mm", [128, 4, 2, 128], BF16)
        AB = sb("AB", [128, 4, 2, 512], BF16)
        fo = sb("fo", [128, 4, TOK], F32)
        ucs = sb("ucs", [128, 2, 512], F32)
        ucb = sb("ucb", [128, 2, 512], BF16)
        tcc = sb("tcc", [128, 2, 2, 32], BF16)
        bank = [st.enter_context(nc.psum_tensor(f"bank{i}", [128, 512], F32)) for i in range(8)]
        P = TP(nc)
        R = {}

        def res(name):
            if name not in R:
                R[name] = Res()
            return R[name]

        P.xdma("sync", wf[:], wfd.rearrange("g j d -> j g d"), "ld_wf", writes=[res("wf")])
        P.xdma("sync", cc[:], ccd, "ld_cc", writes=[res("cc")])
        for g in range(4):
            for t in range(2):
                P.group([lambda e, g=g, t=t: e.matmul(bank[g][:, t * 128:(t + 1) * 128], lhsT=cc[:, t, :], rhs=wf[:, g, :],
                                                      start=True, stop=True)], [res("wf"), res("cc")], [res(f"bank{g}")])
            P.op("vector", lambda e, g=g: e.tensor_copy(out=Mm[:, g].rearrange("p t d -> p (t d)"), in_=bank[g][:, 0:256]),
                 [res(f"bank{g}")], [res("Mm")])

        def stage2(width, c0, ABv):
            for g in range(4):
                P.group([lambda e, g=g, t=t: e.matmul(bank[g][:, :width], lhsT=Mm[:, g, t, :], rhs=ABv(g, t),
                                                      start=(t == 0), stop=(t == 1)) for t in range(2)],
                        [res("Mm"), res("AB")], [res(f"bank{g}")])
                if g % 2 == 0:
                    P.op("vector", lambda e, g=g: e.tensor_copy(out=fo[:, g, c0:c0 + width], in_=bank[g][:, :width]),
                         [res(f"bank{g}")], [res("fo")])
                else:
                    P.op("scalar", lambda e, g=g: e.copy(out=fo[:, g, c0:c0 + width], in_=bank[g][:, :width]),
                         [res(f"bank{g}")], [res("fo")])

        uv = uL.rearrange("(c p) f -> p c f", p=128)
        cv = tabC.rearrange("(c p) k -> p c k", p=128)
        sv = tabS.rearrange("(c p) k -> p c k", p=128)
        n = 0
        for half in range(2):
            ks = slice(half * 512, (half + 1) * 512)
            for nb in range(16):
                b = n % NB
                n += 1
                P.xdma("sync" if half == 1 else "gpsimd", tC[b][:], cv[:, nb * 4:(nb + 1) * 4, ks], f"ld_tc{b}", writes=[res(f"tC{b}")])
                P.xdma("gpsimd", tS[b][:], sv[:, nb * 4:(nb + 1) * 4, ks], f"ld_ts{b}", writes=[res(f"tS{b}")])
                if half == 0:
                    P.xdma("sync", ust[b][:], uv[:, nb * 4:(nb + 1) * 4, :], f"ld_u{b}", writes=[res(f"ust{b}")])
                    if nb % 2 == 0:
                        P.op("vector", lambda e, b=b, nb=nb: e.tensor_copy(out=ubig[:, nb * 4:(nb + 1) * 4, :], in_=ust[b][:]), [res(f"ust{b}")], [res(f"ubig{nb}")])
                    else:
                        P.op("scalar", lambda e, b=b, nb=nb: e.copy(out=ubig[:, nb * 4:(nb + 1) * 4, :], in_=ust[b][:]), [res(f"ust{b}")], [res(f"ubig{nb}")])
                fns = []
                for ci in range(4):
                    first = (nb == 0 and ci == 0)
                    last = (nb == 15 and ci == 3)
                    for g in range(4):
                        for t in range(2):
                            tb = tC[b] if t == 0 else tS[b]
                            fns.append(lambda e, nb=nb, ci=ci, g=g, t=t, tb=tb, first=first, last=last: e.matmul(
                                bank[g * 2 + t][:, :], lhsT=ubig[:, nb * 4 + ci, g * 128:(g + 1) * 128], rhs=tb[:, ci, :],
                                start=first, stop=last))
                P.group(fns, [res(f"ubig{nb}"), res(f"tC{b}"), res(f"tS{b}")], [res(f"bank{i}") for i in range(8)])
            for g in range(4):
                for t in range(2):
                    if t == 0:
                        P.op("vector", lambda e, g=g, t=t: e.tensor_copy(out=AB[:, g, t, :], in_=bank[g * 2 + t][:, :]),
                             [res(f"bank{g * 2 + t}")], [res("AB")])
                    else:
                        P.op("scalar", lambda e, g=g, t=t: e.copy(out=AB[:, g, t, :], in_=bank[g * 2 + t][:, :]),
                             [res(f"bank{g * 2 + t}")], [res("AB")])
            stage2(512, half * 512, lambda g, t: AB[:, g, t, :])
        P.xdma("sync", ucs[:], uC.rearrange("(c p) f -> p c f", p=128), "ld_uc", writes=[res("ucs")])
        P.xdma("sync", tcc[:, 0], tcC.rearrange("(c p) k -> p c k", p=128), "ld_tcc", writes=[res("tcc")])
        P.xdma("sync", tcc[:, 1], tcS.rearrange("(c p) k -> p c k", p=128), "ld_tcs", writes=[res("tcc")])
        P.op("vector", lambda e: e.tensor_copy(out=ucb[:], in_=ucs[:]), [res("ucs")], [res("ucb")])
        for g in range(4):
            for t in range(2):
                P.group([lambda e, g=g, t=t, ci=ci: e.matmul(bank[g * 2 + t][:, :32], lhsT=ucb[:, ci, g * 128:(g + 1) * 128],
                                                            rhs=tcc[:, t, ci, :], start=(ci == 0), stop=(ci == 1)) for ci in range(2)],
                        [res("ucb"), res("tcc")], [res(f"bank{g * 2 + t}")])
                P.op("vector", lambda e, g=g, t=t: e.tensor_copy(out=AB[:, g, t, :32], in_=bank[g * 2 + t][:, :32]),
                     [res(f"bank{g * 2 + t}")], [res("AB")])
        stage2(32, 1024, lambda g, t: AB[:, g, t, :32])
        t_o = P.xdma("sync", fT.rearrange("(g p) t -> p g t", p=128), fo[:], "st_f", reads=[res("fo")])
        P.finish([t_o])
    return nc


def fourier_tables():
    n = np.arange(SEQ, dtype=np.int64)
    ph = (np.outer(n, n) % SEQ).astype(np.float64) * (2 * np.pi / SEQ)
    sc = 1.0 / np.sqrt(SEQ * 128.0)
    C = (np.cos(ph) * sc).astype(NPBF)
    S = (np.sin(ph) * sc).astype(NPBF)
    m = np.arange(CTX, dtype=np.int64)
    phc = (np.outer(m, m) % CTX).astype(np.float64) * (2 * np.pi / CTX)
    scc = 1.0 / np.sqrt(CTX * 128.0)
    Cc_ = (np.cos(phc) * scc).astype(NPBF)
    Sc_ = (np.sin(phc) * scc).astype(NPBF)
    j = np.arange(128, dtype=np.int64)
    p128 = (np.outer(j, j) % 128).astype(np.float64) * (2 * np.pi / 128)
    cc = np.stack([np.cos(p128), -np.sin(p128)], 1).astype(np.float32)
    return C, S, Cc_, Sc_, np.ascontiguousarray(cc)


def build_G():
    nc = bass.Bass("TRN2", target_bir_lowering=False)
    xT = nc.dram_tensor("xT", [D, TOK], F32, kind="ExternalInput").ap()
    fTd = nc.dram_tensor("fT", [512, TOK], F32, kind="ExternalInput").ap()
    yfd = nc.dram_tensor("yf", [1536, TOK], F32, kind="ExternalInput").ap()
    ybd = nc.dram_tensor("yb", [1536, TOK], F32, kind="ExternalInput").ap()
    zd = nc.dram_tensor("zT", [1536, TOK], F32, kind="ExternalInput").ap()
    gnd = nc.dram_tensor("gn", [128, 12], F32, kind="ExternalInput").ap()
    gtd = nc.dram_tensor("gt", [128, 16, 2], F32, kind="ExternalInput").ap()
    wd = nc.dram_tensor("w", [D, D], BF16, kind="ExternalInput").ap()
    xo = nc.dram_tensor("xo", [D, TOK], F32, kind="ExternalOutput").ap()
    with contextlib.ExitStack() as st:
        sb = lambda n, s, d: st.enter_context(nc.sbuf_tensor(n, s, d))
        catT = sb("catT", [128, 16, TOK], BF16)
        gn = sb("gn_s", [128, 12], F32)
        gt = sb("gt_s", [128, 16, 2], F32)
        epsc = sb("epsc", [128, 1], F32)
        ones = sb("ones", [128, 128], BF16)
        fst = [sb(f"fst{i}", [128, TOK], F32) for i in range(2)]
        yfs = [sb(f"yfs{i}", [128, TOK], F32) for i in range(2)]
        ybs = [sb(f"ybs{i}", [128, TOK], F32) for i in range(2)]
        zs = [sb(f"zs{i}", [128, TOK], F32) for i in range(2)]
        uu = sb("uu", [128, 3, TOK], F32)
        sq = [sb(f"sq{i}", [128, TOK], BF16) for i in range(2)]
        rstd = sb("rstd", [128, TOK], F32)
        tmp = [sb(f"tmp{i}", [128, TOK], F32) for i in range(2)]
        wb = [sb(f"wb{i}", [128, 16, 512], BF16) for i in range(2)]
        xc = [sb(f"xc{i}", [128, TOK], F32) for i in range(3)]
        ps_ms = st.enter_context(nc.psum_tensor("ps_ms", [128, 3, 512], F32))
        ps_mm = st.enter_context(nc.psum_tensor("ps_mm", [128, 5, 512], F32))
        P = TP(nc)
        R = {}

        def res(name):
            if name not in R:
                R[name] = Res()
            return R[name]

        P.xdma("sync", gn[:], gnd, "ld_gn", writes=[res("gn")])
        P.xdma("sync", gt[:], gtd, "ld_gt", writes=[res("gt")])
        P.op("vector", lambda e: e.memset(ones[:], 1.0), [], [res("ones")])
        P.op("vector", lambda e: e.memset(epsc[:], EPS), [], [res("epsc")])
        for c in range(4):
            b = c % 2
            P.xdma("sync", fst[b][:], fTd[c * 128:(c + 1) * 128, :], f"ld_f{b}", writes=[res(f"fst{b}")])
            P.op("vector" if c % 2 == 0 else "scalar",
                 (lambda e, c=c, b=b: e.tensor_copy(out=catT[:, c, :], in_=fst[b][:])) if c % 2 == 0 else
                 (lambda e, c=c, b=b: e.copy(out=catT[:, c, :], in_=fst[b][:])),
                 [res(f"fst{b}")], [res("catT")])
        n = 0
        for grp in range(4):
            for c in range(3):
                ch = grp * 3 + c
                b = n % 2
                n += 1
                rows = slice(ch * 128, (ch + 1) * 128)
                P.xdma("sync", yfs[b][:], yfd[rows, :], f"ld_yf{b}", writes=[res(f"yfs{b}")])
                P.xdma("gpsimd", ybs[b][:], ybd[rows, :], f"ld_yb{b}", writes=[res(f"ybs{b}")])
                P.xdma("sync" if ch % 2 == 0 else "gpsimd", zs[b][:], zd[rows, :], f"ld_z{b}", writes=[res(f"zs{b}")])
                P.op("vector", lambda e, b=b: e.tensor_tensor(out=yfs[b][:], in0=yfs[b][:], in1=ybs[b][:], op=ALU.add),
                     [res(f"yfs{b}"), res(f"ybs{b}")], [res(f"yfs{b}")])
                P.op("scalar", lambda e, b=b: e.activation(out=zs[b][:], in_=zs[b][:], func=AF.Silu), [res(f"zs{b}")], [res(f"zs{b}")])
                P.op("vector", lambda e, b=b, c=c: e.tensor_tensor(out=uu[:, c, :], in0=yfs[b][:], in1=zs[b][:], op=ALU.mult),
                     [res(f"yfs{b}"), res(f"zs{b}")], [res(f"uu{c}")])
                P.op("scalar", lambda e, b=b, c=c: e.activation(out=sq[b][:], in_=uu[:, c, :], func=AF.Square),
                     [res(f"uu{c}")], [res(f"sq{b}")])
                P.group([lambda e, b=b, c=c, bi=bi, s0=s0, w=w: e.matmul(ps_ms[:, bi, :w], lhsT=ones[:], rhs=sq[b][:, s0:s0 + w],
                                                                        start=(c == 0), stop=(c == 2))
                         for bi, (s0, w) in enumerate(BLKS)], [res("ones"), res(f"sq{b}")], [res("ps_ms")])
            for bi, (s0, w) in enumerate(BLKS):
                P.op("scalar", lambda e, bi=bi, s0=s0, w=w: e.activation(out=rstd[:, s0:s0 + w], in_=ps_ms[:, bi, :w], func=AF.Sqrt,
                                                                         bias=epsc[:, 0:1], scale=1.0 / 384.0),
                     [res("ps_ms"), res("epsc")], [res("rstd")])
            P.op("vector", lambda e: e.reciprocal(out=rstd[:], in_=rstd[:]), [res("rstd")], [res("rstd")])
            for c in range(3):
                ch = grp * 3 + c
                b = c % 2
                P.op("vector", lambda e, b=b, c=c: e.tensor_tensor(out=tmp[b][:], in0=uu[:, c, :], in1=rstd[:], op=ALU.mult),
                     [res(f"uu{c}"), res("rstd")], [res(f"tmp{b}")])
                P.op("scalar", lambda e, b=b, ch=ch: e.activation(out=catT[:, 4 + ch, :], in_=tmp[b][:], func=AF.Copy, scale=gn[:, ch:ch + 1]),
                     [res(f"tmp{b}"), res("gn")], [res("catT")])
        outs = []
        for nb in range(4):
            b = nb % 2
            P.xdma("sync", wb[b][:], wd[:, nb * 512:(nb + 1) * 512].rearrange("(kc p) n -> p kc n", p=128), f"ld_w{b}", writes=[res(f"wb{b}")])
            for j in range(4):
                nch = nb * 4 + j
                xb = nch % 3
                P.xdma("gpsimd", xc[xb][:], xT[nch * 128:(nch + 1) * 128, :], f"ld_x{xb}", writes=[res(f"xc{xb}")])
                for bi, (s0, w) in enumerate(BLKS):
                    pb = (nch * 3 + bi) % 5
                    P.group([lambda e, b=b, j=j, k=k, s0=s0, w=w, pb=pb: e.matmul(
                        ps_mm[:, pb, :w], lhsT=wb[b][:, k, j * 128:(j + 1) * 128], rhs=catT[:, k, s0:s0 + w],
                        start=(k == 0), stop=(k == 15)) for k in range(16)],
                        [res(f"wb{b}"), res("catT")], [res(f"ps_mm{pb}")])
                    P.op("vector", lambda e, xb=xb, nch=nch, bi=bi, s0=s0, w=w, pb=pb: e.scalar_tensor_tensor(
                        out=xc[xb][:, s0:s0 + w], in0=ps_mm[:, pb, :w], scalar=gt[:, nch, (1 if bi == 2 else 0):(2 if bi == 2 else 1)],
                        in1=xc[xb][:, s0:s0 + w], op0=ALU.mult, op1=ALU.add),
                        [res(f"ps_mm{pb}"), res("gt"), res(f"xc{xb}")], [res(f"xc{xb}")])
                outs.append(P.xdma("sync", xo[nch * 128:(nch + 1) * 128, :], xc[xb][:], f"st_x{xb}", reads=[res(f"xc{xb}")]))
        P.finish(outs[-3:])
    return nc


def build_M():
    nc = bass.Bass("TRN2", target_bir_lowering=False)
    xT = nc.dram_tensor("xT", [D, TOK], F32, kind="ExternalInput").ap()
    scd = nc.dram_tensor("sc", [128, 16, 2], F32, kind="ExternalInput").ap()
    shd = nc.dram_tensor("sh", [128, 16, 2], F32, kind="ExternalInput").ap()
    gtd = nc.dram_tensor("gt", [128, 16, 2], F32, kind="ExternalInput").ap()
    gd = nc.dram_tensor("g", [128, 16], F32, kind="ExternalInput").ap()
    w1d = nc.dram_tensor("w1", [D, DFF], BF16, kind="ExternalInput").ap()
    w2d = nc.dram_tensor("w2", [DFF, D], BF16, kind="ExternalInput").ap()
    xo = nc.dram_tensor("xo", [D, TOK], F32, kind="ExternalOutput").ap()
    with contextlib.ExitStack() as st:
        sb = lambda n, s, d: st.enter_context(nc.sbuf_tensor(n, s, d))
        X = sb("X", [128, 16, TOK], F32)
        hT = sb("hT", [128, 16, TOK], BF16)
        m1 = sb("m1", [128, 8, TOK], BF16)
        sc = sb("sc_s", [128, 16, 2], F32)
        sh = sb("sh_s", [128, 16, 2], F32)
        gt = sb("gt_s", [128, 16, 2], F32)
        g = sb("g_s", [128, 16], F32)
        s1 = sb("s1", [128, 16, 3], F32)
        ones = sb("ones", [128, 128], BF16)
        sq = [sb(f"sq{i}", [128, TOK], BF16) for i in range(2)]
        rstd = sb("rstd", [128, TOK], F32)
        tmp = [sb(f"tmp{i}", [128, TOK], F32) for i in range(2)]
        w1b = [sb(f"w1b{i}", [128, 16, 512], BF16) for i in range(2)]
        w2b = [sb(f"w2b{i}", [128, 8, 512], BF16) for i in range(2)]
        rl = [sb(f"rl{i}", [128, 512], F32) for i in range(2)]
        ps_ms = st.enter_context(nc.psum_tensor("ps_ms", [128, 3, 512], F32))
        ps_mm = st.enter_context(nc.psum_tensor("ps_mm", [128, 5, 512], F32))
        P = TP(nc)
        R = {}

        def res(name):
            if name not in R:
                R[name] = Res()
            return R[name]

        t_ones = P.dve(lambda e: e.memset(ones[:], 1.0 / D))
        xr = []
        xv = xT.rearrange("(c p) t -> p c t", p=128)
        for c4 in range(4):
            t = P.dma("sync" if c4 % 2 == 0 else "gpsimd", X[:, c4 * 4:(c4 + 1) * 4, :], xv[:, c4 * 4:(c4 + 1) * 4, :], f"ld_x{c4}")
            xr += [t] * 4
        t1 = P.dma("gpsimd", sc[:], scd, "ld_sc")
        t2 = P.dma("gpsimd", sh[:], shd, "ld_sh")
        t3 = P.dma("gpsimd", g[:], gd, "ld_g")
        t4 = P.dma("gpsimd", gt[:], gtd, "ld_gt")
        h_ready = emit_norm_mod(P, nc, X, hT, sc, sh, g[:], ones, ps_ms, sq, rstd, tmp, s1, xr, [t1, t2, t3, t_ones])
        rh = res("hT")
        rh.w = h_ready[-1]
        rX = [res(f"X{c}") for c in range(16)]
        for c in range(16):
            rX[c].w = xr[c]
            rX[c].r = {h_ready[-1][0]: h_ready[-1][1], "s_dve": P.cnt["s_dve"]}
        res("gt").w = t4
        npb = 0
        for q in range(8):
            for b2 in range(2):
                wi = (q * 2 + b2) % 2
                c0 = q * 1024 + b2 * 512
                P.xdma("sync", w1b[wi][:], w1d[:, c0:c0 + 512].rearrange("(kc p) n -> p kc n", p=128), f"ld_w1{wi}", writes=[res(f"w1b{wi}")])
                for j in range(4):
                    f = b2 * 4 + j
                    for bi, (s0, w) in enumerate(BLKS):
                        pb = npb % 5
                        npb += 1
                        P.group([lambda e, wi=wi, j=j, k=k, s0=s0, w=w, pb=pb: e.matmul(
                            ps_mm[:, pb, :w], lhsT=w1b[wi][:, k, j * 128:(j + 1) * 128], rhs=hT[:, k, s0:s0 + w],
                            start=(k == 0), stop=(k == 15)) for k in range(16)],
                            [res(f"w1b{wi}"), rh], [res(f"ps_mm{pb}")])
                        rb = npb % 2
                        if npb % 2 == 0:
                            P.op("scalar", lambda e, rb=rb, pb=pb, w=w: e.activation(out=rl[rb][:, :w], in_=ps_mm[:, pb, :w], func=AF.Relu),
                                 [res(f"ps_mm{pb}")], [res(f"rl{rb}")])
                            P.op("vector", lambda e, rb=rb, f=f, s0=s0, w=w: e.tensor_tensor(out=m1[:, f, s0:s0 + w], in0=rl[rb][:, :w], in1=rl[rb][:, :w], op=ALU.mult),
                                 [res(f"rl{rb}")], [res(f"m1_{f}")])
                        else:
                            P.op("vector", lambda e, rb=rb, pb=pb, w=w: e.tensor_scalar_max(out=rl[rb][:, :w], in0=ps_mm[:, pb, :w], scalar1=0.0),
                                 [res(f"ps_mm{pb}")], [res(f"rl{rb}")])
                            P.op("scalar", lambda e, rb=rb, f=f, s0=s0, w=w: e.activation(out=m1[:, f, s0:s0 + w], in_=rl[rb][:, :w], func=AF.Square),
                                 [res(f"rl{rb}")], [res(f"m1_{f}")])
            for nb in range(4):
                wi = (q * 4 + nb) % 2
                P.xdma("gpsimd", w2b[wi][:], w2d[q * 1024:(q + 1) * 1024, nb * 512:(nb + 1) * 512].rearrange("(fc p) n -> p fc n", p=128),
                       f"ld_w2{wi}", writes=[res(f"w2b{wi}")])
                for j in range(4):
                    nch = nb * 4 + j
                    for bi, (s0, w) in enumerate(BLKS):
                        pb = npb % 5
                        npb += 1
                        P.group([lambda e, wi=wi, j=j, f=f, s0=s0, w=w, pb=pb: e.matmul(
                            ps_mm[:, pb, :w], lhsT=w2b[wi][:, f, j * 128:(j + 1) * 128], rhs=m1[:, f, s0:s0 + w],
                            start=(f == 0), stop=(f == 7)) for f in range(8)],
                            [res(f"w2b{wi}")] + [res(f"m1_{ff}") for ff in range(8)], [res(f"ps_mm{pb}")])
                        P.op("vector", lambda e, nch=nch, bi=bi, s0=s0, w=w, pb=pb: e.scalar_tensor_tensor(
                            out=X[:, nch, s0:s0 + w], in0=ps_mm[:, pb, :w], scalar=gt[:, nch, (1 if bi == 2 else 0):(2 if bi == 2 else 1)],
                            in1=X[:, nch, s0:s0 + w], op0=ALU.mult, op1=ALU.add),
                            [res(f"ps_mm{pb}"), res("gt"), rX[nch]], [rX[nch]])
        outs = []
        xov = xo.rearrange("(c p) t -> p c t", p=128)
        for c4 in range(4):
            outs.append(P.xdma("sync" if c4 % 2 == 0 else "gpsimd", xov[:, c4 * 4:(c4 + 1) * 4, :], X[:, c4 * 4:(c4 + 1) * 4, :], f"st_x{c4}",
                               reads=[rX[c] for c in range(c4 * 4, c4 * 4 + 4)]))
        P.finish(outs)
    return nc


def build_N():
    nc = bass.Bass("TRN2", target_bir_lowering=False)
    xT = nc.dram_tensor("xT", [D, TOK], F32, kind="ExternalInput").ap()
    gd = nc.dram_tensor("g", [128, 16], F32, kind="ExternalInput").ap()
    xo = nc.dram_tensor("xo", [D, TOK], F32, kind="ExternalOutput").ap()
    with contextlib.ExitStack() as st:
        sb = lambda n, s, d: st.enter_context(nc.sbuf_tensor(n, s, d))
        X = sb("X", [128, 16, TOK], F32)
        hT = sb("hT", [128, 16, TOK], F32)
        zz = sb("zz", [128, 16, 2], F32)
        g = sb("g_s", [128, 16], F32)
        s1 = sb("s1", [128, 16, 3], F32)
        ones = sb("ones", [128, 128], BF16)
        sq = [sb(f"sq{i}", [128, TOK], BF16) for i in range(2)]
        rstd = sb("rstd", [128, TOK], F32)
        tmp = [sb(f"tmp{i}", [128, TOK], F32) for i in range(2)]
        ps_ms = st.enter_context(nc.psum_tensor("ps_ms", [128, 3, 512], F32))
        P = Prog(nc)
        t_ones = P.dve(lambda e: e.memset(ones[:], 1.0 / D))
        t_z = P.dve(lambda e: e.memset(zz[:], 0.0))
        xr = []
        xv = xT.rearrange("(c p) t -> p c t", p=128)
        for c4 in range(4):
            t = P.dma("sync" if c4 % 2 == 0 else "gpsimd", X[:, c4 * 4:(c4 + 1) * 4, :], xv[:, c4 * 4:(c4 + 1) * 4, :], f"ld_x{c4}")
            xr += [t] * 4
        t3 = P.dma("gpsimd", g[:], gd, "ld_g")
        h_ready = emit_norm_mod(P, nc, X, hT, zz, zz, g[:], ones, ps_ms, sq, rstd, tmp, s1, xr, [t3, t_ones, t_z])
        outs = []
        xov = xo.rearrange("(c p) t -> p c t", p=128)
        for c4 in range(4):
            outs.append(P.dma("sync", xov[:, c4 * 4:(c4 + 1) * 4, :], hT[:, c4 * 4:(c4 + 1) * 4, :], f"st_x{c4}", [h_ready[c4 * 4 + 3]]))
        P.finish(outs)
    return nc


_CACHE = {}


def _prog(name, builder):
    if name not in _CACHE:
        _CACHE[name] = builder()
    return _CACHE[name]


def _mod2(mods, l, k):
    return np.ascontiguousarray(np.stack([fm(mods[l, 0, k * D:(k + 1) * D]), fm(mods[l, 1, k * D:(k + 1) * D])], -1))


def kernel(x, c, ctx, c_ctx, w_ada, b_ada, g_mix, w_in, conv_w, conv_b, dt_bias, a_log, d_skip,
           g_ssd_norm, w_fourier, w_out, g_mlp, w_mlp1, w_mlp2, g_final, _nlayers=DEPTH, _debug=None):
    f32 = lambda a: np.ascontiguousarray(np.asarray(a, dtype=np.float32))
    x, c, ctx, c_ctx = f32(x), f32(c), f32(ctx), f32(c_ctx)
    w_ada, b_ada, g_mix, w_in = f32(w_ada), f32(b_ada), f32(g_mix), f32(w_in)
    conv_w, conv_b, dt_bias, a_log, d_skip = f32(conv_w), f32(conv_b), f32(dt_bias), f32(a_log), f32(d_skip)
    g_ssd_norm, w_fourier, w_out, g_mlp = f32(g_ssd_norm), f32(w_fourier), f32(w_out), f32(g_mlp)
    w_mlp1, w_mlp2, g_final = f32(w_mlp1), f32(w_mlp2), f32(g_final)

    ws = [w_in, w_out, w_mlp1, w_mlp2]
    flat = np.concatenate([w.reshape(-1) for w in ws])
    fb = run_cast(flat)
    wb = []
    o = 0
    for w in ws:
        wb.append(fb[o:o + w.size].reshape(w.shape))
        o += w.size
    w_in_b, w_out_b, w1_b, w2_b = wb
    del flat, fb

    mods = run_mods(c, c_ctx, w_ada, b_ada)
    C, S, Cc_, Sc_, cc = fourier_tables()
    tabs = [(np.ascontiguousarray(C[:, 1024 * i:1024 * (i + 1)]), np.ascontiguousarray(S[:, 1024 * i:1024 * (i + 1)]),
             np.ascontiguousarray(Cc_[:, 32 * i:32 * (i + 1)]), np.ascontiguousarray(Sc_[:, 32 * i:32 * (i + 1)])) for i in range(NCORES)]
    del C, S

    xl, xc = x[0], ctx[0]
    xT = [np.ascontiguousarray(np.concatenate([xl[1024 * i:1024 * (i + 1)], xc[32 * i:32 * (i + 1)]], 0).T) for i in range(NCORES)]

    def to_global(per_core):
        R_ = per_core[0].shape[0]
        G = np.empty((R_, NTOK), np.float32)
        for i in range(NCORES):
            G[:, CTX + 1024 * i:CTX + 1024 * (i + 1)] = per_core[i][:, :1024]
            G[:, 32 * i:32 * (i + 1)] = per_core[i][:, 1024:]
        return G

    def to_core(G, i):
        return np.ascontiguousarray(np.concatenate([G[:, CTX + 1024 * i:CTX + 1024 * (i + 1)], G[:, 32 * i:32 * (i + 1)]], 1))

    for l in range(_nlayers):
        ncB = _prog("B", build_B)
        sc1, sh1, gt1 = _mod2(mods, l, 1), _mod2(mods, l, 0), _mod2(mods, l, 2)
        sh2, sc2, gt2 = _mod2(mods, l, 3), _mod2(mods, l, 4), _mod2(mods, l, 5)
        res = _run(ncB, [{"xT": xT[i], "sc": sc1, "sh": sh1, "g": fm(g_mix[l]), "w": w_in_b[l]} for i in range(NCORES)])
        pT = [r["pT"] for r in res]
        PT = to_global(pT)
        ncC = _prog("C", build_ssd)
        res = _run(ncC, ssd_inputs(PT, conv_w[l], conv_b[l], dt_bias[l], a_log[l], d_skip[l]))
        YF = np.concatenate([r["Y"][0].T for r in res], 0)
        YB = np.concatenate([r["Y"][1].T for r in res], 0)
        ncF = _prog("F", build_fourier)
        uL = np.ascontiguousarray(PT[0:512, CTX:].T)
        uC = np.ascontiguousarray(PT[0:512, :CTX].T)
        res = _run(ncF, [{"uL": uL, "uC": uC, "tabC": tabs[i][0], "tabS": tabs[i][1], "tcC": tabs[i][2], "tcS": tabs[i][3],
                          "wf": w_fourier[l], "cc": cc} for i in range(NCORES)])
        fT = [r["fT"] for r in res]
        ncG = _prog("G", build_G)
        res = _run(ncG, [{"xT": xT[i], "fT": fT[i], "yf": to_core(YF, i), "yb": to_core(YB, i),
                          "zT": np.ascontiguousarray(pT[i][512:2048]), "gn": fm(g_ssd_norm[l]), "gt": gt1, "w": w_out_b[l]}
                         for i in range(NCORES)])
        xm = [r["xo"] for r in res]
        if _debug is not None:
            _debug[f"xm{l}"] = xm
        ncM = _prog("M", build_M)
        res = _run(ncM, [{"xT": xm[i], "sc": sc2, "sh": sh2, "gt": gt2, "g": fm(g_mlp[l]), "w1": w1_b[l], "w2": w2_b[l]}
                         for i in range(NCORES)])
        xT = [r["xo"] for r in res]
        if _debug is not None:
            _debug[f"xo{l}"] = xT
    ncN = _prog("N", build_N)
    res = _run(ncN, [{"xT": xT[i], "g": fm(g_final)} for i in range(NCORES)])
    out = np.empty((1, SEQ, D), np.float32)
    for i in range(NCORES):
        out[0, 1024 * i:1024 * (i + 1), :] = res[i]["xo"][:, :1024].T
    return out
```

```python
import numpy as np
import ml_dtypes
import concourse.bass as bass
import concourse.mybir as mybir
from concourse.bass_utils import run_bass_kernel_spmd

F32 = mybir.dt.float32
BF16 = mybir.dt.bfloat16
AF = mybir.ActivationFunctionType
ALU = mybir.AluOpType
AX = mybir.AxisListType
NPBF = ml_dtypes.bfloat16

NCORES = 8
D = 2048
SEQ = 8192
CTX = 256
DEPTH = 4
DIN = 4656
DFF = 8192
TOK = 1056
EPS = 1e-6


class Prog:
    ENGS = ("sync", "scalar", "vector", "gpsimd", "tensor")

    def __init__(self, nc):
        self.nc = nc
        self.q = {e: [] for e in self.ENGS}
        self.cnt = {}
        self.waited = {e: {} for e in self.ENGS}

    def emit(self, eng, fn, waits=(), sig=None, inc=1):
        ws = []
        for w in waits:
            if w is None:
                continue
            s, v = w
            if self.waited[eng].get(s, 0) >= v:
                continue
            self.waited[eng][s] = v
            ws.append((s, v))
        tok = None
        if sig is not None:
            self.cnt[sig] = self.cnt.get(sig, 0) + inc
            tok = (sig, self.cnt[sig])
        self.q[eng].append((fn, ws, sig, inc))
        return tok

    def pe(self, fn, waits=(), sig=True):
        return self.emit("tensor", fn, waits, "s_pe" if sig else None)

    def act(self, fn, waits=(), sig=True):
        return self.emit("scalar", fn, waits, "s_act" if sig else None)

    def dve(self, fn, waits=(), sig=True):
        return self.emit("vector", fn, waits, "s_dve" if sig else None)

    def pool(self, fn, waits=(), sig=True):
        return self.emit("gpsimd", fn, waits, "s_pool" if sig else None)

    def dma(self, q, out, in_, sem, waits=()):
        return self.emit(q, lambda e: e.dma_start(out=out, in_=in_), waits, sem, 16)

    def simulate(self):
        pos = {e: 0 for e in self.ENGS}
        cnt = {}
        while True:
            prog = False
            for e in self.ENGS:
                while pos[e] < len(self.q[e]):
                    fn, ws, sig, inc = self.q[e][pos[e]]
                    if all(cnt.get(s_, 0) >= v for s_, v in ws):
                        if sig is not None:
                            cnt[sig] = cnt.get(sig, 0) + inc
                        pos[e] += 1
                        prog = True
                    else:
                        break
            if all(pos[e] == len(self.q[e]) for e in self.ENGS):
                return True
            if not prog:
                for e in self.ENGS:
                    if pos[e] < len(self.q[e]):
                        fn, ws, sig, inc = self.q[e][pos[e]]
                        print("DEADLOCK", e, pos[e], len(self.q[e]), [(s_, v, cnt.get(s_, 0)) for s_, v in ws], sig)
                return False

    def finish(self, final_waits):
        import os
        if os.environ.get("PROG_SIM"):
            print("SIM", self.simulate(), {e: len(self.q[e]) for e in self.ENGS})
        nc = self.nc
        names = sorted(self.cnt.keys())
        sems = {}
        import contextlib
        with contextlib.ExitStack() as st:
            for n in names:
                sems[n] = st.enter_context(nc.semaphore(n))
            block = st.enter_context(nc.Block())
            q = self.q
            q["sync"].append((None, [w for w in final_waits if w is not None], None, 0))

            def runner(ename):
                def _(eng):
                    for fn, ws, sig, inc in q[ename]:
                        for s, v in ws:
                            eng.wait_ge(sems[s], v)
                        if fn is None:
                            continue
                        ins = fn(eng)
                        if sig is not None:
                            ins.then_inc(sems[sig], inc)
                return _
            block.sync(runner("sync"))
            block.scalar(runner("scalar"))
            block.vector(runner("vector"))
            block.gpsimd(runner("gpsimd"))
            block.tensor(runner("tensor"))


def _run(nc, in_maps):
    res = run_bass_kernel_spmd(nc, in_maps, core_ids=list(range(NCORES)))
    return res.results


def build_cast(F):
    nc = bass.Bass("TRN2", target_bir_lowering=False)
    src = nc.dram_tensor("src", [128, F], F32, kind="ExternalInput").ap()
    dst = nc.dram_tensor("dst", [128, F], BF16, kind="ExternalOutput").ap()
    T = 4096
    nt = (F + T - 1) // T
    NB = 3
    import contextlib
    with contextlib.ExitStack() as st:
        ins = [st.enter_context(nc.sbuf_tensor(f"in{i}", [128, T], F32)) for i in range(NB)]
        outs = [st.enter_context(nc.sbuf_tensor(f"out{i}", [128, T], BF16)) for i in range(NB)]
        P = Prog(nc)
        cast_tok = [None] * NB
        st_tok = [None] * NB
        for t in range(nt):
            b = t % NB
            w = min(T, F - t * T)
            ld = P.dma("sync", ins[b][:, :w], src[:, t * T:t * T + w], f"ld{b}", [cast_tok[b]])
            if t % 2 == 0:
                cast_tok[b] = P.dve(lambda e, b=b, w=w: e.tensor_copy(out=outs[b][:, :w], in_=ins[b][:, :w]),
                                    [ld, st_tok[b]])
            else:
                cast_tok[b] = P.act(lambda e, b=b, w=w: e.copy(out=outs[b][:, :w], in_=ins[b][:, :w]),
                                    [ld, st_tok[b]])
            st_tok[b] = P.dma("gpsimd", dst[:, t * T:t * T + w], outs[b][:, :w], f"st{b}", [cast_tok[b]])
        P.finish(st_tok)
    return nc


def run_cast(flat):
    n = flat.size
    assert n % (NCORES * 128) == 0
    F = n // (NCORES * 128)
    nc = build_cast(F)
    sh = flat.reshape(NCORES, 128, F)
    res = _run(nc, [{"src": np.ascontiguousarray(sh[i])} for i in range(NCORES)])
    return np.stack([r["dst"] for r in res]).reshape(-1)


import contextlib


def fm(v):
    v = np.asarray(v)
    n = v.shape[-1] // 128
    lead = v.shape[:-1]
    a = v.reshape(lead + (n, 128))
    a = np.moveaxis(a, -1, 0)
    return np.ascontiguousarray(a)


def build_mods():
    nc = bass.Bass("TRN2", target_bir_lowering=False)
    cc = nc.dram_tensor("cc", [128, 16, 2], F32, kind="ExternalInput").ap()
    w = nc.dram_tensor("w", [DEPTH, D, 1536], F32, kind="ExternalInput").ap()
    b = nc.dram_tensor("b", [2, DEPTH, 1536], F32, kind="ExternalInput").ap()
    out = nc.dram_tensor("out", [2, DEPTH, 1536], F32, kind="ExternalOutput").ap()
    with contextlib.ExitStack() as st:
        sb = lambda n, s, d: st.enter_context(nc.sbuf_tensor(n, s, d))
        cs = sb("cs", [128, 16, 2], F32)
        ss = sb("ss", [128, 16, 2], F32)
        bs = sb("bs", [2, DEPTH, 1536], F32)
        os_ = sb("os", [2, DEPTH, 1536], F32)
        wb = [sb(f"wb{i}", [128, 16, 512], F32) for i in range(3)]
        ps = st.enter_context(nc.psum_tensor("ps", [128, 2, 512], F32))
        P = Prog(nc)
        t_c = P.dma("sync", cs[:], cc, "ld_c")
        t_b = P.dma("sync", bs[:], b, "ld_b")
        t_s = P.act(lambda e: e.activation(out=ss[:], in_=cs[:], func=AF.Silu), [t_c])
        wfree = [None, None, None]
        psfree = [None, None]
        evs = []
        n = 0
        for l in range(DEPTH):
            for blk in range(3):
                bi = n % 3
                pi = n % 2
                ld = P.dma(("sync", "gpsimd", "scalar")[n % 3], wb[bi][:],
                           w[l, :, blk * 512:(blk + 1) * 512].rearrange("(kc p) n -> p kc n", p=128),
                           f"ld_w{bi}", [wfree[bi]])
                for k in range(16):
                    t_mm = P.pe(lambda e, bi=bi, k=k, pi=pi: e.matmul(
                        ps[0:2, pi, :], lhsT=ss[:, k, :], rhs=wb[bi][:, k, :], start=(k == 0), stop=(k == 15)),
                        [ld, t_s, psfree[pi]] if k == 0 else [], sig=(k == 15))
                ev = P.dve(lambda e, l=l, blk=blk, pi=pi: e.tensor_tensor(
                    out=os_[:, l, blk * 512:(blk + 1) * 512], in0=ps[0:2, pi, :], in1=bs[:, l, blk * 512:(blk + 1) * 512], op=ALU.add),
                    [t_mm, t_b])
                evs.append(ev)
                psfree[pi] = ev
                wfree[bi] = t_mm
                n += 1
        t_o = P.dma("sync", out, os_[:], "st_o", [evs[-1]])
        P.finish([t_o])
    return nc


def run_mods(c, c_ctx, w_ada, b_ada):
    nc = build_mods()
    cc = np.stack([fm(c.reshape(-1)), fm(c_ctx.reshape(-1))], axis=-1)
    in_maps = []
    for i in range(NCORES):
        sl = slice(1536 * i, 1536 * (i + 1))
        bb = np.ascontiguousarray(np.broadcast_to(b_ada[None, :, sl], (2, DEPTH, 1536)), dtype=np.float32)
        in_maps.append({"cc": cc, "w": np.ascontiguousarray(w_ada[:, :, sl]), "b": bb})
    res = _run(nc, in_maps)
    mods = np.zeros((DEPTH, 2, 6 * D), np.float32)
    for i in range(NCORES):
        o = res[i]["out"]
        mods[:, :, 1536 * i:1536 * (i + 1)] = np.transpose(o, (1, 0, 2))
    return mods


BLKS = [(0, 512), (512, 512), (1024, 32)]


def emit_norm_mod(P, nc, X, hT, sc, sh, g, ones, ps_ms, sq, rstd, tmp, s1, x_ready, extra_waits=()):
    ew = list(extra_waits)
    epsc = s1[:, 0, 2:3]
    t_eps = P.dve(lambda e: e.memset(s1[:, :, 2:3], EPS))
    t_s1 = []
    for j in range(2):
        t_s1.append(P.dve(lambda e, j=j: e.scalar_tensor_tensor(
            out=s1[:, :, j], in0=sc[:, :, j], scalar=1.0, in1=g, op0=ALU.add, op1=ALU.mult), ew))
    sq_tok = [None] * len(sq)
    mm_tok = None
    for c in range(16):
        b = c % len(sq)
        xr = x_ready[c] if isinstance(x_ready, list) else x_ready
        t_sq = P.act(lambda e, c=c, b=b: e.activation(out=sq[b][:], in_=X[:, c, :], func=AF.Square),
                     [xr, sq_tok[b]] + ew)
        for bi, (s0, w) in enumerate(BLKS):
            mm_tok = P.pe(lambda e, c=c, b=b, bi=bi, s0=s0, w=w: e.matmul(
                ps_ms[:, bi, :w], lhsT=ones[:], rhs=sq[b][:, s0:s0 + w], start=(c == 0), stop=(c == 15)),
                [t_sq] + ew, sig=(bi == 2))
        sq_tok[b] = mm_tok
    t_r = None
    for bi, (s0, w) in enumerate(BLKS):
        t_q = P.act(lambda e, bi=bi, s0=s0, w=w: e.activation(
            out=rstd[:, s0:s0 + w], in_=ps_ms[:, bi, :w], func=AF.Sqrt, bias=epsc[:, 0:1], scale=1.0), [mm_tok, t_eps])
        t_r = P.dve(lambda e, bi=bi, s0=s0, w=w: e.reciprocal(
            out=rstd[:, s0:s0 + w], in_=rstd[:, s0:s0 + w]), [t_q])
    toks = []
    tmp_tok = [None] * len(tmp)
    for c in range(16):
        b = c % len(tmp)
        t_m = P.dve(lambda e, c=c, b=b: e.tensor_tensor(out=tmp[b][:], in0=X[:, c, :], in1=rstd[:], op=ALU.mult),
                    [t_r, tmp_tok[b]])
        P.act(lambda e, c=c, b=b: e.activation(out=hT[:, c, 0:1024], in_=tmp[b][:, 0:1024], func=AF.Identity,
                                               scale=s1[:, c, 0:1], bias=sh[:, c, 0:1]), [t_m, t_s1[1]], sig=False)
        t_a = P.act(lambda e, c=c, b=b: e.activation(out=hT[:, c, 1024:TOK], in_=tmp[b][:, 1024:TOK], func=AF.Identity,
                                                     scale=s1[:, c, 1:2], bias=sh[:, c, 1:2]), [t_m])
        tmp_tok[b] = t_a
        toks.append(t_a)
    return toks


def emit_proj(P, nc, wdram, nout, hT, h_ready, wbufs, psb, evac, name, kch=16, wfree0=None, ps_free0=None):
    nblk = (nout + 511) // 512
    wfree = list(wfree0) if wfree0 else [None] * len(wbufs)
    ps_free = list(ps_free0) if ps_free0 else [None] * len(psb)
    pi = 0
    evs = []
    last_mm = None
    for nb in range(nblk):
        b = nb % len(wbufs)
        ncol = min(512, nout - nb * 512)
        ld = P.dma("sync" if nb % 2 == 0 else "gpsimd", wbufs[b][:, :, :ncol],
                   wdram[:, nb * 512:nb * 512 + ncol].rearrange("(kc p) n -> p kc n", p=128),
                   f"ld_{name}{b}", [wfree[b]])
        for j in range((ncol + 127) // 128):
            m = min(128, ncol - j * 128)
            for bi, (s0, w) in enumerate(BLKS):
                pb = pi % len(psb)
                pi += 1
                for k in range(kch):
                    last_mm = P.pe(lambda e, b=b, j=j, m=m, k=k, s0=s0, w=w, pb=pb: e.matmul(
                        psb[pb][:m, :w], lhsT=wbufs[b][:, k, j * 128:j * 128 + m], rhs=hT[:, k, s0:s0 + w],
                        start=(k == 0), stop=(k == kch - 1)),
                        ([ld, ps_free[pb]] if k == 0 else []) + [h_ready[k]], sig=(k == kch - 1))
                ev = evac(nb * 4 + j, m, bi, s0, w, psb[pb], last_mm)
                ps_free[pb] = ev
                evs.append(ev)
        wfree[b] = last_mm
    return last_mm, evs


def build_B():
    nc = bass.Bass("TRN2", target_bir_lowering=False)
    xT = nc.dram_tensor("xT", [D, TOK], F32, kind="ExternalInput").ap()
    scd = nc.dram_tensor("sc", [128, 16, 2], F32, kind="ExternalInput").ap()
    shd = nc.dram_tensor("sh", [128, 16, 2], F32, kind="ExternalInput").ap()
    gd = nc.dram_tensor("g", [128, 16], F32, kind="ExternalInput").ap()
    wd = nc.dram_tensor("w", [D, DIN], BF16, kind="ExternalInput").ap()
    pT = nc.dram_tensor("pT", [DIN, TOK], F32, kind="ExternalOutput").ap()
    with contextlib.ExitStack() as st:
        sb = lambda n, s, d: st.enter_context(nc.sbuf_tensor(n, s, d))
        X = sb("X", [128, 16, TOK], F32)
        hT = sb("hT", [128, 16, TOK], BF16)
        sc = sb("sc_s", [128, 16, 2], F32)
        sh = sb("sh_s", [128, 16, 2], F32)
        g = sb("g_s", [128, 16], F32)
        s1 = sb("s1", [128, 16, 3], F32)
        ones = sb("ones", [128, 128], BF16)
        sq = [sb(f"sq{i}", [128, TOK], BF16) for i in range(3)]
        rstd = sb("rstd", [128, TOK], F32)
        tmp = [sb(f"tmp{i}", [128, TOK], F32) for i in range(2)]
        wb = [sb(f"wb{i}", [128, 16, 512], BF16) for i in range(2)]
        ot = [sb(f"ot{i}", [128, TOK], F32) for i in range(3)]
        ps_ms = st.enter_context(nc.psum_tensor("ps_ms", [128, 3, 512], F32))
        ps_mm = st.enter_context(nc.psum_tensor("ps_mm", [128, 5, 512], F32))
        P = Prog(nc)
        t_ones = P.dve(lambda e: e.memset(ones[:], 1.0 / D))
        xr = []
        xv = xT.rearrange("(c p) t -> p c t", p=128)
        for c4 in range(4):
            t = P.dma(("sync", "scalar", "gpsimd", "sync")[c4], X[:, c4 * 4:(c4 + 1) * 4, :], xv[:, c4 * 4:(c4 + 1) * 4, :], f"ld_x{c4}")
            xr += [t] * 4
        t1 = P.dma("gpsimd", sc[:], scd, "ld_sc")
        t2 = P.dma("gpsimd", sh[:], shd, "ld_sh")
        t3 = P.dma("gpsimd", g[:], gd, "ld_g")
        h_ready = emit_norm_mod(P, nc, X, hT, sc, sh, g[:], ones, ps_ms, sq, rstd, tmp, s1, xr, [t1, t2, t3, t_ones])

        ot_free = [None] * 3
        state = {"n": 0, "evs": []}
        out_toks = []

        def evac(nchunk, m, bi, s0, w, ps, mm):
            b = nchunk % 3
            if bi == 1:
                tk = P.act(lambda e: e.copy(out=ot[b][:m, s0:s0 + w], in_=ps[:m, :w]), [mm, ot_free[b]])
            else:
                tk = P.dve(lambda e: e.tensor_copy(out=ot[b][:m, s0:s0 + w], in_=ps[:m, :w]), [mm, ot_free[b]])
            state["evs"].append(tk)
            if bi == 2:
                d = P.dma("scalar", pT[nchunk * 128:nchunk * 128 + m, :], ot[b][:m, :], f"st_p{b}", state["evs"][-3:])
                ot_free[b] = d
                out_toks.append(d)
            return tk

        emit_proj(P, nc, wd, DIN, hT, h_ready, wb, [ps_mm[:, i, :] for i in range(5)], evac, "w")
        P.finish(out_toks[-3:])
    return nc


class Res:
    __slots__ = ("w", "r")

    def __init__(self):
        self.w = None
        self.r = {}


class TP(Prog):
    def _deps(self, reads, writes):
        waits = []
        for r in reads:
            waits.append(r.w)
        for w in writes:
            waits.append(w.w)
            waits += list(w.r.items())
        return waits

    def _mark(self, tok, reads, writes):
        for r in reads:
            s, v = tok
            if r.r.get(s, 0) < v:
                r.r[s] = v
        for w in writes:
            w.w = tok
            w.r = {}

    def op(self, eng, fn, reads=(), writes=()):
        sem = {"tensor": "s_pe", "scalar": "s_act", "vector": "s_dve", "gpsimd": "s_pool"}[eng]
        tok = self.emit(eng, fn, self._deps(reads, writes), sem)
        self._mark(tok, reads, writes)
        return tok

    def group(self, fns, reads=(), writes=()):
        waits = self._deps(reads, writes)
        tok = None
        for i, fn in enumerate(fns):
            tok = self.emit("tensor", fn, waits if i == 0 else (), "s_pe" if i == len(fns) - 1 else None)
        self._mark(tok, reads, writes)
        return tok

    def xdma(self, q, out, in_, sem, reads=(), writes=()):
        tok = self.emit(q, lambda e: e.dma_start(out=out, in_=in_), self._deps(reads, writes), sem, 16)
        self._mark(tok, reads, writes)
        return tok


NCH = 66
NTOK = NCH * 128


def build_ssd():
    nc = bass.Bass("TRN2", target_bir_lowering=False)
    uT = nc.dram_tensor("uT", [448, NTOK], F32, kind="ExternalInput").ap()
    dtr = nc.dram_tensor("dtr", [128, 2, NCH, 3], F32, kind="ExternalInput").ap()
    cwd = nc.dram_tensor("cw", [128, 4, 5], F32, kind="ExternalInput").ap()
    cbd = nc.dram_tensor("cb", [128, 4], F32, kind="ExternalInput").ap()
    smd = nc.dram_tensor("sm", [128, 15], F32, kind="ExternalInput").ap()
    cst = nc.dram_tensor("cst", [128, 8, 128], F32, kind="ExternalInput").ap()
    Y = nc.dram_tensor("Y", [2, NTOK, 192], F32, kind="ExternalOutput").ap()
    with contextlib.ExitStack() as st:
        sb = lambda n, s, d: st.enter_context(nc.sbuf_tensor(n, s, d))
        BT = sb("BT", [128, NTOK], BF16)
        CT = sb("CT", [128, NTOK], BF16)
        Btm = sb("Btm", [128, NCH, 128], BF16)
        Xtm = sb("Xtm", [128, NCH, 192], F32)
        cw = sb("cw_s", [128, 4, 5], F32)
        cb = sb("cb_s", [128, 4], F32)
        sm = sb("sm_s", [128, 15], F32)
        K8 = sb("K8", [128, 8, 128], F32)
        identb = sb("identb", [128, 128], BF16)
        dts = {n: sb(n, [128, 2, NCH, 3], F32) for n in
               ("raw", "t1", "dtv", "dta", "acum", "tot", "sdec", "ea", "dec", "dtsd")}
        avec = sb("avec", [128, 6], F32)
        PIECE = 1024
        raw = [sb(f"rawb{i}", [128, PIECE], F32) for i in range(2)]
        acc = [sb(f"accb{i}", [128, PIECE], F32) for i in range(2)]
        sil = [sb(f"silb{i}", [128, PIECE], F32) for i in range(2)]
        Rb = [sb(f"R{i}", [128, 2, 3, 128], BF16) for i in range(2)]
        K8b = sb("K8b", [128, 4, 128], BF16)
        dth = sb("dth", [128, 2, NCH, 3], BF16)
        dthf = sb("dthf", [128, 2, NCH, 3], F32)
        dtl = sb("dtl", [128, 2, NCH, 3], F32)
        Eb = [sb(f"E{i}", [128, 3, 128], F32) for i in range(2)]
        Mb = [sb(f"M{i}", [128, 3, 128], BF16) for i in range(2)]
        rtmp = [sb(f"rtmp{i}", [128, 192], F32) for i in range(2)]
        CBm = [sb(f"CBm{i}", [128, 128], F32) for i in range(2)]
        xdt2 = [[sb(f"xdt{i}_{j}", [128, 192], BF16) for j in range(2)] for i in range(2)]
        xs2 = [[sb(f"xs{i}_{j}", [128, 192], BF16) for j in range(2)] for i in range(2)]
        xd2 = [sb(f"xd_{j}", [128, 192], BF16) for j in range(2)]
        hst = [sb(f"hst{i}", [128, 192], F32) for i in range(2)]
        hbf = [[sb(f"hbf{i}_{j}", [128, 192], BF16) for j in range(2)] for i in range(2)]
        yo_t = [sb(f"yot{i}", [128, 192], F32) for i in range(2)]
        yo = [sb(f"yo{i}", [128, 192], F32) for i in range(4)]
        ps_seg = [st.enter_context(nc.psum_tensor(f"ps_seg{i}", [128, 512], F32)) for i in range(2)]
        ps_cb = st.enter_context(nc.psum_tensor("ps_cb", [128, 512], F32))
        ps_y = [st.enter_context(nc.psum_tensor(f"ps_y{i}", [128, 512], F32)) for i in range(2)]
        ps_st = st.enter_context(nc.psum_tensor("ps_st", [128, 512], F32))
        ps_tr = [st.enter_context(nc.psum_tensor(f"ps_tr{i}", [128, 512], F32)) for i in range(2)]

        P = TP(nc)
        R = {}

        def res(name):
            if name not in R:
                R[name] = Res()
            return R[name]

        Tm, Um, SU, SL, MF, MB, ID, ON = [K8[:, i, :] for i in range(8)]
        P.xdma("sync", K8[:], cst, "ld_k8", writes=[res("K8")])
        P.xdma("sync", cw[:], cwd, "ld_cw", writes=[res("cw")])
        P.xdma("sync", cb[:], cbd, "ld_cb", writes=[res("cb")])
        P.xdma("sync", sm[:], smd, "ld_sm", writes=[res("sm")])
        P.xdma("sync", dts["raw"][:], dtr, "ld_dt", writes=[res("raw")])
        P.op("vector", lambda e: e.tensor_copy(out=identb[:], in_=ID), [res("K8")], [res("identb")])
        P.op("vector", lambda e: e.tensor_copy(out=K8b[:], in_=K8[:, 0:4, :]), [res("K8")], [res("K8b")])

        fl = lambda n: dts[n][:].rearrange("p a c j -> p (a c j)")
        col = lambda n, d, j: dts[n][:, d, :, j]
        for d in range(2):
            for j in range(3):
                P.op("vector", lambda e, d=d, j=j: e.tensor_scalar(
                    out=col("raw", d, j), in0=col("raw", d, j), scalar1=sm[:, d * 3 + j:d * 3 + j + 1], scalar2=None,
                    op0=ALU.add), [res("raw"), res("sm")], [res("raw")])
        P.op("scalar", lambda e: e.activation(out=fl("t1"), in_=fl("raw"), func=AF.Abs), [res("raw")], [res("t1")])
        P.op("scalar", lambda e: e.activation(out=fl("t1"), in_=fl("t1"), func=AF.Exp, scale=-1.0), [res("t1")], [res("t1")])
        P.op("vector", lambda e: e.tensor_scalar_add(out=fl("t1"), in0=fl("t1"), scalar1=1.0), [res("t1")], [res("t1")])
        P.op("scalar", lambda e: e.activation(out=fl("t1"), in_=fl("t1"), func=AF.Ln), [res("t1")], [res("t1")])
        P.op("vector", lambda e: e.scalar_tensor_tensor(out=fl("dtv"), in0=fl("raw"), scalar=0.0, in1=fl("t1"),
                                                         op0=ALU.max, op1=ALU.add), [res("raw"), res("t1")], [res("dtv")])
        P.op("scalar", lambda e: e.activation(out=avec[:], in_=sm[:, 6:12], func=AF.Exp), [res("sm")], [res("avec")])
        P.op("vector", lambda e: e.tensor_scalar_mul(out=avec[:], in0=avec[:], scalar1=-1.0), [res("avec")], [res("avec")])
        for d in range(2):
            for j in range(3):
                P.op("vector", lambda e, d=d, j=j: e.tensor_scalar(
                    out=col("dta", d, j), in0=col("dtv", d, j), scalar1=avec[:, d * 3 + j:d * 3 + j + 1], scalar2=None,
                    op0=ALU.mult), [res("dtv"), res("avec")], [res("dta")])
        fl2 = lambda t: t[:].rearrange("p a c j -> p (a c j)")
        P.op("vector", lambda e: e.tensor_copy(out=fl2(dth), in_=fl("dta")), [res("dta")], [res("dth")])
        P.op("vector", lambda e: e.tensor_copy(out=fl2(dthf), in_=fl2(dth)), [res("dth")], [res("dthf")])
        P.op("vector", lambda e: e.tensor_tensor(out=fl2(dtl), in0=fl("dta"), in1=fl2(dthf), op=ALU.subtract), [res("dta"), res("dthf")], [res("dtl")])
        for d in range(2):
            lhs = Tm if d == 0 else Um
            P.group([lambda e, d=d, lhs=lhs: e.matmul(ps_tr[0][:, d * 256:d * 256 + 198], lhsT=lhs,
                                                       rhs=dts["dta"][:, d].rearrange("p c j -> p (c j)"),
                                                       start=True, stop=True)],
                    [res("K8"), res("dta")], [res("ps_tr0")])
            P.group([lambda e, d=d: e.matmul(ps_tr[1][:, d * 256:d * 256 + 198], lhsT=ON,
                                             rhs=dts["dta"][:, d].rearrange("p c j -> p (c j)"),
                                             start=True, stop=True)],
                    [res("K8"), res("dta")], [res("ps_tr1")])
        for d in range(2):
            P.op("vector", lambda e, d=d: e.tensor_copy(out=dts["acum"][:, d].rearrange("p c j -> p (c j)"),
                                                       in_=ps_tr[0][:, d * 256:d * 256 + 198]),
                 [res("ps_tr0")], [res("acum")])
            P.op("vector", lambda e, d=d: e.tensor_copy(out=dts["tot"][:, d].rearrange("p c j -> p (c j)"),
                                                       in_=ps_tr[1][:, d * 256:d * 256 + 198]),
                 [res("ps_tr1")], [res("tot")])
        P.op("vector", lambda e: e.tensor_tensor(out=fl("sdec"), in0=fl("tot"), in1=fl("acum"), op=ALU.subtract),
             [res("tot"), res("acum")], [res("sdec")])
        P.op("scalar", lambda e: e.activation(out=fl("sdec"), in_=fl("sdec"), func=AF.Exp), [res("sdec")], [res("sdec")])
        P.op("scalar", lambda e: e.activation(out=fl("ea"), in_=fl("acum"), func=AF.Exp), [res("acum")], [res("ea")])
        P.op("scalar", lambda e: e.activation(out=fl("dec"), in_=fl("tot"), func=AF.Exp), [res("tot")], [res("dec")])
        P.op("vector", lambda e: e.tensor_tensor(out=fl("dtsd"), in0=fl("dtv"), in1=fl("sdec"), op=ALU.mult),
             [res("dtv"), res("sdec")], [res("dtsd")])

        tiles = [(0, 128), (128, 64), (192, 128), (320, 128)]
        pieces = [(0, 256, 256)] + [(256 + i * 1024, 1024, 64) for i in range(8)]
        n = 0
        ntr = 0
        for (t0, nt, rl) in pieces:
            for ti, (r0, npart) in enumerate(tiles):
                b = n % 2
                n += 1
                rw, ac, so = raw[b], acc[b], sil[b]
                rr, ra, rs = res(f"raw{b}"), res(f"acc{b}"), res(f"sil{b}")
                P.xdma("sync" if n % 2 else "gpsimd", rw[:npart, :nt], uT[r0:r0 + npart, t0:t0 + nt], f"ld_raw{b}", writes=[rr])
                v = lambda a, npart=npart, nt=nt, rl=rl: a[:npart, :nt].rearrange("p (r t) -> p r t", t=rl)
                P.op("vector", lambda e, ti=ti, rw=rw, ac=ac, npart=npart, nt=nt: e.tensor_scalar(
                    out=ac[:npart, :nt], in0=rw[:npart, :nt], scalar1=cw[:npart, ti, 2:3], scalar2=None, op0=ALU.mult),
                    [rr, res("cw")], [ra])
                for j in (0, 1, 3, 4):
                    sh_ = j - 2
                    a0, a1 = max(0, -sh_), min(rl, rl - sh_)
                    P.op("vector", lambda e, ti=ti, j=j, v=v, rw=rw, ac=ac, a0=a0, a1=a1, sh_=sh_, npart=npart: e.scalar_tensor_tensor(
                        out=v(ac)[:, :, a0:a1], in0=v(rw)[:, :, a0 + sh_:a1 + sh_], scalar=cw[:npart, ti, j:j + 1],
                        in1=v(ac)[:, :, a0:a1], op0=ALU.mult, op1=ALU.add), [rr, ra, res("cw")], [ra])
                P.op("scalar", lambda e, ti=ti, ac=ac, so=so, npart=npart, nt=nt: e.activation(
                    out=so[:npart, :nt], in_=ac[:npart, :nt], func=AF.Silu, bias=cb[:npart, ti:ti + 1], scale=1.0),
                    [ra, res("cb")], [rs])
                if ti == 2:
                    P.op("vector", lambda e, so=so, nt=nt, t0=t0: e.tensor_copy(out=BT[:, t0:t0 + nt], in_=so[:, :nt]), [rs], [res("BT")])
                if ti == 3:
                    P.op("vector", lambda e, so=so, nt=nt, t0=t0: e.tensor_copy(out=CT[:, t0:t0 + nt], in_=so[:, :nt]), [rs], [res("CT")])
                    continue
                for cc in range(nt // 128):
                    ch = t0 // 128 + cc
                    pt = ntr % 2
                    ntr += 1
                    rp = res(f"ps_tr{pt}")
                    P.group([lambda e, so=so, cc=cc, npart=npart, pt=pt: e.transpose(
                        ps_tr[pt][:, :npart], so[:npart, cc * 128:(cc + 1) * 128], K8[:npart, 6, :npart])],
                        [rs, res("K8")], [rp])
                    if ti == 2:
                        P.op("scalar", lambda e, ch=ch, pt=pt: e.copy(out=Btm[:, ch, :], in_=ps_tr[pt][:, :128]), [rp], [res("Btm")])
                    else:
                        c0 = 0 if ti == 0 else 128
                        P.op("vector" if ti == 0 else "scalar",
                             (lambda e, ch=ch, pt=pt, c0=c0, npart=npart: e.tensor_copy(out=Xtm[:, ch, c0:c0 + npart], in_=ps_tr[pt][:, :npart]))
                             if ti == 0 else
                             (lambda e, ch=ch, pt=pt, c0=c0, npart=npart: e.copy(out=Xtm[:, ch, c0:c0 + npart], in_=ps_tr[pt][:, :npart])),
                             [rp], [res("Xtm")])

        for i in range(2):
            P.op("vector", lambda e, i=i: e.memset(hst[i][:], 0.0), [], [res(f"hst{i}")])
            P.op("vector", lambda e, i=i: e.memset(hbf[i][0][:], 0.0), [], [res(f"hbf{i}_0")])
        border = [1, 0] + list(range(65, 1, -1))
        dvec = sm[:, 12:15]
        out_tok = []
        h3 = lambda ap: ap.rearrange("p (h c) -> p h c", h=3)
        stb = [ps_st, ps_tr[1]]
        stn = ["ps_st0", "ps_tr1"]
        cbb = [ps_cb, ps_tr[0]]
        cbn = ["ps_cb", "ps_tr0"]

        def unit(u):
            step, d = u // 2, u % 2
            ch = step if d == 0 else border[step]
            return step, d, ch

        def front(u):
            front_a(u)
            front_b(u)

        def front_a(u):
            cbpart(u)
            prep(u)

        def cbpart(u):
            step, d, ch = unit(u)
            cs = slice(ch * 128, (ch + 1) * 128)
            P.group([lambda e, cs=cs, d=d: e.matmul(cbb[d][:, :128], lhsT=BT[:, cs], rhs=CT[:, cs], start=True, stop=True)],
                    [res("BT"), res("CT")], [res(cbn[d])])
            P.op("vector", lambda e, d=d: e.tensor_tensor(out=CBm[d][:], in0=cbb[d][:, :128], in1=(MF if d == 0 else MB), op=ALU.mult),
                 [res(cbn[d]), res("K8")], [res(f"CBm{d}")])

        def prep(u):
            step, d, ch = unit(u)
            par = step % 2
            xdt_, xs_, xd_ = xdt2[d][par], xs2[d][par], xd2[par]
            P.op("vector", lambda e, d=d, ch=ch, xdt_=xdt_: e.tensor_tensor(
                out=h3(xdt_[:]), in0=h3(Xtm[:, ch, :]), in1=dts["dtv"][:, d, ch, 0:3].unsqueeze(2).to_broadcast([128, 3, 64]), op=ALU.mult),
                [res("Xtm"), res("dtv")], [res(f"xdt{d}_{par}")])
            P.op("vector", lambda e, d=d, ch=ch, xs_=xs_: e.tensor_tensor(
                out=h3(xs_[:]), in0=h3(Xtm[:, ch, :]), in1=dts["dtsd"][:, d, ch, 0:3].unsqueeze(2).to_broadcast([128, 3, 64]), op=ALU.mult),
                [res("Xtm"), res("dtsd")], [res(f"xs{d}_{par}")])
            if d == 0:
                P.op("vector", lambda e, ch=ch, xd_=xd_: e.tensor_tensor(
                    out=h3(xd_[:]), in0=h3(Xtm[:, ch, :]), in1=dvec.unsqueeze(2).to_broadcast([128, 3, 64]), op=ALU.mult),
                    [res("Xtm"), res("sm")], [res(f"xd_{par}")])
            Tsel = K8[:, (0 if d == 0 else 1), :]
            P.op("vector", lambda e, d=d, ch=ch, Tsel=Tsel: e.tensor_tensor(
                out=Rb[d][:, 0], in0=Tsel.unsqueeze(1).to_broadcast([128, 3, 128]),
                in1=dthf[:, d, ch, 0:3].unsqueeze(2).to_broadcast([128, 3, 128]), op=ALU.mult),
                [res("K8"), res("dthf")], [res(f"R{d}")])
            for h in range(3):
                P.op("scalar", lambda e, d=d, ch=ch, h=h, Tsel=Tsel: e.activation(
                    out=Rb[d][:, 1, h, :], in_=Tsel, func=AF.Copy, scale=dtl[:, d, ch, h:h + 1]),
                    [res("K8"), res("dtl")], [res(f"R{d}")])
        def front_b(u):
            step, d, ch = unit(u)
            Ssel = K8b[:, (2 if d == 0 else 3), :]
            P.group([lambda e, d=d, h=h, t=t, Ssel=Ssel: e.matmul(ps_seg[d][:, h * 128:(h + 1) * 128], lhsT=Ssel,
                                                  rhs=Rb[d][:, t, h, :], start=(t == 0), stop=(t == 1)) for h in range(3) for t in range(2)],
                    [res("K8b"), res(f"R{d}")], [res(f"ps_seg{d}")])
            P.op("scalar", lambda e, d=d: e.activation(out=Eb[d][:].rearrange("p h l -> p (h l)"), in_=ps_seg[d][:, 0:384], func=AF.Exp),
                 [res(f"ps_seg{d}")], [res(f"E{d}")])
            P.op("vector", lambda e, d=d: e.tensor_tensor(out=Mb[d][:], in0=Eb[d][:], in1=CBm[d][:].unsqueeze(1).to_broadcast([128, 3, 128]), op=ALU.mult),
                 [res(f"E{d}"), res(f"CBm{d}")], [res(f"M{d}")])

        def back(u):
            back_pe(u)
            back_rest(u)

        def back_pe(u):
            step, d, ch = unit(u)
            cs = slice(ch * 128, (ch + 1) * 128)
            hp = step % 2
            par = step % 2
            xdt_, xs_, xd_ = xdt2[d][par], xs2[d][par], xd2[par]
            fns = []
            for h in range(3):
                hs = slice(h * 64, (h + 1) * 64)
                fns.append(lambda e, d=d, h=h, hs=hs, xdt_=xdt_: e.matmul(ps_y[d][:, hs], lhsT=Mb[d][:, h, :], rhs=xdt_[:, hs],
                                                               start=True, stop=(d == 1)))
                if d == 0:
                    fns.append(lambda e, hs=hs, xd_=xd_: e.matmul(ps_y[0][:, hs], lhsT=identb[:], rhs=xd_[:, hs], start=False, stop=True))
            fns.append(lambda e, d=d, cs=cs, hp=hp: e.matmul(ps_y[d][:, 256:448], lhsT=CT[:, cs], rhs=hbf[d][hp][:], start=True, stop=True))
            fns.append(lambda e, d=d, ch=ch, xs_=xs_: e.matmul(stb[d][:, 0:192], lhsT=Btm[:, ch, :], rhs=xs_[:], start=True, stop=True))
            P.group(fns, [res(f"M{d}"), res(f"xdt{d}_{par}"), res("identb"), res(f"xd_{par}"), res("CT"), res(f"hbf{d}_{hp}"), res("Btm"), res(f"xs{d}_{par}")],
                    [res(f"ps_y{d}"), res(stn[d])])
        def back_rest(u):
            step, d, ch = unit(u)
            hp = step % 2
            P.op("vector", lambda e, d=d, ch=ch: e.tensor_tensor(
                out=h3(rtmp[d][:]), in0=h3(hst[d][:]), in1=dts["dec"][:, d, ch, 0:3].unsqueeze(2).to_broadcast([128, 3, 64]), op=ALU.mult),
                [res(f"hst{d}"), res("dec")], [res(f"rtmp{d}")])
            P.op("vector", lambda e, d=d: e.tensor_tensor(out=hst[d][:], in0=rtmp[d][:], in1=stb[d][:, 0:192], op=ALU.add),
                 [res(f"rtmp{d}"), res(stn[d])], [res(f"hst{d}")])
            P.op("scalar", lambda e, d=d, hp=hp: e.copy(out=hbf[d][1 - hp][:], in_=hst[d][:]),
                 [res(f"hst{d}")], [res(f"hbf{d}_{1 - hp}")])
            P.op("vector", lambda e, d=d, ch=ch: e.tensor_tensor(
                out=h3(yo_t[d][:]), in0=h3(ps_y[d][:, 256:448]), in1=dts["ea"][:, d, ch, 0:3].unsqueeze(2).to_broadcast([128, 3, 64]), op=ALU.mult),
                [res(f"ps_y{d}"), res("ea")], [res(f"yot{d}")])
            yb = u % 4
            P.op("vector", lambda e, d=d, yb=yb: e.tensor_tensor(out=yo[yb][:], in0=ps_y[d][:, 0:192], in1=yo_t[d][:], op=ALU.add),
                 [res(f"ps_y{d}"), res(f"yot{d}")], [res(f"yo{yb}")])
            out_tok.append(P.xdma("sync", Y[d, ch * 128:(ch + 1) * 128, :], yo[yb][:], f"st_y{yb}", reads=[res(f"yo{yb}")]))

        NU = 2 * NCH
        import os
        mode = os.environ.get("SSD_ORDER", "pair")
        if mode == "plain":
            for u in range(NU):
                front(u)
                back(u)
        elif mode == "prep":
            prep(0)
            for u in range(NU):
                cbpart(u)
                front_b(u)
                if u + 1 < NU:
                    prep(u + 1)
                back(u)
        elif mode == "pair":
            prep(0)
            prep(1)
            for st_ in range(NCH):
                a, b = 2 * st_, 2 * st_ + 1
                cbpart(a)
                cbpart(b)
                front_b(a)
                front_b(b)
                if st_ + 1 < NCH:
                    prep(a + 2)
                    prep(b + 2)
                back(a)
                back(b)
        elif mode == "pipe":
            prep(0)
            cbpart(0)
            front_b(0)
            for u in range(NU):
                if u + 1 < NU:
                    prep(u + 1)
                    cbpart(u + 1)
                    front_b(u + 1)
                back(u)
        elif mode == "late":
            prep(0)
            cbpart(0)
            front_b(0)
            for u in range(NU):
                if u + 1 < NU:
                    prep(u + 1)
                back_pe(u)
                if u + 1 < NU:
                    cbpart(u + 1)
                    front_b(u + 1)
                back_rest(u)
        elif mode == "split":
            front_a(0)
            front_b(0)
            for u in range(NU):
                if u + 1 < NU:
                    front_a(u + 1)
                back(u)
                if u + 1 < NU:
                    front_b(u + 1)
        P.finish(out_tok[-4:])
    return nc


def ssd_consts():
    k = np.arange(128)[:, None]
    l = np.arange(128)[None, :]
    mats = [k <= l, k >= l, k > l, k < l, l >= k, l <= k, k == l, np.ones((128, 128), bool)]
    return np.ascontiguousarray(np.stack([m.astype(np.float32) for m in mats], 1))


def ssd_inputs(PT, conv_w, conv_b, dt_bias, a_log, d_skip):
    cst = ssd_consts()
    maps = []
    for core in range(NCORES):
        g, half = core // 2, core % 2
        hg0 = g * 6 + half * 3
        xc = 2048 + g * 384 + half * 192
        bc = 2048 + 1536 + g * 128
        cc = 2048 + 2048 + g * 128
        uT = np.concatenate([PT[xc:xc + 192], PT[bc:bc + 128], PT[cc:cc + 128]], 0)
        dt = np.stack([PT[4608 + d * 24 + hg0:4608 + d * 24 + hg0 + 3] for d in range(2)], 0)
        dt = dt.reshape(2, 3, NCH, 128).transpose(3, 0, 2, 1)
        cols = [np.arange(xc - 2048, xc - 2048 + 128), np.arange(xc - 2048 + 128, xc - 2048 + 192),
                np.arange(bc - 2048, bc - 2048 + 128), np.arange(cc - 2048, cc - 2048 + 128)]
        cw = np.zeros((128, 4, 5), np.float32)
        cb = np.zeros((128, 4), np.float32)
        for ti, cidx in enumerate(cols):
            cw[:len(cidx), ti, :] = conv_w[:, cidx].T
            cb[:len(cidx), ti] = conv_b[cidx]
        sm = np.concatenate([dt_bias[:, hg0:hg0 + 3].reshape(-1), a_log[:, hg0:hg0 + 3].reshape(-1), d_skip[hg0:hg0 + 3]])
        sm = np.broadcast_to(sm[None, :], (128, 15))
        maps.append({"uT": np.ascontiguousarray(uT), "dtr": np.ascontiguousarray(dt), "cw": cw, "cb": cb,
                     "sm": np.ascontiguousarray(sm, dtype=np.float32), "cst": cst})
    return maps


def build_fourier():
    nc = bass.Bass("TRN2", target_bir_lowering=False)
    uL = nc.dram_tensor("uL", [SEQ, 512], F32, kind="ExternalInput").ap()
    uC = nc.dram_tensor("uC", [CTX, 512], F32, kind="ExternalInput").ap()
    tabC = nc.dram_tensor("tabC", [SEQ, 1024], BF16, kind="ExternalInput").ap()
    tabS = nc.dram_tensor("tabS", [SEQ, 1024], BF16, kind="ExternalInput").ap()
    tcC = nc.dram_tensor("tcC", [CTX, 32], BF16, kind="ExternalInput").ap()
    tcS = nc.dram_tensor("tcS", [CTX, 32], BF16, kind="ExternalInput").ap()
    wfd = nc.dram_tensor("wf", [4, 128, 128], F32, kind="ExternalInput").ap()
    ccd = nc.dram_tensor("cc", [128, 2, 128], F32, kind="ExternalInput").ap()
    fT = nc.dram_tensor("fT", [512, TOK], F32, kind="ExternalOutput").ap()
    with contextlib.ExitStack() as st:
        sb = lambda n, s, d: st.enter_context(nc.sbuf_tensor(n, s, d))
        NB = 3
        ust = [sb(f"ust{i}", [128, 4, 512], F32) for i in range(NB)]
        ubig = sb("ubig", [128, 64, 512], BF16)
        tC = [sb(f"tC{i}", [128, 4, 512], BF16) for i in range(NB)]
        tS = [sb(f"tS{i}", [128, 4, 512], BF16) for i in range(NB)]
        wf = sb("wf_s", [128, 4, 128], F32)
        cc = sb("cc_s", [128, 2, 128], F32)
        Mm = sb("Mm", [128, 4, 2, 128], BF16)
        AB = sb("AB", [128, 4, 2, 512], BF16)
        fo = sb("fo", [128, 4, TOK], F32)
        ucs = sb("ucs", [128, 2, 512], F32)
        ucb = sb("ucb", [128, 2, 512], BF16)
        tcc = sb("tcc", [128, 2, 2, 32], BF16)
        bank = [st.enter_context(nc.psum_tensor(f"bank{i}", [128, 512], F32)) for i in range(8)]
        P = TP(nc)
        R = {}

        def res(name):
            if name not in R:
                R[name] = Res()
            return R[name]

        P.xdma("sync", wf[:], wfd.rearrange("g j d -> j g d"), "ld_wf", writes=[res("wf")])
        P.xdma("sync", cc[:], ccd, "ld_cc", writes=[res("cc")])
        for g in range(4):
            for t in range(2):
                P.group([lambda e, g=g, t=t: e.matmul(bank[g][:, t * 128:(t + 1) * 128], lhsT=cc[:, t, :], rhs=wf[:, g, :],
                                                      start=True, stop=True)], [res("wf"), res("cc")], [res(f"bank{g}")])
            P.op("vector", lambda e, g=g: e.tensor_copy(out=Mm[:, g].rearrange("p t d -> p (t d)"), in_=bank[g][:, 0:256]),
                 [res(f"bank{g}")], [res("Mm")])

        def stage2(width, c0, ABv):
            for g in range(4):
                P.group([lambda e, g=g, t=t: e.matmul(bank[g][:, :width], lhsT=Mm[:, g, t, :], rhs=ABv(g, t),
                                                      start=(t == 0), stop=(t == 1)) for t in range(2)],
                        [res("Mm"), res("AB")], [res(f"bank{g}")])
                if g % 2 == 0:
                    P.op("vector", lambda e, g=g: e.tensor_copy(out=fo[:, g, c0:c0 + width], in_=bank[g][:, :width]),
                         [res(f"bank{g}")], [res("fo")])
                else:
                    P.op("scalar", lambda e, g=g: e.copy(out=fo[:, g, c0:c0 + width], in_=bank[g][:, :width]),
                         [res(f"bank{g}")], [res("fo")])

        uv = uL.rearrange("(c p) f -> p c f", p=128)
        cv = tabC.rearrange("(c p) k -> p c k", p=128)
        sv = tabS.rearrange("(c p) k -> p c k", p=128)
        n = 0
        for half in range(2):
            ks = slice(half * 512, (half + 1) * 512)
            for nb in range(16):
                b = n % NB
                n += 1
                P.xdma("sync" if half == 1 else "gpsimd", tC[b][:], cv[:, nb * 4:(nb + 1) * 4, ks], f"ld_tc{b}", writes=[res(f"tC{b}")])
                P.xdma("gpsimd", tS[b][:], sv[:, nb * 4:(nb + 1) * 4, ks], f"ld_ts{b}", writes=[res(f"tS{b}")])
                if half == 0:
                    P.xdma("sync", ust[b][:], uv[:, nb * 4:(nb + 1) * 4, :], f"ld_u{b}", writes=[res(f"ust{b}")])
                    if nb % 2 == 0:
                        P.op("vector", lambda e, b=b, nb=nb: e.tensor_copy(out=ubig[:, nb * 4:(nb + 1) * 4, :], in_=ust[b][:]), [res(f"ust{b}")], [res(f"ubig{nb}")])
                    else:
                        P.op("scalar", lambda e, b=b, nb=nb: e.copy(out=ubig[:, nb * 4:(nb + 1) * 4, :], in_=ust[b][:]), [res(f"ust{b}")], [res(f"ubig{nb}")])
                fns = []
                for ci in range(4):
                    first = (nb == 0 and ci == 0)
                    last = (nb == 15 and ci == 3)
                    for g in range(4):
                        for t in range(2):
                            tb = tC[b] if t == 0 else tS[b]
                            fns.append(lambda e, nb=nb, ci=ci, g=g, t=t, tb=tb, first=first, last=last: e.matmul(
                                bank[g * 2 + t][:, :], lhsT=ubig[:, nb * 4 + ci, g * 128:(g + 1) * 128], rhs=tb[:, ci, :],
                                start=first, stop=last))
                P.group(fns, [res(f"ubig{nb}"), res(f"tC{b}"), res(f"tS{b}")], [res(f"bank{i}") for i in range(8)])
            for g in range(4):
                for t in range(2):
                    if t == 0:
                        P.op("vector", lambda e, g=g, t=t: e.tensor_copy(out=AB[:, g, t, :], in_=bank[g * 2 + t][:, :]),
                             [res(f"bank{g * 2 + t}")], [res("AB")])
                    else:
                        P.op("scalar", lambda e, g=g, t=t: e.copy(out=AB[:, g, t, :], in_=bank[g * 2 + t][:, :]),
                             [res(f"bank{g * 2 + t}")], [res("AB")])
            stage2(512, half * 512, lambda g, t: AB[:, g, t, :])
        P.xdma("sync", ucs[:], uC.rearrange("(c p) f -> p c f", p=128), "ld_uc", writes=[res("ucs")])
        P.xdma("sync", tcc[:, 0], tcC.rearrange("(c p) k -> p c k", p=128), "ld_tcc", writes=[res("tcc")])
        P.xdma("sync", tcc[:, 1], tcS.rearrange("(c p) k -> p c k", p=128), "ld_tcs", writes=[res("tcc")])
        P.op("vector", lambda e: e.tensor_copy(out=ucb[:], in_=ucs[:]), [res("ucs")], [res("ucb")])
        for g in range(4):
            for t in range(2):
                P.group([lambda e, g=g, t=t, ci=ci: e.matmul(bank[g * 2 + t][:, :32], lhsT=ucb[:, ci, g * 128:(g + 1) * 128],
                                                            rhs=tcc[:, t, ci, :], start=(ci == 0), stop=(ci == 1)) for ci in range(2)],
                        [res("ucb"), res("tcc")], [res(f"bank{g * 2 + t}")])
                P.op("vector", lambda e, g=g, t=t: e.tensor_copy(out=AB[:, g, t, :32], in_=bank[g * 2 + t][:, :32]),
                     [res(f"bank{g * 2 + t}")], [res("AB")])
        stage2(32, 1024, lambda g, t: AB[:, g, t, :32])
        t_o = P.xdma("sync", fT.rearrange("(g p) t -> p g t", p=128), fo[:], "st_f", reads=[res("fo")])
        P.finish([t_o])
    return nc


def fourier_tables():
    n = np.arange(SEQ, dtype=np.int64)
    ph = (np.outer(n, n) % SEQ).astype(np.float64) * (2 * np.pi / SEQ)
    sc = 1.0 / np.sqrt(SEQ * 128.0)
    C = (np.cos(ph) * sc).astype(NPBF)
    S = (np.sin(ph) * sc).astype(NPBF)
    m = np.arange(CTX, dtype=np.int64)
    phc = (np.outer(m, m) % CTX).astype(np.float64) * (2 * np.pi / CTX)
    scc = 1.0 / np.sqrt(CTX * 128.0)
    Cc_ = (np.cos(phc) * scc).astype(NPBF)
    Sc_ = (np.sin(phc) * scc).astype(NPBF)
    j = np.arange(128, dtype=np.int64)
    p128 = (np.outer(j, j) % 128).astype(np.float64) * (2 * np.pi / 128)
    cc = np.stack([np.cos(p128), -np.sin(p128)], 1).astype(np.float32)
    return C, S, Cc_, Sc_, np.ascontiguousarray(cc)


def build_G():
    nc = bass.Bass("TRN2", target_bir_lowering=False)
    xT = nc.dram_tensor("xT", [D, TOK], F32, kind="ExternalInput").ap()
    fTd = nc.dram_tensor("fT", [512, TOK], F32, kind="ExternalInput").ap()
    yfd = nc.dram_tensor("yf", [1536, TOK], F32, kind="ExternalInput").ap()
    ybd = nc.dram_tensor("yb", [1536, TOK], F32, kind="ExternalInput").ap()
    zd = nc.dram_tensor("zT", [1536, TOK], F32, kind="ExternalInput").ap()
    gnd = nc.dram_tensor("gn", [128, 12], F32, kind="ExternalInput").ap()
    gtd = nc.dram_tensor("gt", [128, 16, 2], F32, kind="ExternalInput").ap()
    wd = nc.dram_tensor("w", [D, D], BF16, kind="ExternalInput").ap()
    xo = nc.dram_tensor("xo", [D, TOK], F32, kind="ExternalOutput").ap()
    with contextlib.ExitStack() as st:
        sb = lambda n, s, d: st.enter_context(nc.sbuf_tensor(n, s, d))
        catT = sb("catT", [128, 16, TOK], BF16)
        gn = sb("gn_s", [128, 12], F32)
        gt = sb("gt_s", [128, 16, 2], F32)
        epsc = sb("epsc", [128, 1], F32)
        ones = sb("ones", [128, 128], BF16)
        fst = [sb(f"fst{i}", [128, TOK], F32) for i in range(2)]
        yfs = [sb(f"yfs{i}", [128, TOK], F32) for i in range(2)]
        ybs = [sb(f"ybs{i}", [128, TOK], F32) for i in range(2)]
        zs = [sb(f"zs{i}", [128, TOK], F32) for i in range(2)]
        uu = sb("uu", [128, 3, TOK], F32)
        sq = [sb(f"sq{i}", [128, TOK], BF16) for i in range(2)]
        rstd = sb("rstd", [128, TOK], F32)
        tmp = [sb(f"tmp{i}", [128, TOK], F32) for i in range(2)]
        wb = [sb(f"wb{i}", [128, 16, 512], BF16) for i in range(2)]
        xc = [sb(f"xc{i}", [128, TOK], F32) for i in range(3)]
        ps_ms = st.enter_context(nc.psum_tensor("ps_ms", [128, 3, 512], F32))
        ps_mm = st.enter_context(nc.psum_tensor("ps_mm", [128, 5, 512], F32))
        P = TP(nc)
        R = {}

        def res(name):
            if name not in R:
                R[name] = Res()
            return R[name]

        P.xdma("sync", gn[:], gnd, "ld_gn", writes=[res("gn")])
        P.xdma("sync", gt[:], gtd, "ld_gt", writes=[res("gt")])
        P.op("vector", lambda e: e.memset(ones[:], 1.0), [], [res("ones")])
        P.op("vector", lambda e: e.memset(epsc[:], EPS), [], [res("epsc")])
        for c in range(4):
            b = c % 2
            P.xdma("sync", fst[b][:], fTd[c * 128:(c + 1) * 128, :], f"ld_f{b}", writes=[res(f"fst{b}")])
            P.op("vector" if c % 2 == 0 else "scalar",
                 (lambda e, c=c, b=b: e.tensor_copy(out=catT[:, c, :], in_=fst[b][:])) if c % 2 == 0 else
                 (lambda e, c=c, b=b: e.copy(out=catT[:, c, :], in_=fst[b][:])),
                 [res(f"fst{b}")], [res("catT")])
        n = 0
        for grp in range(4):
            for c in range(3):
                ch = grp * 3 + c
                b = n % 2
                n += 1
                rows = slice(ch * 128, (ch + 1) * 128)
                P.xdma("sync", yfs[b][:], yfd[rows, :], f"ld_yf{b}", writes=[res(f"yfs{b}")])
                P.xdma("gpsimd", ybs[b][:], ybd[rows, :], f"ld_yb{b}", writes=[res(f"ybs{b}")])
                P.xdma("sync" if ch % 2 == 0 else "gpsimd", zs[b][:], zd[rows, :], f"ld_z{b}", writes=[res(f"zs{b}")])
                P.op("vector", lambda e, b=b: e.tensor_tensor(out=yfs[b][:], in0=yfs[b][:], in1=ybs[b][:], op=ALU.add),
                     [res(f"yfs{b}"), res(f"ybs{b}")], [res(f"yfs{b}")])
                P.op("scalar", lambda e, b=b: e.activation(out=zs[b][:], in_=zs[b][:], func=AF.Silu), [res(f"zs{b}")], [res(f"zs{b}")])
                P.op("vector", lambda e, b=b, c=c: e.tensor_tensor(out=uu[:, c, :], in0=yfs[b][:], in1=zs[b][:], op=ALU.mult),
                     [res(f"yfs{b}"), res(f"zs{b}")], [res(f"uu{c}")])
                P.op("scalar", lambda e, b=b, c=c: e.activation(out=sq[b][:], in_=uu[:, c, :], func=AF.Square),
                     [res(f"uu{c}")], [res(f"sq{b}")])
                P.group([lambda e, b=b, c=c, bi=bi, s0=s0, w=w: e.matmul(ps_ms[:, bi, :w], lhsT=ones[:], rhs=sq[b][:, s0:s0 + w],
                                                                        start=(c == 0), stop=(c == 2))
                         for bi, (s0, w) in enumerate(BLKS)], [res("ones"), res(f"sq{b}")], [res("ps_ms")])
            for bi, (s0, w) in enumerate(BLKS):
                P.op("scalar", lambda e, bi=bi, s0=s0, w=w: e.activation(out=rstd[:, s0:s0 + w], in_=ps_ms[:, bi, :w], func=AF.Sqrt,
                                                                         bias=epsc[:, 0:1], scale=1.0 / 384.0),
                     [res("ps_ms"), res("epsc")], [res("rstd")])
            P.op("vector", lambda e: e.reciprocal(out=rstd[:], in_=rstd[:]), [res("rstd")], [res("rstd")])
            for c in range(3):
                ch = grp * 3 + c
                b = c % 2
                P.op("vector", lambda e, b=b, c=c: e.tensor_tensor(out=tmp[b][:], in0=uu[:, c, :], in1=rstd[:], op=ALU.mult),
                     [res(f"uu{c}"), res("rstd")], [res(f"tmp{b}")])
                P.op("scalar", lambda e, b=b, ch=ch: e.activation(out=catT[:, 4 + ch, :], in_=tmp[b][:], func=AF.Copy, scale=gn[:, ch:ch + 1]),
                     [res(f"tmp{b}"), res("gn")], [res("catT")])
        outs = []
        for nb in range(4):
            b = nb % 2
            P.xdma("sync", wb[b][:], wd[:, nb * 512:(nb + 1) * 512].rearrange("(kc p) n -> p kc n", p=128), f"ld_w{b}", writes=[res(f"wb{b}")])
            for j in range(4):
                nch = nb * 4 + j
                xb = nch % 3
                P.xdma("gpsimd", xc[xb][:], xT[nch * 128:(nch + 1) * 128, :], f"ld_x{xb}", writes=[res(f"xc{xb}")])
                for bi, (s0, w) in enumerate(BLKS):
                    pb = (nch * 3 + bi) % 5
                    P.group([lambda e, b=b, j=j, k=k, s0=s0, w=w, pb=pb: e.matmul(
                        ps_mm[:, pb, :w], lhsT=wb[b][:, k, j * 128:(j + 1) * 128], rhs=catT[:, k, s0:s0 + w],
                        start=(k == 0), stop=(k == 15)) for k in range(16)],
                        [res(f"wb{b}"), res("catT")], [res(f"ps_mm{pb}")])
                    P.op("vector", lambda e, xb=xb, nch=nch, bi=bi, s0=s0, w=w, pb=pb: e.scalar_tensor_tensor(
                        out=xc[xb][:, s0:s0 + w], in0=ps_mm[:, pb, :w], scalar=gt[:, nch, (1 if bi == 2 else 0):(2 if bi == 2 else 1)],
                        in1=xc[xb][:, s0:s0 + w], op0=ALU.mult, op1=ALU.add),
                        [res(f"ps_mm{pb}"), res("gt"), res(f"xc{xb}")], [res(f"xc{xb}")])
                outs.append(P.xdma("sync", xo[nch * 128:(nch + 1) * 128, :], xc[xb][:], f"st_x{xb}", reads=[res(f"xc{xb}")]))
        P.finish(outs[-3:])
    return nc


def build_M():
    nc = bass.Bass("TRN2", target_bir_lowering=False)
    xT = nc.dram_tensor("xT", [D, TOK], F32, kind="ExternalInput").ap()
    scd = nc.dram_tensor("sc", [128, 16, 2], F32, kind="ExternalInput").ap()
    shd = nc.dram_tensor("sh", [128, 16, 2], F32, kind="ExternalInput").ap()
    gtd = nc.dram_tensor("gt", [128, 16, 2], F32, kind="ExternalInput").ap()
    gd = nc.dram_tensor("g", [128, 16], F32, kind="ExternalInput").ap()
    w1d = nc.dram_tensor("w1", [D, DFF], BF16, kind="ExternalInput").ap()
    w2d = nc.dram_tensor("w2", [DFF, D], BF16, kind="ExternalInput").ap()
    xo = nc.dram_tensor("xo", [D, TOK], F32, kind="ExternalOutput").ap()
    with contextlib.ExitStack() as st:
        sb = lambda n, s, d: st.enter_context(nc.sbuf_tensor(n, s, d))
        X = sb("X", [128, 16, TOK], F32)
        hT = sb("hT", [128, 16, TOK], BF16)
        m1 = sb("m1", [128, 8, TOK], BF16)
        sc = sb("sc_s", [128, 16, 2], F32)
        sh = sb("sh_s", [128, 16, 2], F32)
        gt = sb("gt_s", [128, 16, 2], F32)
        g = sb("g_s", [128, 16], F32)
        s1 = sb("s1", [128, 16, 3], F32)
        ones = sb("ones", [128, 128], BF16)
        sq = [sb(f"sq{i}", [128, TOK], BF16) for i in range(2)]
        rstd = sb("rstd", [128, TOK], F32)
        tmp = [sb(f"tmp{i}", [128, TOK], F32) for i in range(2)]
        w1b = [sb(f"w1b{i}", [128, 16, 512], BF16) for i in range(2)]
        w2b = [sb(f"w2b{i}", [128, 8, 512], BF16) for i in range(2)]
        rl = [sb(f"rl{i}", [128, 512], F32) for i in range(2)]
        ps_ms = st.enter_context(nc.psum_tensor("ps_ms", [128, 3, 512], F32))
        ps_mm = st.enter_context(nc.psum_tensor("ps_mm", [128, 5, 512], F32))
        P = TP(nc)
        R = {}

        def res(name):
            if name not in R:
                R[name] = Res()
            return R[name]

        t_ones = P.dve(lambda e: e.memset(ones[:], 1.0 / D))
        xr = []
        xv = xT.rearrange("(c p) t -> p c t", p=128)
        for c4 in range(4):
            t = P.dma(("sync", "scalar", "gpsimd", "sync")[c4], X[:, c4 * 4:(c4 + 1) * 4, :], xv[:, c4 * 4:(c4 + 1) * 4, :], f"ld_x{c4}")
            xr += [t] * 4
        t1 = P.dma("gpsimd", sc[:], scd, "ld_sc")
        t2 = P.dma("gpsimd", sh[:], shd, "ld_sh")
        t3 = P.dma("gpsimd", g[:], gd, "ld_g")
        t4 = P.dma("gpsimd", gt[:], gtd, "ld_gt")
        h_ready = emit_norm_mod(P, nc, X, hT, sc, sh, g[:], ones, ps_ms, sq, rstd, tmp, s1, xr, [t1, t2, t3, t_ones])
        rh = res("hT")
        rh.w = h_ready[-1]
        rX = [res(f"X{c}") for c in range(16)]
        for c in range(16):
            rX[c].w = xr[c]
            rX[c].r = {h_ready[-1][0]: h_ready[-1][1], "s_dve": P.cnt["s_dve"]}
        res("gt").w = t4
        npb = 0
        for q in range(8):
            for b2 in range(2):
                wi = (q * 2 + b2) % 2
                c0 = q * 1024 + b2 * 512
                P.xdma("sync", w1b[wi][:], w1d[:, c0:c0 + 512].rearrange("(kc p) n -> p kc n", p=128), f"ld_w1{wi}", writes=[res(f"w1b{wi}")])
                for j in range(4):
                    f = b2 * 4 + j
                    for bi, (s0, w) in enumerate(BLKS):
                        pb = npb % 5
                        npb += 1
                        P.group([lambda e, wi=wi, j=j, k=k, s0=s0, w=w, pb=pb: e.matmul(
                            ps_mm[:, pb, :w], lhsT=w1b[wi][:, k, j * 128:(j + 1) * 128], rhs=hT[:, k, s0:s0 + w],
                            start=(k == 0), stop=(k == 15)) for k in range(16)],
                            [res(f"w1b{wi}"), rh], [res(f"ps_mm{pb}")])
                        rb = npb % 2
                        if npb % 2 == 0:
                            P.op("scalar", lambda e, rb=rb, pb=pb, w=w: e.activation(out=rl[rb][:, :w], in_=ps_mm[:, pb, :w], func=AF.Relu),
                                 [res(f"ps_mm{pb}")], [res(f"rl{rb}")])
                            P.op("vector", lambda e, rb=rb, f=f, s0=s0, w=w: e.tensor_tensor(out=m1[:, f, s0:s0 + w], in0=rl[rb][:, :w], in1=rl[rb][:, :w], op=ALU.mult),
                                 [res(f"rl{rb}")], [res(f"m1_{f}")])
                        else:
                            P.op("vector", lambda e, rb=rb, pb=pb, w=w: e.tensor_scalar_max(out=rl[rb][:, :w], in0=ps_mm[:, pb, :w], scalar1=0.0),
                                 [res(f"ps_mm{pb}")], [res(f"rl{rb}")])
                            P.op("scalar", lambda e, rb=rb, f=f, s0=s0, w=w: e.activation(out=m1[:, f, s0:s0 + w], in_=rl[rb][:, :w], func=AF.Square),
                                 [res(f"rl{rb}")], [res(f"m1_{f}")])
            for nb in range(4):
                wi = (q * 4 + nb) % 2
                P.xdma("gpsimd", w2b[wi][:], w2d[q * 1024:(q + 1) * 1024, nb * 512:(nb + 1) * 512].rearrange("(fc p) n -> p fc n", p=128),
                       f"ld_w2{wi}", writes=[res(f"w2b{wi}")])
                for j in range(4):
                    nch = nb * 4 + j
                    for bi, (s0, w) in enumerate(BLKS):
                        pb = npb % 5
                        npb += 1
                        P.group([lambda e, wi=wi, j=j, f=f, s0=s0, w=w, pb=pb: e.matmul(
                            ps_mm[:, pb, :w], lhsT=w2b[wi][:, f, j * 128:(j + 1) * 128], rhs=m1[:, f, s0:s0 + w],
                            start=(f == 0), stop=(f == 7)) for f in range(8)],
                            [res(f"w2b{wi}")] + [res(f"m1_{ff}") for ff in range(8)], [res(f"ps_mm{pb}")])
                        P.op("vector", lambda e, nch=nch, bi=bi, s0=s0, w=w, pb=pb: e.scalar_tensor_tensor(
                            out=X[:, nch, s0:s0 + w], in0=ps_mm[:, pb, :w], scalar=gt[:, nch, (1 if bi == 2 else 0):(2 if bi == 2 else 1)],
                            in1=X[:, nch, s0:s0 + w], op0=ALU.mult, op1=ALU.add),
                            [res(f"ps_mm{pb}"), res("gt"), rX[nch]], [rX[nch]])
        outs = []
        xov = xo.rearrange("(c p) t -> p c t", p=128)
        for c4 in range(4):
            outs.append(P.xdma("sync" if c4 % 2 == 0 else "gpsimd", xov[:, c4 * 4:(c4 + 1) * 4, :], X[:, c4 * 4:(c4 + 1) * 4, :], f"st_x{c4}",
                               reads=[rX[c] for c in range(c4 * 4, c4 * 4 + 4)]))
        P.finish(outs)
    return nc


def build_N():
    nc = bass.Bass("TRN2", target_bir_lowering=False)
    xT = nc.dram_tensor("xT", [D, TOK], F32, kind="ExternalInput").ap()
    gd = nc.dram_tensor("g", [128, 16], F32, kind="ExternalInput").ap()
    xo = nc.dram_tensor("xo", [D, TOK], F32, kind="ExternalOutput").ap()
    with contextlib.ExitStack() as st:
        sb = lambda n, s, d: st.enter_context(nc.sbuf_tensor(n, s, d))
        X = sb("X", [128, 16, TOK], F32)
        hT = sb("hT", [128, 16, TOK], F32)
        zz = sb("zz", [128, 16, 2], F32)
        g = sb("g_s", [128, 16], F32)
        s1 = sb("s1", [128, 16, 3], F32)
        ones = sb("ones", [128, 128], BF16)
        sq = [sb(f"sq{i}", [128, TOK], BF16) for i in range(2)]
        rstd = sb("rstd", [128, TOK], F32)
        tmp = [sb(f"tmp{i}", [128, TOK], F32) for i in range(2)]
        ps_ms = st.enter_context(nc.psum_tensor("ps_ms", [128, 3, 512], F32))
        P = Prog(nc)
        t_ones = P.dve(lambda e: e.memset(ones[:], 1.0 / D))
        t_z = P.dve(lambda e: e.memset(zz[:], 0.0))
        xr = []
        xv = xT.rearrange("(c p) t -> p c t", p=128)
        for c4 in range(4):
            t = P.dma(("sync", "scalar", "gpsimd", "sync")[c4], X[:, c4 * 4:(c4 + 1) * 4, :], xv[:, c4 * 4:(c4 + 1) * 4, :], f"ld_x{c4}")
            xr += [t] * 4
        t3 = P.dma("gpsimd", g[:], gd, "ld_g")
        h_ready = emit_norm_mod(P, nc, X, hT, zz, zz, g[:], ones, ps_ms, sq, rstd, tmp, s1, xr, [t3, t_ones, t_z])
        outs = []
        xov = xo.rearrange("(c p) t -> p c t", p=128)
        for c4 in range(4):
            outs.append(P.dma("sync", xov[:, c4 * 4:(c4 + 1) * 4, :], hT[:, c4 * 4:(c4 + 1) * 4, :], f"st_x{c4}", [h_ready[c4 * 4 + 3]]))
        P.finish(outs)
    return nc


_CACHE = {}


def _prog(name, builder):
    if name not in _CACHE:
        _CACHE[name] = builder()
    return _CACHE[name]


def _mod2(mods, l, k):
    return np.ascontiguousarray(np.stack([fm(mods[l, 0, k * D:(k + 1) * D]), fm(mods[l, 1, k * D:(k + 1) * D])], -1))


def kernel(x, c, ctx, c_ctx, w_ada, b_ada, g_mix, w_in, conv_w, conv_b, dt_bias, a_log, d_skip,
           g_ssd_norm, w_fourier, w_out, g_mlp, w_mlp1, w_mlp2, g_final, _nlayers=DEPTH, _debug=None):
    f32 = lambda a: np.ascontiguousarray(np.asarray(a, dtype=np.float32))
    x, c, ctx, c_ctx = f32(x), f32(c), f32(ctx), f32(c_ctx)
    w_ada, b_ada, g_mix, w_in = f32(w_ada), f32(b_ada), f32(g_mix), f32(w_in)
    conv_w, conv_b, dt_bias, a_log, d_skip = f32(conv_w), f32(conv_b), f32(dt_bias), f32(a_log), f32(d_skip)
    g_ssd_norm, w_fourier, w_out, g_mlp = f32(g_ssd_norm), f32(w_fourier), f32(w_out), f32(g_mlp)
    w_mlp1, w_mlp2, g_final = f32(w_mlp1), f32(w_mlp2), f32(g_final)

    ws = [w_in, w_out, w_mlp1, w_mlp2]
    flat = np.concatenate([w.reshape(-1) for w in ws])
    fb = run_cast(flat)
    wb = []
    o = 0
    for w in ws:
        wb.append(fb[o:o + w.size].reshape(w.shape))
        o += w.size
    w_in_b, w_out_b, w1_b, w2_b = wb
    del flat, fb

    mods = run_mods(c, c_ctx, w_ada, b_ada)
    C, S, Cc_, Sc_, cc = fourier_tables()
    tabs = [(np.ascontiguousarray(C[:, 1024 * i:1024 * (i + 1)]), np.ascontiguousarray(S[:, 1024 * i:1024 * (i + 1)]),
             np.ascontiguousarray(Cc_[:, 32 * i:32 * (i + 1)]), np.ascontiguousarray(Sc_[:, 32 * i:32 * (i + 1)])) for i in range(NCORES)]
    del C, S

    xl, xc = x[0], ctx[0]
    xT = [np.ascontiguousarray(np.concatenate([xl[1024 * i:1024 * (i + 1)], xc[32 * i:32 * (i + 1)]], 0).T) for i in range(NCORES)]

    def to_global(per_core):
        R_ = per_core[0].shape[0]
        G = np.empty((R_, NTOK), np.float32)
        for i in range(NCORES):
            G[:, CTX + 1024 * i:CTX + 1024 * (i + 1)] = per_core[i][:, :1024]
            G[:, 32 * i:32 * (i + 1)] = per_core[i][:, 1024:]
        return G

    def to_core(G, i):
        return np.ascontiguousarray(np.concatenate([G[:, CTX + 1024 * i:CTX + 1024 * (i + 1)], G[:, 32 * i:32 * (i + 1)]], 1))

    for l in range(_nlayers):
        ncB = _prog("B", build_B)
        sc1, sh1, gt1 = _mod2(mods, l, 1), _mod2(mods, l, 0), _mod2(mods, l, 2)
        sh2, sc2, gt2 = _mod2(mods, l, 3), _mod2(mods, l, 4), _mod2(mods, l, 5)
        res = _run(ncB, [{"xT": xT[i], "sc": sc1, "sh": sh1, "g": fm(g_mix[l]), "w": w_in_b[l]} for i in range(NCORES)])
        pT = [r["pT"] for r in res]
        PT = to_global(pT)
        ncC = _prog("C", build_ssd)
        res = _run(ncC, ssd_inputs(PT, conv_w[l], conv_b[l], dt_bias[l], a_log[l], d_skip[l]))
        YF = np.concatenate([r["Y"][0].T for r in res], 0)
        YB = np.concatenate([r["Y"][1].T for r in res], 0)
        ncF = _prog("F", build_fourier)
        uL = np.ascontiguousarray(PT[0:512, CTX:].T)
        uC = np.ascontiguousarray(PT[0:512, :CTX].T)
        res = _run(ncF, [{"uL": uL, "uC": uC, "tabC": tabs[i][0], "tabS": tabs[i][1], "tcC": tabs[i][2], "tcS": tabs[i][3],
                          "wf": w_fourier[l], "cc": cc} for i in range(NCORES)])
        fT = [r["fT"] for r in res]
        ncG = _prog("G", build_G)
        res = _run(ncG, [{"xT": xT[i], "fT": fT[i], "yf": to_core(YF, i), "yb": to_core(YB, i),
                          "zT": np.ascontiguousarray(pT[i][512:2048]), "gn": fm(g_ssd_norm[l]), "gt": gt1, "w": w_out_b[l]}
                         for i in range(NCORES)])
        xm = [r["xo"] for r in res]
        if _debug is not None:
            _debug[f"xm{l}"] = xm
        ncM = _prog("M", build_M)
        res = _run(ncM, [{"xT": xm[i], "sc": sc2, "sh": sh2, "gt": gt2, "g": fm(g_mlp[l]), "w1": w1_b[l], "w2": w2_b[l]}
                         for i in range(NCORES)])
        xT = [r["xo"] for r in res]
        if _debug is not None:
            _debug[f"xo{l}"] = xT
    ncN = _prog("N", build_N)
    res = _run(ncN, [{"xT": xT[i], "g": fm(g_final)} for i in range(NCORES)])
    out = np.empty((1, SEQ, D), np.float32)
    for i in range(NCORES):
        out[0, 1024 * i:1024 * (i + 1), :] = res[i]["xo"][:, :1024].T
    return out
```

```python
import numpy as np
import ml_dtypes
import concourse.bass as bass
import concourse.mybir as mybir
from concourse.bass_utils import run_bass_kernel_spmd

F32 = mybir.dt.float32
BF16 = mybir.dt.bfloat16
AF = mybir.ActivationFunctionType
ALU = mybir.AluOpType
AX = mybir.AxisListType
NPBF = ml_dtypes.bfloat16

NCORES = 8
D = 2048
SEQ = 8192
CTX = 256
DEPTH = 4
DIN = 4656
DFF = 8192
TOK = 1056
EPS = 1e-6


class Prog:
    ENGS = ("sync", "scalar", "vector", "gpsimd", "tensor")

    def __init__(self, nc):
        self.nc = nc
        self.q = {e: [] for e in self.ENGS}
        self.cnt = {}
        self.waited = {e: {} for e in self.ENGS}

    def emit(self, eng, fn, waits=(), sig=None, inc=1):
        ws = []
        for w in waits:
            if w is None:
                continue
            s, v = w
            if self.waited[eng].get(s, 0) >= v:
                continue
            self.waited[eng][s] = v
            ws.append((s, v))
        tok = None
        if sig is not None:
            self.cnt[sig] = self.cnt.get(sig, 0) + inc
            tok = (sig, self.cnt[sig])
        self.q[eng].append((fn, ws, sig, inc))
        return tok

    def pe(self, fn, waits=(), sig=True):
        return self.emit("tensor", fn, waits, "s_pe" if sig else None)

    def act(self, fn, waits=(), sig=True):
        return self.emit("scalar", fn, waits, "s_act" if sig else None)

    def dve(self, fn, waits=(), sig=True):
        return self.emit("vector", fn, waits, "s_dve" if sig else None)

    def pool(self, fn, waits=(), sig=True):
        return self.emit("gpsimd", fn, waits, "s_pool" if sig else None)

    def dma(self, q, out, in_, sem, waits=()):
        return self.emit(q, lambda e: e.dma_start(out=out, in_=in_), waits, sem, 16)

    def simulate(self):
        pos = {e: 0 for e in self.ENGS}
        cnt = {}
        while True:
            prog = False
            for e in self.ENGS:
                while pos[e] < len(self.q[e]):
                    fn, ws, sig, inc = self.q[e][pos[e]]
                    if all(cnt.get(s_, 0) >= v for s_, v in ws):
                        if sig is not None:
                            cnt[sig] = cnt.get(sig, 0) + inc
                        pos[e] += 1
                        prog = True
                    else:
                        break
            if all(pos[e] == len(self.q[e]) for e in self.ENGS):
                return True
            if not prog:
                for e in self.ENGS:
                    if pos[e] < len(self.q[e]):
                        fn, ws, sig, inc = self.q[e][pos[e]]
                        print("DEADLOCK", e, pos[e], len(self.q[e]), [(s_, v, cnt.get(s_, 0)) for s_, v in ws], sig)
                return False

    def finish(self, final_waits):
        import os
        if os.environ.get("PROG_SIM"):
            print("SIM", self.simulate(), {e: len(self.q[e]) for e in self.ENGS})
        nc = self.nc
        names = sorted(self.cnt.keys())
        sems = {}
        import contextlib
        with contextlib.ExitStack() as st:
            for n in names:
                sems[n] = st.enter_context(nc.semaphore(n))
            block = st.enter_context(nc.Block())
            q = self.q
            q["sync"].append((None, [w for w in final_waits if w is not None], None, 0))

            def runner(ename):
                def _(eng):
                    for fn, ws, sig, inc in q[ename]:
                        for s, v in ws:
                            eng.wait_ge(sems[s], v)
                        if fn is None:
                            continue
                        ins = fn(eng)
                        if sig is not None:
                            ins.then_inc(sems[sig], inc)
                return _
            block.sync(runner("sync"))
            block.scalar(runner("scalar"))
            block.vector(runner("vector"))
            block.gpsimd(runner("gpsimd"))
            block.tensor(runner("tensor"))


def _run(nc, in_maps):
    res = run_bass_kernel_spmd(nc, in_maps, core_ids=list(range(NCORES)))
    return res.results


def build_cast(F):
    nc = bass.Bass("TRN2", target_bir_lowering=False)
    src = nc.dram_tensor("src", [128, F], F32, kind="ExternalInput").ap()
    dst = nc.dram_tensor("dst", [128, F], BF16, kind="ExternalOutput").ap()
    T = 4096
    nt = (F + T - 1) // T
    NB = 3
    import contextlib
    with contextlib.ExitStack() as st:
        ins = [st.enter_context(nc.sbuf_tensor(f"in{i}", [128, T], F32)) for i in range(NB)]
        outs = [st.enter_context(nc.sbuf_tensor(f"out{i}", [128, T], BF16)) for i in range(NB)]
        P = Prog(nc)
        cast_tok = [None] * NB
        st_tok = [None] * NB
        for t in range(nt):
            b = t % NB
            w = min(T, F - t * T)
            ld = P.dma("sync", ins[b][:, :w], src[:, t * T:t * T + w], f"ld{b}", [cast_tok[b]])
            if t % 2 == 0:
                cast_tok[b] = P.dve(lambda e, b=b, w=w: e.tensor_copy(out=outs[b][:, :w], in_=ins[b][:, :w]),
                                    [ld, st_tok[b]])
            else:
                cast_tok[b] = P.act(lambda e, b=b, w=w: e.copy(out=outs[b][:, :w], in_=ins[b][:, :w]),
                                    [ld, st_tok[b]])
            st_tok[b] = P.dma("gpsimd", dst[:, t * T:t * T + w], outs[b][:, :w], f"st{b}", [cast_tok[b]])
        P.finish(st_tok)
    return nc


def run_cast(flat):
    n = flat.size
    assert n % (NCORES * 128) == 0
    F = n // (NCORES * 128)
    nc = build_cast(F)
    sh = flat.reshape(NCORES, 128, F)
    res = _run(nc, [{"src": np.ascontiguousarray(sh[i])} for i in range(NCORES)])
    return np.stack([r["dst"] for r in res]).reshape(-1)


import contextlib


def fm(v):
    v = np.asarray(v)
    n = v.shape[-1] // 128
    lead = v.shape[:-1]
    a = v.reshape(lead + (n, 128))
    a = np.moveaxis(a, -1, 0)
    return np.ascontiguousarray(a)


def build_mods():
    nc = bass.Bass("TRN2", target_bir_lowering=False)
    cc = nc.dram_tensor("cc", [128, 16, 2], F32, kind="ExternalInput").ap()
    w = nc.dram_tensor("w", [DEPTH, D, 1536], F32, kind="ExternalInput").ap()
    b = nc.dram_tensor("b", [2, DEPTH, 1536], F32, kind="ExternalInput").ap()
    out = nc.dram_tensor("out", [2, DEPTH, 1536], F32, kind="ExternalOutput").ap()
    with contextlib.ExitStack() as st:
        sb = lambda n, s, d: st.enter_context(nc.sbuf_tensor(n, s, d))
        cs = sb("cs", [128, 16, 2], F32)
        ss = sb("ss", [128, 16, 2], F32)
        bs = sb("bs", [2, DEPTH, 1536], F32)
        os_ = sb("os", [2, DEPTH, 1536], F32)
        wb = [sb(f"wb{i}", [128, 16, 512], F32) for i in range(3)]
        ps = st.enter_context(nc.psum_tensor("ps", [128, 2, 512], F32))
        P = Prog(nc)
        t_c = P.dma("sync", cs[:], cc, "ld_c")
        t_b = P.dma("sync", bs[:], b, "ld_b")
        t_s = P.act(lambda e: e.activation(out=ss[:], in_=cs[:], func=AF.Silu), [t_c])
        wfree = [None, None, None]
        psfree = [None, None]
        evs = []
        n = 0
        for l in range(DEPTH):
            for blk in range(3):
                bi = n % 3
                pi = n % 2
                ld = P.dma(("sync", "gpsimd", "scalar")[n % 3], wb[bi][:],
                           w[l, :, blk * 512:(blk + 1) * 512].rearrange("(kc p) n -> p kc n", p=128),
                           f"ld_w{bi}", [wfree[bi]])
                for k in range(16):
                    t_mm = P.pe(lambda e, bi=bi, k=k, pi=pi: e.matmul(
                        ps[0:2, pi, :], lhsT=ss[:, k, :], rhs=wb[bi][:, k, :], start=(k == 0), stop=(k == 15)),
                        [ld, t_s, psfree[pi]] if k == 0 else [], sig=(k == 15))
                ev = P.dve(lambda e, l=l, blk=blk, pi=pi: e.tensor_tensor(
                    out=os_[:, l, blk * 512:(blk + 1) * 512], in0=ps[0:2, pi, :], in1=bs[:, l, blk * 512:(blk + 1) * 512], op=ALU.add),
                    [t_mm, t_b])
                evs.append(ev)
                psfree[pi] = ev
                wfree[bi] = t_mm
                n += 1
        t_o = P.dma("sync", out, os_[:], "st_o", [evs[-1]])
        P.finish([t_o])
    return nc


def run_mods(c, c_ctx, w_ada, b_ada):
    nc = build_mods()
    cc = np.stack([fm(c.reshape(-1)), fm(c_ctx.reshape(-1))], axis=-1)
    in_maps = []
    for i in range(NCORES):
        sl = slice(1536 * i, 1536 * (i + 1))
        bb = np.ascontiguousarray(np.broadcast_to(b_ada[None, :, sl], (2, DEPTH, 1536)), dtype=np.float32)
        in_maps.append({"cc": cc, "w": np.ascontiguousarray(w_ada[:, :, sl]), "b": bb})
    res = _run(nc, in_maps)
    mods = np.zeros((DEPTH, 2, 6 * D), np.float32)
    for i in range(NCORES):
        o = res[i]["out"]
        mods[:, :, 1536 * i:1536 * (i + 1)] = np.transpose(o, (1, 0, 2))
    return mods


BLKS = [(0, 512), (512, 512), (1024, 32)]


def emit_norm_mod(P, nc, X, hT, sc, sh, g, ones, ps_ms, sq, rstd, tmp, s1, x_ready, extra_waits=()):
    ew = list(extra_waits)
    epsc = s1[:, 0, 2:3]
    t_eps = P.dve(lambda e: e.memset(s1[:, :, 2:3], EPS))
    t_s1 = []
    for j in range(2):
        t_s1.append(P.dve(lambda e, j=j: e.scalar_tensor_tensor(
            out=s1[:, :, j], in0=sc[:, :, j], scalar=1.0, in1=g, op0=ALU.add, op1=ALU.mult), ew))
    sq_tok = [None] * len(sq)
    mm_tok = None
    for c in range(16):
        b = c % len(sq)
        xr = x_ready[c] if isinstance(x_ready, list) else x_ready
        t_sq = P.act(lambda e, c=c, b=b: e.activation(out=sq[b][:], in_=X[:, c, :], func=AF.Square),
                     [xr, sq_tok[b]] + ew)
        for bi, (s0, w) in enumerate(BLKS):
            mm_tok = P.pe(lambda e, c=c, b=b, bi=bi, s0=s0, w=w: e.matmul(
                ps_ms[:, bi, :w], lhsT=ones[:], rhs=sq[b][:, s0:s0 + w], start=(c == 0), stop=(c == 15)),
                [t_sq] + ew, sig=(bi == 2))
        sq_tok[b] = mm_tok
    t_r = None
    for bi, (s0, w) in enumerate(BLKS):
        t_q = P.act(lambda e, bi=bi, s0=s0, w=w: e.activation(
            out=rstd[:, s0:s0 + w], in_=ps_ms[:, bi, :w], func=AF.Sqrt, bias=epsc[:, 0:1], scale=1.0), [mm_tok, t_eps])
        t_r = P.dve(lambda e, bi=bi, s0=s0, w=w: e.reciprocal(
            out=rstd[:, s0:s0 + w], in_=rstd[:, s0:s0 + w]), [t_q])
    toks = []
    tmp_tok = [None] * len(tmp)
    for c in range(16):
        b = c % len(tmp)
        t_m = P.dve(lambda e, c=c, b=b: e.tensor_tensor(out=tmp[b][:], in0=X[:, c, :], in1=rstd[:], op=ALU.mult),
                    [t_r, tmp_tok[b]])
        P.act(lambda e, c=c, b=b: e.activation(out=hT[:, c, 0:1024], in_=tmp[b][:, 0:1024], func=AF.Identity,
                                               scale=s1[:, c, 0:1], bias=sh[:, c, 0:1]), [t_m, t_s1[1]], sig=False)
        t_a = P.act(lambda e, c=c, b=b: e.activation(out=hT[:, c, 1024:TOK], in_=tmp[b][:, 1024:TOK], func=AF.Identity,
                                                     scale=s1[:, c, 1:2], bias=sh[:, c, 1:2]), [t_m])
        tmp_tok[b] = t_a
        toks.append(t_a)
    return toks


def emit_proj(P, nc, wdram, nout, hT, h_ready, wbufs, psb, evac, name, kch=16, wfree0=None, ps_free0=None):
    nblk = (nout + 511) // 512
    wfree = list(wfree0) if wfree0 else [None] * len(wbufs)
    ps_free = list(ps_free0) if ps_free0 else [None] * len(psb)
    pi = 0
    evs = []
    last_mm = None
    for nb in range(nblk):
        b = nb % len(wbufs)
        ncol = min(512, nout - nb * 512)
        ld = P.dma("sync" if nb % 2 == 0 else "gpsimd", wbufs[b][:, :, :ncol],
                   wdram[:, nb * 512:nb * 512 + ncol].rearrange("(kc p) n -> p kc n", p=128),
                   f"ld_{name}{b}", [wfree[b]])
        for j in range((ncol + 127) // 128):
            m = min(128, ncol - j * 128)
            for bi, (s0, w) in enumerate(BLKS):
                pb = pi % len(psb)
                pi += 1
                for k in range(kch):
                    last_mm = P.pe(lambda e, b=b, j=j, m=m, k=k, s0=s0, w=w, pb=pb: e.matmul(
                        psb[pb][:m, :w], lhsT=wbufs[b][:, k, j * 128:j * 128 + m], rhs=hT[:, k, s0:s0 + w],
                        start=(k == 0), stop=(k == kch - 1)),
                        ([ld, ps_free[pb]] if k == 0 else []) + [h_ready[k]], sig=(k == kch - 1))
                ev = evac(nb * 4 + j, m, bi, s0, w, psb[pb], last_mm)
                ps_free[pb] = ev
                evs.append(ev)
        wfree[b] = last_mm
    return last_mm, evs


def build_B():
    nc = bass.Bass("TRN2", target_bir_lowering=False)
    xT = nc.dram_tensor("xT", [D, TOK], F32, kind="ExternalInput").ap()
    scd = nc.dram_tensor("sc", [128, 16, 2], F32, kind="ExternalInput").ap()
    shd = nc.dram_tensor("sh", [128, 16, 2], F32, kind="ExternalInput").ap()
    gd = nc.dram_tensor("g", [128, 16], F32, kind="ExternalInput").ap()
    wd = nc.dram_tensor("w", [D, DIN], BF16, kind="ExternalInput").ap()
    pT = nc.dram_tensor("pT", [DIN, TOK], F32, kind="ExternalOutput").ap()
    with contextlib.ExitStack() as st:
        sb = lambda n, s, d: st.enter_context(nc.sbuf_tensor(n, s, d))
        X = sb("X", [128, 16, TOK], F32)
        hT = sb("hT", [128, 16, TOK], BF16)
        sc = sb("sc_s", [128, 16, 2], F32)
        sh = sb("sh_s", [128, 16, 2], F32)
        g = sb("g_s", [128, 16], F32)
        s1 = sb("s1", [128, 16, 3], F32)
        ones = sb("ones", [128, 128], BF16)
        sq = [sb(f"sq{i}", [128, TOK], BF16) for i in range(3)]
        rstd = sb("rstd", [128, TOK], F32)
        tmp = [sb(f"tmp{i}", [128, TOK], F32) for i in range(2)]
        wb = [sb(f"wb{i}", [128, 16, 512], BF16) for i in range(2)]
        ot = [sb(f"ot{i}", [128, TOK], F32) for i in range(3)]
        ps_ms = st.enter_context(nc.psum_tensor("ps_ms", [128, 3, 512], F32))
        ps_mm = st.enter_context(nc.psum_tensor("ps_mm", [128, 5, 512], F32))
        P = Prog(nc)
        t_ones = P.dve(lambda e: e.memset(ones[:], 1.0 / D))
        xr = []
        xv = xT.rearrange("(c p) t -> p c t", p=128)
        for c4 in range(4):
            t = P.dma(("sync", "scalar", "gpsimd", "sync")[c4], X[:, c4 * 4:(c4 + 1) * 4, :], xv[:, c4 * 4:(c4 + 1) * 4, :], f"ld_x{c4}")
            xr += [t] * 4
        t1 = P.dma("gpsimd", sc[:], scd, "ld_sc")
        t2 = P.dma("gpsimd", sh[:], shd, "ld_sh")
        t3 = P.dma("gpsimd", g[:], gd, "ld_g")
        h_ready = emit_norm_mod(P, nc, X, hT, sc, sh, g[:], ones, ps_ms, sq, rstd, tmp, s1, xr, [t1, t2, t3, t_ones])

        ot_free = [None] * 3
        state = {"n": 0, "evs": []}
        out_toks = []

        def evac(nchunk, m, bi, s0, w, ps, mm):
            b = nchunk % 3
            if bi == 1:
                tk = P.act(lambda e: e.copy(out=ot[b][:m, s0:s0 + w], in_=ps[:m, :w]), [mm, ot_free[b]])
            else:
                tk = P.dve(lambda e: e.tensor_copy(out=ot[b][:m, s0:s0 + w], in_=ps[:m, :w]), [mm, ot_free[b]])
            state["evs"].append(tk)
            if bi == 2:
                d = P.dma("scalar", pT[nchunk * 128:nchunk * 128 + m, :], ot[b][:m, :], f"st_p{b}", state["evs"][-3:])
                ot_free[b] = d
                out_toks.append(d)
            return tk

        emit_proj(P, nc, wd, DIN, hT, h_ready, wb, [ps_mm[:, i, :] for i in range(5)], evac, "w")
        P.finish(out_toks[-3:])
    return nc


class Res:
    __slots__ = ("w", "r")

    def __init__(self):
        self.w = None
        self.r = {}


class TP(Prog):
    def _deps(self, reads, writes):
        waits = []
        for r in reads:
            waits.append(r.w)
        for w in writes:
            waits.append(w.w)
            waits += list(w.r.items())
        return waits

    def _mark(self, tok, reads, writes):
        for r in reads:
            s, v = tok
            if r.r.get(s, 0) < v:
                r.r[s] = v
        for w in writes:
            w.w = tok
            w.r = {}

    def op(self, eng, fn, reads=(), writes=()):
        sem = {"tensor": "s_pe", "scalar": "s_act", "vector": "s_dve", "gpsimd": "s_pool"}[eng]
        tok = self.emit(eng, fn, self._deps(reads, writes), sem)
        self._mark(tok, reads, writes)
        return tok

    def group(self, fns, reads=(), writes=()):
        waits = self._deps(reads, writes)
        tok = None
        for i, fn in enumerate(fns):
            tok = self.emit("tensor", fn, waits if i == 0 else (), "s_pe" if i == len(fns) - 1 else None)
        self._mark(tok, reads, writes)
        return tok

    def xdma(self, q, out, in_, sem, reads=(), writes=()):
        tok = self.emit(q, lambda e: e.dma_start(out=out, in_=in_), self._deps(reads, writes), sem, 16)
        self._mark(tok, reads, writes)
        return tok


NCH = 66
NTOK = NCH * 128


def build_ssd():
    nc = bass.Bass("TRN2", target_bir_lowering=False)
    uT = nc.dram_tensor("uT", [448, NTOK], F32, kind="ExternalInput").ap()
    dtr = nc.dram_tensor("dtr", [128, 2, NCH, 3], F32, kind="ExternalInput").ap()
    cwd = nc.dram_tensor("cw", [128, 4, 5], F32, kind="ExternalInput").ap()
    cbd = nc.dram_tensor("cb", [128, 4], F32, kind="ExternalInput").ap()
    smd = nc.dram_tensor("sm", [128, 15], F32, kind="ExternalInput").ap()
    cst = nc.dram_tensor("cst", [128, 8, 128], F32, kind="ExternalInput").ap()
    Y = nc.dram_tensor("Y", [2, NTOK, 192], F32, kind="ExternalOutput").ap()
    with contextlib.ExitStack() as st:
        sb = lambda n, s, d: st.enter_context(nc.sbuf_tensor(n, s, d))
        BT = sb("BT", [128, NTOK], BF16)
        CT = sb("CT", [128, NTOK], BF16)
        Btm = sb("Btm", [128, NCH, 128], BF16)
        Xtm = sb("Xtm", [128, NCH, 192], F32)
        cw = sb("cw_s", [128, 4, 5], F32)
        cb = sb("cb_s", [128, 4], F32)
        sm = sb("sm_s", [128, 15], F32)
        K8 = sb("K8", [128, 8, 128], F32)
        identb = sb("identb", [128, 128], BF16)
        dts = {n: sb(n, [128, 2, NCH, 3], F32) for n in
               ("raw", "t1", "dtv", "dta", "acum", "tot", "sdec", "ea", "dec", "dtsd")}
        avec = sb("avec", [128, 6], F32)
        PIECE = 1024
        raw = [sb(f"rawb{i}", [128, PIECE], F32) for i in range(2)]
        acc = [sb(f"accb{i}", [128, PIECE], F32) for i in range(2)]
        sil = [sb(f"silb{i}", [128, PIECE], F32) for i in range(2)]
        Rb = [sb(f"R{i}", [128, 2, 3, 128], BF16) for i in range(2)]
        K8b = sb("K8b", [128, 4, 128], BF16)
        dth = sb("dth", [128, 2, NCH, 3], BF16)
        dthf = sb("dthf", [128, 2, NCH, 3], F32)
        dtl = sb("dtl", [128, 2, NCH, 3], F32)
        Eb = [sb(f"E{i}", [128, 3, 128], F32) for i in range(2)]
        Mb = [sb(f"M{i}", [128, 3, 128], BF16) for i in range(2)]
        rtmp = [sb(f"rtmp{i}", [128, 192], F32) for i in range(2)]
        CBm = [sb(f"CBm{i}", [128, 128], F32) for i in range(2)]
        xdt2 = [[sb(f"xdt{i}_{j}", [128, 192], BF16) for j in range(2)] for i in range(2)]
        xs2 = [[sb(f"xs{i}_{j}", [128, 192], BF16) for j in range(2)] for i in range(2)]
        xd2 = [sb(f"xd_{j}", [128, 192], BF16) for j in range(2)]
        hst = [sb(f"hst{i}", [128, 192], F32) for i in range(2)]
        hbf = [[sb(f"hbf{i}_{j}", [128, 192], BF16) for j in range(2)] for i in range(2)]
        yo_t = [sb(f"yot{i}", [128, 192], F32) for i in range(2)]
        yo = [sb(f"yo{i}", [128, 192], F32) for i in range(4)]
        ps_seg = [st.enter_context(nc.psum_tensor(f"ps_seg{i}", [128, 512], F32)) for i in range(2)]
        ps_cb = st.enter_context(nc.psum_tensor("ps_cb", [128, 512], F32))
        ps_y = [st.enter_context(nc.psum_tensor(f"ps_y{i}", [128, 512], F32)) for i in range(2)]
        ps_st = st.enter_context(nc.psum_tensor("ps_st", [128, 512], F32))
        ps_tr = [st.enter_context(nc.psum_tensor(f"ps_tr{i}", [128, 512], F32)) for i in range(2)]

        P = TP(nc)
        R = {}

        def res(name):
            if name not in R:
                R[name] = Res()
            return R[name]

        Tm, Um, SU, SL, MF, MB, ID, ON = [K8[:, i, :] for i in range(8)]
        P.xdma("sync", K8[:], cst, "ld_k8", writes=[res("K8")])
        P.xdma("sync", cw[:], cwd, "ld_cw", writes=[res("cw")])
        P.xdma("sync", cb[:], cbd, "ld_cb", writes=[res("cb")])
        P.xdma("sync", sm[:], smd, "ld_sm", writes=[res("sm")])
        P.xdma("sync", dts["raw"][:], dtr, "ld_dt", writes=[res("raw")])
        P.op("vector", lambda e: e.tensor_copy(out=identb[:], in_=ID), [res("K8")], [res("identb")])
        P.op("vector", lambda e: e.tensor_copy(out=K8b[:], in_=K8[:, 0:4, :]), [res("K8")], [res("K8b")])

        fl = lambda n: dts[n][:].rearrange("p a c j -> p (a c j)")
        col = lambda n, d, j: dts[n][:, d, :, j]
        for d in range(2):
            for j in range(3):
                P.op("vector", lambda e, d=d, j=j: e.tensor_scalar(
                    out=col("raw", d, j), in0=col("raw", d, j), scalar1=sm[:, d * 3 + j:d * 3 + j + 1], scalar2=None,
                    op0=ALU.add), [res("raw"), res("sm")], [res("raw")])
        P.op("scalar", lambda e: e.activation(out=fl("t1"), in_=fl("raw"), func=AF.Abs), [res("raw")], [res("t1")])
        P.op("scalar", lambda e: e.activation(out=fl("t1"), in_=fl("t1"), func=AF.Exp, scale=-1.0), [res("t1")], [res("t1")])
        P.op("vector", lambda e: e.tensor_scalar_add(out=fl("t1"), in0=fl("t1"), scalar1=1.0), [res("t1")], [res("t1")])
        P.op("scalar", lambda e: e.activation(out=fl("t1"), in_=fl("t1"), func=AF.Ln), [res("t1")], [res("t1")])
        P.op("vector", lambda e: e.scalar_tensor_tensor(out=fl("dtv"), in0=fl("raw"), scalar=0.0, in1=fl("t1"),
                                                         op0=ALU.max, op1=ALU.add), [res("raw"), res("t1")], [res("dtv")])
        P.op("scalar", lambda e: e.activation(out=avec[:], in_=sm[:, 6:12], func=AF.Exp), [res("sm")], [res("avec")])
        P.op("vector", lambda e: e.tensor_scalar_mul(out=avec[:], in0=avec[:], scalar1=-1.0), [res("avec")], [res("avec")])
        for d in range(2):
            for j in range(3):
                P.op("vector", lambda e, d=d, j=j: e.tensor_scalar(
                    out=col("dta", d, j), in0=col("dtv", d, j), scalar1=avec[:, d * 3 + j:d * 3 + j + 1], scalar2=None,
                    op0=ALU.mult), [res("dtv"), res("avec")], [res("dta")])
        fl2 = lambda t: t[:].rearrange("p a c j -> p (a c j)")
        P.op("vector", lambda e: e.tensor_copy(out=fl2(dth), in_=fl("dta")), [res("dta")], [res("dth")])
        P.op("vector", lambda e: e.tensor_copy(out=fl2(dthf), in_=fl2(dth)), [res("dth")], [res("dthf")])
        P.op("vector", lambda e: e.tensor_tensor(out=fl2(dtl), in0=fl("dta"), in1=fl2(dthf), op=ALU.subtract), [res("dta"), res("dthf")], [res("dtl")])
        for d in range(2):
            lhs = Tm if d == 0 else Um
            P.group([lambda e, d=d, lhs=lhs: e.matmul(ps_tr[0][:, d * 256:d * 256 + 198], lhsT=lhs,
                                                       rhs=dts["dta"][:, d].rearrange("p c j -> p (c j)"),
                                                       start=True, stop=True)],
                    [res("K8"), res("dta")], [res("ps_tr0")])
            P.group([lambda e, d=d: e.matmul(ps_tr[1][:, d * 256:d * 256 + 198], lhsT=ON,
                                             rhs=dts["dta"][:, d].rearrange("p c j -> p (c j)"),
                                             start=True, stop=True)],
                    [res("K8"), res("dta")], [res("ps_tr1")])
        for d in range(2):
            P.op("vector", lambda e, d=d: e.tensor_copy(out=dts["acum"][:, d].rearrange("p c j -> p (c j)"),
                                                       in_=ps_tr[0][:, d * 256:d * 256 + 198]),
                 [res("ps_tr0")], [res("acum")])
            P.op("vector", lambda e, d=d: e.tensor_copy(out=dts["tot"][:, d].rearrange("p c j -> p (c j)"),
                                                       in_=ps_tr[1][:, d * 256:d * 256 + 198]),
                 [res("ps_tr1")], [res("tot")])
        P.op("vector", lambda e: e.tensor_tensor(out=fl("sdec"), in0=fl("tot"), in1=fl("acum"), op=ALU.subtract),
             [res("tot"), res("acum")], [res("sdec")])
        P.op("scalar", lambda e: e.activation(out=fl("sdec"), in_=fl("sdec"), func=AF.Exp), [res("sdec")], [res("sdec")])
        P.op("scalar", lambda e: e.activation(out=fl("ea"), in_=fl("acum"), func=AF.Exp), [res("acum")], [res("ea")])
        P.op("scalar", lambda e: e.activation(out=fl("dec"), in_=fl("tot"), func=AF.Exp), [res("tot")], [res("dec")])
        P.op("vector", lambda e: e.tensor_tensor(out=fl("dtsd"), in0=fl("dtv"), in1=fl("sdec"), op=ALU.mult),
             [res("dtv"), res("sdec")], [res("dtsd")])

        tiles = [(0, 128), (128, 64), (192, 128), (320, 128)]
        pieces = [(0, 256, 256)] + [(256 + i * 1024, 1024, 64) for i in range(8)]
        n = 0
        ntr = 0
        for (t0, nt, rl) in pieces:
            for ti, (r0, npart) in enumerate(tiles):
                b = n % 2
                n += 1
                rw, ac, so = raw[b], acc[b], sil[b]
                rr, ra, rs = res(f"raw{b}"), res(f"acc{b}"), res(f"sil{b}")
                P.xdma("sync" if n % 2 else "gpsimd", rw[:npart, :nt], uT[r0:r0 + npart, t0:t0 + nt], f"ld_raw{b}", writes=[rr])
                v = lambda a, npart=npart, nt=nt, rl=rl: a[:npart, :nt].rearrange("p (r t) -> p r t", t=rl)
                P.op("vector", lambda e, ti=ti, rw=rw, ac=ac, npart=npart, nt=nt: e.tensor_scalar(
                    out=ac[:npart, :nt], in0=rw[:npart, :nt], scalar1=cw[:npart, ti, 2:3], scalar2=None, op0=ALU.mult),
                    [rr, res("cw")], [ra])
                for j in (0, 1, 3, 4):
                    sh_ = j - 2
                    a0, a1 = max(0, -sh_), min(rl, rl - sh_)
                    P.op("vector", lambda e, ti=ti, j=j, v=v, rw=rw, ac=ac, a0=a0, a1=a1, sh_=sh_, npart=npart: e.scalar_tensor_tensor(
                        out=v(ac)[:, :, a0:a1], in0=v(rw)[:, :, a0 + sh_:a1 + sh_], scalar=cw[:npart, ti, j:j + 1],
                        in1=v(ac)[:, :, a0:a1], op0=ALU.mult, op1=ALU.add), [rr, ra, res("cw")], [ra])
                P.op("scalar", lambda e, ti=ti, ac=ac, so=so, npart=npart, nt=nt: e.activation(
                    out=so[:npart, :nt], in_=ac[:npart, :nt], func=AF.Silu, bias=cb[:npart, ti:ti + 1], scale=1.0),
                    [ra, res("cb")], [rs])
                if ti == 2:
                    P.op("vector", lambda e, so=so, nt=nt, t0=t0: e.tensor_copy(out=BT[:, t0:t0 + nt], in_=so[:, :nt]), [rs], [res("BT")])
                if ti == 3:
                    P.op("vector", lambda e, so=so, nt=nt, t0=t0: e.tensor_copy(out=CT[:, t0:t0 + nt], in_=so[:, :nt]), [rs], [res("CT")])
                    continue
                for cc in range(nt // 128):
                    ch = t0 // 128 + cc
                    pt = ntr % 2
                    ntr += 1
                    rp = res(f"ps_tr{pt}")
                    P.group([lambda e, so=so, cc=cc, npart=npart, pt=pt: e.transpose(
                        ps_tr[pt][:, :npart], so[:npart, cc * 128:(cc + 1) * 128], K8[:npart, 6, :npart])],
                        [rs, res("K8")], [rp])
                    if ti == 2:
                        P.op("scalar", lambda e, ch=ch, pt=pt: e.copy(out=Btm[:, ch, :], in_=ps_tr[pt][:, :128]), [rp], [res("Btm")])
                    else:
                        c0 = 0 if ti == 0 else 128
                        P.op("vector" if ti == 0 else "scalar",
                             (lambda e, ch=ch, pt=pt, c0=c0, npart=npart: e.tensor_copy(out=Xtm[:, ch, c0:c0 + npart], in_=ps_tr[pt][:, :npart]))
                             if ti == 0 else
                             (lambda e, ch=ch, pt=pt, c0=c0, npart=npart: e.copy(out=Xtm[:, ch, c0:c0 + npart], in_=ps_tr[pt][:, :npart])),
                             [rp], [res("Xtm")])

        for i in range(2):
            P.op("vector", lambda e, i=i: e.memset(hst[i][:], 0.0), [], [res(f"hst{i}")])
            P.op("vector", lambda e, i=i: e.memset(hbf[i][0][:], 0.0), [], [res(f"hbf{i}_0")])
        border = [1, 0] + list(range(65, 1, -1))
        dvec = sm[:, 12:15]
        out_tok = []
        h3 = lambda ap: ap.rearrange("p (h c) -> p h c", h=3)
        stb = [ps_st, ps_tr[1]]
        stn = ["ps_st0", "ps_tr1"]
        cbb = [ps_cb, ps_tr[0]]
        cbn = ["ps_cb", "ps_tr0"]

        def unit(u):
            step, d = u // 2, u % 2
            ch = step if d == 0 else border[step]
            return step, d, ch

        def front(u):
            front_a(u)
            front_b(u)

        def front_a(u):
            cbpart(u)
            prep(u)

        def cbpart(u):
            step, d, ch = unit(u)
            cs = slice(ch * 128, (ch + 1) * 128)
            P.group([lambda e, cs=cs, d=d: e.matmul(cbb[d][:, :128], lhsT=BT[:, cs], rhs=CT[:, cs], start=True, stop=True)],
                    [res("BT"), res("CT")], [res(cbn[d])])
            P.op("vector", lambda e, d=d: e.tensor_tensor(out=CBm[d][:], in0=cbb[d][:, :128], in1=(MF if d == 0 else MB), op=ALU.mult),
                 [res(cbn[d]), res("K8")], [res(f"CBm{d}")])

        def prep(u):
            step, d, ch = unit(u)
            par = step % 2
            xdt_, xs_, xd_ = xdt2[d][par], xs2[d][par], xd2[par]
            P.op("vector", lambda e, d=d, ch=ch, xdt_=xdt_: e.tensor_tensor(
                out=h3(xdt_[:]), in0=h3(Xtm[:, ch, :]), in1=dts["dtv"][:, d, ch, 0:3].unsqueeze(2).to_broadcast([128, 3, 64]), op=ALU.mult),
                [res("Xtm"), res("dtv")], [res(f"xdt{d}_{par}")])
            P.op("vector", lambda e, d=d, ch=ch, xs_=xs_: e.tensor_tensor(
                out=h3(xs_[:]), in0=h3(Xtm[:, ch, :]), in1=dts["dtsd"][:, d, ch, 0:3].unsqueeze(2).to_broadcast([128, 3, 64]), op=ALU.mult),
                [res("Xtm"), res("dtsd")], [res(f"xs{d}_{par}")])
            if d == 0:
                P.op("vector", lambda e, ch=ch, xd_=xd_: e.tensor_tensor(
                    out=h3(xd_[:]), in0=h3(Xtm[:, ch, :]), in1=dvec.unsqueeze(2).to_broadcast([128, 3, 64]), op=ALU.mult),
                    [res("Xtm"), res("sm")], [res(f"xd_{par}")])
            Tsel = K8[:, (0 if d == 0 else 1), :]
            P.op("vector", lambda e, d=d, ch=ch, Tsel=Tsel: e.tensor_tensor(
                out=Rb[d][:, 0], in0=Tsel.unsqueeze(1).to_broadcast([128, 3, 128]),
                in1=dthf[:, d, ch, 0:3].unsqueeze(2).to_broadcast([128, 3, 128]), op=ALU.mult),
                [res("K8"), res("dthf")], [res(f"R{d}")])
            for h in range(3):
                P.op("scalar", lambda e, d=d, ch=ch, h=h, Tsel=Tsel: e.activation(
                    out=Rb[d][:, 1, h, :], in_=Tsel, func=AF.Copy, scale=dtl[:, d, ch, h:h + 1]),
                    [res("K8"), res("dtl")], [res(f"R{d}")])
        def front_b(u):
            step, d, ch = unit(u)
            Ssel = K8b[:, (2 if d == 0 else 3), :]
            P.group([lambda e, d=d, h=h, t=t, Ssel=Ssel: e.matmul(ps_seg[d][:, h * 128:(h + 1) * 128], lhsT=Ssel,
                                                  rhs=Rb[d][:, t, h, :], start=(t == 0), stop=(t == 1)) for h in range(3) for t in range(2)],
                    [res("K8b"), res(f"R{d}")], [res(f"ps_seg{d}")])
            P.op("scalar", lambda e, d=d: e.activation(out=Eb[d][:].rearrange("p h l -> p (h l)"), in_=ps_seg[d][:, 0:384], func=AF.Exp),
                 [res(f"ps_seg{d}")], [res(f"E{d}")])
            P.op("vector", lambda e, d=d: e.tensor_tensor(out=Mb[d][:], in0=Eb[d][:], in1=CBm[d][:].unsqueeze(1).to_broadcast([128, 3, 128]), op=ALU.mult),
                 [res(f"E{d}"), res(f"CBm{d}")], [res(f"M{d}")])

        def back(u):
            back_pe(u)
            back_rest(u)

        def back_pe(u):
            step, d, ch = unit(u)
            cs = slice(ch * 128, (ch + 1) * 128)
            hp = step % 2
            par = step % 2
            xdt_, xs_, xd_ = xdt2[d][par], xs2[d][par], xd2[par]
            fns = []
            for h in range(3):
                hs = slice(h * 64, (h + 1) * 64)
                fns.append(lambda e, d=d, h=h, hs=hs, xdt_=xdt_: e.matmul(ps_y[d][:, hs], lhsT=Mb[d][:, h, :], rhs=xdt_[:, hs],
                                                               start=True, stop=(d == 1)))
                if d == 0:
                    fns.append(lambda e, hs=hs, xd_=xd_: e.matmul(ps_y[0][:, hs], lhsT=identb[:], rhs=xd_[:, hs], start=False, stop=True))
            fns.append(lambda e, d=d, cs=cs, hp=hp: e.matmul(ps_y[d][:, 256:448], lhsT=CT[:, cs], rhs=hbf[d][hp][:], start=True, stop=True))
            fns.append(lambda e, d=d, ch=ch, xs_=xs_: e.matmul(stb[d][:, 0:192], lhsT=Btm[:, ch, :], rhs=xs_[:], start=True, stop=True))
            P.group(fns, [res(f"M{d}"), res(f"xdt{d}_{par}"), res("identb"), res(f"xd_{par}"), res("CT"), res(f"hbf{d}_{hp}"), res("Btm"), res(f"xs{d}_{par}")],
                    [res(f"ps_y{d}"), res(stn[d])])
        def back_rest(u):
            step, d, ch = unit(u)
            hp = step % 2
            P.op("vector", lambda e, d=d, ch=ch: e.tensor_tensor(
                out=h3(rtmp[d][:]), in0=h3(hst[d][:]), in1=dts["dec"][:, d, ch, 0:3].unsqueeze(2).to_broadcast([128, 3, 64]), op=ALU.mult),
                [res(f"hst{d}"), res("dec")], [res(f"rtmp{d}")])
            P.op("vector", lambda e, d=d: e.tensor_tensor(out=hst[d][:], in0=rtmp[d][:], in1=stb[d][:, 0:192], op=ALU.add),
                 [res(f"rtmp{d}"), res(stn[d])], [res(f"hst{d}")])
            P.op("scalar", lambda e, d=d, hp=hp: e.copy(out=hbf[d][1 - hp][:], in_=hst[d][:]),
                 [res(f"hst{d}")], [res(f"hbf{d}_{1 - hp}")])
            P.op("vector", lambda e, d=d, ch=ch: e.tensor_tensor(
                out=h3(yo_t[d][:]), in0=h3(ps_y[d][:, 256:448]), in1=dts["ea"][:, d, ch, 0:3].unsqueeze(2).to_broadcast([128, 3, 64]), op=ALU.mult),
                [res(f"ps_y{d}"), res("ea")], [res(f"yot{d}")])
            yb = u % 4
            P.op("vector", lambda e, d=d, yb=yb: e.tensor_tensor(out=yo[yb][:], in0=ps_y[d][:, 0:192], in1=yo_t[d][:], op=ALU.add),
                 [res(f"ps_y{d}"), res(f"yot{d}")], [res(f"yo{yb}")])
            out_tok.append(P.xdma("sync", Y[d, ch * 128:(ch + 1) * 128, :], yo[yb][:], f"st_y{yb}", reads=[res(f"yo{yb}")]))

        NU = 2 * NCH
        import os
        mode = os.environ.get("SSD_ORDER", "pair")
        if mode == "plain":
            for u in range(NU):
                front(u)
                back(u)
        elif mode == "prep":
            prep(0)
            for u in range(NU):
                cbpart(u)
                front_b(u)
                if u + 1 < NU:
                    prep(u + 1)
                back(u)
        elif mode == "pair":
            prep(0)
            prep(1)
            for st_ in range(NCH):
                a, b = 2 * st_, 2 * st_ + 1
                cbpart(a)
                cbpart(b)
                front_b(a)
                front_b(b)
                if st_ + 1 < NCH:
                    prep(a + 2)
                    prep(b + 2)
                back(a)
                back(b)
        elif mode == "pipe":
            prep(0)
            cbpart(0)
            front_b(0)
            for u in range(NU):
                if u + 1 < NU:
                    prep(u + 1)
                    cbpart(u + 1)
                    front_b(u + 1)
                back(u)
        elif mode == "late":
            prep(0)
            cbpart(0)
            front_b(0)
            for u in range(NU):
                if u + 1 < NU:
                    prep(u + 1)
                back_pe(u)
                if u + 1 < NU:
                    cbpart(u + 1)
                    front_b(u + 1)
                back_rest(u)
        elif mode == "split":
            front_a(0)
            front_b(0)
            for u in range(NU):
                if u + 1 < NU:
                    front_a(u + 1)
                back(u)
                if u + 1 < NU:
                    front_b(u + 1)
        P.finish(out_tok[-4:])
    return nc


def ssd_consts():
    k = np.arange(128)[:, None]
    l = np.arange(128)[None, :]
    mats = [k <= l, k >= l, k > l, k < l, l >= k, l <= k, k == l, np.ones((128, 128), bool)]
    return np.ascontiguousarray(np.stack([m.astype(np.float32) for m in mats], 1))


def ssd_inputs(PT, conv_w, conv_b, dt_bias, a_log, d_skip):
    cst = ssd_consts()
    maps = []
    for core in range(NCORES):
        g, half = core // 2, core % 2
        hg0 = g * 6 + half * 3
        xc = 2048 + g * 384 + half * 192
        bc = 2048 + 1536 + g * 128
        cc = 2048 + 2048 + g * 128
        uT = np.concatenate([PT[xc:xc + 192], PT[bc:bc + 128], PT[cc:cc + 128]], 0)
        dt = np.stack([PT[4608 + d * 24 + hg0:4608 + d * 24 + hg0 + 3] for d in range(2)], 0)
        dt = dt.reshape(2, 3, NCH, 128).transpose(3, 0, 2, 1)
        cols = [np.arange(xc - 2048, xc - 2048 + 128), np.arange(xc - 2048 + 128, xc - 2048 + 192),
                np.arange(bc - 2048, bc - 2048 + 128), np.arange(cc - 2048, cc - 2048 + 128)]
        cw = np.zeros((128, 4, 5), np.float32)
        cb = np.zeros((128, 4), np.float32)
        for ti, cidx in enumerate(cols):
            cw[:len(cidx), ti, :] = conv_w[:, cidx].T
            cb[:len(cidx), ti] = conv_b[cidx]
        sm = np.concatenate([dt_bias[:, hg0:hg0 + 3].reshape(-1), a_log[:, hg0:hg0 + 3].reshape(-1), d_skip[hg0:hg0 + 3]])
        sm = np.broadcast_to(sm[None, :], (128, 15))
        maps.append({"uT": np.ascontiguousarray(uT), "dtr": np.ascontiguousarray(dt), "cw": cw, "cb": cb,
                     "sm": np.ascontiguousarray(sm, dtype=np.float32), "cst": cst})
    return maps


def build_fourier():
    nc = bass.Bass("TRN2", target_bir_lowering=False)
    uL = nc.dram_tensor("uL", [SEQ, 512], F32, kind="ExternalInput").ap()
    uC = nc.dram_tensor("uC", [CTX, 512], F32, kind="ExternalInput").ap()
    tabC = nc.dram_tensor("tabC", [SEQ, 1024], BF16, kind="ExternalInput").ap()
    tabS = nc.dram_tensor("tabS", [SEQ, 1024], BF16, kind="ExternalInput").ap()
    tcC = nc.dram_tensor("tcC", [CTX, 32], BF16, kind="ExternalInput").ap()
    tcS = nc.dram_tensor("tcS", [CTX, 32], BF16, kind="ExternalInput").ap()
    wfd = nc.dram_tensor("wf", [4, 128, 128], F32, kind="ExternalInput").ap()
    ccd = nc.dram_tensor("cc", [128, 2, 128], F32, kind="ExternalInput").ap()
    fT = nc.dram_tensor("fT", [512, TOK], F32, kind="ExternalOutput").ap()
    with contextlib.ExitStack() as st:
        sb = lambda n, s, d: st.enter_context(nc.sbuf_tensor(n, s, d))
        NB = 3
        ust = [sb(f"ust{i}", [128, 4, 512], F32) for i in range(NB)]
        ubig = sb("ubig", [128, 64, 512], BF16)
        tC = [sb(f"tC{i}", [128, 4, 512], BF16) for i in range(NB)]
        tS = [sb(f"tS{i}", [128, 4, 512], BF16) for i in range(NB)]
        wf = sb("wf_s", [128, 4, 128], F32)
        cc = sb("cc_s", [128, 2, 128], F32)
        Mm = sb("Mm", [128, 4, 2, 128], BF16)
        AB = sb("AB", [128, 4, 2, 512], BF16)
        fo = sb("fo", [128, 4, TOK], F32)
        ucs = sb("ucs", [128, 2, 512], F32)
        ucb = sb("ucb", [128, 2, 512], BF16)
        tcc = sb("tcc", [128, 2, 2, 32], BF16)
        bank = [st.enter_context(nc.psum_tensor(f"bank{i}", [128, 512], F32)) for i in range(8)]
        P = TP(nc)
        R = {}

        def res(name):
            if name not in R:
                R[name] = Res()
            return R[name]

        P.xdma("sync", wf[:], wfd.rearrange("g j d -> j g d"), "ld_wf", writes=[res("wf")])
        P.xdma("sync", cc[:], ccd, "ld_cc", writes=[res("cc")])
        for g in range(4):
            for t in range(2):
                P.group([lambda e, g=g, t=t: e.matmul(bank[g][:, t * 128:(t + 1) * 128], lhsT=cc[:, t, :], rhs=wf[:, g, :],
                                                      start=True, stop=True)], [res("wf"), res("cc")], [res(f"bank{g}")])
            P.op("vector", lambda e, g=g: e.tensor_copy(out=Mm[:, g].rearrange("p t d -> p (t d)"), in_=bank[g][:, 0:256]),
                 [res(f"bank{g}")], [res("Mm")])

        def stage2(width, c0, ABv):
            for g in range(4):
                P.group([lambda e, g=g, t=t: e.matmul(bank[g][:, :width], lhsT=Mm[:, g, t, :], rhs=ABv(g, t),
                                                      start=(t == 0), stop=(t == 1)) for t in range(2)],
                        [res("Mm"), res("AB")], [res(f"bank{g}")])
                if g % 2 == 0:
                    P.op("vector", lambda e, g=g: e.tensor_copy(out=fo[:, g, c0:c0 + width], in_=bank[g][:, :width]),
                         [res(f"bank{g}")], [res("fo")])
                else:
                    P.op("scalar", lambda e, g=g: e.copy(out=fo[:, g, c0:c0 + width], in_=bank[g][:, :width]),
                         [res(f"bank{g}")], [res("fo")])

        uv = uL.rearrange("(c p) f -> p c f", p=128)
        cv = tabC.rearrange("(c p) k -> p c k", p=128)
        sv = tabS.rearrange("(c p) k -> p c k", p=128)
        n = 0
        for half in range(2):
            ks = slice(half * 512, (half + 1) * 512)
            for nb in range(16):
                b = n % NB
                n += 1
                P.xdma("sync" if half == 1 else "gpsimd", tC[b][:], cv[:, nb * 4:(nb + 1) * 4, ks], f"ld_tc{b}", writes=[res(f"tC{b}")])
                P.xdma("gpsimd", tS[b][:], sv[:, nb * 4:(nb + 1) * 4, ks], f"ld_ts{b}", writes=[res(f"tS{b}")])
                if half == 0:
                    P.xdma("sync", ust[b][:], uv[:, nb * 4:(nb + 1) * 4, :], f"ld_u{b}", writes=[res(f"ust{b}")])
                    if nb % 2 == 0:
                        P.op("vector", lambda e, b=b, nb=nb: e.tensor_copy(out=ubig[:, nb * 4:(nb + 1) * 4, :], in_=ust[b][:]), [res(f"ust{b}")], [res(f"ubig{nb}")])
                    else:
                        P.op("scalar", lambda e, b=b, nb=nb: e.copy(out=ubig[:, nb * 4:(nb + 1) * 4, :], in_=ust[b][:]), [res(f"ust{b}")], [res(f"ubig{nb}")])
                fns = []
                for ci in range(4):
                    first = (nb == 0 and ci == 0)
                    last = (nb == 15 and ci == 3)
                    for g in range(4):
                        for t in range(2):
                            tb = tC[b] if t == 0 else tS[b]
                            fns.append(lambda e, nb=nb, ci=ci, g=g, t=t, tb=tb, first=first, last=last: e.matmul(
                                bank[g * 2 + t][:, :], lhsT=ubig[:, nb * 4 + ci, g * 128:(g + 1) * 128], rhs=tb[:, ci, :],
                                start=first, stop=last))
                P.group(fns, [res(f"ubig{nb}"), res(f"tC{b}"), res(f"tS{b}")], [res(f"bank{i}") for i in range(8)])
            for g in range(4):
                for t in range(2):
                    if t == 0:
                        P.op("vector", lambda e, g=g, t=t: e.tensor_copy(out=AB[:, g, t, :], in_=bank[g * 2 + t][:, :]),
                             [res(f"bank{g * 2 + t}")], [res("AB")])
                    else:
                        P.op("scalar", lambda e, g=g, t=t: e.copy(out=AB[:, g, t, :], in_=bank[g * 2 + t][:, :]),
                             [res(f"bank{g * 2 + t}")], [res("AB")])
            stage2(512, half * 512, lambda g, t: AB[:, g, t, :])
        P.xdma("sync", ucs[:], uC.rearrange("(c p) f -> p c f", p=128), "ld_uc", writes=[res("ucs")])
        P.xdma("sync", tcc[:, 0], tcC.rearrange("(c p) k -> p c k", p=128), "ld_tcc", writes=[res("tcc")])
        P.xdma("sync", tcc[:, 1], tcS.rearrange("(c p) k -> p c k", p=128), "ld_tcs", writes=[res("tcc")])
        P.op("vector", lambda e: e.tensor_copy(out=ucb[:], in_=ucs[:]), [res("ucs")], [res("ucb")])
        for g in range(4):
            for t in range(2):
                P.group([lambda e, g=g, t=t, ci=ci: e.matmul(bank[g * 2 + t][:, :32], lhsT=ucb[:, ci, g * 128:(g + 1) * 128],
                                                            rhs=tcc[:, t, ci, :], start=(ci == 0), stop=(ci == 1)) for ci in range(2)],
                        [res("ucb"), res("tcc")], [res(f"bank{g * 2 + t}")])
                P.op("vector", lambda e, g=g, t=t: e.tensor_copy(out=AB[:, g, t, :32], in_=bank[g * 2 + t][:, :32]),
                     [res(f"bank{g * 2 + t}")], [res("AB")])
        stage2(32, 1024, lambda g, t: AB[:, g, t, :32])
        t_o = P.xdma("sync", fT.rearrange("(g p) t -> p g t", p=128), fo[:], "st_f", reads=[res("fo")])
        P.finish([t_o])
    return nc


def fourier_tables():
    n = np.arange(SEQ, dtype=np.int64)
    ph = (np.outer(n, n) % SEQ).astype(np.float64) * (2 * np.pi / SEQ)
    sc = 1.0 / np.sqrt(SEQ * 128.0)
    C = (np.cos(ph) * sc).astype(NPBF)
    S = (np.sin(ph) * sc).astype(NPBF)
    m = np.arange(CTX, dtype=np.int64)
    phc = (np.outer(m, m) % CTX).astype(np.float64) * (2 * np.pi / CTX)
    scc = 1.0 / np.sqrt(CTX * 128.0)
    Cc_ = (np.cos(phc) * scc).astype(NPBF)
    Sc_ = (np.sin(phc) * scc).astype(NPBF)
    j = np.arange(128, dtype=np.int64)
    p128 = (np.outer(j, j) % 128).astype(np.float64) * (2 * np.pi / 128)
    cc = np.stack([np.cos(p128), -np.sin(p128)], 1).astype(np.float32)
    return C, S, Cc_, Sc_, np.ascontiguousarray(cc)


def build_G():
    nc = bass.Bass("TRN2", target_bir_lowering=False)
    xT = nc.dram_tensor("xT", [D, TOK], F32, kind="ExternalInput").ap()
    fTd = nc.dram_tensor("fT", [512, TOK], F32, kind="ExternalInput").ap()
    yfd = nc.dram_tensor("yf", [1536, TOK], F32, kind="ExternalInput").ap()
    ybd = nc.dram_tensor("yb", [1536, TOK], F32, kind="ExternalInput").ap()
    zd = nc.dram_tensor("zT", [1536, TOK], F32, kind="ExternalInput").ap()
    gnd = nc.dram_tensor("gn", [128, 12], F32, kind="ExternalInput").ap()
    gtd = nc.dram_tensor("gt", [128, 16, 2], F32, kind="ExternalInput").ap()
    wd = nc.dram_tensor("w", [D, D], BF16, kind="ExternalInput").ap()
    xo = nc.dram_tensor("xo", [D, TOK], F32, kind="ExternalOutput").ap()
    with contextlib.ExitStack() as st:
        sb = lambda n, s, d: st.enter_context(nc.sbuf_tensor(n, s, d))
        catT = sb("catT", [128, 16, TOK], BF16)
        gn = sb("gn_s", [128, 12], F32)
        gt = sb("gt_s", [128, 16, 2], F32)
        epsc = sb("epsc", [128, 1], F32)
        ones = sb("ones", [128, 128], BF16)
        fst = [sb(f"fst{i}", [128, TOK], F32) for i in range(2)]
        yfs = [sb(f"yfs{i}", [128, TOK], F32) for i in range(2)]
        ybs = [sb(f"ybs{i}", [128, TOK], F32) for i in range(2)]
        zs = [sb(f"zs{i}", [128, TOK], F32) for i in range(2)]
        uu = sb("uu", [128, 3, TOK], F32)
        sq = [sb(f"sq{i}", [128, TOK], BF16) for i in range(2)]
        rstd = sb("rstd", [128, TOK], F32)
        tmp = [sb(f"tmp{i}", [128, TOK], F32) for i in range(2)]
        wb = [sb(f"wb{i}", [128, 16, 512], BF16) for i in range(2)]
        xc = [sb(f"xc{i}", [128, TOK], F32) for i in range(3)]
        ps_ms = st.enter_context(nc.psum_tensor("ps_ms", [128, 3, 512], F32))
        ps_mm = st.enter_context(nc.psum_tensor("ps_mm", [128, 5, 512], F32))
        P = TP(nc)
        R = {}

        def res(name):
            if name not in R:
                R[name] = Res()
            return R[name]

        P.xdma("sync", gn[:], gnd, "ld_gn", writes=[res("gn")])
        P.xdma("sync", gt[:], gtd, "ld_gt", writes=[res("gt")])
        P.op("vector", lambda e: e.memset(ones[:], 1.0), [], [res("ones")])
        P.op("vector", lambda e: e.memset(epsc[:], EPS), [], [res("epsc")])
        for c in range(4):
            b = c % 2
            P.xdma("sync", fst[b][:], fTd[c * 128:(c + 1) * 128, :], f"ld_f{b}", writes=[res(f"fst{b}")])
            P.op("vector" if c % 2 == 0 else "scalar",
                 (lambda e, c=c, b=b: e.tensor_copy(out=catT[:, c, :], in_=fst[b][:])) if c % 2 == 0 else
                 (lambda e, c=c, b=b: e.copy(out=catT[:, c, :], in_=fst[b][:])),
                 [res(f"fst{b}")], [res("catT")])
        n = 0
        for grp in range(4):
            for c in range(3):
                ch = grp * 3 + c
                b = n % 2
                n += 1
                rows = slice(ch * 128, (ch + 1) * 128)
                P.xdma("sync", yfs[b][:], yfd[rows, :], f"ld_yf{b}", writes=[res(f"yfs{b}")])
                P.xdma("gpsimd", ybs[b][:], ybd[rows, :], f"ld_yb{b}", writes=[res(f"ybs{b}")])
                P.xdma("sync" if ch % 2 == 0 else "gpsimd", zs[b][:], zd[rows, :], f"ld_z{b}", writes=[res(f"zs{b}")])
                P.op("vector", lambda e, b=b: e.tensor_tensor(out=yfs[b][:], in0=yfs[b][:], in1=ybs[b][:], op=ALU.add),
                     [res(f"yfs{b}"), res(f"ybs{b}")], [res(f"yfs{b}")])
                P.op("scalar", lambda e, b=b: e.activation(out=zs[b][:], in_=zs[b][:], func=AF.Silu), [res(f"zs{b}")], [res(f"zs{b}")])
                P.op("vector", lambda e, b=b, c=c: e.tensor_tensor(out=uu[:, c, :], in0=yfs[b][:], in1=zs[b][:], op=ALU.mult),
                     [res(f"yfs{b}"), res(f"zs{b}")], [res(f"uu{c}")])
                P.op("scalar", lambda e, b=b, c=c: e.activation(out=sq[b][:], in_=uu[:, c, :], func=AF.Square),
                     [res(f"uu{c}")], [res(f"sq{b}")])
                P.group([lambda e, b=b, c=c, bi=bi, s0=s0, w=w: e.matmul(ps_ms[:, bi, :w], lhsT=ones[:], rhs=sq[b][:, s0:s0 + w],
                                                                        start=(c == 0), stop=(c == 2))
                         for bi, (s0, w) in enumerate(BLKS)], [res("ones"), res(f"sq{b}")], [res("ps_ms")])
            for bi, (s0, w) in enumerate(BLKS):
                P.op("scalar", lambda e, bi=bi, s0=s0, w=w: e.activation(out=rstd[:, s0:s0 + w], in_=ps_ms[:, bi, :w], func=AF.Sqrt,
                                                                         bias=epsc[:, 0:1], scale=1.0 / 384.0),
                     [res("ps_ms"), res("epsc")], [res("rstd")])
            P.op("vector", lambda e: e.reciprocal(out=rstd[:], in_=rstd[:]), [res("rstd")], [res("rstd")])
            for c in range(3):
                ch = grp * 3 + c
                b = c % 2
                P.op("vector", lambda e, b=b, c=c: e.tensor_tensor(out=tmp[b][:], in0=uu[:, c, :], in1=rstd[:], op=ALU.mult),
                     [res(f"uu{c}"), res("rstd")], [res(f"tmp{b}")])
                P.op("scalar", lambda e, b=b, ch=ch: e.activation(out=catT[:, 4 + ch, :], in_=tmp[b][:], func=AF.Copy, scale=gn[:, ch:ch + 1]),
                     [res(f"tmp{b}"), res("gn")], [res("catT")])
        outs = []
        for nb in range(4):
            b = nb % 2
            P.xdma("sync", wb[b][:], wd[:, nb * 512:(nb + 1) * 512].rearrange("(kc p) n -> p kc n", p=128), f"ld_w{b}", writes=[res(f"wb{b}")])
            for j in range(4):
                nch = nb * 4 + j
                xb = nch % 3
                P.xdma("gpsimd", xc[xb][:], xT[nch * 128:(nch + 1) * 128, :], f"ld_x{xb}", writes=[res(f"xc{xb}")])
                for bi, (s0, w) in enumerate(BLKS):
                    pb = (nch * 3 + bi) % 5
                    P.group([lambda e, b=b, j=j, k=k, s0=s0, w=w, pb=pb: e.matmul(
                        ps_mm[:, pb, :w], lhsT=wb[b][:, k, j * 128:(j + 1) * 128], rhs=catT[:, k, s0:s0 + w],
                        start=(k == 0), stop=(k == 15)) for k in range(16)],
                        [res(f"wb{b}"), res("catT")], [res(f"ps_mm{pb}")])
                    P.op("vector", lambda e, xb=xb, nch=nch, bi=bi, s0=s0, w=w, pb=pb: e.scalar_tensor_tensor(
                        out=xc[xb][:, s0:s0 + w], in0=ps_mm[:, pb, :w], scalar=gt[:, nch, (1 if bi == 2 else 0):(2 if bi == 2 else 1)],
                        in1=xc[xb][:, s0:s0 + w], op0=ALU.mult, op1=ALU.add),
                        [res(f"ps_mm{pb}"), res("gt"), res(f"xc{xb}")], [res(f"xc{xb}")])
                outs.append(P.xdma("scalar", xo[nch * 128:(nch + 1) * 128, :], xc[xb][:], f"st_x{xb}", reads=[res(f"xc{xb}")]))
        P.finish(outs[-3:])
    return nc


def build_M():
    nc = bass.Bass("TRN2", target_bir_lowering=False)
    xT = nc.dram_tensor("xT", [D, TOK], F32, kind="ExternalInput").ap()
    scd = nc.dram_tensor("sc", [128, 16, 2], F32, kind="ExternalInput").ap()
    shd = nc.dram_tensor("sh", [128, 16, 2], F32, kind="ExternalInput").ap()
    gtd = nc.dram_tensor("gt", [128, 16, 2], F32, kind="ExternalInput").ap()
    gd = nc.dram_tensor("g", [128, 16], F32, kind="ExternalInput").ap()
    w1d = nc.dram_tensor("w1", [D, DFF], BF16, kind="ExternalInput").ap()
    w2d = nc.dram_tensor("w2", [DFF, D], BF16, kind="ExternalInput").ap()
    xo = nc.dram_tensor("xo", [D, TOK], F32, kind="ExternalOutput").ap()
    with contextlib.ExitStack() as st:
        sb = lambda n, s, d: st.enter_context(nc.sbuf_tensor(n, s, d))
        X = sb("X", [128, 16, TOK], F32)
        hT = sb("hT", [128, 16, TOK], BF16)
        m1 = sb("m1", [128, 8, TOK], BF16)
        sc = sb("sc_s", [128, 16, 2], F32)
        sh = sb("sh_s", [128, 16, 2], F32)
        gt = sb("gt_s", [128, 16, 2], F32)
        g = sb("g_s", [128, 16], F32)
        s1 = sb("s1", [128, 16, 3], F32)
        ones = sb("ones", [128, 128], BF16)
        sq = [sb(f"sq{i}", [128, TOK], BF16) for i in range(2)]
        rstd = sb("rstd", [128, TOK], F32)
        tmp = [sb(f"tmp{i}", [128, TOK], F32) for i in range(2)]
        w1b = [sb(f"w1b{i}", [128, 16, 512], BF16) for i in range(2)]
        w2b = [sb(f"w2b{i}", [128, 8, 512], BF16) for i in range(2)]
        rl = [sb(f"rl{i}", [128, 512], F32) for i in range(2)]
        ps_ms = st.enter_context(nc.psum_tensor("ps_ms", [128, 3, 512], F32))
        ps_mm = st.enter_context(nc.psum_tensor("ps_mm", [128, 5, 512], F32))
        P = TP(nc)
        R = {}

        def res(name):
            if name not in R:
                R[name] = Res()
            return R[name]

        t_ones = P.dve(lambda e: e.memset(ones[:], 1.0 / D))
        xr = []
        xv = xT.rearrange("(c p) t -> p c t", p=128)
        for c4 in range(4):
            t = P.dma(("sync", "scalar", "gpsimd", "sync")[c4], X[:, c4 * 4:(c4 + 1) * 4, :], xv[:, c4 * 4:(c4 + 1) * 4, :], f"ld_x{c4}")
            xr += [t] * 4
        t1 = P.dma("gpsimd", sc[:], scd, "ld_sc")
        t2 = P.dma("gpsimd", sh[:], shd, "ld_sh")
        t3 = P.dma("gpsimd", g[:], gd, "ld_g")
        t4 = P.dma("gpsimd", gt[:], gtd, "ld_gt")
        h_ready = emit_norm_mod(P, nc, X, hT, sc, sh, g[:], ones, ps_ms, sq, rstd, tmp, s1, xr, [t1, t2, t3, t_ones])
        rh = res("hT")
        rh.w = h_ready[-1]
        rX = [res(f"X{c}") for c in range(16)]
        for c in range(16):
            rX[c].w = xr[c]
            rX[c].r = {h_ready[-1][0]: h_ready[-1][1], "s_dve": P.cnt["s_dve"]}
        res("gt").w = t4
        npb = 0
        for q in range(8):
            for b2 in range(2):
                wi = (q * 2 + b2) % 2
                c0 = q * 1024 + b2 * 512
                P.xdma("sync", w1b[wi][:], w1d[:, c0:c0 + 512].rearrange("(kc p) n -> p kc n", p=128), f"ld_w1{wi}", writes=[res(f"w1b{wi}")])
                for j in range(4):
                    f = b2 * 4 + j
                    for bi, (s0, w) in enumerate(BLKS):
                        pb = npb % 5
                        npb += 1
                        P.group([lambda e, wi=wi, j=j, k=k, s0=s0, w=w, pb=pb: e.matmul(
                            ps_mm[:, pb, :w], lhsT=w1b[wi][:, k, j * 128:(j + 1) * 128], rhs=hT[:, k, s0:s0 + w],
                            start=(k == 0), stop=(k == 15)) for k in range(16)],
                            [res(f"w1b{wi}"), rh], [res(f"ps_mm{pb}")])
                        rb = npb % 2
                        if npb % 2 == 0:
                            P.op("scalar", lambda e, rb=rb, pb=pb, w=w: e.activation(out=rl[rb][:, :w], in_=ps_mm[:, pb, :w], func=AF.Relu),
                                 [res(f"ps_mm{pb}")], [res(f"rl{rb}")])
                            P.op("vector", lambda e, rb=rb, f=f, s0=s0, w=w: e.tensor_tensor(out=m1[:, f, s0:s0 + w], in0=rl[rb][:, :w], in1=rl[rb][:, :w], op=ALU.mult),
                                 [res(f"rl{rb}")], [res(f"m1_{f}")])
                        else:
                            P.op("vector", lambda e, rb=rb, pb=pb, w=w: e.tensor_scalar_max(out=rl[rb][:, :w], in0=ps_mm[:, pb, :w], scalar1=0.0),
                                 [res(f"ps_mm{pb}")], [res(f"rl{rb}")])
                            P.op("scalar", lambda e, rb=rb, f=f, s0=s0, w=w: e.activation(out=m1[:, f, s0:s0 + w], in_=rl[rb][:, :w], func=AF.Square),
                                 [res(f"rl{rb}")], [res(f"m1_{f}")])
            for nb in range(4):
                wi = (q * 4 + nb) % 2
                P.xdma("gpsimd", w2b[wi][:], w2d[q * 1024:(q + 1) * 1024, nb * 512:(nb + 1) * 512].rearrange("(fc p) n -> p fc n", p=128),
                       f"ld_w2{wi}", writes=[res(f"w2b{wi}")])
                for j in range(4):
                    nch = nb * 4 + j
                    for bi, (s0, w) in enumerate(BLKS):
                        pb = npb % 5
                        npb += 1
                        P.group([lambda e, wi=wi, j=j, f=f, s0=s0, w=w, pb=pb: e.matmul(
                            ps_mm[:, pb, :w], lhsT=w2b[wi][:, f, j * 128:(j + 1) * 128], rhs=m1[:, f, s0:s0 + w],
                            start=(f == 0), stop=(f == 7)) for f in range(8)],
                            [res(f"w2b{wi}")] + [res(f"m1_{ff}") for ff in range(8)], [res(f"ps_mm{pb}")])
                        P.op("vector", lambda e, nch=nch, bi=bi, s0=s0, w=w, pb=pb: e.scalar_tensor_tensor(
                            out=X[:, nch, s0:s0 + w], in0=ps_mm[:, pb, :w], scalar=gt[:, nch, (1 if bi == 2 else 0):(2 if bi == 2 else 1)],
                            in1=X[:, nch, s0:s0 + w], op0=ALU.mult, op1=ALU.add),
                            [res(f"ps_mm{pb}"), res("gt"), rX[nch]], [rX[nch]])
        outs = []
        xov = xo.rearrange("(c p) t -> p c t", p=128)
        for c4 in range(4):
            outs.append(P.xdma("sync" if c4 % 2 == 0 else "gpsimd", xov[:, c4 * 4:(c4 + 1) * 4, :], X[:, c4 * 4:(c4 + 1) * 4, :], f"st_x{c4}",
                               reads=[rX[c] for c in range(c4 * 4, c4 * 4 + 4)]))
        P.finish(outs)
    return nc


def build_N():
    nc = bass.Bass("TRN2", target_bir_lowering=False)
    xT = nc.dram_tensor("xT", [D, TOK], F32, kind="ExternalInput").ap()
    gd = nc.dram_tensor("g", [128, 16], F32, kind="ExternalInput").ap()
    xo = nc.dram_tensor("xo", [D, TOK], F32, kind="ExternalOutput").ap()
    with contextlib.ExitStack() as st:
        sb = lambda n, s, d: st.enter_context(nc.sbuf_tensor(n, s, d))
        X = sb("X", [128, 16, TOK], F32)
        hT = sb("hT", [128, 16, TOK], F32)
        zz = sb("zz", [128, 16, 2], F32)
        g = sb("g_s", [128, 16], F32)
        s1 = sb("s1", [128, 16, 3], F32)
        ones = sb("ones", [128, 128], BF16)
        sq = [sb(f"sq{i}", [128, TOK], BF16) for i in range(2)]
        rstd = sb("rstd", [128, TOK], F32)
        tmp = [sb(f"tmp{i}", [128, TOK], F32) for i in range(2)]
        ps_ms = st.enter_context(nc.psum_tensor("ps_ms", [128, 3, 512], F32))
        P = Prog(nc)
        t_ones = P.dve(lambda e: e.memset(ones[:], 1.0 / D))
        t_z = P.dve(lambda e: e.memset(zz[:], 0.0))
        xr = []
        xv = xT.rearrange("(c p) t -> p c t", p=128)
        for c4 in range(4):
            t = P.dma(("sync", "scalar", "gpsimd", "sync")[c4], X[:, c4 * 4:(c4 + 1) * 4, :], xv[:, c4 * 4:(c4 + 1) * 4, :], f"ld_x{c4}")
            xr += [t] * 4
        t3 = P.dma("gpsimd", g[:], gd, "ld_g")
        h_ready = emit_norm_mod(P, nc, X, hT, zz, zz, g[:], ones, ps_ms, sq, rstd, tmp, s1, xr, [t3, t_ones, t_z])
        outs = []
        xov = xo.rearrange("(c p) t -> p c t", p=128)
        for c4 in range(4):
            outs.append(P.dma("sync", xov[:, c4 * 4:(c4 + 1) * 4, :], hT[:, c4 * 4:(c4 + 1) * 4, :], f"st_x{c4}", [h_ready[c4 * 4 + 3]]))
        P.finish(outs)
    return nc


_CACHE = {}


def _prog(name, builder):
    if name not in _CACHE:
        _CACHE[name] = builder()
    return _CACHE[name]


def _mod2(mods, l, k):
    return np.ascontiguousarray(np.stack([fm(mods[l, 0, k * D:(k + 1) * D]), fm(mods[l, 1, k * D:(k + 1) * D])], -1))


def kernel(x, c, ctx, c_ctx, w_ada, b_ada, g_mix, w_in, conv_w, conv_b, dt_bias, a_log, d_skip,
           g_ssd_norm, w_fourier, w_out, g_mlp, w_mlp1, w_mlp2, g_final, _nlayers=DEPTH, _debug=None):
    f32 = lambda a: np.ascontiguousarray(np.asarray(a, dtype=np.float32))
    x, c, ctx, c_ctx = f32(x), f32(c), f32(ctx), f32(c_ctx)
    w_ada, b_ada, g_mix, w_in = f32(w_ada), f32(b_ada), f32(g_mix), f32(w_in)
    conv_w, conv_b, dt_bias, a_log, d_skip = f32(conv_w), f32(conv_b), f32(dt_bias), f32(a_log), f32(d_skip)
    g_ssd_norm, w_fourier, w_out, g_mlp = f32(g_ssd_norm), f32(w_fourier), f32(w_out), f32(g_mlp)
    w_mlp1, w_mlp2, g_final = f32(w_mlp1), f32(w_mlp2), f32(g_final)

    ws = [w_in, w_out, w_mlp1, w_mlp2]
    flat = np.concatenate([w.reshape(-1) for w in ws])
    fb = run_cast(flat)
    wb = []
    o = 0
    for w in ws:
        wb.append(fb[o:o + w.size].reshape(w.shape))
        o += w.size
    w_in_b, w_out_b, w1_b, w2_b = wb
    del flat, fb

    mods = run_mods(c, c_ctx, w_ada, b_ada)
    C, S, Cc_, Sc_, cc = fourier_tables()
    tabs = [(np.ascontiguousarray(C[:, 1024 * i:1024 * (i + 1)]), np.ascontiguousarray(S[:, 1024 * i:1024 * (i + 1)]),
             np.ascontiguousarray(Cc_[:, 32 * i:32 * (i + 1)]), np.ascontiguousarray(Sc_[:, 32 * i:32 * (i + 1)])) for i in range(NCORES)]
    del C, S

    xl, xc = x[0], ctx[0]
    xT = [np.ascontiguousarray(np.concatenate([xl[1024 * i:1024 * (i + 1)], xc[32 * i:32 * (i + 1)]], 0).T) for i in range(NCORES)]

    def to_global(per_core):
        R_ = per_core[0].shape[0]
        G = np.empty((R_, NTOK), np.float32)
        for i in range(NCORES):
            G[:, CTX + 1024 * i:CTX + 1024 * (i + 1)] = per_core[i][:, :1024]
            G[:, 32 * i:32 * (i + 1)] = per_core[i][:, 1024:]
        return G

    def to_core(G, i):
        return np.ascontiguousarray(np.concatenate([G[:, CTX + 1024 * i:CTX + 1024 * (i + 1)], G[:, 32 * i:32 * (i + 1)]], 1))

    for l in range(_nlayers):
        ncB = _prog("B", build_B)
        sc1, sh1, gt1 = _mod2(mods, l, 1), _mod2(mods, l, 0), _mod2(mods, l, 2)
        sh2, sc2, gt2 = _mod2(mods, l, 3), _mod2(mods, l, 4), _mod2(mods, l, 5)
        res = _run(ncB, [{"xT": xT[i], "sc": sc1, "sh": sh1, "g": fm(g_mix[l]), "w": w_in_b[l]} for i in range(NCORES)])
        pT = [r["pT"] for r in res]
        PT = to_global(pT)
        ncC = _prog("C", build_ssd)
        res = _run(ncC, ssd_inputs(PT, conv_w[l], conv_b[l], dt_bias[l], a_log[l], d_skip[l]))
        YF = np.concatenate([r["Y"][0].T for r in res], 0)
        YB = np.concatenate([r["Y"][1].T for r in res], 0)
        ncF = _prog("F", build_fourier)
        uL = np.ascontiguousarray(PT[0:512, CTX:].T)
        uC = np.ascontiguousarray(PT[0:512, :CTX].T)
        res = _run(ncF, [{"uL": uL, "uC": uC, "tabC": tabs[i][0], "tabS": tabs[i][1], "tcC": tabs[i][2], "tcS": tabs[i][3],
                          "wf": w_fourier[l], "cc": cc} for i in range(NCORES)])
        fT = [r["fT"] for r in res]
        ncG = _prog("G", build_G)
        res = _run(ncG, [{"xT": xT[i], "fT": fT[i], "yf": to_core(YF, i), "yb": to_core(YB, i),
                          "zT": np.ascontiguousarray(pT[i][512:2048]), "gn": fm(g_ssd_norm[l]), "gt": gt1, "w": w_out_b[l]}
                         for i in range(NCORES)])
        xm = [r["xo"] for r in res]
        if _debug is not None:
            _debug[f"xm{l}"] = xm
        ncM = _prog("M", build_M)
        res = _run(ncM, [{"xT": xm[i], "sc": sc2, "sh": sh2, "gt": gt2, "g": fm(g_mlp[l]), "w1": w1_b[l], "w2": w2_b[l]}
                         for i in range(NCORES)])
        xT = [r["xo"] for r in res]
        if _debug is not None:
            _debug[f"xo{l}"] = xT
    ncN = _prog("N", build_N)
    res = _run(ncN, [{"xT": xT[i], "g": fm(g_final)} for i in range(NCORES)])
    out = np.empty((1, SEQ, D), np.float32)
    for i in range(NCORES):
        out[0, 1024 * i:1024 * (i + 1), :] = res[i]["xo"][:, :1024].T
    return out
```
